# Optimizing a Trainium2 kernel written in Bass

```python
import math
import jax, jax.numpy as jnp
from jax import lax
import numpy as np

D_MODEL = 1024
BATCH = 2
SEQ = 16384
DEPTH = 4
DEC_BATCH = 16
DEC_SEQ = 16
PAST_LEN = 2048

CHUNK = 64
Q_BLOCK = 128
N_AB = (DEPTH + 1) // 2
N_CD = DEPTH // 2
HALF = D_MODEL // 2
HG_HEADS = 4
HG_DK = HALF // HG_HEADS
HG_DV = HALF // HG_HEADS
LRU_WIDTH = HALF
LRU_BLOCKS = 8
LRU_BLOCK = LRU_WIDTH // LRU_BLOCKS
LRU_CONV = 4
LRU_C = 8.0
GD_HEADS = 4
GD_DK = HALF // GD_HEADS
GD_DV = HALF // GD_HEADS
GD_CONV = 4
MLA_HEADS = 4
MLA_NOPE = 128
MLA_ROPE = 64
MLA_V = HALF // MLA_HEADS
MLA_Q_RANK = 384
MLA_KV_RANK = 256
MLA_SCALE = (MLA_NOPE + MLA_ROPE) ** -0.5
ROPE_THETA = 10000.0
D_FF = 2816
FFN_CONV = 3
EPS = 1e-6
NEG_BIG = -1e30
SQRT_FLOOR = 1e-12
AB_IN = 6 * HALF
CD_SPLITS = (3 * HALF, 4 * HALF, 4 * HALF + GD_HEADS, 4 * HALF + 2 * GD_HEADS,
             4 * HALF + 2 * GD_HEADS + MLA_Q_RANK, 4 * HALF + 2 * GD_HEADS + MLA_Q_RANK + MLA_KV_RANK)
CD_IN = CD_SPLITS[-1] + MLA_ROPE
F32 = jnp.float32

kernel_name = 'hybrid_streaming_encoder_step'


def rmsnorm(x, g):
    xf = x.astype(F32)
    y = xf * lax.rsqrt(jnp.mean(xf * xf, axis=-1, keepdims=True) + EPS)
    return (y * g.astype(F32)).astype(x.dtype)


def l2norm(x):
    return x * lax.rsqrt(jnp.sum(x * x, axis=-1, keepdims=True) + EPS)


def rope(x, pos):
    half = x.shape[-1] // 2
    freqs = jnp.exp(-math.log(ROPE_THETA) * jnp.arange(half, dtype=F32) / half)
    ang = pos.astype(F32)[:, None] * freqs
    shape = (ang.shape[0],) + (1,) * (x.ndim - 3) + (half,)
    cos = jnp.cos(ang).reshape(shape)
    sin = jnp.sin(ang).reshape(shape)
    xf = x.astype(F32)
    x1, x2 = xf[..., :half], xf[..., half:]
    return jnp.concatenate([x1 * cos - x2 * sin, x2 * cos + x1 * sin], axis=-1).astype(x.dtype)


def causal_dwconv(x, buf, w, b=None):
    width = w.shape[0]
    T = x.shape[1]
    xp = jnp.concatenate([buf.astype(x.dtype), x], axis=1)
    y = xp[:, 0:T] * w[0]
    for tap in range(1, width):
        y = y + xp[:, tap:tap + T] * w[tap]
    if b is not None:
        y = y + b
    return y, xp[:, T:]


def ada_mod(c, w, b):
    m = (jax.nn.silu(c) @ w + b)[:, None, :]
    return jnp.split(m, 6, axis=-1)


def hgrn2_chunked(q, k, v, log_f, s0):
    B, T, H, DK = q.shape
    DV = v.shape[-1]
    L = CHUNK if T % CHUNK == 0 else T
    n = T // L
    qc, kc, vc, gc = (a.reshape(B, n, L, H, a.shape[-1]).transpose(1, 0, 3, 2, 4) for a in (q, k, v, log_f))
    incl = jnp.tril(jnp.ones((L, L), bool))[:, :, None]

    def step(S, inp):
        q_, k_, v_, g_ = inp
        b = jnp.cumsum(g_, axis=2)
        diff = b[:, :, :, None, :] - b[:, :, None, :, :]
        decay = jnp.exp(jnp.where(incl, diff, NEG_BIG))
        att = jnp.einsum('bhtsd,bhtd,bhsd->bhts', decay, q_, k_)
        o = jnp.einsum('bhts,bhsv->bhtv', att, v_) + jnp.einsum('bhtd,bhdv->bhtv', q_ * jnp.exp(b), S)
        b_last = b[:, :, -1:, :]
        S = jnp.exp(b_last[:, :, 0, :, None]) * S + jnp.einsum('bhsd,bhsv->bhdv', k_ * jnp.exp(b_last - b), v_)
        return S, o

    S, oc = lax.scan(step, s0, (qc, kc, vc, gc))
    return oc.transpose(1, 0, 3, 2, 4).reshape(B, T, H, DV), S


def gated_delta_chunked(q, k, v, log_alpha, beta, s0):
    B, T, H, DK = q.shape
    DV = v.shape[-1]
    L = CHUNK if T % CHUNK == 0 else T
    n = T // L
    qc, kc, vc = (a.reshape(B, n, L, H, a.shape[-1]).transpose(1, 0, 3, 2, 4) for a in (q, k, v))
    gc, bc = (a.reshape(B, n, L, H).transpose(1, 0, 3, 2) for a in (log_alpha, beta))
    g = jnp.cumsum(gc, axis=-1)
    diff = g[..., :, None] - g[..., None, :]
    incl = jnp.tril(jnp.ones((L, L), bool))
    strict = jnp.tril(jnp.ones((L, L), bool), -1)
    dec_incl = jnp.exp(jnp.where(incl, diff, NEG_BIG))
    dec_strict = jnp.where(strict, dec_incl, 0.0)
    m = bc[..., :, None] * jnp.einsum('nbhtd,nbhsd->nbhts', kc, kc) * dec_strict
    eye = jnp.eye(L, dtype=F32)
    rhs = jnp.concatenate([bc[..., None] * vc, (bc * jnp.exp(g))[..., None] * kc], axis=-1)
    sol = lax.linalg.triangular_solve(eye + m, rhs, left_side=True, lower=True, unit_diagonal=True)
    u_v, w_k = sol[..., :DV], sol[..., DV:]
    qk = jnp.einsum('nbhtd,nbhsd->nbhts', qc, kc) * dec_incl

    def step(S, inp):
        u_v_, w_k_, qk_, q_, k_, g_ = inp
        u = u_v_ - jnp.einsum('bhtd,bhdv->bhtv', w_k_, S)
        o = jnp.einsum('bhts,bhsv->bhtv', qk_, u) + jnp.exp(g_)[..., None] * jnp.einsum('bhtd,bhdv->bhtv', q_, S)
        g_last = g_[..., -1:]
        S = jnp.exp(g_last)[..., None] * S + jnp.einsum('bhsd,bhsv->bhdv', k_ * jnp.exp(g_last - g_)[..., None], u)
        return S, o

    S, oc = lax.scan(step, s0, (u_v, w_k, qk, qc, kc, g))
    return oc.transpose(1, 0, 3, 2, 4).reshape(B, T, H, DV), S


def rglru(xc, h0, w_a, b_a, w_x, b_x, lam):
    B, T, W = xc.shape
    xf = xc.astype(F32)
    xb = xf.reshape(B, T, LRU_BLOCKS, LRU_BLOCK)
    r = jax.nn.sigmoid(jnp.einsum('btki,kij->btkj', xb, w_a.astype(F32)).reshape(B, T, W) + b_a.astype(F32))
    i = jax.nn.sigmoid(jnp.einsum('btki,kij->btkj', xb, w_x.astype(F32)).reshape(B, T, W) + b_x.astype(F32))
    log_a = -LRU_C * r * jax.nn.softplus(-lam.astype(F32))
    a = jnp.exp(log_a)
    u = jnp.sqrt(jnp.maximum(-jnp.expm1(2.0 * log_a), SQRT_FLOOR)) * i * xf

    def combine(left, right):
        a_l, u_l = left
        a_r, u_r = right
        return a_l * a_r, a_r * u_l + u_r

    a_cum, h = lax.associative_scan(combine, (a, u), axis=1)
    h = h + a_cum * h0.astype(F32)[:, None, :]
    return h, h[:, -1]


def mla_attend(q_nope, q_rope, k_nope, k_rope, v, q_pos, k_pos):
    B, T, H, _ = q_nope.shape
    blk = Q_BLOCK if T % Q_BLOCK == 0 else T
    k_chunk = k_pos // CHUNK

    def one_block(i):
        start = i * blk
        qn = lax.dynamic_slice_in_dim(q_nope, start, blk, axis=1)
        qr = lax.dynamic_slice_in_dim(q_rope, start, blk, axis=1)
        q_chunk = lax.dynamic_slice_in_dim(q_pos, start, blk, axis=0) // CHUNK
        s = (jnp.einsum('bqhd,bkhd->bhqk', qn, k_nope) + jnp.einsum('bqhd,bkd->bhqk', qr, k_rope)).astype(F32) * MLA_SCALE
        s = jnp.where(k_chunk[None, :] <= q_chunk[:, None], s, NEG_BIG)
        p = jax.nn.softmax(s, axis=-1).astype(v.dtype)
        return jnp.einsum('bhqk,bkhv->bqhv', p, v)

    out = lax.map(one_block, jnp.arange(T // blk))
    return out.transpose(1, 0, 2, 3, 4).reshape(B, T, H * v.shape[-1])


def ab_mixer(h, hg_s0, lru_h0, lru_buf0, w_in, w_out, lower_bound, hg_norm_g,
             conv_w, conv_b, w_a, b_a, w_x, b_x, lam):
    B, T, _ = h.shape
    hq, hf, hi, hz, lx, ly = jnp.split(h @ w_in, 6, axis=-1)
    lb = lower_bound.astype(F32).reshape(HG_HEADS, HG_DK)
    z = hf.astype(F32).reshape(B, T, HG_HEADS, HG_DK)
    log_f = jnp.log(lb + (1.0 - lb) * jax.nn.sigmoid(z))
    k = (1.0 - lb) * jax.nn.sigmoid(-z)
    q = jax.nn.silu(hq.astype(F32)).reshape(B, T, HG_HEADS, HG_DK)
    v = hi.astype(F32).reshape(B, T, HG_HEADS, HG_DV)
    o, hg_s = hgrn2_chunked(q, k, v, log_f, hg_s0.astype(F32))
    o = rmsnorm(o, hg_norm_g) * jax.nn.silu(hz.astype(F32).reshape(B, T, HG_HEADS, HG_DV))
    o_a = o.reshape(B, T, HALF).astype(h.dtype)
    xc, lru_buf = causal_dwconv(lx, lru_buf0, conv_w, conv_b)
    hr, lru_h = rglru(xc, lru_h0, w_a, b_a, w_x, b_x, lam)
    o_b = hr.astype(h.dtype) * jax.nn.gelu(ly)
    y = jnp.concatenate([o_a, o_b], axis=-1) @ w_out
    return y, hg_s, lru_h, lru_buf


def cd_mixer(h, gd_s0, gd_buf0, lat_past, kr_past, w_in, w_out, gd_conv_w, a_log, dt_bias, gd_norm_g,
             q_norm_g, w_qb, kv_norm_g, w_kvb):
    B, T, _ = h.shape
    past_len = lat_past.shape[1]
    qkv, gz, gb, ga, qa, kva, kr = jnp.split(h @ w_in, CD_SPLITS, axis=-1)
    qkv, gd_buf = causal_dwconv(qkv, gd_buf0, gd_conv_w)
    qkv = jax.nn.silu(qkv.astype(F32))
    q, k, v = jnp.split(qkv, 3, axis=-1)
    q = l2norm(q.reshape(B, T, GD_HEADS, GD_DK)) * GD_DK ** -0.5
    k = l2norm(k.reshape(B, T, GD_HEADS, GD_DK))
    v = v.reshape(B, T, GD_HEADS, GD_DV)
    beta = jax.nn.sigmoid(gb.astype(F32))
    log_alpha = -jnp.exp(a_log.astype(F32)) * jax.nn.softplus(ga.astype(F32) + dt_bias.astype(F32))
    o, gd_s = gated_delta_chunked(q, k, v, log_alpha, beta, gd_s0.astype(F32))
    o = rmsnorm(o, gd_norm_g) * jax.nn.silu(gz.astype(F32).reshape(B, T, GD_HEADS, GD_DV))
    o_c = o.reshape(B, T, HALF).astype(h.dtype)
    q_pos = past_len + jnp.arange(T, dtype=jnp.int32)
    qh = (rmsnorm(qa, q_norm_g) @ w_qb).reshape(B, T, MLA_HEADS, MLA_NOPE + MLA_ROPE)
    q_nope = qh[..., :MLA_NOPE]
    q_rope = rope(qh[..., MLA_NOPE:], q_pos)
    c_kv = rmsnorm(kva, kv_norm_g)
    k_r = rope(kr, q_pos)
    lat_all = jnp.concatenate([lat_past.astype(h.dtype), c_kv], axis=1)
    kr_all = jnp.concatenate([kr_past.astype(h.dtype), k_r], axis=1)
    kv = (lat_all @ w_kvb).reshape(B, lat_all.shape[1], MLA_HEADS, MLA_NOPE + MLA_V)
    k_pos = jnp.arange(lat_all.shape[1], dtype=jnp.int32)
    o_d = mla_attend(q_nope, q_rope, kv[..., :MLA_NOPE], kr_all, kv[..., MLA_NOPE:], q_pos, k_pos)
    y = jnp.concatenate([o_c, o_d], axis=-1) @ w_out
    return y, gd_s, gd_buf, c_kv, k_r


def conv_ffn(h, buf, w_up, conv_w, w_down):
    u, buf = causal_dwconv(h @ w_up, buf, conv_w)
    gate, val = jnp.split(u, 2, axis=-1)
    return (jax.nn.silu(gate) * val) @ w_down, buf


def run_group(x, c, hg_s, lru_h, lru_buf, gd_s, gd_buf, lat_past, kr_past, ffn_buf, wts):
    lb_p = jax.nn.softmax(wts['hgrn_lb_logits'].astype(F32), axis=0)
    lower_bounds = jnp.cumsum(lb_p, axis=0) - lb_p[0:1]
    n_hg, n_lru, n_lrub, n_gd, n_gdb, n_lat, n_kr, n_ffn = ([] for _ in range(8))
    for l in range(DEPTH):
        j = l // 2
        shift1, scale1, gate1, shift2, scale2, gate2 = ada_mod(c, wts['ada_w'][l], wts['ada_b'][l])
        g = wts['norm_g'][l]
        h = rmsnorm(x, g[0]) * (1 + scale1) + shift1
        if l % 2 == 0:
            y, s_hg, s_lru, s_lrub = ab_mixer(
                h, hg_s[j], lru_h[j], lru_buf[j], wts['ab_w_in'][j], wts['ab_w_out'][j], lower_bounds[j],
                wts['hgrn_norm_g'][j], wts['lru_conv_w'][j], wts['lru_conv_b'][j], wts['lru_w_a'][j],
                wts['lru_b_a'][j], wts['lru_w_x'][j], wts['lru_b_x'][j], wts['lru_lambda'][j])
            n_hg.append(s_hg)
            n_lru.append(s_lru)
            n_lrub.append(s_lrub)
        else:
            y, s_gd, s_gdb, s_lat, s_kr = cd_mixer(
                h, gd_s[j], gd_buf[j], lat_past[j], kr_past[j], wts['cd_w_in'][j], wts['cd_w_out'][j],
                wts['gdn_conv_w'][j], wts['gdn_a_log'][j], wts['gdn_dt_bias'][j], wts['gdn_norm_g'][j],
                wts['mla_q_norm_g'][j], wts['mla_w_qb'][j], wts['mla_kv_norm_g'][j], wts['mla_w_kvb'][j])
            n_gd.append(s_gd)
            n_gdb.append(s_gdb)
            n_lat.append(s_lat)
            n_kr.append(s_kr)
        x = x + gate1 * rmsnorm(y, g[1])
        h = rmsnorm(x, g[2]) * (1 + scale2) + shift2
        y, s_ffn = conv_ffn(h, ffn_buf[l], wts['ffn_w_up'][l], wts['ffn_conv_w'][l], wts['ffn_w_down'][l])
        n_ffn.append(s_ffn)
        x = x + gate2 * rmsnorm(y, g[3])
    return x, (jnp.stack(n_hg), jnp.stack(n_lru), jnp.stack(n_lrub), jnp.stack(n_gd), jnp.stack(n_gdb),
               jnp.stack(n_lat), jnp.stack(n_kr), jnp.stack(n_ffn))


def setup_inputs(seed: int = 0) -> dict:
    key = jax.random.key(seed)
    it = iter(jax.random.split(key, 48))

    def nrm(shape, scale):
        return jax.random.normal(next(it), shape, F32) * scale

    u = jax.random.uniform(next(it), (N_AB, LRU_WIDTH), F32, 0.9, 0.999)
    a0 = u ** (1.0 / LRU_C)
    dt = jnp.exp(jax.random.uniform(next(it), (N_CD, GD_HEADS), F32, math.log(1e-3), math.log(1e-1)))
    return {
        'x_prompt': nrm((BATCH, SEQ, D_MODEL), 1.0),
        'x_sample': nrm((DEC_BATCH, DEC_SEQ, D_MODEL), 1.0),
        'c_prompt': nrm((BATCH, D_MODEL), 1.0),
        'c_sample': nrm((DEC_BATCH, D_MODEL), 1.0),
        'state_hgrn': nrm((N_AB, DEC_BATCH, HG_HEADS, HG_DK, HG_DV), 0.1),
        'state_rglru': nrm((N_AB, DEC_BATCH, LRU_WIDTH), 0.5),
        'state_rglru_conv': nrm((N_AB, DEC_BATCH, LRU_CONV - 1, LRU_WIDTH), 1.0),
        'state_gdn': nrm((N_CD, DEC_BATCH, GD_HEADS, GD_DK, GD_DV), 0.1),
        'state_gdn_conv': nrm((N_CD, DEC_BATCH, GD_CONV - 1, 3 * HALF), 1.0),
        'cache_mla_latent': nrm((N_CD, DEC_BATCH, PAST_LEN, MLA_KV_RANK), 1.0),
        'cache_mla_krope': nrm((N_CD, DEC_BATCH, PAST_LEN, MLA_ROPE), 1.0),
        'state_ffn_conv': nrm((DEPTH, DEC_BATCH, FFN_CONV - 1, 2 * D_FF), 1.0),
        'ada_w': nrm((DEPTH, D_MODEL, 6 * D_MODEL), 0.5 * D_MODEL ** -0.5),
        'ada_b': nrm((DEPTH, 6 * D_MODEL), 0.02),
        'norm_g': 1.0 + nrm((DEPTH, 4, D_MODEL), 0.05),
        'ab_w_in': nrm((N_AB, D_MODEL, AB_IN), D_MODEL ** -0.5),
        'ab_w_out': nrm((N_AB, 2 * HALF, D_MODEL), (2 * HALF) ** -0.5),
        'hgrn_lb_logits': nrm((N_AB, HALF), 0.5),
        'hgrn_norm_g': 1.0 + nrm((N_AB, HG_DV), 0.05),
        'lru_conv_w': nrm((N_AB, LRU_CONV, LRU_WIDTH), LRU_CONV ** -0.5),
        'lru_conv_b': nrm((N_AB, LRU_WIDTH), 0.02),
        'lru_w_a': nrm((N_AB, LRU_BLOCKS, LRU_BLOCK, LRU_BLOCK), LRU_BLOCK ** -0.5),
        'lru_b_a': nrm((N_AB, LRU_WIDTH), 0.02),
        'lru_w_x': nrm((N_AB, LRU_BLOCKS, LRU_BLOCK, LRU_BLOCK), LRU_BLOCK ** -0.5),
        'lru_b_x': nrm((N_AB, LRU_WIDTH), 0.02),
        'lru_lambda': jnp.log(a0) - jnp.log1p(-a0),
        'cd_w_in': nrm((N_CD, D_MODEL, CD_IN), D_MODEL ** -0.5),
        'cd_w_out': nrm((N_CD, 2 * HALF, D_MODEL), (2 * HALF) ** -0.5),
        'gdn_conv_w': nrm((N_CD, GD_CONV, 3 * HALF), GD_CONV ** -0.5),
        'gdn_a_log': jnp.log(jax.random.uniform(next(it), (N_CD, GD_HEADS), F32, 1.0, 16.0)),
        'gdn_dt_bias': dt + jnp.log(-jnp.expm1(-dt)),
        'gdn_norm_g': 1.0 + nrm((N_CD, GD_DV), 0.05),
        'mla_q_norm_g': 1.0 + nrm((N_CD, MLA_Q_RANK), 0.05),
        'mla_w_qb': nrm((N_CD, MLA_Q_RANK, MLA_HEADS * (MLA_NOPE + MLA_ROPE)), MLA_Q_RANK ** -0.5),
        'mla_kv_norm_g': 1.0 + nrm((N_CD, MLA_KV_RANK), 0.05),
        'mla_w_kvb': nrm((N_CD, MLA_KV_RANK, MLA_HEADS * (MLA_NOPE + MLA_V)), MLA_KV_RANK ** -0.5),
        'ffn_w_up': nrm((DEPTH, D_MODEL, 2 * D_FF), D_MODEL ** -0.5),
        'ffn_conv_w': nrm((DEPTH, FFN_CONV, 2 * D_FF), FFN_CONV ** -0.5),
        'ffn_w_down': nrm((DEPTH, D_FF, D_MODEL), D_FF ** -0.5),
    }


def reference(x_prompt, x_sample, c_prompt, c_sample, state_hgrn, state_rglru, state_rglru_conv, state_gdn,
              state_gdn_conv, cache_mla_latent, cache_mla_krope, state_ffn_conv, ada_w, ada_b, norm_g,
              ab_w_in, ab_w_out, hgrn_lb_logits, hgrn_norm_g, lru_conv_w, lru_conv_b, lru_w_a, lru_b_a,
              lru_w_x, lru_b_x, lru_lambda, cd_w_in, cd_w_out, gdn_conv_w, gdn_a_log, gdn_dt_bias, gdn_norm_g,
              mla_q_norm_g, mla_w_qb, mla_kv_norm_g, mla_w_kvb, ffn_w_up, ffn_conv_w, ffn_w_down):
    wts = dict(ada_w=ada_w, ada_b=ada_b, norm_g=norm_g, ab_w_in=ab_w_in, ab_w_out=ab_w_out,
               hgrn_lb_logits=hgrn_lb_logits, hgrn_norm_g=hgrn_norm_g, lru_conv_w=lru_conv_w,
               lru_conv_b=lru_conv_b, lru_w_a=lru_w_a, lru_b_a=lru_b_a, lru_w_x=lru_w_x, lru_b_x=lru_b_x,
               lru_lambda=lru_lambda, cd_w_in=cd_w_in, cd_w_out=cd_w_out, gdn_conv_w=gdn_conv_w,
               gdn_a_log=gdn_a_log, gdn_dt_bias=gdn_dt_bias, gdn_norm_g=gdn_norm_g, mla_q_norm_g=mla_q_norm_g,
               mla_w_qb=mla_w_qb, mla_kv_norm_g=mla_kv_norm_g, mla_w_kvb=mla_w_kvb, ffn_w_up=ffn_w_up,
               ffn_conv_w=ffn_conv_w, ffn_w_down=ffn_w_down)
    bp = x_prompt.shape[0]
    dt_ = x_prompt.dtype
    y_prompt, p_states = run_group(
        x_prompt, c_prompt,
        jnp.zeros((N_AB, bp, HG_HEADS, HG_DK, HG_DV), F32),
        jnp.zeros((N_AB, bp, LRU_WIDTH), F32),
        jnp.zeros((N_AB, bp, LRU_CONV - 1, LRU_WIDTH), dt_),
        jnp.zeros((N_CD, bp, GD_HEADS, GD_DK, GD_DV), F32),
        jnp.zeros((N_CD, bp, GD_CONV - 1, 3 * HALF), dt_),
        jnp.zeros((N_CD, bp, 0, MLA_KV_RANK), dt_),
        jnp.zeros((N_CD, bp, 0, MLA_ROPE), dt_),
        jnp.zeros((DEPTH, bp, FFN_CONV - 1, 2 * D_FF), dt_),
        wts)
    y_sample, s_states = run_group(
        x_sample, c_sample, state_hgrn, state_rglru, state_rglru_conv, state_gdn, state_gdn_conv,
        cache_mla_latent, cache_mla_krope, state_ffn_conv, wts)
    p_hgrn, p_rglru, p_rglru_conv, p_gdn, p_gdn_conv, p_mla_latent, p_mla_krope, p_ffn_conv = p_states
    s_hgrn, s_rglru, s_rglru_conv, s_gdn, s_gdn_conv, s_mla_latent, s_mla_krope, s_ffn_conv = s_states
    return (y_prompt, y_sample,
            p_hgrn, p_rglru, p_rglru_conv, p_gdn, p_gdn_conv, p_mla_latent, p_mla_krope, p_ffn_conv,
            s_hgrn, s_rglru, s_rglru_conv, s_gdn, s_gdn_conv, s_mla_latent, s_mla_krope, s_ffn_conv)
```

```python
import math
import os
from contextlib import ExitStack
import numpy as np
import concourse.bass as bass
import concourse.mybir as mybir
from concourse.bass_utils import run_bass_kernel_spmd

F32 = mybir.dt.float32
BF16 = mybir.dt.bfloat16
AF = mybir.ActivationFunctionType
ALU = mybir.AluOpType
AX = mybir.AxisListType

D = 1024
DEPTH = 4
HALF = 512
DFF = 2816
EPS = 1e-6
NCORES = 8
PAST = 2048
SAME_ENGINE_SYNC = True


class Ctx:
    def __init__(self, nc, es):
        self.nc = nc
        self.es = es
        self.eng = {'pe': nc.tensor, 'dve': nc.vector, 'act': nc.scalar, 'pool': nc.gpsimd, 'sp': nc.sync}
        self.sem = {}
        self.cnt = {}
        for e in self.eng:
            self.sem[e] = es.enter_context(nc.semaphore("sem_" + e))
            self.cnt[e] = 0
        self.ndma = 20
        self.dslots = {}
        for q in ('sp', 'pool'):
            self.dslots[q] = []
            for i in range(self.ndma):
                k = "d_%s_%d" % (q, i)
                self.sem[k] = es.enter_context(nc.semaphore(k))
                self.cnt[k] = 0
                self.dslots[q].append(k)
        self.dnext = {'sp': 0, 'pool': 0}
        self.waited = {e: {} for e in self.eng}
        self.lastw = {}
        self.readers = {}
        self.nins = 0

    def _wait(self, e, deps):
        best = {}
        for (k, v) in deps:
            if v > best.get(k, 0):
                best[k] = v
        for k, v in best.items():
            if k == e and (e == 'pe' or not SAME_ENGINE_SYNC):
                continue
            if self.waited[e].get(k, 0) >= v:
                continue
            self.eng[e].wait_ge(self.sem[k], v)
            self.waited[e][k] = v

    def _deps(self, r, w):
        deps = []
        for x in r:
            if x in self.lastw:
                deps.append(self.lastw[x])
        for x in w:
            if x in self.lastw:
                deps.append(self.lastw[x])
            rd = self.readers.get(x)
            if rd:
                deps.extend(rd.items())
        return deps

    def _record(self, tok, r, w):
        for x in r:
            rd = self.readers.setdefault(x, {})
            if tok[1] > rd.get(tok[0], 0):
                rd[tok[0]] = tok[1]
        for x in w:
            self.lastw[x] = tok
            self.readers[x] = {}

    def op(self, e, fn, r=(), w=()):
        self._wait(e, self._deps(r, w))
        ins = fn(self.eng[e])
        self.cnt[e] += 1
        ins.then_inc(self.sem[e], 1)
        self._record((e, self.cnt[e]), r, w)
        self.nins += 1

    def dma(self, q, out, in_, r=(), w=(), **kw):
        k = self.dslots[q][self.dnext[q]]
        self.dnext[q] = (self.dnext[q] + 1) % self.ndma
        deps = self._deps(r, w)
        if self.cnt[k] > 0:
            deps.append((k, self.cnt[k]))
        self._wait(q, deps)
        ins = self.eng[q].dma_start(out=out, in_=in_, **kw)
        self.cnt[k] += 16
        ins.then_inc(self.sem[k], 16)
        self._record((k, self.cnt[k]), r, w)
        self.nins += 1

    def barrier(self):
        allk = [(k, v) for k, v in self.cnt.items() if v > 0]
        for e in self.eng:
            self._wait(e, [kv for kv in allk if kv[0] != e])
        self.lastw = {}
        self.readers = {}

    def final_wait(self):
        allk = [(k, v) for k, v in self.cnt.items() if v > 0 and k != 'sp']
        self._wait('sp', allk)


def tiles_of(T, n=512):
    out = []
    t = 0
    while t < T:
        out.append((t, min(n, T - t)))
        t += n
    return out


class Seq:
    pass


_uid = [0]


def SB(nc, name, shape, dt):
    _uid[0] += 1
    return nc.sbuf_tensor("%s_u%d" % (name, _uid[0]), shape, dt)


def build_program(SEQ, DT, NS, depth=DEPTH, dbg=None):
    nc = bass.Bass("TRN2", target_bir_lowering=False)
    es = ExitStack()
    with es:
        cx = Ctx(nc, es)
        _emit(nc, es, cx, SEQ, DT, NS, depth, dbg)
        cx.final_wait()
        print("instructions:", cx.nins)
    return nc


def _emit(nc, es, cx, SEQ, DT, NS, depth, dbg):
    NSEQ = 1 + NS
    seqs = []
    for i in range(NSEQ):
        s = Seq()
        s.i = i
        s.T = SEQ if i == 0 else DT
        s.tiles = tiles_of(s.T)
        s.xin = nc.dram_tensor("xT%d" % i, [D, s.T], F32, kind="ExternalInput").ap()
        s.yout = nc.dram_tensor("yT%d" % i, [D, s.T], F32, kind="ExternalOutput").ap()
        s.ffnc_out = nc.dram_tensor("ffnc_o%d" % i, [depth, 128, 44, 2], F32, kind="ExternalOutput").ap()
        s.xT = nc.dram_tensor("s_xT%d" % i, [D, s.T], F32).ap()
        s.h2T = nc.dram_tensor("s_h2T%d" % i, [D, s.T], BF16).ap()
        s.oT = nc.dram_tensor("s_oT%d" % i, [D, s.T], BF16).ap()
        s.aT = nc.dram_tensor("s_aT%d" % i, [DFF, s.T], BF16).ap()
        nab = (depth + 1) // 2
        s.hg_out = nc.dram_tensor("hg_o%d" % i, [nab, 128, 4, 128], F32, kind="ExternalOutput").ap()
        s.lru_out = nc.dram_tensor("lru_o%d" % i, [nab, 128, 4], F32, kind="ExternalOutput").ap()
        s.lruc_out = nc.dram_tensor("lruc_o%d" % i, [nab, 128, 4, 3], F32, kind="ExternalOutput").ap()
        ncd = depth // 2
        s.past = 0 if i == 0 else PAST
        if ncd > 0:
            s.gd_out = nc.dram_tensor("gd_o%d" % i, [ncd, 128, 4, 128], F32, kind="ExternalOutput").ap()
            s.gdc_out = nc.dram_tensor("gdc_o%d" % i, [ncd, 128, 12, 3], F32, kind="ExternalOutput").ap()
            s.lat_out = nc.dram_tensor("lat_o%d" % i, [ncd, s.T, 256], F32, kind="ExternalOutput").ap()
            s.kr_out = nc.dram_tensor("kr_o%d" % i, [ncd, s.T, 64], F32, kind="ExternalOutput").ap()
            s.ropeC = nc.dram_tensor("ropeC%d" % i, [64, s.T], F32, kind="ExternalInput").ap()
            s.ropeS = nc.dram_tensor("ropeS%d" % i, [64, s.T], F32, kind="ExternalInput").ap()
            s.qanT = nc.dram_tensor("s_qanT%d" % i, [384, s.T], BF16).ap()
            s.ckvT = nc.dram_tensor("s_ckvT%d" % i, [256, s.past + s.T], F32).ap()
            s.krT = nc.dram_tensor("s_krT%d" % i, [64, s.past + s.T], F32).ap()
            if i > 0:
                s.gd_in = nc.dram_tensor("gd_i%d" % i, [ncd, 128, 4, 128], F32, kind="ExternalInput").ap()
                s.gdc_in = nc.dram_tensor("gdc_i%d" % i, [ncd, 128, 12, 3], F32, kind="ExternalInput").ap()
                s.lat_inT = nc.dram_tensor("lat_iT%d" % i, [ncd, 256, PAST], F32, kind="ExternalInput").ap()
                s.kr_inT = nc.dram_tensor("kr_iT%d" % i, [ncd, 64, PAST], F32, kind="ExternalInput").ap()
        if i > 0:
            s.ffnc_in = nc.dram_tensor("ffnc_i%d" % i, [depth, 128, 44, 2], F32, kind="ExternalInput").ap()
            s.hg_in = nc.dram_tensor("hg_i%d" % i, [nab, 128, 4, 128], F32, kind="ExternalInput").ap()
            s.lru_in = nc.dram_tensor("lru_i%d" % i, [nab, 128, 4], F32, kind="ExternalInput").ap()
            s.lruc_in = nc.dram_tensor("lruc_i%d" % i, [nab, 128, 4, 3], F32, kind="ExternalInput").ap()
        seqs.append(s)
    cT = nc.dram_tensor("cT", [128, 8, NSEQ], F32, kind="ExternalInput").ap()
    ada_w = nc.dram_tensor("ada_w", [depth, D, 6 * D], F32, kind="ExternalInput").ap()
    ada_bT = nc.dram_tensor("ada_bT", [depth, 128, 48], F32, kind="ExternalInput").ap()
    norm_gT = nc.dram_tensor("norm_gT", [depth, 128, 4, 8], F32, kind="ExternalInput").ap()
    w_out = nc.dram_tensor("w_out", [depth, D, D], F32, kind="ExternalInput").ap()
    ffn_up = nc.dram_tensor("ffn_up", [depth, D, 2 * DFF], F32, kind="ExternalInput").ap()
    ffn_cw = nc.dram_tensor("ffn_cw", [depth, 128, 44, 3], F32, kind="ExternalInput").ap()
    ffn_dn = nc.dram_tensor("ffn_dn", [depth, DFF, D], F32, kind="ExternalInput").ap()

    nab = (depth + 1) // 2
    W = {}
    W['consts'] = nc.dram_tensor("consts", [128, 13, 128], F32, kind="ExternalInput").ap()
    ncd = depth // 2
    if ncd > 0:
        W['cd_w_in'] = nc.dram_tensor("cd_w_in", [ncd, D, 3072], F32, kind="ExternalInput").ap()
        W['gd_cw'] = nc.dram_tensor("gd_cw", [ncd, 128, 12, 4], F32, kind="ExternalInput").ap()
        W['gd_vec'] = nc.dram_tensor("gd_vec", [ncd, 128, 4, 2], F32, kind="ExternalInput").ap()
        W['gd_g'] = nc.dram_tensor("gd_g", [ncd, 128, 1], F32, kind="ExternalInput").ap()
        W['q_g'] = nc.dram_tensor("q_g", [ncd, 128, 3], F32, kind="ExternalInput").ap()
        W['kv_g'] = nc.dram_tensor("kv_g", [ncd, 128, 2], F32, kind="ExternalInput").ap()
        W['w_qb'] = nc.dram_tensor("w_qb", [ncd, 384, 1024], F32, kind="ExternalInput").ap()
        W['w_kvb'] = nc.dram_tensor("w_kvb", [ncd, 256, 1024], F32, kind="ExternalInput").ap()
    W['ab_w_in'] = nc.dram_tensor("ab_w_in", [nab, D, 3072], F32, kind="ExternalInput").ap()
    W['hg_lb'] = nc.dram_tensor("hg_lb", [128, 2, 4], F32, kind="ExternalInput").ap()
    W['hg_g'] = nc.dram_tensor("hg_g", [nab, 128, 1], F32, kind="ExternalInput").ap()
    W['lru_cw'] = nc.dram_tensor("lru_cw", [nab, 128, 4, 4], F32, kind="ExternalInput").ap()
    W['lru_vec'] = nc.dram_tensor("lru_vec", [nab, 128, 4, 4], F32, kind="ExternalInput").ap()
    W['lru_gw'] = nc.dram_tensor("lru_gw", [nab, 128, 2, 4, 128], F32, kind="ExternalInput").ap()
    ps = [es.enter_context(nc.psum_tensor("ps%d" % i, [128, 512], F32)) for i in range(8)]
    PS = ["ps%d" % i for i in range(8)]
    ones_f = es.enter_context(nc.sbuf_tensor("ones_f", [128, 128], F32))
    mod = es.enter_context(nc.sbuf_tensor("mod", [128, 48, NSEQ], F32))
    ng = es.enter_context(nc.sbuf_tensor("ng", [128, 4, 8], F32))
    gsc = es.enter_context(nc.sbuf_tensor("gsc", [128, 2, 8, NSEQ], F32))
    csil = es.enter_context(nc.sbuf_tensor("csil", [128, 8, NSEQ], F32))
    cx.op('dve', lambda e: e.memset(ones_f[:], 1.0), w=['ones_f'])
    cx.dma('sp', csil[:], cT[:, :, :], w=['csil'])
    cx.op('act', lambda e: e.activation(out=csil[:], in_=csil[:], func=AF.Silu), r=['csil'], w=['csil'])

    with SB(nc, "cp", [128, 2, 8, 512], F32) as cp:
        k = 0
        for s in seqs:
            for (t0, n) in s.tiles:
                sl = k % 2
                k += 1
                cx.dma('sp', cp[:, sl, :, :n], s.xin.rearrange("(c p) t -> p c t", p=128)[:, :, t0:t0 + n],
                       w=['cp%d' % sl])
                cx.dma('pool', s.xT.rearrange("(c p) t -> p c t", p=128)[:, :, t0:t0 + n], cp[:, sl, :, :n],
                       r=['cp%d' % sl], w=['xT%d' % s.i])
        cx.barrier()

    def rms_stats(src, n, sq, rstd, psn, srcres, nfeat_chunks=8):
        for c in range(nfeat_chunks):
            cx.op('act', lambda e, c=c: e.activation(out=sq[:, c, :n], in_=src(c), func=AF.Square),
                  r=srcres, w=['sq'])
        for c in range(nfeat_chunks):
            cx.op('pe', lambda e, c=c: e.matmul(ps[psn][:, :n], ones_f[:], sq[:, c, :n],
                                                 start=(c == 0), stop=(c == nfeat_chunks - 1)),
                  r=['sq', 'ones_f'], w=[PS[psn]])
        cx.op('dve', lambda e: e.tensor_scalar(rstd[:, :n], ps[psn][:, :n], 1.0 / (128 * nfeat_chunks), EPS,
                                               ALU.mult, ALU.add), r=[PS[psn]], w=['rstd'])
        cx.op('dve', lambda e: e.reciprocal(rstd[:, :n], rstd[:, :n]), r=['rstd'], w=['rstd'])
        cx.op('act', lambda e: e.activation(out=rstd[:, :n], in_=rstd[:, :n], func=AF.Sqrt), r=['rstd'], w=['rstd'])

    for l in range(depth):
        with SB(nc, "aw", [128, 2, 8, 768], F32) as aw, SB(nc, "adb", [128, 48], F32) as adb:
            cx.dma('sp', adb[:], ada_bT[l], w=['adb'])
            cx.dma('sp', ng[:], norm_gT[l], w=['ng'])
            for g in range(8):
                sl = g % 2
                cx.dma('sp', aw[:, sl], ada_w[l].rearrange("(c p) f -> p c f", p=128)[:, :, g * 768:(g + 1) * 768],
                       w=['aw%d' % sl])
                for j in range(6):
                    fc = g * 6 + j
                    pn = fc % 2
                    for c in range(8):
                        cx.op('pe', lambda e, c=c, j=j, sl=sl, pn=pn: e.matmul(
                            ps[pn][:, :NSEQ], aw[:, sl, c, j * 128:(j + 1) * 128], csil[:, c, :],
                            start=(c == 0), stop=(c == 7)), r=['aw%d' % sl, 'csil'], w=[PS[pn]])
                    cx.op('dve', lambda e, fc=fc, pn=pn: e.tensor_scalar(
                        mod[:, fc, :], ps[pn][:, :NSEQ], adb[:, fc:fc + 1], None, ALU.add),
                        r=[PS[pn], 'adb'], w=['mod'])
            for k2, (gi, sc0) in enumerate(((0, 8), (2, 32))):
                for c in range(8):
                    cx.op('dve', lambda e, k2=k2, gi=gi, sc0=sc0, c=c: e.tensor_scalar(
                        gsc[:, k2, c, :], mod[:, sc0 + c, :], 1.0, ng[:, gi, c:c + 1], ALU.add, ALU.mult),
                        r=['mod', 'ng'], w=['gsc'])
            cx.barrier()

        if l % 2 == 0:
            emit_mixer(nc, cx, ps, PS, l, seqs, mod, gsc, ng, ones_f, rms_stats, W)
        else:
            emit_cd(nc, cx, ps, PS, l, seqs, mod, gsc, ng, ones_f, rms_stats, W)

        with (SB(nc, "wst", [128, 2, 8, 512], F32) as wst,
              SB(nc, "wo", [128, 8, D], BF16) as wo,
              SB(nc, "ot", [128, 2, 8, 512], BF16) as ot,
              SB(nc, "xt", [128, 2, 8, 512], F32) as xt,
              SB(nc, "yt", [128, 8, 512], F32) as yt,
              SB(nc, "sq", [128, 8, 512], F32) as sq,
              SB(nc, "rstd", [128, 512], F32) as rstd,
              SB(nc, "tmp", [128, 512], F32) as tmp,
              SB(nc, "h2", [128, 2, 8, 512], BF16) as h2):
            for g in range(2):
                cx.dma('sp', wst[:, g], w_out[l].rearrange("(c p) f -> p c f", p=128)[:, :, g * 512:(g + 1) * 512],
                       w=['wst%d' % g])
                cx.op('act', lambda e, g=g: e.activation(out=wo[:, :, g * 512:(g + 1) * 512], in_=wst[:, g],
                                                         func=AF.Copy), r=['wst%d' % g], w=['wo'])
            k = 0
            for s in seqs:
                for (t0, n) in s.tiles:
                    sl = k % 2
                    k += 1
                    XT, OT, H2 = 'xt%d' % sl, 'ot%d' % sl, 'h2%d' % sl
                    cx.dma('sp', ot[:, sl, :, :n], s.oT.rearrange("(c p) t -> p c t", p=128)[:, :, t0:t0 + n],
                           r=['oT%d' % s.i], w=[OT])
                    cx.dma('sp', xt[:, sl, :, :n], s.xT.rearrange("(c p) t -> p c t", p=128)[:, :, t0:t0 + n],
                           r=['xT%d' % s.i], w=[XT])
                    for m in range(8):
                        pn = m % 4
                        for c in range(8):
                            cx.op('pe', lambda e, m=m, c=c, pn=pn: e.matmul(
                                ps[pn][:, :n], wo[:, c, m * 128:(m + 1) * 128], ot[:, sl, c, :n],
                                start=(c == 0), stop=(c == 7)), r=['wo', OT], w=[PS[pn]])
                        cx.op('act', lambda e, m=m, pn=pn: e.activation(out=yt[:, m, :n], in_=ps[pn][:, :n],
                                                                        func=AF.Copy), r=[PS[pn]], w=['yt'])
                    rms_stats(lambda c: yt[:, c, :n], n, sq, rstd, 4, ['yt'])
                    for c in range(8):
                        cx.op('dve', lambda e, c=c: e.tensor_tensor(tmp[:, :n], yt[:, c, :n], rstd[:, :n], ALU.mult),
                              r=['yt', 'rstd'], w=['tmp'])
                        cx.op('dve', lambda e, c=c: e.tensor_scalar(
                            tmp[:, :n], tmp[:, :n], ng[:, 1, c:c + 1], mod[:, 16 + c, s.i:s.i + 1],
                            ALU.mult, ALU.mult), r=['tmp', 'ng', 'mod'], w=['tmp'])
                        cx.op('dve', lambda e, c=c: e.tensor_tensor(xt[:, sl, c, :n], xt[:, sl, c, :n], tmp[:, :n],
                                                                    ALU.add), r=['tmp', XT], w=[XT])
                    cx.dma('pool', s.xT.rearrange("(c p) t -> p c t", p=128)[:, :, t0:t0 + n], xt[:, sl, :, :n],
                           r=[XT], w=['xT%d' % s.i])
                    rms_stats(lambda c: xt[:, sl, c, :n], n, sq, rstd, 5, [XT])
                    for c in range(8):
                        cx.op('dve', lambda e, c=c: e.tensor_tensor(tmp[:, :n], xt[:, sl, c, :n], rstd[:, :n], ALU.mult),
                              r=[XT, 'rstd'], w=['tmp'])
                        cx.op('act', lambda e, c=c: e.activation(
                            out=h2[:, sl, c, :n], in_=tmp[:, :n], func=AF.Identity,
                            scale=gsc[:, 1, c, s.i:s.i + 1], bias=mod[:, 24 + c, s.i:s.i + 1]),
                            r=['tmp', 'gsc', 'mod'], w=[H2])
                    cx.dma('pool', s.h2T.rearrange("(c p) t -> p c t", p=128)[:, :, t0:t0 + n], h2[:, sl, :, :n],
                           r=[H2], w=['h2T%d' % s.i])
            cx.barrier()

        with (SB(nc, "wst", [128, 2, 2816], F32) as wst,
              SB(nc, "wu", [128, 8, 2 * DFF], BF16) as wu,
              SB(nc, "cw", [128, 44, 3], F32) as cw,
              SB(nc, "h2", [128, 2, 8, 512], BF16) as h2,
              SB(nc, "halo", [128, 44, 2], F32) as halo,
              SB(nc, "u", [128, 4, 514], F32) as u,
              SB(nc, "cv", [128, 4, 512], F32) as cv,
              SB(nc, "at", [128, 2, 22, 512], BF16) as at):
            cx.dma('sp', cw[:], ffn_cw[l], w=['cw'])
            k = 0
            for c in range(8):
                for g in range(2):
                    sl = k % 2
                    k += 1
                    cx.dma('sp', wst[:, sl], ffn_up[l][c * 128:(c + 1) * 128, g * 2816:(g + 1) * 2816],
                           w=['wst%d' % sl])
                    cx.op('act' if g else 'dve', (lambda e, c=c, g=g, sl=sl: e.activation(
                        out=wu[:, c, g * 2816:(g + 1) * 2816], in_=wst[:, sl], func=AF.Copy)) if g else
                        (lambda e, c=c, g=g, sl=sl: e.tensor_copy(wu[:, c, g * 2816:(g + 1) * 2816], wst[:, sl])),
                        r=['wst%d' % sl], w=['wu'])
            k = 0
            for s in seqs:
                if s.i == 0:
                    cx.op('dve', lambda e: e.memset(halo[:], 0.0), w=['halo'])
                else:
                    cx.dma('sp', halo[:], s.ffnc_in[l], w=['halo'])
                for (t0, n) in s.tiles:
                    sl = k % 2
                    k += 1
                    H2, AT = 'h2%d' % sl, 'at%d' % sl
                    cx.dma('sp', h2[:, sl, :, :n], s.h2T.rearrange("(c p) t -> p c t", p=128)[:, :, t0:t0 + n],
                           r=['h2T%d' % s.i], w=[H2])
                    for j in range(22):
                        for gv in range(2):
                            ch = gv * 22 + j
                            ui = (j % 2) * 2 + gv
                            pn = ui
                            U, CV = 'u%d' % ui, 'cv%d' % ui
                            for c in range(8):
                                cx.op('pe', lambda e, c=c, ch=ch, pn=pn: e.matmul(
                                    ps[pn][:, :n], wu[:, c, ch * 128:(ch + 1) * 128], h2[:, sl, c, :n],
                                    start=(c == 0), stop=(c == 7)), r=['wu', H2], w=[PS[pn]])
                            cx.op('act', lambda e, ui=ui, ch=ch: e.activation(out=u[:, ui, 0:2], in_=halo[:, ch, :],
                                                                            func=AF.Copy), r=['halo'], w=[U])
                            cx.op('act', lambda e, ui=ui, pn=pn: e.activation(out=u[:, ui, 2:2 + n], in_=ps[pn][:, :n],
                                                                            func=AF.Copy), r=[PS[pn]], w=[U])
                            cx.op('act', lambda e, ui=ui, ch=ch: e.activation(out=halo[:, ch, :], in_=u[:, ui, n:n + 2],
                                                                            func=AF.Copy), r=[U], w=['halo'])
                            cx.op('dve', lambda e, ui=ui, ch=ch: e.tensor_scalar(
                                cv[:, ui, :n], u[:, ui, 0:n], cw[:, ch, 0:1], None, ALU.mult),
                                r=[U, 'cw'], w=[CV])
                            cx.op('dve', lambda e, ui=ui, ch=ch: e.scalar_tensor_tensor(
                                cv[:, ui, :n], u[:, ui, 1:1 + n], cw[:, ch, 1:2], cv[:, ui, :n], ALU.mult, ALU.add),
                                r=[U, 'cw', CV], w=[CV])
                            cx.op('dve', lambda e, ui=ui, ch=ch: e.scalar_tensor_tensor(
                                cv[:, ui, :n], u[:, ui, 2:2 + n], cw[:, ch, 2:3], cv[:, ui, :n], ALU.mult, ALU.add),
                                r=[U, 'cw', CV], w=[CV])
                        gi = (j % 2) * 2
                        cx.op('act', lambda e, gi=gi: e.activation(out=cv[:, gi, :n], in_=cv[:, gi, :n], func=AF.Silu),
                              r=['cv%d' % gi], w=['cv%d' % gi])
                        cx.op('dve', lambda e, gi=gi, j=j: e.tensor_tensor(at[:, sl, j, :n], cv[:, gi, :n],
                                                                         cv[:, gi + 1, :n], ALU.mult),
                              r=['cv%d' % gi, 'cv%d' % (gi + 1)], w=[AT])
                    cx.dma('pool', s.aT.rearrange("(c p) t -> p c t", p=128)[:, :, t0:t0 + n], at[:, sl, :, :n],
                           r=[AT], w=['aT%d' % s.i])
                cx.dma('pool', s.ffnc_out[l], halo[:], r=['halo'], w=['ffnc_out'])
            cx.barrier()

        with (SB(nc, "wst", [128, 2, 11, 512], F32) as wst,
              SB(nc, "wd", [128, 22, D], BF16) as wd,
              SB(nc, "at", [128, 2, 22, 512], BF16) as at,
              SB(nc, "xt", [128, 2, 8, 512], F32) as xt,
              SB(nc, "yt", [128, 8, 512], F32) as yt,
              SB(nc, "sq", [128, 8, 512], F32) as sq,
              SB(nc, "rstd", [128, 512], F32) as rstd,
              SB(nc, "tmp", [128, 512], F32) as tmp):
            k = 0
            for g in range(2):
                for hh in range(2):
                    sl = k % 2
                    k += 1
                    cx.dma('sp', wst[:, sl], ffn_dn[l].rearrange("(c p) f -> p c f", p=128)[
                        :, hh * 11:(hh + 1) * 11, g * 512:(g + 1) * 512], w=['wst%d' % sl])
                    cx.op('act', lambda e, g=g, hh=hh, sl=sl: e.activation(
                        out=wd[:, hh * 11:(hh + 1) * 11, g * 512:(g + 1) * 512], in_=wst[:, sl], func=AF.Copy),
                        r=['wst%d' % sl], w=['wd'])
            k = 0
            for s in seqs:
                for (t0, n) in s.tiles:
                    sl = k % 2
                    k += 1
                    XT, AT = 'xt%d' % sl, 'at%d' % sl
                    cx.dma('sp', at[:, sl, :, :n], s.aT.rearrange("(c p) t -> p c t", p=128)[:, :, t0:t0 + n],
                           r=['aT%d' % s.i], w=[AT])
                    cx.dma('sp', xt[:, sl, :, :n], s.xT.rearrange("(c p) t -> p c t", p=128)[:, :, t0:t0 + n],
                           r=['xT%d' % s.i], w=[XT])
                    for m in range(8):
                        pn = m % 4
                        for c in range(22):
                            cx.op('pe', lambda e, m=m, c=c, pn=pn: e.matmul(
                                ps[pn][:, :n], wd[:, c, m * 128:(m + 1) * 128], at[:, sl, c, :n],
                                start=(c == 0), stop=(c == 21)), r=['wd', AT], w=[PS[pn]])
                        cx.op('act', lambda e, m=m, pn=pn: e.activation(out=yt[:, m, :n], in_=ps[pn][:, :n],
                                                                        func=AF.Copy), r=[PS[pn]], w=['yt'])
                    rms_stats(lambda c: yt[:, c, :n], n, sq, rstd, 4, ['yt'])
                    for c in range(8):
                        cx.op('dve', lambda e, c=c: e.tensor_tensor(tmp[:, :n], yt[:, c, :n], rstd[:, :n], ALU.mult),
                              r=['yt', 'rstd'], w=['tmp'])
                        cx.op('dve', lambda e, c=c: e.tensor_scalar(
                            tmp[:, :n], tmp[:, :n], ng[:, 3, c:c + 1], mod[:, 40 + c, s.i:s.i + 1],
                            ALU.mult, ALU.mult), r=['tmp', 'ng', 'mod'], w=['tmp'])
                        cx.op('dve', lambda e, c=c: e.tensor_tensor(xt[:, sl, c, :n], xt[:, sl, c, :n], tmp[:, :n],
                                                                    ALU.add), r=['tmp', XT], w=[XT])
                    dst = s.yout if l == depth - 1 else s.xT
                    cx.dma('pool', dst.rearrange("(c p) t -> p c t", p=128)[:, :, t0:t0 + n], xt[:, sl, :, :n],
                           r=[XT], w=['xT%d' % s.i])
            cx.barrier()


def emit_mixer(nc, cx, ps, PS, l, seqs, mod, gsc, ng, ones_f, rms_stats, W):
    ab = (l % 2 == 0)
    jl = l // 2
    with ExitStack() as st:
        def T(name, shape, dt):
            return st.enter_context(SB(nc, name, shape, dt))
        xt = T("xt", [128, 2, 8, 512], F32)
        sq = T("sq", [128, 8, 512], F32)
        rstd = T("rstd", [128, 512], F32)
        tmp = T("tmp", [128, 512], F32)
        h = T("h", [128, 8, 512], BF16)
        ot = T("otile", [128, 2, 8, 512], BF16)
        if ab:
            wst = T("wst", [128, 2, 1536], F32)
            win = T("win", [128, 8, 3072], BF16)
            P4 = T("P4", [128, 4, 512], F32)
            sig = T("sig", [128, 512], F32)
            kk = T("kk", [128, 512], F32)
            qq = T("qq", [128, 512], F32)
            lf = T("lf", [128, 512], F32)
            bb = T("bb", [128, 513], F32)
            dd = T("dd", [128, 512], F32)
            ee = T("ee", [128, 512], F32)
            qt = T("qt", [128, 512], BF16)
            kt = T("kt", [128, 512], BF16)
            qe = T("qe", [128, 512], BF16)
            kl = T("kl", [128, 512], BF16)
            vb = T("vb", [128, 512], BF16)
            edl = T("edl", [128, 8], F32)
            ob = T("ob", [128, 512], F32)
            attm = T("attm", [128, 128], BF16)
            attf = T("attf", [128, 128], F32)
            vtok = T("vtok", [128, 128], BF16)
            kltok = T("kltok", [128, 128], BF16)
            S = T("S", [128, 4, 128], F32)
            Sb = T("Sb", [128, 4, 128], BF16)
            onesr = T("onesr", [128, 512], F32)
            cst = T("cst", [128, 3, 128], F32)
            identb = T("identb", [128, 128], BF16)
            lbl = T("lbl", [128, 2, 4], F32)
            lbv = T("lbv", [128, 4], F32)
            oml = T("oml", [128, 4], F32)
            noml = T("noml", [128, 4], F32)
            hgg = T("hgg", [128, 1], F32)
            lxb = T("lxb", [128, 515], F32)
            lhalo = T("lhalo", [128, 4, 3], F32)
            lcw = T("lcw", [128, 4, 4], F32)
            lvec = T("lvec", [128, 4, 4], F32)
            m8sp = T("m8sp", [128, 4], F32)
            gst = T("gst", [128, 2, 4, 128], F32)
            gw = T("gw", [128, 2, 4, 128], BF16)
            hst = T("hst", [128, 4], F32)
            xc = T("xc", [128, 512], F32)
            xcb = T("xcb", [128, 512], BF16)
            rr = sig
            ii = kk
            aa = qq
            uu = lf
            hh = dd
            ly = ee
            gl = ob
            for c2 in range(16):
                c, hf = c2 // 2, c2 % 2
                sl = hf
                cx.dma('sp', wst[:, sl], W['ab_w_in'][jl][c * 128:(c + 1) * 128, hf * 1536:(hf + 1) * 1536],
                       w=['wst%d' % sl])
                cx.op('act' if sl else 'dve',
                      (lambda e: e.activation(out=win[:, c, hf * 1536:(hf + 1) * 1536], in_=wst[:, sl], func=AF.Copy))
                      if sl else (lambda e: e.tensor_copy(win[:, c, hf * 1536:(hf + 1) * 1536], wst[:, sl])),
                      r=['wst%d' % sl], w=['win'])
            cx.dma('sp', cst[:], W['consts'][:, 0:3, :], w=['cst'])
            cx.op('dve', lambda e: e.tensor_copy(identb[:], cst[:, 0, :]), r=['cst'], w=['identb'])
            cx.op('dve', lambda e: e.memset(onesr[:], 1.0), w=['onesr'])
            cx.dma('sp', lbl[:], W['hg_lb'][:, :, :], w=['lbl'])
            cx.dma('sp', hgg[:], W['hg_g'][jl], w=['hgg'])
            if jl == 0:
                cx.op('dve', lambda e: e.memset(lbv[:], 0.0), w=['lbv'])
            else:
                cx.op('dve', lambda e: e.tensor_tensor(lbv[:], lbl[:, 1, :], lbl[:, 0, :], ALU.subtract),
                      r=['lbl'], w=['lbv'])
                cx.op('act', lambda e: e.activation(out=lbv[:], in_=lbv[:], func=AF.Sigmoid), r=['lbv'], w=['lbv'])
            cx.op('dve', lambda e: e.tensor_scalar(oml[:], lbv[:], -1.0, 1.0, ALU.mult, ALU.add), r=['lbv'], w=['oml'])
            cx.op('dve', lambda e: e.tensor_scalar(noml[:], oml[:], -1.0, None, ALU.mult), r=['oml'], w=['noml'])
            cx.dma('sp', lcw[:], W['lru_cw'][jl], w=['lcw'])
            cx.dma('sp', lvec[:], W['lru_vec'][jl], w=['lvec'])
            cx.dma('sp', gst[:], W['lru_gw'][jl], w=['gst'])
            cx.op('dve', lambda e: e.tensor_copy(gw[:], gst[:]), r=['gst'], w=['gw'])
            cx.op('act', lambda e: e.activation(out=m8sp[:], in_=lvec[:, :, 3], func=AF.Exp, scale=-1.0),
                  r=['lvec'], w=['m8sp'])
            cx.op('dve', lambda e: e.tensor_scalar(m8sp[:], m8sp[:], 1.0, None, ALU.add), r=['m8sp'], w=['m8sp'])
            cx.op('act', lambda e: e.activation(out=m8sp[:], in_=m8sp[:], func=AF.Ln), r=['m8sp'], w=['m8sp'])
            cx.op('dve', lambda e: e.tensor_scalar(m8sp[:], m8sp[:], -8.0, None, ALU.mult), r=['m8sp'], w=['m8sp'])
        k = 0
        for s in seqs:
            L = 64 if s.T % 64 == 0 else s.T
            if ab:
                if s.i == 0:
                    cx.op('dve', lambda e: e.memset(S[:], 0.0), w=['S'])
                    cx.op('dve', lambda e: e.memset(hst[:], 0.0), w=['hst'])
                    cx.op('dve', lambda e: e.memset(lhalo[:], 0.0), w=['lhalo'])
                else:
                    cx.dma('sp', S[:], s.hg_in[jl], w=['S'])
                    cx.dma('sp', hst[:], s.lru_in[jl], w=['hst'])
                    cx.dma('sp', lhalo[:], s.lruc_in[jl], w=['lhalo'])
                cx.op('act', lambda e: e.activation(out=Sb[:], in_=S[:], func=AF.Copy), r=['S'], w=['Sb'])
            for (t0, n) in s.tiles:
                sl = k % 2
                k += 1
                XT, OT = 'xt%d' % sl, 'ot%d' % sl
                cx.dma('sp', xt[:, sl, :, :n], s.xT.rearrange("(c p) t -> p c t", p=128)[:, :, t0:t0 + n],
                       r=['xT%d' % s.i], w=[XT])
                rms_stats(lambda c: xt[:, sl, c, :n], n, sq, rstd, 6, [XT])
                for c in range(8):
                    cx.op('dve', lambda e, c=c: e.tensor_tensor(tmp[:, :n], xt[:, sl, c, :n], rstd[:, :n], ALU.mult),
                          r=[XT, 'rstd'], w=['tmp'])
                    cx.op('act', lambda e, c=c: e.activation(
                        out=(h[:, c, :n] if ab else ot[:, sl, c, :n]), in_=tmp[:, :n], func=AF.Identity,
                        scale=gsc[:, 0, c, s.i:s.i + 1], bias=mod[:, 0 + c, s.i:s.i + 1]),
                        r=['tmp', 'gsc', 'mod'], w=(['h'] if ab else [OT]))
                if ab:
                    def proj(ch, pn):
                        for c in range(8):
                            cx.op('pe', lambda e, c=c: e.matmul(ps[pn][:, :n], win[:, c, ch * 128:(ch + 1) * 128],
                                                                 h[:, c, :n], start=(c == 0), stop=(c == 7)),
                                  r=['win', 'h'], w=[PS[pn]])
                    nch = n // L
                    G = min(128, n)
                    mi = 1 if L == 64 else 2
                    for hd in range(4):
                        for j4 in range(4):
                            pn = j4 % 2
                            proj(j4 * 4 + hd, pn)
                            cx.op('act', lambda e, j4=j4, pn=pn: e.activation(out=P4[:, j4, :n], in_=ps[pn][:, :n],
                                                                              func=AF.Copy), r=[PS[pn]], w=['P4'])
                        cx.op('act', lambda e: e.activation(out=sig[:, :n], in_=P4[:, 1, :n], func=AF.Sigmoid),
                              r=['P4'], w=['sig'])
                        cx.op('dve', lambda e: e.tensor_scalar(lf[:, :n], sig[:, :n], oml[:, hd:hd + 1], lbv[:, hd:hd + 1],
                                                               ALU.mult, ALU.add), r=['sig', 'oml', 'lbv'], w=['lf'])
                        cx.op('act', lambda e: e.activation(out=lf[:, :n], in_=lf[:, :n], func=AF.Ln), r=['lf'], w=['lf'])
                        cx.op('dve', lambda e: e.tensor_scalar(kk[:, :n], sig[:, :n], noml[:, hd:hd + 1], oml[:, hd:hd + 1],
                                                               ALU.mult, ALU.add), r=['sig', 'oml', 'noml'], w=['kk'])
                        cx.op('act', lambda e: e.activation(out=qq[:, :n], in_=P4[:, 0, :n], func=AF.Silu),
                              r=['P4'], w=['qq'])
                        cx.op('act', lambda e: e.activation(out=vb[:, :n], in_=P4[:, 2, :n], func=AF.Copy),
                              r=['P4'], w=['vb'])
                        cx.op('dve', lambda e: e.memset(bb[:, 0:1], 0.0), w=['bb'])
                        cx.op('dve', lambda e: e.tensor_tensor_scan(bb[:, 1:1 + n], onesr[:, :n], lf[:, :n], 0.0,
                                                                    ALU.mult, ALU.add), r=['onesr', 'lf'], w=['bb'])
                        b3 = bb[:, 1:1 + n].rearrange("p (c l) -> p c l", l=L)
                        mid3 = b3[:, :, L // 2:L // 2 + 1].to_broadcast([128, nch, L])
                        last3 = b3[:, :, L - 1:L].to_broadcast([128, nch, L])
                        prev3 = bb[:, 0:n].rearrange("p (c l) -> p c l", l=L)[:, :, 0:1].to_broadcast([128, nch, L])
                        d3 = dd[:, :n].rearrange("p (c l) -> p c l", l=L)

                        def expmul(ref3, scale, src, dst, DST):
                            cx.op('dve', lambda e: e.tensor_tensor(d3, b3, ref3, ALU.subtract), r=['bb'], w=['dd'])
                            cx.op('act', lambda e: e.activation(out=ee[:, :n], in_=dd[:, :n], func=AF.Exp, scale=scale),
                                  r=['dd'], w=['ee'])
                            cx.op('dve', lambda e: e.tensor_tensor(dst[:, :n], src[:, :n], ee[:, :n], ALU.mult),
                                  r=['ee', 'qq', 'kk'], w=[DST])
                        expmul(mid3, 1.0, qq, qt, 'qt')
                        expmul(mid3, -1.0, kk, kt, 'kt')
                        expmul(prev3, 1.0, qq, qe, 'qe')
                        expmul(last3, -1.0, kk, kl, 'kl')
                        cx.op('dve', lambda e: e.tensor_tensor(
                            edl[:, :nch], bb[:, 1:1 + n].rearrange("p (c l) -> p c l", l=L)[:, :, L - 1],
                            bb[:, 0:n].rearrange("p (c l) -> p c l", l=L)[:, :, 0], ALU.subtract), r=['bb'], w=['edl'])
                        cx.op('act', lambda e: e.activation(out=edl[:, :nch], in_=edl[:, :nch], func=AF.Exp),
                              r=['edl'], w=['edl'])
                        psb = ps[5].bitcast(BF16)
                        for g0 in range(0, n, G):
                            cx.op('pe', lambda e, g0=g0: e.matmul(ps[2][:G, :G], kt[:, g0:g0 + G], qt[:, g0:g0 + G],
                                                                   start=True, stop=True), r=['kt', 'qt'], w=[PS[2]])
                            cx.op('dve', lambda e: e.tensor_scalar(attf[:G, :G], ps[2][:G, :G], -1e30, 1e30, ALU.max, ALU.min),
                                  r=[PS[2]], w=['attf'])
                            cx.op('dve', lambda e: e.tensor_tensor(attm[:G, :G], attf[:G, :G], cst[:G, mi, :G], ALU.mult),
                                  r=['attf', 'cst'], w=['attm'])
                            cx.op('pe', lambda e, g0=g0: e.transpose(psb[:G, 0:128], vb[:, g0:g0 + G], identb[:]),
                                  r=['vb', 'identb'], w=[PS[5]])
                            cx.op('act', lambda e: e.activation(out=vtok[:G, :], in_=psb[:G, 0:128], func=AF.Copy),
                                  r=[PS[5]], w=['vtok'])
                            cx.op('pe', lambda e, g0=g0: e.transpose(psb[:G, 0:128], kl[:, g0:g0 + G], identb[:]),
                                  r=['kl', 'identb'], w=[PS[5]])
                            cx.op('act', lambda e: e.activation(out=kltok[:G, :], in_=psb[:G, 0:128], func=AF.Copy),
                                  r=[PS[5]], w=['kltok'])
                            for ci in range(G // L):
                                c0 = ci * L
                                cidx = (g0 + c0) // L
                                cx.op('pe', lambda e, c0=c0: e.matmul(ps[3][:, c0:c0 + L], vtok[c0:c0 + L, :],
                                                                       attm[c0:c0 + L, c0:c0 + L], start=True, stop=False),
                                      r=['vtok', 'attm'], w=[PS[3]])
                                cx.op('pe', lambda e, c0=c0, g0=g0: e.matmul(ps[3][:, c0:c0 + L], Sb[:, hd, :],
                                                                              qe[:, g0 + c0:g0 + c0 + L], start=False, stop=True),
                                      r=['Sb', 'qe'], w=[PS[3]])
                                cx.op('pe', lambda e, c0=c0: e.matmul(ps[4][:, :128], kltok[c0:c0 + L, :], vtok[c0:c0 + L, :],
                                                                       start=True, stop=True), r=['kltok', 'vtok'], w=[PS[4]])
                                cx.op('dve', lambda e, cidx=cidx: e.scalar_tensor_tensor(
                                    S[:, hd, :], S[:, hd, :], edl[:, cidx:cidx + 1], ps[4][:, :128], ALU.mult, ALU.add),
                                    r=['S', 'edl', PS[4]], w=['S'])
                                cx.op('act', lambda e: e.activation(out=Sb[:, hd, :], in_=S[:, hd, :], func=AF.Copy),
                                      r=['S'], w=['Sb'])
                            cx.op('act', lambda e, g0=g0: e.activation(out=ob[:, g0:g0 + G], in_=ps[3][:, :G], func=AF.Copy),
                                  r=[PS[3]], w=['ob'])
                        cx.op('act', lambda e: e.activation(out=dd[:, :n], in_=ob[:, :n], func=AF.Square), r=['ob'], w=['dd'])
                        cx.op('pe', lambda e: e.matmul(ps[7][:, :n], ones_f[:], dd[:, :n], start=True, stop=True),
                              r=['dd', 'ones_f'], w=[PS[7]])
                        cx.op('dve', lambda e: e.tensor_scalar(ee[:, :n], ps[7][:, :n], 1.0 / 128, EPS, ALU.mult, ALU.add),
                              r=[PS[7]], w=['ee'])
                        cx.op('dve', lambda e: e.reciprocal(ee[:, :n], ee[:, :n]), r=['ee'], w=['ee'])
                        cx.op('act', lambda e: e.activation(out=ee[:, :n], in_=ee[:, :n], func=AF.Sqrt), r=['ee'], w=['ee'])
                        cx.op('dve', lambda e: e.tensor_tensor(ob[:, :n], ob[:, :n], ee[:, :n], ALU.mult),
                              r=['ob', 'ee'], w=['ob'])
                        cx.op('act', lambda e: e.activation(out=dd[:, :n], in_=P4[:, 3, :n], func=AF.Silu), r=['P4'], w=['dd'])
                        cx.op('dve', lambda e: e.scalar_tensor_tensor(ot[:, sl, hd, :n], ob[:, :n], hgg[:, 0:1], dd[:, :n],
                                                                      ALU.mult, ALU.mult), r=['ob', 'hgg', 'dd'], w=[OT])
                    for ch in range(4):
                        proj(16 + ch, 0)
                        cx.op('act', lambda e: e.activation(out=lxb[:, 0:3], in_=lhalo[:, ch, :], func=AF.Copy),
                              r=['lhalo'], w=['lxb'])
                        cx.op('act', lambda e: e.activation(out=lxb[:, 3:3 + n], in_=ps[0][:, :n], func=AF.Copy),
                              r=[PS[0]], w=['lxb'])
                        cx.op('act', lambda e: e.activation(out=lhalo[:, ch, :], in_=lxb[:, n:n + 3], func=AF.Copy),
                              r=['lxb'], w=['lhalo'])
                        proj(20 + ch, 1)
                        cx.op('act', lambda e: e.activation(out=ly[:, :n], in_=ps[1][:, :n], func=AF.Copy),
                              r=[PS[1]], w=['ee'])
                        cx.op('dve', lambda e: e.tensor_scalar(xc[:, :n], lxb[:, 0:n], lcw[:, ch, 0:1], lvec[:, ch, 0:1],
                                                               ALU.mult, ALU.add), r=['lxb', 'lcw', 'lvec'], w=['xc'])
                        for tp in range(1, 4):
                            cx.op('dve', lambda e, tp=tp: e.scalar_tensor_tensor(
                                xc[:, :n], lxb[:, tp:tp + n], lcw[:, ch, tp:tp + 1], xc[:, :n], ALU.mult, ALU.add),
                                r=['lxb', 'lcw', 'xc'], w=['xc'])
                        cx.op('act', lambda e: e.activation(out=xcb[:, :n], in_=xc[:, :n], func=AF.Copy), r=['xc'], w=['xcb'])
                        cx.op('pe', lambda e: e.matmul(ps[2][:, :n], gw[:, 0, ch, :], xcb[:, :n], start=True, stop=True),
                              r=['gw', 'xcb'], w=[PS[2]])
                        cx.op('act', lambda e: e.activation(out=rr[:, :n], in_=ps[2][:, :n], func=AF.Sigmoid,
                                                            bias=lvec[:, ch, 1:2]), r=[PS[2], 'lvec'], w=['sig'])
                        cx.op('pe', lambda e: e.matmul(ps[3][:, :n], gw[:, 1, ch, :], xcb[:, :n], start=True, stop=True),
                              r=['gw', 'xcb'], w=[PS[3]])
                        cx.op('act', lambda e: e.activation(out=ii[:, :n], in_=ps[3][:, :n], func=AF.Sigmoid,
                                                            bias=lvec[:, ch, 2:3]), r=[PS[3], 'lvec'], w=['kk'])
                        cx.op('dve', lambda e: e.tensor_scalar(rr[:, :n], rr[:, :n], m8sp[:, ch:ch + 1], None, ALU.mult),
                              r=['sig', 'm8sp'], w=['sig'])
                        cx.op('act', lambda e: e.activation(out=aa[:, :n], in_=rr[:, :n], func=AF.Exp), r=['sig'], w=['qq'])
                        cx.op('act', lambda e: e.activation(out=uu[:, :n], in_=rr[:, :n], func=AF.Exp, scale=2.0),
                              r=['sig'], w=['lf'])
                        cx.op('dve', lambda e: e.tensor_scalar(uu[:, :n], uu[:, :n], -1.0, 1.0, ALU.mult, ALU.add),
                              r=['lf'], w=['lf'])
                        cx.op('dve', lambda e: e.tensor_scalar(uu[:, :n], uu[:, :n], 1e-12, None, ALU.max),
                              r=['lf'], w=['lf'])
                        cx.op('act', lambda e: e.activation(out=uu[:, :n], in_=uu[:, :n], func=AF.Sqrt), r=['lf'], w=['lf'])
                        cx.op('dve', lambda e: e.tensor_tensor(uu[:, :n], uu[:, :n], ii[:, :n], ALU.mult),
                              r=['lf', 'kk'], w=['lf'])
                        cx.op('dve', lambda e: e.tensor_tensor(uu[:, :n], uu[:, :n], xc[:, :n], ALU.mult),
                              r=['lf', 'xc'], w=['lf'])
                        cx.op('dve', lambda e: e.tensor_tensor_scan(hh[:, :n], aa[:, :n], uu[:, :n], hst[:, ch:ch + 1],
                                                                    ALU.mult, ALU.add), r=['qq', 'lf', 'hst'], w=['dd'])
                        cx.op('act', lambda e: e.activation(out=hst[:, ch:ch + 1], in_=hh[:, n - 1:n], func=AF.Copy),
                              r=['dd'], w=['hst'])
                        cx.op('dve', lambda e: e.tensor_tensor(gl[:, :n], ly[:, :n], ly[:, :n], ALU.mult), r=['ee'], w=['ob'])
                        cx.op('dve', lambda e: e.tensor_scalar(gl[:, :n], gl[:, :n], 0.044715, 1.0, ALU.mult, ALU.add),
                              r=['ob'], w=['ob'])
                        cx.op('dve', lambda e: e.tensor_tensor(gl[:, :n], gl[:, :n], ly[:, :n], ALU.mult),
                              r=['ob', 'ee'], w=['ob'])
                        cx.op('act', lambda e: e.activation(out=gl[:, :n], in_=gl[:, :n], func=AF.Sigmoid,
                                                            scale=1.5957691216057308), r=['ob'], w=['ob'])
                        cx.op('dve', lambda e: e.tensor_tensor(gl[:, :n], gl[:, :n], ly[:, :n], ALU.mult),
                              r=['ob', 'ee'], w=['ob'])
                        cx.op('dve', lambda e: e.tensor_tensor(ot[:, sl, 4 + ch, :n], hh[:, :n], gl[:, :n], ALU.mult),
                              r=['dd', 'ob'], w=[OT])
                cx.dma('pool', s.oT.rearrange("(c p) t -> p c t", p=128)[:, :, t0:t0 + n], ot[:, sl, :, :n],
                       r=[OT], w=['oT%d' % s.i])
            if ab:
                cx.dma('pool', s.hg_out[jl], S[:], r=['S'], w=['hg_out'])
                cx.dma('pool', s.lru_out[jl], hst[:], r=['hst'], w=['lru_out'])
                cx.dma('pool', s.lruc_out[jl], lhalo[:], r=['lhalo'], w=['lruc_out'])
        cx.barrier()


MLA_SCALE = (128 + 64) ** -0.5
CD_STAGE = int(os.environ.get('CD_STAGE', '99'))
CD_SUB = int(os.environ.get('CD_SUB', '99'))
M2_SUB = int(os.environ.get('M2_SUB', '99'))


def emit_cd(nc, cx, ps, PS, l, seqs, mod, gsc, ng, ones_f, rms_stats, W):
    jl = l // 2
    RT = lambda a: a.rearrange("(c p) t -> p c t", p=128)
    with ExitStack() as st:
        def T(name, shape, dt):
            return st.enter_context(SB(nc, name, shape, dt))
        xt = T("xt", [128, 8, 512], F32)
        sq = T("sq", [128, 8, 512], F32)
        rstd = T("rstd", [128, 512], F32)
        tmp = T("tmp", [128, 512], F32)
        h = T("h", [128, 8, 512], BF16)
        ot = T("otile", [128, 2, 4, 512], BF16)
        wst = T("wst", [128, 2, 1536], F32)
        win = T("win", [128, 8, 3072], BF16)
        cst = T("cst", [128, 13, 128], F32)
        identb = T("identb", [128, 128], BF16)
        onesr = T("onesr", [128, 512], F32)
        pre = T("pre", [128, 3, 515], F32)
        cvx = T("cvx", [128, 3, 512], F32)
        ghalo = T("ghalo", [128, 12, 3], F32)
        gcw = T("gcw", [128, 12, 4], F32)
        gvec = T("gvec", [128, 4, 2], F32)
        nexpa = T("nexpa", [128, 4], F32)
        gdg = T("gdg", [128, 1], F32)
        c21 = T("c21", [128, 512], F32)
        c22 = T("c22", [128, 512], F32)
        c23 = T("c23", [128, 512], F32)
        gz = T("gz", [128, 512], F32)
        betaB = T("betaB", [128, 512], F32)
        gg = T("gg", [128, 513], F32)
        egr = T("egr", [128, 512], F32)
        ela = T("ela", [128, 512], F32)
        edl = T("edl", [128, 8], F32)
        w1 = T("w1", [128, 512], F32)
        w2 = T("w2", [128, 512], F32)
        vbet = T("vbet", [128, 512], F32)
        kbe = T("kbe", [128, 512], F32)
        kd = T("kd", [128, 512], F32)
        qbf = T("qbf", [128, 512], BF16)
        kbf = T("kbf", [128, 512], BF16)
        qg = T("qg", [128, 512], BF16)
        ob = T("ob", [128, 512], F32)
        gcol = T("gcol", [128, 1], F32)
        DBe = T("DBe", [128, 128], F32)
        XY = T("XY", [128, 2, 2, 128], F32)
        Rm = T("Rm", [128, 256], F32)
        qkT = T("qkT", [128, 128], BF16)
        kdtok = T("kdtok", [128, 128], BF16)
        wkT = T("wkT", [128, 128], BF16)
        utok = T("utok", [128, 128], BF16)
        S = T("S", [128, 4, 128], F32)
        Sb = T("Sb", [128, 4, 128], BF16)
        PQ = T("PQ", [128, 3, 512], F32)
        qng = T("qng", [128, 3], F32)
        kvg = T("kvg", [128, 2], F32)
        qan = T("qan", [128, 3, 512], BF16)
        ckv = T("ckv", [128, 2, 512], F32)
        tokst = T("tokst", [128, 4, 320], F32)
        rope = T("rope", [64, 2, 512], F32)
        krr = T("krr", [64, 512], F32)
        for c2 in range(16):
            c, hf = c2 // 2, c2 % 2
            sl = hf
            cx.dma('sp', wst[:, sl], W['cd_w_in'][jl][c * 128:(c + 1) * 128, hf * 1536:(hf + 1) * 1536],
                   w=['wst%d' % sl])
            cx.op('act' if sl else 'dve',
                  (lambda e: e.activation(out=win[:, c, hf * 1536:(hf + 1) * 1536], in_=wst[:, sl], func=AF.Copy))
                  if sl else (lambda e: e.tensor_copy(win[:, c, hf * 1536:(hf + 1) * 1536], wst[:, sl])),
                  r=['wst%d' % sl], w=['win'])
        cx.dma('sp', cst[:], W['consts'][:, :, :], w=['cst'])
        cx.op('dve', lambda e: e.tensor_copy(identb[:], cst[:, 0, :]), r=['cst'], w=['identb'])
        cx.op('dve', lambda e: e.memset(onesr[:], 1.0), w=['onesr'])
        cx.dma('sp', gcw[:], W['gd_cw'][jl], w=['gcw'])
        cx.dma('sp', gvec[:], W['gd_vec'][jl], w=['gvec'])
        cx.dma('sp', gdg[:], W['gd_g'][jl], w=['gdg'])
        cx.dma('sp', qng[:], W['q_g'][jl], w=['qng'])
        cx.dma('sp', kvg[:], W['kv_g'][jl], w=['kvg'])
        cx.op('act', lambda e: e.activation(out=nexpa[:], in_=gvec[:, :, 0], func=AF.Exp), r=['gvec'], w=['nexpa'])
        cx.op('dve', lambda e: e.tensor_scalar(nexpa[:], nexpa[:], -1.0, None, ALU.mult), r=['nexpa'], w=['nexpa'])
        k = 0
        for s in seqs:
            L = 64 if s.T % 64 == 0 else s.T
            nsteps = int(round(math.log2(L)))
            past = s.past
            if s.i == 0:
                cx.op('dve', lambda e: e.memset(S[:], 0.0), w=['S'])
                cx.op('dve', lambda e: e.memset(ghalo[:], 0.0), w=['ghalo'])
            else:
                cx.dma('sp', S[:], s.gd_in[jl], w=['S'])
                cx.dma('sp', ghalo[:], s.gdc_in[jl], w=['ghalo'])
                cx.dma('pool', s.ckvT[:, 0:past], s.lat_inT[jl], w=['ckvT%d' % s.i])
                cx.dma('pool', s.krT[:, 0:past], s.kr_inT[jl], w=['krT%d' % s.i])
            cx.op('act', lambda e: e.activation(out=Sb[:], in_=S[:], func=AF.Copy), r=['S'], w=['Sb'])
            for (t0, n) in s.tiles:
                sl = k % 2
                k += 1
                OT = 'ot%d' % sl
                cx.dma('sp', xt[:, :, :n], RT(s.xT)[:, :, t0:t0 + n], r=['xT%d' % s.i], w=['xt'])
                cx.dma('sp', rope[:, 0, :n], s.ropeC[:, t0:t0 + n], w=['rope'])
                cx.dma('sp', rope[:, 1, :n], s.ropeS[:, t0:t0 + n], w=['rope'])
                rms_stats(lambda c: xt[:, c, :n], n, sq, rstd, 6, ['xt'])
                for c in range(8):
                    cx.op('dve', lambda e: e.tensor_tensor(tmp[:, :n], xt[:, c, :n], rstd[:, :n], ALU.mult),
                          r=['xt', 'rstd'], w=['tmp'])
                    cx.op('act', lambda e: e.activation(out=h[:, c, :n], in_=tmp[:, :n], func=AF.Identity,
                                                        scale=gsc[:, 0, c, s.i:s.i + 1], bias=mod[:, c, s.i:s.i + 1]),
                          r=['tmp', 'gsc', 'mod'], w=['h'])

                def proj(ch, pn):
                    for c in range(8):
                        cx.op('pe', lambda e, c=c: e.matmul(ps[pn][:, :n], win[:, c, ch * 128:(ch + 1) * 128],
                                                             h[:, c, :n], start=(c == 0), stop=(c == 7)),
                              r=['win', 'h'], w=[PS[pn]])

                def pcopy(ch, pn, dst, DST):
                    proj(ch, pn)
                    cx.op('act', lambda e: e.activation(out=dst, in_=ps[pn][:, :n], func=AF.Copy), r=[PS[pn]], w=[DST])

                def stat128(src, SRC, dst, DST, scl, bias_eps):
                    cx.op('act', lambda e: e.activation(out=w2[:, :n], in_=src, func=AF.Square), r=[SRC], w=['w2'])
                    cx.op('pe', lambda e: e.matmul(ps[6][:, :n], ones_f[:], w2[:, :n], start=True, stop=True),
                          r=['w2', 'ones_f'], w=[PS[6]])
                    cx.op('dve', lambda e: e.tensor_scalar(dst, ps[6][:, :n], scl, bias_eps, ALU.mult, ALU.add),
                          r=[PS[6]], w=[DST])
                    cx.op('dve', lambda e: e.reciprocal(dst, dst), r=[DST], w=[DST])
                    cx.op('act', lambda e: e.activation(out=dst, in_=dst, func=AF.Sqrt), r=[DST], w=[DST])

                pcopy(21, 0, c21[:, :n], 'c21')
                pcopy(22, 1, c22[:, :n], 'c22')
                pcopy(23, 0, c23[:, :n], 'c23')
                for c in range(3):
                    pcopy(16 + c, c % 2, PQ[:, c, :n], 'PQ')
                for c in range(3):
                    cx.op('act', lambda e: e.activation(out=sq[:, c, :n], in_=PQ[:, c, :n], func=AF.Square), r=['PQ'], w=['sq'])
                for c in range(3):
                    cx.op('pe', lambda e: e.matmul(ps[6][:, :n], ones_f[:], sq[:, c, :n], start=(c == 0), stop=(c == 2)),
                          r=['sq', 'ones_f'], w=[PS[6]])
                cx.op('dve', lambda e: e.tensor_scalar(w1[:, :n], ps[6][:, :n], 1.0 / 384, EPS, ALU.mult, ALU.add),
                      r=[PS[6]], w=['w1'])
                cx.op('dve', lambda e: e.reciprocal(w1[:, :n], w1[:, :n]), r=['w1'], w=['w1'])
                cx.op('act', lambda e: e.activation(out=w1[:, :n], in_=w1[:, :n], func=AF.Sqrt), r=['w1'], w=['w1'])
                for c in range(3):
                    cx.op('dve', lambda e: e.tensor_tensor(PQ[:, c, :n], PQ[:, c, :n], w1[:, :n], ALU.mult),
                          r=['PQ', 'w1'], w=['PQ'])
                    cx.op('dve', lambda e: e.tensor_scalar(qan[:, c, :n], PQ[:, c, :n], qng[:, c:c + 1], None, ALU.mult),
                          r=['PQ', 'qng'], w=['qan'])
                cx.dma('pool', RT(s.qanT)[:, :, t0:t0 + n], qan[:, :, :n], r=['qan'], w=['qanT%d' % s.i])
                for c in range(2):
                    pcopy(19 + c, c % 2, PQ[:, c, :n], 'PQ')
                for c in range(2):
                    cx.op('act', lambda e: e.activation(out=sq[:, c, :n], in_=PQ[:, c, :n], func=AF.Square), r=['PQ'], w=['sq'])
                for c in range(2):
                    cx.op('pe', lambda e: e.matmul(ps[6][:, :n], ones_f[:], sq[:, c, :n], start=(c == 0), stop=(c == 1)),
                          r=['sq', 'ones_f'], w=[PS[6]])
                cx.op('dve', lambda e: e.tensor_scalar(w1[:, :n], ps[6][:, :n], 1.0 / 256, EPS, ALU.mult, ALU.add),
                      r=[PS[6]], w=['w1'])
                cx.op('dve', lambda e: e.reciprocal(w1[:, :n], w1[:, :n]), r=['w1'], w=['w1'])
                cx.op('act', lambda e: e.activation(out=w1[:, :n], in_=w1[:, :n], func=AF.Sqrt), r=['w1'], w=['w1'])
                for c in range(2):
                    cx.op('dve', lambda e: e.tensor_tensor(PQ[:, c, :n], PQ[:, c, :n], w1[:, :n], ALU.mult),
                          r=['PQ', 'w1'], w=['PQ'])
                    cx.op('dve', lambda e: e.tensor_scalar(ckv[:, c, :n], PQ[:, c, :n], kvg[:, c:c + 1], None, ALU.mult),
                          r=['PQ', 'kvg'], w=['ckv'])
                cx.dma('pool', RT(s.ckvT)[:, :, past + t0:past + t0 + n], ckv[:, :, :n], r=['ckv'], w=['ckvT%d' % s.i])
                cx.op('dve', lambda e: e.tensor_tensor(krr[:, :n], c21[0:64, :n], rope[:, 0, :n], ALU.mult),
                      r=['c21', 'rope'], w=['krr'])
                cx.op('dve', lambda e: e.tensor_tensor(w1[0:64, :n], c23[0:64, :n], rope[:, 1, :n], ALU.mult),
                      r=['c23', 'rope'], w=['w1'])
                cx.op('dve', lambda e: e.tensor_tensor(krr[:, :n], krr[:, :n], w1[0:64, :n], ALU.add),
                      r=['krr', 'w1'], w=['krr'])
                cx.dma('pool', s.krT[:, past + t0:past + t0 + n], krr[:, :n], r=['krr'], w=['krT%d' % s.i])
                nsub = (n + 127) // 128
                for sb_ in range(nsub):
                    a0 = sb_ * 128
                    an = min(128, n - a0)
                    for c in range(2):
                        cx.op('pe', lambda e: e.transpose(ps[4][:an, c * 128:(c + 1) * 128], ckv[:, c, a0:a0 + an], cst[:, 0, :]),
                              r=['ckv', 'cst'], w=[PS[4]])
                    cx.op('pe', lambda e: e.transpose(ps[4][:an, 256:320], krr[:, a0:a0 + an], cst[0:64, 0, 0:64]),
                          r=['krr', 'cst'], w=[PS[4]])
                    cx.op('act', lambda e: e.activation(out=tokst[:an, sb_, :], in_=ps[4][:an, 0:320], func=AF.Copy),
                          r=[PS[4]], w=['tokst'])
                    cx.dma('pool', s.lat_out[jl][t0 + a0:t0 + a0 + an, :], tokst[:an, sb_, 0:256], r=['tokst'], w=['lat_out'])
                    cx.dma('pool', s.kr_out[jl][t0 + a0:t0 + a0 + an, :], tokst[:an, sb_, 256:320], r=['tokst'], w=['kr_out'])
                nch = n // L
                G = min(128, n)
                mI, mS = (1, 4) if L == 64 else (2, 3)
                for hd in range(4 if CD_STAGE >= 2 else 0):
                    for idx, ch in enumerate((hd, 4 + hd, 8 + hd)):
                        proj(ch, idx % 2)
                        cx.op('act', lambda e: e.activation(out=pre[:, idx, 0:3], in_=ghalo[:, ch, :], func=AF.Copy),
                              r=['ghalo'], w=['pre'])
                        cx.op('act', lambda e: e.activation(out=pre[:, idx, 3:3 + n], in_=ps[idx % 2][:, :n], func=AF.Copy),
                              r=[PS[idx % 2]], w=['pre'])
                        cx.op('act', lambda e: e.activation(out=ghalo[:, ch, :], in_=pre[:, idx, n:n + 3], func=AF.Copy),
                              r=['pre'], w=['ghalo'])
                        cx.op('dve', lambda e: e.tensor_scalar(cvx[:, idx, :n], pre[:, idx, 0:n], gcw[:, ch, 0:1], None, ALU.mult),
                              r=['pre', 'gcw'], w=['cvx'])
                        for tp in range(1, 4):
                            cx.op('dve', lambda e: e.scalar_tensor_tensor(cvx[:, idx, :n], pre[:, idx, tp:tp + n],
                                                                          gcw[:, ch, tp:tp + 1], cvx[:, idx, :n], ALU.mult, ALU.add),
                                  r=['pre', 'gcw', 'cvx'], w=['cvx'])
                        cx.op('act', lambda e: e.activation(out=cvx[:, idx, :n], in_=cvx[:, idx, :n], func=AF.Silu),
                              r=['cvx'], w=['cvx'])
                    pcopy(12 + hd, 0, gz[:, :n], 'gz')
                    stat128(cvx[:, 0, :n], 'cvx', w1[:, :n], 'w1', 1.0, EPS)
                    cx.op('dve', lambda e: e.scalar_tensor_tensor(cvx[:, 0, :n], cvx[:, 0, :n], 128 ** -0.5, w1[:, :n],
                                                                  ALU.mult, ALU.mult), r=['cvx', 'w1'], w=['cvx'])
                    stat128(cvx[:, 1, :n], 'cvx', w1[:, :n], 'w1', 1.0, EPS)
                    cx.op('dve', lambda e: e.tensor_tensor(cvx[:, 1, :n], cvx[:, 1, :n], w1[:, :n], ALU.mult),
                          r=['cvx', 'w1'], w=['cvx'])
                    cx.op('act', lambda e: e.activation(out=qbf[:, :n], in_=cvx[:, 0, :n], func=AF.Copy), r=['cvx'], w=['qbf'])
                    cx.op('act', lambda e: e.activation(out=kbf[:, :n], in_=cvx[:, 1, :n], func=AF.Copy), r=['cvx'], w=['kbf'])
                    cx.op('pe', lambda e: e.matmul(ps[2][:, :n], cst[:, 5 + hd, :], c21[:, :n], start=True, stop=True),
                          r=['cst', 'c21'], w=[PS[2]])
                    cx.op('act', lambda e: e.activation(out=betaB[:, :n], in_=ps[2][:, :n], func=AF.Sigmoid),
                          r=[PS[2]], w=['betaB'])
                    cx.op('pe', lambda e: e.matmul(ps[3][:, :n], cst[:, 9 + hd, :], c22[:, :n], start=True, stop=True),
                          r=['cst', 'c22'], w=[PS[3]])
                    cx.op('dve', lambda e: e.tensor_scalar(w1[:, :n], ps[3][:, :n], gvec[:, hd, 1:2], None, ALU.add),
                          r=[PS[3], 'gvec'], w=['w1'])
                    cx.op('act', lambda e: e.activation(out=w2[:, :n], in_=w1[:, :n], func=AF.Abs), r=['w1'], w=['w2'])
                    cx.op('act', lambda e: e.activation(out=w2[:, :n], in_=w2[:, :n], func=AF.Exp, scale=-1.0), r=['w2'], w=['w2'])
                    cx.op('dve', lambda e: e.tensor_scalar(w2[:, :n], w2[:, :n], 1.0, None, ALU.add), r=['w2'], w=['w2'])
                    cx.op('act', lambda e: e.activation(out=w2[:, :n], in_=w2[:, :n], func=AF.Ln), r=['w2'], w=['w2'])
                    cx.op('dve', lambda e: e.scalar_tensor_tensor(w1[:, :n], w1[:, :n], 0.0, w2[:, :n], ALU.max, ALU.add),
                          r=['w1', 'w2'], w=['w1'])
                    cx.op('dve', lambda e: e.tensor_scalar(w1[:, :n], w1[:, :n], nexpa[:, hd:hd + 1], None, ALU.mult),
                          r=['w1', 'nexpa'], w=['w1'])
                    cx.op('dve', lambda e: e.memset(gg[:, 0:1], 0.0), w=['gg'])
                    cx.op('dve', lambda e: e.tensor_tensor_scan(gg[:, 1:1 + n], onesr[:, :n], w1[:, :n], 0.0, ALU.mult, ALU.add),
                          r=['onesr', 'w1'], w=['gg'])
                    b3 = gg[:, 1:1 + n].rearrange("p (c l) -> p c l", l=L)
                    last3 = b3[:, :, L - 1:L].to_broadcast([128, nch, L])
                    prev3 = gg[:, 0:n].rearrange("p (c l) -> p c l", l=L)[:, :, 0:1].to_broadcast([128, nch, L])
                    cx.op('dve', lambda e: e.tensor_tensor(egr[:, :n].rearrange("p (c l) -> p c l", l=L), b3, prev3, ALU.subtract),
                          r=['gg'], w=['egr'])
                    cx.op('act', lambda e: e.activation(out=egr[:, :n], in_=egr[:, :n], func=AF.Exp), r=['egr'], w=['egr'])
                    cx.op('dve', lambda e: e.tensor_tensor(ela[:, :n].rearrange("p (c l) -> p c l", l=L), b3, last3, ALU.subtract),
                          r=['gg'], w=['ela'])
                    cx.op('act', lambda e: e.activation(out=ela[:, :n], in_=ela[:, :n], func=AF.Exp, scale=-1.0), r=['ela'], w=['ela'])
                    cx.op('dve', lambda e: e.tensor_tensor(
                        edl[:, :nch], gg[:, 1:1 + n].rearrange("p (c l) -> p c l", l=L)[:, :, L - 1],
                        gg[:, 0:n].rearrange("p (c l) -> p c l", l=L)[:, :, 0], ALU.subtract), r=['gg'], w=['edl'])
                    cx.op('act', lambda e: e.activation(out=edl[:, :nch], in_=edl[:, :nch], func=AF.Exp), r=['edl'], w=['edl'])
                    cx.op('dve', lambda e: e.tensor_tensor(vbet[:, :n], cvx[:, 2, :n], betaB[:, :n], ALU.mult),
                          r=['cvx', 'betaB'], w=['vbet'])
                    cx.op('dve', lambda e: e.tensor_tensor(kbe[:, :n], cvx[:, 1, :n], betaB[:, :n], ALU.mult),
                          r=['cvx', 'betaB'], w=['kbe'])
                    cx.op('dve', lambda e: e.tensor_tensor(kbe[:, :n], kbe[:, :n], egr[:, :n], ALU.mult), r=['kbe', 'egr'], w=['kbe'])
                    cx.op('dve', lambda e: e.tensor_tensor(kd[:, :n], cvx[:, 1, :n], ela[:, :n], ALU.mult), r=['cvx', 'ela'], w=['kd'])
                    cx.op('dve', lambda e: e.tensor_tensor(qg[:, :n], cvx[:, 0, :n], egr[:, :n], ALU.mult), r=['cvx', 'egr'], w=['qg'])
                    for g0 in range(0, n if CD_SUB >= 2 else 0, G):
                        gs = slice(g0, g0 + G)
                        cx.op('pe', lambda e: e.transpose(ps[4][:G, 0:128], gg[:, 1 + g0:1 + g0 + G], cst[:, 0, :]),
                              r=['gg', 'cst'], w=[PS[4]])
                        cx.op('act', lambda e: e.activation(out=gcol[:G, :], in_=ps[4][:G, 0:1], func=AF.Copy), r=[PS[4]], w=['gcol'])
                        cx.op('dve', lambda e: e.tensor_scalar(DBe[:G, :G], gg[:G, 1 + g0:1 + g0 + G], gcol[:G, 0:1], 0.0,
                                                               ALU.subtract, ALU.min), r=['gg', 'gcol'], w=['DBe'])
                        cx.op('act', lambda e: e.activation(out=DBe[:G, :G], in_=DBe[:G, :G], func=AF.Exp), r=['DBe'], w=['DBe'])
                        cx.op('pe', lambda e: e.matmul(ps[2][:G, :G], kbf[:, gs], kbf[:, gs], start=True, stop=True), r=['kbf'], w=[PS[2]])
                        X0, Y0 = XY[:G, 0, 0, :G], XY[:G, 0, 1, :G]
                        cx.op('dve', lambda e: e.scalar_tensor_tensor(X0, ps[2][:G, :G], -1.0, DBe[:G, :G], ALU.mult, ALU.mult),
                              r=[PS[2], 'DBe'], w=['XY0'])
                        cx.op('dve', lambda e: e.tensor_tensor(X0, X0, betaB[:G, gs], ALU.mult), r=['XY0', 'betaB'], w=['XY0'])
                        cx.op('dve', lambda e: e.tensor_tensor(X0, X0, cst[:G, mS, :G], ALU.mult), r=['XY0', 'cst'], w=['XY0'])
                        cx.op('pe', lambda e: e.transpose(ps[5][:G, :G], X0, cst[:G, 0, :G]), r=['XY0', 'cst'], w=[PS[5]])
                        cx.op('act', lambda e: e.activation(out=Y0, in_=ps[5][:G, :G], func=AF.Copy), r=[PS[5]], w=['XY0'])
                        cx.op('pe', lambda e: e.matmul(ps[3][:G, :G], kbf[:, gs], qbf[:, gs], start=True, stop=True),
                              r=['kbf', 'qbf'], w=[PS[3]])
                        cx.op('dve', lambda e: e.tensor_tensor(DBe[:G, :G], DBe[:G, :G], cst[:G, mI, :G], ALU.mult), r=['DBe', 'cst'], w=['DBe'])
                        cx.op('dve', lambda e: e.tensor_tensor(qkT[:G, :G], ps[3][:G, :G], DBe[:G, :G], ALU.mult), r=[PS[3], 'DBe'], w=['qkT'])
                        cx.op('pe', lambda e: e.transpose(ps[4][:G, 0:128], vbet[:, gs], cst[:, 0, :]), r=['vbet', 'cst'], w=[PS[4]])
                        cx.op('pe', lambda e: e.transpose(ps[4][:G, 128:256], kbe[:, gs], cst[:, 0, :]), r=['kbe', 'cst'], w=[PS[4]])
                        cx.op('pe', lambda e: e.transpose(ps[4][:G, 256:384], kd[:, gs], cst[:, 0, :]), r=['kd', 'cst'], w=[PS[4]])
                        cx.op('act', lambda e: e.activation(out=Rm[:G, :], in_=ps[4][:G, 0:256], func=AF.Copy), r=[PS[4]], w=['Rm'])
                        cx.op('act', lambda e: e.activation(out=kdtok[:G, :], in_=ps[4][:G, 256:384], func=AF.Copy), r=[PS[4]], w=['kdtok'])
                        for kk_ in range(nsteps if CD_SUB >= 3 else 0):
                            pg = kk_ % 2
                            Xc, Yc = XY[:G, pg, 0, :G], XY[:G, pg, 1, :G]
                            cx.op('pe', lambda e: e.matmul(ps[2][:G, 0:256], Xc, Rm[:G, :], start=True, stop=True),
                                  r=['XY%d' % pg, 'Rm'], w=[PS[2]])
                            cx.op('dve', lambda e: e.tensor_tensor(Rm[:G, :], Rm[:G, :], ps[2][:G, 0:256], ALU.add),
                                  r=['Rm', PS[2]], w=['Rm'])
                            if kk_ < nsteps - 1:
                                Xn, Yn = XY[:G, 1 - pg, 0, :G], XY[:G, 1 - pg, 1, :G]
                                cx.op('pe', lambda e: e.matmul(ps[3][:G, 0:G], Yc, Xc, start=True, stop=True), r=['XY%d' % pg], w=[PS[3]])
                                cx.op('pe', lambda e: e.matmul(ps[5][:G, 0:G], Xc, Yc, start=True, stop=True), r=['XY%d' % pg], w=[PS[5]])
                                cx.op('act', lambda e: e.activation(out=Xn, in_=ps[3][:G, 0:G], func=AF.Copy), r=[PS[3]], w=['XY%d' % (1 - pg)])
                                cx.op('dve', lambda e: e.tensor_copy(Yn, ps[5][:G, 0:G]), r=[PS[5]], w=['XY%d' % (1 - pg)])
                        cx.op('pe', lambda e: e.transpose(ps[4][:, 0:G], Rm[:G, 128:256], cst[:G, 0, :G]), r=['Rm', 'cst'], w=[PS[4]])
                        cx.op('act', lambda e: e.activation(out=wkT[:, :G], in_=ps[4][:, 0:G], func=AF.Copy), r=[PS[4]], w=['wkT'])
                        for ci in range(G // L if CD_SUB >= 4 else 0):
                            c0 = ci * L
                            cs_ = slice(c0, c0 + L)
                            cidx = (g0 + c0) // L
                            cx.op('pe', lambda e: e.matmul(ps[5][:G, 0:128], wkT[:, :G], Sb[:, hd, :], start=True, stop=True),
                                  r=['wkT', 'Sb'], w=[PS[5]])
                            cx.op('dve', lambda e: e.tensor_tensor(utok[cs_, :], Rm[cs_, 0:128], ps[5][cs_, 0:128], ALU.subtract),
                                  r=['Rm', PS[5]], w=['utok'])
                            if CD_SUB < 5:
                                continue
                            cx.op('pe', lambda e: e.matmul(ps[7][:, cs_], utok[cs_, :], qkT[cs_, cs_], start=True, stop=False),
                                  r=['utok', 'qkT'], w=[PS[7]])
                            cx.op('pe', lambda e: e.matmul(ps[7][:, cs_], Sb[:, hd, :], qg[:, g0 + c0:g0 + c0 + L], start=False, stop=True),
                                  r=['Sb', 'qg'], w=[PS[7]])
                            if CD_SUB < 6:
                                continue
                            cx.op('pe', lambda e: e.matmul(ps[3][:, 0:128], kdtok[cs_, :], utok[cs_, :], start=True, stop=True),
                                  r=['kdtok', 'utok'], w=[PS[3]])
                            cx.op('dve', lambda e: e.scalar_tensor_tensor(S[:, hd, :], S[:, hd, :], edl[:, cidx:cidx + 1], ps[3][:, 0:128],
                                                                          ALU.mult, ALU.add), r=['S', 'edl', PS[3]], w=['S'])
                            cx.op('act', lambda e: e.activation(out=Sb[:, hd, :], in_=S[:, hd, :], func=AF.Copy), r=['S'], w=['Sb'])
                        cx.op('act', lambda e: e.activation(out=ob[:, gs], in_=ps[7][:, :G], func=AF.Copy), r=[PS[7]], w=['ob'])
                    stat128(ob[:, :n], 'ob', w1[:, :n], 'w1', 1.0 / 128, EPS)
                    cx.op('dve', lambda e: e.tensor_tensor(ob[:, :n], ob[:, :n], w1[:, :n], ALU.mult), r=['ob', 'w1'], w=['ob'])
                    cx.op('act', lambda e: e.activation(out=gz[:, :n], in_=gz[:, :n], func=AF.Silu), r=['gz'], w=['gz'])
                    cx.op('dve', lambda e: e.scalar_tensor_tensor(ot[:, sl, hd, :n], ob[:, :n], gdg[:, 0:1], gz[:, :n], ALU.mult, ALU.mult),
                          r=['ob', 'gdg', 'gz'], w=[OT])
                cx.dma('pool', RT(s.oT)[:, 0:4, t0:t0 + n], ot[:, sl, :, :n], r=[OT], w=['oT%d' % s.i])
            cx.dma('pool', s.gd_out[jl], S[:], r=['S'], w=['gd_out'])
            cx.dma('pool', s.gdc_out[jl], ghalo[:], r=['ghalo'], w=['gdc_out'])
        cx.barrier()

    if CD_STAGE < 3:
        return
    with ExitStack() as st:
        def T(name, shape, dt):
            return st.enter_context(SB(nc, name, shape, dt))
        TA = max(s.past + s.T for s in seqs)
        NT128 = (TA + 127) // 128
        wst = T("wst", [128, 3, 1024], F32)
        wqb = T("wqb", [128, 3, 1024], BF16)
        wkvb = T("wkvb", [128, 2, 1024], BF16)
        onesb = T("onesb", [128, 128], BF16)
        KT = T("KT", [128, TA], BF16)
        VT = T("VT", [128, NT128, 128], BF16)
        KR = T("KR", [64, TA], BF16)
        negm = T("negm", [1, 512], BF16)
        onerow = T("onerow", [1, 128], BF16)
        cst_ = T("cstf", [128, 512], F32)
        ckb = T("ckb", [128, 2, 512], BF16)
        ksq = T("ksq", [128, 512], F32)
        kmax = T("kmax", [128, 2], F32)
        qan = T("qan", [128, 2, 3, 512], BF16)
        rope = T("rope", [64, 2, 512], F32)
        Qn = T("Qn", [128, 512], BF16)
        Qf = T("Qf", [128, 512], F32)
        Qr = T("Qr", [64, 512], BF16)
        qr1 = T("qr1", [64, 512], F32)
        qr2 = T("qr2", [64, 512], F32)
        Pt = T("Pt", [128, 2, 512], BF16)
        rs = T("rs", [128, 512], F32)
        od = T("od", [128, 2, 512], BF16)
        for c in range(3):
            cx.dma('sp', wst[:, c, :], W['w_qb'][jl][c * 128:(c + 1) * 128, :], w=['wst'])
        cx.op('dve', lambda e: e.tensor_copy(wqb[:], wst[:]), r=['wst'], w=['wqb'])
        for c in range(2):
            cx.dma('sp', wst[:, c, :], W['w_kvb'][jl][c * 128:(c + 1) * 128, :], w=['wst'])
        cx.op('dve', lambda e: e.tensor_copy(wkvb[:], wst[:, 0:2, :]), r=['wst'], w=['wkvb'])
        cx.op('dve', lambda e: e.memset(onesb[:], 1.0), w=['onesb'])
        cx.op('dve', lambda e: e.memset(onerow[:], 1.0), w=['onerow'])
        kq = 0
        for s in seqs:
            past = s.past
            Tall = past + s.T
            chunked = (s.T % 64 == 0)
            ktiles = tiles_of(Tall, 512)
            for (k0, nk) in ktiles:
                cx.dma('sp', cst_[0:64, :nk], s.krT[:, k0:k0 + nk], r=['krT%d' % s.i], w=['cstf'])
                cx.op('act', lambda e: e.activation(out=KR[0:64, k0:k0 + nk], in_=cst_[0:64, :nk], func=AF.Copy), r=['cstf'], w=['KR'])
            for hd in range(4 if M2_SUB >= 2 else 0):
                cx.op('dve', lambda e: e.memset(kmax[:], 0.0), w=['kmax'])
                for (k0, nk) in ktiles:
                    cx.dma('sp', cst_[:, :nk], s.ckvT[0:128, k0:k0 + nk], r=['ckvT%d' % s.i], w=['cstf'])
                    cx.op('act', lambda e: e.activation(out=ckb[:, 0, :nk], in_=cst_[:, :nk], func=AF.Copy), r=['cstf'], w=['ckb'])
                    cx.dma('sp', cst_[:, :nk], s.ckvT[128:256, k0:k0 + nk], r=['ckvT%d' % s.i], w=['cstf'])
                    cx.op('act', lambda e: e.activation(out=ckb[:, 1, :nk], in_=cst_[:, :nk], func=AF.Copy), r=['cstf'], w=['ckb'])
                    for c in range(2):
                        cx.op('pe', lambda e: e.matmul(ps[4][:, :nk], wkvb[:, c, hd * 256:hd * 256 + 128], ckb[:, c, :nk],
                                                       start=(c == 0), stop=(c == 1)), r=['wkvb', 'ckb'], w=[PS[4]])
                    cx.op('act', lambda e: e.activation(out=KT[:, k0:k0 + nk], in_=ps[4][:, :nk], func=AF.Copy), r=[PS[4]], w=['KT'])
                    cx.op('act', lambda e: e.activation(out=ksq[:, :nk], in_=ps[4][:, :nk], func=AF.Square), r=[PS[4]], w=['ksq'])
                    cx.op('pe', lambda e: e.matmul(ps[5][:, :nk], ones_f[:], ksq[:, :nk], start=True, stop=False),
                          r=['ksq', 'ones_f'], w=[PS[5]])
                    cx.op('act', lambda e: e.activation(out=ksq[0:64, :nk], in_=KR[0:64, k0:k0 + nk], func=AF.Square), r=['KR', 'ksq'], w=['ksq'])
                    cx.op('pe', lambda e: e.matmul(ps[5][:, :nk], ones_f[0:64, :], ksq[0:64, :nk], start=False, stop=True),
                          r=['ksq', 'ones_f'], w=[PS[5]])
                    cx.op('dve', lambda e: e.tensor_reduce(kmax[:, 1:2], ps[5][:, :nk], AX.X, ALU.max), r=[PS[5]], w=['kmax'])
                    cx.op('dve', lambda e: e.tensor_tensor(kmax[:, 0:1], kmax[:, 0:1], kmax[:, 1:2], ALU.max), r=['kmax'], w=['kmax'])
                    for a0 in range(0, nk, 128):
                        an = min(128, nk - a0)
                        ti = (k0 + a0) // 128
                        for c in range(2):
                            cx.op('pe', lambda e: e.matmul(ps[6][:an, 0:128], ckb[:, c, a0:a0 + an],
                                                           wkvb[:, c, hd * 256 + 128:hd * 256 + 256], start=(c == 0), stop=(c == 1)),
                                  r=['ckb', 'wkvb'], w=[PS[6]])
                        cx.op('dve', lambda e: e.tensor_copy(VT[:an, ti, :], ps[6][:an, 0:128]), r=[PS[6]], w=['VT'])
                for (q0, nq) in (s.tiles if M2_SUB >= 3 else []):
                    sl = kq % 2
                    kq += 1
                    QA, OD, PTn = 'qan%d' % sl, 'od%d' % sl, None
                    cx.dma('sp', qan[:, sl, :, :nq], RT(s.qanT)[:, :, q0:q0 + nq], r=['qanT%d' % s.i], w=[QA])
                    cx.dma('sp', rope[:, 0, :nq], s.ropeC[:, q0:q0 + nq], w=['rope'])
                    cx.dma('sp', rope[:, 1, :nq], s.ropeS[:, q0:q0 + nq], w=['rope'])
                    for c in range(3):
                        cx.op('pe', lambda e: e.matmul(ps[4][:, :nq], wqb[:, c, hd * 256:hd * 256 + 128], qan[:, sl, c, :nq],
                                                       start=(c == 0), stop=(c == 2)), r=['wqb', QA], w=[PS[4]])
                    cx.op('act', lambda e: e.activation(out=Qf[:, :nq], in_=ps[4][:, :nq], func=AF.Copy, scale=MLA_SCALE), r=[PS[4]], w=['Qf'])
                    cx.op('dve', lambda e: e.tensor_copy(Qn[:, :nq], Qf[:, :nq]), r=['Qf'], w=['Qn'])
                    for c in range(3):
                        cx.op('pe', lambda e: e.matmul(ps[5][0:64, :nq], wqb[:, c, hd * 256 + 128:hd * 256 + 192], qan[:, sl, c, :nq],
                                                       start=(c == 0), stop=(c == 2)), r=['wqb', QA], w=[PS[5]])
                    for c in range(3):
                        cx.op('pe', lambda e: e.matmul(ps[6][0:64, :nq], wqb[:, c, hd * 256 + 192:hd * 256 + 256], qan[:, sl, c, :nq],
                                                       start=(c == 0), stop=(c == 2)), r=['wqb', QA], w=[PS[6]])
                    cx.op('dve', lambda e: e.tensor_tensor(qr1[:, :nq], ps[5][0:64, :nq], rope[:, 0, :nq], ALU.mult), r=[PS[5], 'rope'], w=['qr1'])
                    cx.op('dve', lambda e: e.tensor_tensor(qr2[:, :nq], ps[6][0:64, :nq], rope[:, 1, :nq], ALU.mult), r=[PS[6], 'rope'], w=['qr2'])
                    cx.op('dve', lambda e: e.scalar_tensor_tensor(qr1[:, :nq], qr1[:, :nq], 1.0, qr2[:, :nq], ALU.mult, ALU.add),
                          r=['qr1', 'qr2'], w=['qr1'])
                    cx.op('act', lambda e: e.activation(out=qr1[:, :nq], in_=qr1[:, :nq], func=AF.Copy, scale=MLA_SCALE), r=['qr1'], w=['qr1'])
                    cx.op('dve', lambda e: e.tensor_copy(Qr[0:64, :nq], qr1[:, :nq]), r=['qr1'], w=['Qr'])
                    if M2_SUB == 30:
                        continue
                    cx.op('act', lambda e: e.activation(out=Qf[:, :nq], in_=Qf[:, :nq], func=AF.Square), r=['Qf'], w=['Qf'])
                    cx.op('act', lambda e: e.activation(out=qr2[:, :nq], in_=qr1[:, :nq], func=AF.Square), r=['qr1'], w=['qr2'])
                    cx.op('pe', lambda e: e.matmul(ps[7][0:1, :nq], ones_f[:, 0:1], Qf[:, :nq], start=True, stop=False), r=['Qf', 'ones_f'], w=[PS[7]])
                    cx.op('pe', lambda e: e.matmul(ps[7][0:1, :nq], ones_f[0:64, 0:1], qr2[:, :nq], start=False, stop=True), r=['qr2', 'ones_f'], w=[PS[7]])
                    cx.op('dve', lambda e: e.tensor_scalar(rs[0:1, :nq], ps[7][0:1, :nq], kmax[0:1, 0:1], None, ALU.mult),
                          r=[PS[7], 'kmax'], w=['rs'])
                    cx.op('act', lambda e: e.activation(out=rs[0:1, :nq], in_=rs[0:1, :nq], func=AF.Sqrt), r=['rs'], w=['rs'])
                    cx.op('dve', lambda e: e.tensor_scalar(negm[0:1, :nq], rs[0:1, :nq], -1.0, None, ALU.mult), r=['rs'], w=['negm'])
                    if M2_SUB == 31:
                        continue
                    if chunked:
                        qb = q0 // 512
                        klist = [(kt, 0, False) for kt in range(4 * qb)] + [(4 * qb + j, 128 * j, True) for j in range((nq + 127) // 128)]
                    else:
                        klist = [(kt, 0, False) for kt in range((Tall + 127) // 128)]
                    if M2_SUB < 4 or M2_SUB in (32, 33, 34, 35):
                        klist = klist[:1]
                    nkl = len(klist)
                    for ki, (kt, qlo, diag) in enumerate(klist):
                        kn = min(128, Tall - kt * 128)
                        pn = ki % 2
                        PTn = 'Pt%d' % pn
                        ksl = slice(kt * 128, kt * 128 + kn)
                        cx.op('pe', lambda e: e.matmul(ps[pn][:kn, qlo:nq], KT[:, ksl], Qn[:, qlo:nq], start=True, stop=False),
                              r=['KT', 'Qn'], w=[PS[pn]])
                        cx.op('pe', lambda e: e.matmul(ps[pn][:kn, qlo:nq], KR[0:64, ksl], Qr[0:64, qlo:nq], start=False, stop=False),
                              r=['KR', 'Qr'], w=[PS[pn]])
                        cx.op('pe', lambda e: e.matmul(ps[pn][:kn, qlo:nq], onerow[0:1, 0:kn], negm[0:1, qlo:nq], start=False, stop=True),
                              r=['onerow', 'negm'], w=[PS[pn]])
                        cx.op('act', lambda e: e.activation(out=Pt[:kn, pn, qlo:nq], in_=ps[pn][:kn, qlo:nq], func=AF.Exp),
                              r=[PS[pn]], w=[PTn])
                        first, last = (ki == 0), (ki == nkl - 1)
                        if not diag:
                            parts = [(0, kn, qlo, nq)]
                        else:
                            parts = [(0, 64, qlo, min(qlo + 64, nq))]
                            if qlo + 64 < nq:
                                parts.append((0, 128, qlo + 64, nq))
                        for pi, (r0, r1, ca, cb) in enumerate(parts):
                            lastp = last and (pi == len(parts) - 1)
                            cx.op('pe', lambda e: e.matmul(ps[2][:, ca:cb], VT[r0:r1, kt, :], Pt[r0:r1, pn, ca:cb],
                                                           start=first, stop=lastp), r=['VT', PTn], w=[PS[2]])
                            cx.op('pe', lambda e: e.matmul(ps[3][:, ca:cb], onesb[r0:r1, :], Pt[r0:r1, pn, ca:cb],
                                                           start=first, stop=lastp), r=['onesb', PTn], w=[PS[3]])
                    if M2_SUB in (32, 33, 34, 35):
                        continue
                    cx.op('dve', lambda e: e.reciprocal(rs[:, :nq], ps[3][:, :nq]), r=[PS[3]], w=['rs'])
                    cx.op('dve', lambda e: e.tensor_tensor(od[:, sl, :nq], ps[2][:, :nq], rs[:, :nq], ALU.mult), r=[PS[2], 'rs'], w=[OD])
                    cx.dma('pool', s.oT[512 + hd * 128:512 + (hd + 1) * 128, q0:q0 + nq], od[:, sl, :nq], r=[OD], w=['oT%d' % s.i])
        cx.barrier()


def _pc(v, nchunk):
    sh = v.shape[:-1]
    return np.ascontiguousarray(np.moveaxis(v.reshape(sh + (nchunk, 128)), -1, -2))


def make_consts():
    c = np.zeros((128, 13, 128), np.float32)
    c[:, 0, :] = np.eye(128, dtype=np.float32)
    s_ = np.arange(128)[:, None]
    t_ = np.arange(128)[None, :]
    c[:, 1, :] = ((s_ // 64 == t_ // 64) & (s_ <= t_)).astype(np.float32)
    c[:, 2, :] = (s_ <= t_).astype(np.float32)
    c[:, 3, :] = (s_ < t_).astype(np.float32)
    c[:, 4, :] = ((s_ // 64 == t_ // 64) & (s_ < t_)).astype(np.float32)
    for h in range(4):
        c[64 + h, 5 + h, :] = 1.0
        c[h, 9 + h, :] = 1.0
    return c


def rope_tables(past, T):
    half = 32
    freqs = np.exp(np.float32(-math.log(10000.0)) * np.arange(half, dtype=np.float32) / np.float32(half)).astype(np.float32)
    pos = (past + np.arange(T)).astype(np.float32)
    ang = (pos[:, None] * freqs[None, :]).astype(np.float32)
    cos = np.cos(ang).astype(np.float32).T
    sin = np.sin(ang).astype(np.float32).T
    return (np.ascontiguousarray(np.concatenate([cos, cos], 0)),
            np.ascontiguousarray(np.concatenate([-sin, sin], 0)))


def make_core_inputs(inp, xs, cs, sidx, depth):
    nab = (depth + 1) // 2
    im = {}
    nseq = len(xs)
    for i in range(nseq):
        im["xT%d" % i] = np.ascontiguousarray(xs[i].T)
        if sidx[i] is not None:
            b = sidx[i]
            im["ffnc_i%d" % i] = np.ascontiguousarray(
                inp['state_ffn_conv'][:depth, b].reshape(depth, 2, 44, 128).transpose(0, 3, 2, 1))
            im["hg_i%d" % i] = np.ascontiguousarray(inp['state_hgrn'][:nab, b].transpose(0, 2, 1, 3))
            im["lru_i%d" % i] = _pc(inp['state_rglru'][:nab, b], 4)
            im["lruc_i%d" % i] = np.ascontiguousarray(
                inp['state_rglru_conv'][:nab, b].reshape(nab, 3, 4, 128).transpose(0, 3, 2, 1))
    im["cT"] = np.ascontiguousarray(np.asarray(cs).reshape(nseq, 8, 128).transpose(2, 1, 0))
    im["ada_w"] = np.ascontiguousarray(inp['ada_w'][:depth])
    im["ada_bT"] = _pc(inp['ada_b'][:depth], 48)
    im["norm_gT"] = np.ascontiguousarray(inp['norm_g'][:depth].reshape(depth, 4, 8, 128).transpose(0, 3, 1, 2))
    wo = np.zeros((depth, D, D), np.float32)
    for l in range(depth):
        wo[l] = inp['ab_w_out'][l // 2] if l % 2 == 0 else inp['cd_w_out'][l // 2]
    im["w_out"] = wo
    im["ffn_up"] = np.ascontiguousarray(inp['ffn_w_up'][:depth])
    im["ffn_cw"] = np.ascontiguousarray(inp['ffn_conv_w'][:depth].reshape(depth, 3, 44, 128).transpose(0, 3, 2, 1))
    im["ffn_dn"] = np.ascontiguousarray(inp['ffn_w_down'][:depth])
    im["consts"] = make_consts()
    im["ab_w_in"] = np.ascontiguousarray(inp['ab_w_in'][:nab])
    im["hg_lb"] = np.ascontiguousarray(inp['hgrn_lb_logits'].reshape(2, 4, 128).transpose(2, 0, 1))
    im["hg_g"] = np.ascontiguousarray(inp['hgrn_norm_g'][:nab].reshape(nab, 128, 1))
    im["lru_cw"] = np.ascontiguousarray(inp['lru_conv_w'][:nab].reshape(nab, 4, 4, 128).transpose(0, 3, 2, 1))
    vec = np.stack([inp['lru_conv_b'][:nab], inp['lru_b_a'][:nab], inp['lru_b_x'][:nab], inp['lru_lambda'][:nab]], -1)
    im["lru_vec"] = np.ascontiguousarray(vec.reshape(nab, 4, 128, 4).transpose(0, 2, 1, 3))
    gw = np.zeros((nab, 128, 2, 4, 128), np.float32)
    for k, nm in enumerate(('lru_w_a', 'lru_w_x')):
        w = inp[nm][:nab]
        for ch in range(4):
            for bb in range(2):
                gw[:, bb * 64:(bb + 1) * 64, k, ch, bb * 64:(bb + 1) * 64] = w[:, 2 * ch + bb]
    im["lru_gw"] = gw
    ncd = depth // 2
    if ncd > 0:
        src = inp['cd_w_in'][:ncd]
        w = np.zeros((ncd, D, 3072), np.float32)
        w[:, :, 0:2048] = src[:, :, 0:2048]
        w[:, :, 2048:2432] = src[:, :, 2056:2440]
        w[:, :, 2432:2688] = src[:, :, 2440:2696]
        w[:, :, 2688:2752] = src[:, :, 2696:2760]
        w[:, :, 2752:2756] = src[:, :, 2048:2052]
        w[:, :, 2816:2820] = src[:, :, 2052:2056]
        w[:, :, 2944:2976] = src[:, :, 2728:2760]
        w[:, :, 2976:3008] = src[:, :, 2696:2728]
        im["cd_w_in"] = w
        im["gd_cw"] = np.ascontiguousarray(inp['gdn_conv_w'][:ncd].reshape(ncd, 4, 12, 128).transpose(0, 3, 2, 1))
        gv = np.stack([inp['gdn_a_log'][:ncd], inp['gdn_dt_bias'][:ncd]], -1)
        im["gd_vec"] = np.ascontiguousarray(np.broadcast_to(gv[:, None], (ncd, 128, 4, 2)))
        im["gd_g"] = np.ascontiguousarray(inp['gdn_norm_g'][:ncd].reshape(ncd, 128, 1))
        im["q_g"] = _pc(inp['mla_q_norm_g'][:ncd], 3)
        im["kv_g"] = _pc(inp['mla_kv_norm_g'][:ncd], 2)
        wq = inp['mla_w_qb'][:ncd].reshape(ncd, 384, 4, 192)
        wqe = np.zeros((ncd, 384, 4, 256), np.float32)
        wqe[..., 0:192] = wq
        wqe[..., 192:224] = wq[..., 160:192]
        wqe[..., 224:256] = wq[..., 128:160]
        im["w_qb"] = np.ascontiguousarray(wqe.reshape(ncd, 384, 1024))
        im["w_kvb"] = np.ascontiguousarray(inp['mla_w_kvb'][:ncd])
        for i in range(nseq):
            T_ = xs[i].shape[0]
            past = 0 if sidx[i] is None else PAST
            im["ropeC%d" % i], im["ropeS%d" % i] = rope_tables(past, T_)
            if sidx[i] is not None:
                b = sidx[i]
                im["gd_i%d" % i] = np.ascontiguousarray(inp['state_gdn'][:ncd, b].transpose(0, 2, 1, 3))
                im["gdc_i%d" % i] = np.ascontiguousarray(
                    inp['state_gdn_conv'][:ncd, b].reshape(ncd, 3, 12, 128).transpose(0, 3, 2, 1))
                im["lat_iT%d" % i] = np.ascontiguousarray(inp['cache_mla_latent'][:ncd, b].transpose(0, 2, 1))
                im["kr_iT%d" % i] = np.ascontiguousarray(inp['cache_mla_krope'][:ncd, b].transpose(0, 2, 1))
    return im


_PROG = {}


def kernel(**inputs):
    inp = {k: np.asarray(v) for k, v in inputs.items()}
    xp, xsm = inp['x_prompt'], inp['x_sample']
    B, SEQ, _ = xp.shape
    NB, DT, _ = xsm.shape
    NS = NB // NCORES
    depth = DEPTH
    nab, ncd = (depth + 1) // 2, depth // 2
    key = (SEQ, DT, NS, depth)
    if key not in _PROG:
        _PROG[key] = build_program(SEQ, DT, NS, depth=depth)
    nc = _PROG[key]
    in_maps = []
    shared = None
    for c in range(NCORES):
        b = c * B // NCORES
        sidx = [None] + [c * NS + j for j in range(NS)]
        xs = [xp[b]] + [xsm[i] for i in sidx[1:]]
        cs = np.stack([inp['c_prompt'][b]] + [inp['c_sample'][i] for i in sidx[1:]])
        im = make_core_inputs(inp, xs, cs, sidx, depth)
        if shared is None:
            shared = im
        else:
            for k in ('ada_w', 'w_out', 'ffn_up', 'ffn_dn', 'ab_w_in', 'consts', 'cd_w_in', 'w_qb', 'w_kvb', 'ropeC0', 'ropeS0'):
                im[k] = shared[k]
        in_maps.append(im)
    res = run_bass_kernel_spmd(nc, in_maps, core_ids=list(range(NCORES))).results
    pc = [b * NCORES // B for b in range(B)]

    def un_ffnc(a):
        return a.transpose(0, 3, 2, 1).reshape(depth, 2, 2 * DFF)

    def un_hg(a):
        return a.transpose(0, 2, 1, 3)

    def un_lru(a):
        return a.transpose(0, 2, 1).reshape(nab, HALF)

    def un_gdc(a):
        return a.transpose(0, 3, 2, 1).reshape(ncd, 3, 3 * HALF)

    def un_lruc(a):
        return a.transpose(0, 3, 2, 1).reshape(nab, 3, HALF)

    def gather(fn, name, prompt):
        if prompt:
            return np.ascontiguousarray(np.stack([fn(res[pc[b]][name + "0"]) for b in range(B)], axis=1))
        return np.ascontiguousarray(np.stack(
            [fn(res[i // NS][name + str(1 + i % NS)]) for i in range(NB)], axis=1))

    y_prompt = np.ascontiguousarray(np.stack([res[pc[b]]["yT0"].T for b in range(B)]))
    y_sample = np.ascontiguousarray(np.stack([res[i // NS]["yT%d" % (1 + i % NS)].T for i in range(NB)]))
    outs = [y_prompt, y_sample]
    for prompt in (True, False):
        nb = B if prompt else NB
        tt = SEQ if prompt else DT
        outs += [
            gather(un_hg, "hg_o", prompt), gather(un_lru, "lru_o", prompt), gather(un_lruc, "lruc_o", prompt),
            gather(un_hg, "gd_o", prompt), gather(un_gdc, "gdc_o", prompt),
            gather(lambda a: a, "lat_o", prompt), gather(lambda a: a, "kr_o", prompt),
            gather(un_ffnc, "ffnc_o", prompt),
        ]
    return tuple(outs)
```

```python
import math
import os
from contextlib import ExitStack
import numpy as np
import concourse.bass as bass
import concourse.mybir as mybir
from concourse.bass_utils import run_bass_kernel_spmd

F32 = mybir.dt.float32
BF16 = mybir.dt.bfloat16
AF = mybir.ActivationFunctionType
ALU = mybir.AluOpType
AX = mybir.AxisListType

D = 1024
DEPTH = 4
HALF = 512
DFF = 2816
EPS = 1e-6
NCORES = 8
PAST = 2048
SAME_ENGINE_SYNC = os.environ.get('SES', '1') == '1'


class Ctx:
    def __init__(self, nc, es):
        self.nc = nc
        self.es = es
        self.eng = {'pe': nc.tensor, 'dve': nc.vector, 'act': nc.scalar, 'pool': nc.gpsimd, 'sp': nc.sync}
        self.sem = {}
        self.cnt = {}
        for e in self.eng:
            self.sem[e] = es.enter_context(nc.semaphore("sem_" + e))
            self.cnt[e] = 0
        self.ndma = 20
        self.dslots = {}
        for q in ('sp', 'pool'):
            self.dslots[q] = []
            for i in range(self.ndma):
                k = "d_%s_%d" % (q, i)
                self.sem[k] = es.enter_context(nc.semaphore(k))
                self.cnt[k] = 0
                self.dslots[q].append(k)
        self.dnext = {'sp': 0, 'pool': 0}
        self.waited = {e: {} for e in self.eng}
        self.lastw = {}
        self.readers = {}
        self.nins = 0

    def _wait(self, e, deps):
        best = {}
        for (k, v) in deps:
            if v > best.get(k, 0):
                best[k] = v
        for k, v in best.items():
            if k == e and (e == 'pe' or not SAME_ENGINE_SYNC):
                continue
            if self.waited[e].get(k, 0) >= v:
                continue
            self.eng[e].wait_ge(self.sem[k], v)
            self.waited[e][k] = v

    def _deps(self, r, w):
        deps = []
        for x in r:
            if x in self.lastw:
                deps.append(self.lastw[x])
        for x in w:
            if x in self.lastw:
                deps.append(self.lastw[x])
            rd = self.readers.get(x)
            if rd:
                deps.extend(rd.items())
        return deps

    def _record(self, tok, r, w):
        for x in r:
            rd = self.readers.setdefault(x, {})
            if tok[1] > rd.get(tok[0], 0):
                rd[tok[0]] = tok[1]
        for x in w:
            self.lastw[x] = tok
            self.readers[x] = {}

    def op(self, e, fn, r=(), w=()):
        self._wait(e, self._deps(r, w))
        ins = fn(self.eng[e])
        self.cnt[e] += 1
        ins.then_inc(self.sem[e], 1)
        self._record((e, self.cnt[e]), r, w)
        self.nins += 1

    def dma(self, q, out, in_, r=(), w=(), **kw):
        k = self.dslots[q][self.dnext[q]]
        self.dnext[q] = (self.dnext[q] + 1) % self.ndma
        deps = self._deps(r, w)
        if self.cnt[k] > 0:
            deps.append((k, self.cnt[k]))
        self._wait(q, deps)
        ins = self.eng[q].dma_start(out=out, in_=in_, **kw)
        self.cnt[k] += 16
        ins.then_inc(self.sem[k], 16)
        self._record((k, self.cnt[k]), r, w)
        self.nins += 1

    def barrier(self):
        allk = [(k, v) for k, v in self.cnt.items() if v > 0]
        for e in self.eng:
            self._wait(e, [kv for kv in allk if kv[0] != e])
        self.lastw = {}
        self.readers = {}

    def final_wait(self):
        allk = [(k, v) for k, v in self.cnt.items() if v > 0 and k != 'sp']
        self._wait('sp', allk)


def tiles_of(T, n=512):
    out = []
    t = 0
    while t < T:
        out.append((t, min(n, T - t)))
        t += n
    return out


class Seq:
    pass


_uid = [0]


def SB(nc, name, shape, dt):
    _uid[0] += 1
    return nc.sbuf_tensor("%s_u%d" % (name, _uid[0]), shape, dt)


def build_program(SEQ, DT, NS, depth=DEPTH, dbg=None):
    nc = bass.Bass("TRN2", target_bir_lowering=False)
    es = ExitStack()
    with es:
        cx = Ctx(nc, es)
        _emit(nc, es, cx, SEQ, DT, NS, depth, dbg)
        cx.final_wait()
        print("instructions:", cx.nins)
    return nc


def _emit(nc, es, cx, SEQ, DT, NS, depth, dbg):
    NSEQ = 1 + NS
    seqs = []
    for i in range(NSEQ):
        s = Seq()
        s.i = i
        s.T = SEQ if i == 0 else DT
        s.tiles = tiles_of(s.T)
        s.xin = nc.dram_tensor("xT%d" % i, [D, s.T], F32, kind="ExternalInput").ap()
        s.yout = nc.dram_tensor("yT%d" % i, [D, s.T], F32, kind="ExternalOutput").ap()
        s.ffnc_out = nc.dram_tensor("ffnc_o%d" % i, [depth, 128, 44, 2], F32, kind="ExternalOutput").ap()
        s.xT = nc.dram_tensor("s_xT%d" % i, [D, s.T], F32).ap()
        s.h2T = nc.dram_tensor("s_h2T%d" % i, [D, s.T], BF16).ap()
        s.oT = nc.dram_tensor("s_oT%d" % i, [D, s.T], BF16).ap()
        s.aT = nc.dram_tensor("s_aT%d" % i, [DFF, s.T], BF16).ap()
        nab = (depth + 1) // 2
        s.hg_out = nc.dram_tensor("hg_o%d" % i, [nab, 128, 4, 128], F32, kind="ExternalOutput").ap()
        s.lru_out = nc.dram_tensor("lru_o%d" % i, [nab, 128, 4], F32, kind="ExternalOutput").ap()
        s.lruc_out = nc.dram_tensor("lruc_o%d" % i, [nab, 128, 4, 3], F32, kind="ExternalOutput").ap()
        ncd = depth // 2
        s.past = 0 if i == 0 else PAST
        if ncd > 0:
            s.gd_out = nc.dram_tensor("gd_o%d" % i, [ncd, 128, 4, 128], F32, kind="ExternalOutput").ap()
            s.gdc_out = nc.dram_tensor("gdc_o%d" % i, [ncd, 128, 12, 3], F32, kind="ExternalOutput").ap()
            s.lat_out = nc.dram_tensor("lat_o%d" % i, [ncd, s.T, 256], F32, kind="ExternalOutput").ap()
            s.kr_out = nc.dram_tensor("kr_o%d" % i, [ncd, s.T, 64], F32, kind="ExternalOutput").ap()
            s.ropeC = nc.dram_tensor("ropeC%d" % i, [64, s.T], F32, kind="ExternalInput").ap()
            s.ropeS = nc.dram_tensor("ropeS%d" % i, [64, s.T], F32, kind="ExternalInput").ap()
            s.qanT = nc.dram_tensor("s_qanT%d" % i, [384, s.T], BF16).ap()
            s.ckvT = nc.dram_tensor("s_ckvT%d" % i, [256, s.past + s.T], F32).ap()
            s.krT = nc.dram_tensor("s_krT%d" % i, [64, s.past + s.T], F32).ap()
            if i > 0:
                s.gd_in = nc.dram_tensor("gd_i%d" % i, [ncd, 128, 4, 128], F32, kind="ExternalInput").ap()
                s.gdc_in = nc.dram_tensor("gdc_i%d" % i, [ncd, 128, 12, 3], F32, kind="ExternalInput").ap()
                s.lat_inT = nc.dram_tensor("lat_iT%d" % i, [ncd, 256, PAST], F32, kind="ExternalInput").ap()
                s.kr_inT = nc.dram_tensor("kr_iT%d" % i, [ncd, 64, PAST], F32, kind="ExternalInput").ap()
        if i > 0:
            s.ffnc_in = nc.dram_tensor("ffnc_i%d" % i, [depth, 128, 44, 2], F32, kind="ExternalInput").ap()
            s.hg_in = nc.dram_tensor("hg_i%d" % i, [nab, 128, 4, 128], F32, kind="ExternalInput").ap()
            s.lru_in = nc.dram_tensor("lru_i%d" % i, [nab, 128, 4], F32, kind="ExternalInput").ap()
            s.lruc_in = nc.dram_tensor("lruc_i%d" % i, [nab, 128, 4, 3], F32, kind="ExternalInput").ap()
        seqs.append(s)
    cT = nc.dram_tensor("cT", [128, 8, NSEQ], F32, kind="ExternalInput").ap()
    ada_w = nc.dram_tensor("ada_w", [depth, D, 6 * D], F32, kind="ExternalInput").ap()
    ada_bT = nc.dram_tensor("ada_bT", [depth, 128, 48], F32, kind="ExternalInput").ap()
    norm_gT = nc.dram_tensor("norm_gT", [depth, 128, 4, 8], F32, kind="ExternalInput").ap()
    w_out = nc.dram_tensor("w_out", [depth, D, D], F32, kind="ExternalInput").ap()
    ffn_up = nc.dram_tensor("ffn_up", [depth, D, 2 * DFF], F32, kind="ExternalInput").ap()
    ffn_cw = nc.dram_tensor("ffn_cw", [depth, 128, 44, 3], F32, kind="ExternalInput").ap()
    ffn_dn = nc.dram_tensor("ffn_dn", [depth, DFF, D], F32, kind="ExternalInput").ap()

    nab = (depth + 1) // 2
    W = {}
    W['consts'] = nc.dram_tensor("consts", [128, 13, 128], F32, kind="ExternalInput").ap()
    ncd = depth // 2
    if ncd > 0:
        W['cd_w_in'] = nc.dram_tensor("cd_w_in", [ncd, D, 3072], F32, kind="ExternalInput").ap()
        W['gd_cw'] = nc.dram_tensor("gd_cw", [ncd, 128, 12, 4], F32, kind="ExternalInput").ap()
        W['gd_vec'] = nc.dram_tensor("gd_vec", [ncd, 128, 4, 2], F32, kind="ExternalInput").ap()
        W['gd_g'] = nc.dram_tensor("gd_g", [ncd, 128, 1], F32, kind="ExternalInput").ap()
        W['q_g'] = nc.dram_tensor("q_g", [ncd, 128, 3], F32, kind="ExternalInput").ap()
        W['kv_g'] = nc.dram_tensor("kv_g", [ncd, 128, 2], F32, kind="ExternalInput").ap()
        W['w_qb'] = nc.dram_tensor("w_qb", [ncd, 384, 1024], F32, kind="ExternalInput").ap()
        W['w_kvb'] = nc.dram_tensor("w_kvb", [ncd, 256, 1024], F32, kind="ExternalInput").ap()
    W['ab_w_in'] = nc.dram_tensor("ab_w_in", [nab, D, 3072], F32, kind="ExternalInput").ap()
    W['hg_lb'] = nc.dram_tensor("hg_lb", [128, 2, 4], F32, kind="ExternalInput").ap()
    W['hg_g'] = nc.dram_tensor("hg_g", [nab, 128, 1], F32, kind="ExternalInput").ap()
    W['lru_cw'] = nc.dram_tensor("lru_cw", [nab, 128, 4, 4], F32, kind="ExternalInput").ap()
    W['lru_vec'] = nc.dram_tensor("lru_vec", [nab, 128, 4, 4], F32, kind="ExternalInput").ap()
    W['lru_gw'] = nc.dram_tensor("lru_gw", [nab, 128, 2, 4, 128], F32, kind="ExternalInput").ap()
    ps = [es.enter_context(nc.psum_tensor("ps%d" % i, [128, 512], F32)) for i in range(8)]
    PS = ["ps%d" % i for i in range(8)]
    ones_f = es.enter_context(nc.sbuf_tensor("ones_f", [128, 128], F32))
    mod = es.enter_context(nc.sbuf_tensor("mod", [128, 48, NSEQ], F32))
    ng = es.enter_context(nc.sbuf_tensor("ng", [128, 4, 8], F32))
    gsc = es.enter_context(nc.sbuf_tensor("gsc", [128, 2, 8, NSEQ], F32))
    csil = es.enter_context(nc.sbuf_tensor("csil", [128, 8, NSEQ], F32))
    cx.op('dve', lambda e: e.memset(ones_f[:], 1.0), w=['ones_f'])
    cx.dma('sp', csil[:], cT[:, :, :], w=['csil'])
    cx.op('act', lambda e: e.activation(out=csil[:], in_=csil[:], func=AF.Silu), r=['csil'], w=['csil'])

    with SB(nc, "cp", [128, 2, 8, 512], F32) as cp:
        k = 0
        for s in seqs:
            for (t0, n) in s.tiles:
                sl = k % 2
                k += 1
                cx.dma('sp', cp[:, sl, :, :n], s.xin.rearrange("(c p) t -> p c t", p=128)[:, :, t0:t0 + n],
                       w=['cp%d' % sl])
                cx.dma('pool', s.xT.rearrange("(c p) t -> p c t", p=128)[:, :, t0:t0 + n], cp[:, sl, :, :n],
                       r=['cp%d' % sl], w=['xT%d' % s.i])
        cx.barrier()

    def rms_stats(src, n, sq, rstd, psn, srcres, nfeat_chunks=8):
        for c in range(nfeat_chunks):
            cx.op('act', lambda e, c=c: e.activation(out=sq[:, c, :n], in_=src(c), func=AF.Square),
                  r=srcres, w=['sq'])
        for c in range(nfeat_chunks):
            cx.op('pe', lambda e, c=c: e.matmul(ps[psn][:, :n], ones_f[:], sq[:, c, :n],
                                                 start=(c == 0), stop=(c == nfeat_chunks - 1)),
                  r=['sq', 'ones_f'], w=[PS[psn]])
        cx.op('dve', lambda e: e.tensor_scalar(rstd[:, :n], ps[psn][:, :n], 1.0 / (128 * nfeat_chunks), EPS,
                                               ALU.mult, ALU.add), r=[PS[psn]], w=['rstd'])
        cx.op('dve', lambda e: e.reciprocal(rstd[:, :n], rstd[:, :n]), r=['rstd'], w=['rstd'])
        cx.op('act', lambda e: e.activation(out=rstd[:, :n], in_=rstd[:, :n], func=AF.Sqrt), r=['rstd'], w=['rstd'])

    for l in range(depth):
        with SB(nc, "aw", [128, 2, 8, 768], F32) as aw, SB(nc, "adb", [128, 48], F32) as adb:
            cx.dma('sp', adb[:], ada_bT[l], w=['adb'])
            cx.dma('sp', ng[:], norm_gT[l], w=['ng'])
            for g in range(8):
                sl = g % 2
                cx.dma('sp', aw[:, sl], ada_w[l].rearrange("(c p) f -> p c f", p=128)[:, :, g * 768:(g + 1) * 768],
                       w=['aw%d' % sl])
                for j in range(6):
                    fc = g * 6 + j
                    pn = fc % 2
                    for c in range(8):
                        cx.op('pe', lambda e, c=c, j=j, sl=sl, pn=pn: e.matmul(
                            ps[pn][:, :NSEQ], aw[:, sl, c, j * 128:(j + 1) * 128], csil[:, c, :],
                            start=(c == 0), stop=(c == 7)), r=['aw%d' % sl, 'csil'], w=[PS[pn]])
                    cx.op('dve', lambda e, fc=fc, pn=pn: e.tensor_scalar(
                        mod[:, fc, :], ps[pn][:, :NSEQ], adb[:, fc:fc + 1], None, ALU.add),
                        r=[PS[pn], 'adb'], w=['mod'])
            for k2, (gi, sc0) in enumerate(((0, 8), (2, 32))):
                for c in range(8):
                    cx.op('dve', lambda e, k2=k2, gi=gi, sc0=sc0, c=c: e.tensor_scalar(
                        gsc[:, k2, c, :], mod[:, sc0 + c, :], 1.0, ng[:, gi, c:c + 1], ALU.add, ALU.mult),
                        r=['mod', 'ng'], w=['gsc'])
            cx.barrier()

        if l % 2 == 0:
            emit_mixer(nc, cx, ps, PS, l, seqs, mod, gsc, ng, ones_f, rms_stats, W)
        else:
            emit_cd(nc, cx, ps, PS, l, seqs, mod, gsc, ng, ones_f, rms_stats, W)

        with (SB(nc, "wst", [128, 2, 8, 512], F32) as wst,
              SB(nc, "wo", [128, 8, D], BF16) as wo,
              SB(nc, "ot", [128, 2, 8, 512], BF16) as ot,
              SB(nc, "xt", [128, 2, 8, 512], F32) as xt,
              SB(nc, "yt", [128, 8, 512], F32) as yt,
              SB(nc, "sq", [128, 8, 512], F32) as sq,
              SB(nc, "rstd", [128, 512], F32) as rstd,
              SB(nc, "tmp", [128, 512], F32) as tmp,
              SB(nc, "h2", [128, 2, 8, 512], BF16) as h2):
            for g in range(2):
                cx.dma('sp', wst[:, g], w_out[l].rearrange("(c p) f -> p c f", p=128)[:, :, g * 512:(g + 1) * 512],
                       w=['wst%d' % g])
                cx.op('act', lambda e, g=g: e.activation(out=wo[:, :, g * 512:(g + 1) * 512], in_=wst[:, g],
                                                         func=AF.Copy), r=['wst%d' % g], w=['wo'])
            k = 0
            for s in seqs:
                for (t0, n) in s.tiles:
                    sl = k % 2
                    k += 1
                    XT, OT, H2 = 'xt%d' % sl, 'ot%d' % sl, 'h2%d' % sl
                    cx.dma('sp', ot[:, sl, :, :n], s.oT.rearrange("(c p) t -> p c t", p=128)[:, :, t0:t0 + n],
                           r=['oT%d' % s.i], w=[OT])
                    cx.dma('sp', xt[:, sl, :, :n], s.xT.rearrange("(c p) t -> p c t", p=128)[:, :, t0:t0 + n],
                           r=['xT%d' % s.i], w=[XT])
                    for m in range(8):
                        pn = m % 4
                        for c in range(8):
                            cx.op('pe', lambda e, m=m, c=c, pn=pn: e.matmul(
                                ps[pn][:, :n], wo[:, c, m * 128:(m + 1) * 128], ot[:, sl, c, :n],
                                start=(c == 0), stop=(c == 7)), r=['wo', OT], w=[PS[pn]])
                        cx.op('act', lambda e, m=m, pn=pn: e.activation(out=yt[:, m, :n], in_=ps[pn][:, :n],
                                                                        func=AF.Copy), r=[PS[pn]], w=['yt'])
                    rms_stats(lambda c: yt[:, c, :n], n, sq, rstd, 4, ['yt'])
                    for c in range(8):
                        cx.op('dve', lambda e, c=c: e.tensor_tensor(tmp[:, :n], yt[:, c, :n], rstd[:, :n], ALU.mult),
                              r=['yt', 'rstd'], w=['tmp'])
                        cx.op('dve', lambda e, c=c: e.tensor_scalar(
                            tmp[:, :n], tmp[:, :n], ng[:, 1, c:c + 1], mod[:, 16 + c, s.i:s.i + 1],
                            ALU.mult, ALU.mult), r=['tmp', 'ng', 'mod'], w=['tmp'])
                        cx.op('dve', lambda e, c=c: e.tensor_tensor(xt[:, sl, c, :n], xt[:, sl, c, :n], tmp[:, :n],
                                                                    ALU.add), r=['tmp', XT], w=[XT])
                    cx.dma('pool', s.xT.rearrange("(c p) t -> p c t", p=128)[:, :, t0:t0 + n], xt[:, sl, :, :n],
                           r=[XT], w=['xT%d' % s.i])
                    rms_stats(lambda c: xt[:, sl, c, :n], n, sq, rstd, 5, [XT])
                    for c in range(8):
                        cx.op('dve', lambda e, c=c: e.tensor_tensor(tmp[:, :n], xt[:, sl, c, :n], rstd[:, :n], ALU.mult),
                              r=[XT, 'rstd'], w=['tmp'])
                        cx.op('act', lambda e, c=c: e.activation(
                            out=h2[:, sl, c, :n], in_=tmp[:, :n], func=AF.Identity,
                            scale=gsc[:, 1, c, s.i:s.i + 1], bias=mod[:, 24 + c, s.i:s.i + 1]),
                            r=['tmp', 'gsc', 'mod'], w=[H2])
                    cx.dma('pool', s.h2T.rearrange("(c p) t -> p c t", p=128)[:, :, t0:t0 + n], h2[:, sl, :, :n],
                           r=[H2], w=['h2T%d' % s.i])
            cx.barrier()

        with (SB(nc, "wst", [128, 2, 2816], F32) as wst,
              SB(nc, "wu", [128, 8, 2 * DFF], BF16) as wu,
              SB(nc, "cw", [128, 44, 3], F32) as cw,
              SB(nc, "h2", [128, 2, 8, 512], BF16) as h2,
              SB(nc, "halo", [128, 44, 2], F32) as halo,
              SB(nc, "u", [128, 4, 514], F32) as u,
              SB(nc, "cv", [128, 4, 512], F32) as cv,
              SB(nc, "at", [128, 2, 22, 512], BF16) as at):
            cx.dma('sp', cw[:], ffn_cw[l], w=['cw'])
            k = 0
            for c in range(8):
                for g in range(2):
                    sl = k % 2
                    k += 1
                    cx.dma('sp', wst[:, sl], ffn_up[l][c * 128:(c + 1) * 128, g * 2816:(g + 1) * 2816],
                           w=['wst%d' % sl])
                    cx.op('act' if g else 'dve', (lambda e, c=c, g=g, sl=sl: e.activation(
                        out=wu[:, c, g * 2816:(g + 1) * 2816], in_=wst[:, sl], func=AF.Copy)) if g else
                        (lambda e, c=c, g=g, sl=sl: e.tensor_copy(wu[:, c, g * 2816:(g + 1) * 2816], wst[:, sl])),
                        r=['wst%d' % sl], w=['wu'])
            k = 0
            for s in seqs:
                if s.i == 0:
                    cx.op('dve', lambda e: e.memset(halo[:], 0.0), w=['halo'])
                else:
                    cx.dma('sp', halo[:], s.ffnc_in[l], w=['halo'])
                for (t0, n) in s.tiles:
                    sl = k % 2
                    k += 1
                    H2, AT = 'h2%d' % sl, 'at%d' % sl
                    cx.dma('sp', h2[:, sl, :, :n], s.h2T.rearrange("(c p) t -> p c t", p=128)[:, :, t0:t0 + n],
                           r=['h2T%d' % s.i], w=[H2])
                    for j in range(22):
                        for gv in range(2):
                            ch = gv * 22 + j
                            ui = (j % 2) * 2 + gv
                            pn = ui
                            U, CV = 'u%d' % ui, 'cv%d' % ui
                            for c in range(8):
                                cx.op('pe', lambda e, c=c, ch=ch, pn=pn: e.matmul(
                                    ps[pn][:, :n], wu[:, c, ch * 128:(ch + 1) * 128], h2[:, sl, c, :n],
                                    start=(c == 0), stop=(c == 7)), r=['wu', H2], w=[PS[pn]])
                            cx.op('act', lambda e, ui=ui, ch=ch: e.activation(out=u[:, ui, 0:2], in_=halo[:, ch, :],
                                                                            func=AF.Copy), r=['halo'], w=[U])
                            cx.op('act', lambda e, ui=ui, pn=pn: e.activation(out=u[:, ui, 2:2 + n], in_=ps[pn][:, :n],
                                                                            func=AF.Copy), r=[PS[pn]], w=[U])
                            cx.op('act', lambda e, ui=ui, ch=ch: e.activation(out=halo[:, ch, :], in_=u[:, ui, n:n + 2],
                                                                            func=AF.Copy), r=[U], w=['halo'])
                            cx.op('dve', lambda e, ui=ui, ch=ch: e.tensor_scalar(
                                cv[:, ui, :n], u[:, ui, 0:n], cw[:, ch, 0:1], None, ALU.mult),
                                r=[U, 'cw'], w=[CV])
                            cx.op('dve', lambda e, ui=ui, ch=ch: e.scalar_tensor_tensor(
                                cv[:, ui, :n], u[:, ui, 1:1 + n], cw[:, ch, 1:2], cv[:, ui, :n], ALU.mult, ALU.add),
                                r=[U, 'cw', CV], w=[CV])
                            cx.op('dve', lambda e, ui=ui, ch=ch: e.scalar_tensor_tensor(
                                cv[:, ui, :n], u[:, ui, 2:2 + n], cw[:, ch, 2:3], cv[:, ui, :n], ALU.mult, ALU.add),
                                r=[U, 'cw', CV], w=[CV])
                        gi = (j % 2) * 2
                        cx.op('act', lambda e, gi=gi: e.activation(out=cv[:, gi, :n], in_=cv[:, gi, :n], func=AF.Silu),
                              r=['cv%d' % gi], w=['cv%d' % gi])
                        cx.op('dve', lambda e, gi=gi, j=j: e.tensor_tensor(at[:, sl, j, :n], cv[:, gi, :n],
                                                                         cv[:, gi + 1, :n], ALU.mult),
                              r=['cv%d' % gi, 'cv%d' % (gi + 1)], w=[AT])
                    cx.dma('pool', s.aT.rearrange("(c p) t -> p c t", p=128)[:, :, t0:t0 + n], at[:, sl, :, :n],
                           r=[AT], w=['aT%d' % s.i])
                cx.dma('pool', s.ffnc_out[l], halo[:], r=['halo'], w=['ffnc_out'])
            cx.barrier()

        with (SB(nc, "wst", [128, 2, 11, 512], F32) as wst,
              SB(nc, "wd", [128, 22, D], BF16) as wd,
              SB(nc, "at", [128, 2, 22, 512], BF16) as at,
              SB(nc, "xt", [128, 2, 8, 512], F32) as xt,
              SB(nc, "yt", [128, 8, 512], F32) as yt,
              SB(nc, "sq", [128, 8, 512], F32) as sq,
              SB(nc, "rstd", [128, 512], F32) as rstd,
              SB(nc, "tmp", [128, 512], F32) as tmp):
            k = 0
            for g in range(2):
                for hh in range(2):
                    sl = k % 2
                    k += 1
                    cx.dma('sp', wst[:, sl], ffn_dn[l].rearrange("(c p) f -> p c f", p=128)[
                        :, hh * 11:(hh + 1) * 11, g * 512:(g + 1) * 512], w=['wst%d' % sl])
                    cx.op('act', lambda e, g=g, hh=hh, sl=sl: e.activation(
                        out=wd[:, hh * 11:(hh + 1) * 11, g * 512:(g + 1) * 512], in_=wst[:, sl], func=AF.Copy),
                        r=['wst%d' % sl], w=['wd'])
            k = 0
            for s in seqs:
                for (t0, n) in s.tiles:
                    sl = k % 2
                    k += 1
                    XT, AT = 'xt%d' % sl, 'at%d' % sl
                    cx.dma('sp', at[:, sl, :, :n], s.aT.rearrange("(c p) t -> p c t", p=128)[:, :, t0:t0 + n],
                           r=['aT%d' % s.i], w=[AT])
                    cx.dma('sp', xt[:, sl, :, :n], s.xT.rearrange("(c p) t -> p c t", p=128)[:, :, t0:t0 + n],
                           r=['xT%d' % s.i], w=[XT])
                    for m in range(8):
                        pn = m % 4
                        for c in range(22):
                            cx.op('pe', lambda e, m=m, c=c, pn=pn: e.matmul(
                                ps[pn][:, :n], wd[:, c, m * 128:(m + 1) * 128], at[:, sl, c, :n],
                                start=(c == 0), stop=(c == 21)), r=['wd', AT], w=[PS[pn]])
                        cx.op('act', lambda e, m=m, pn=pn: e.activation(out=yt[:, m, :n], in_=ps[pn][:, :n],
                                                                        func=AF.Copy), r=[PS[pn]], w=['yt'])
                    rms_stats(lambda c: yt[:, c, :n], n, sq, rstd, 4, ['yt'])
                    for c in range(8):
                        cx.op('dve', lambda e, c=c: e.tensor_tensor(tmp[:, :n], yt[:, c, :n], rstd[:, :n], ALU.mult),
                              r=['yt', 'rstd'], w=['tmp'])
                        cx.op('dve', lambda e, c=c: e.tensor_scalar(
                            tmp[:, :n], tmp[:, :n], ng[:, 3, c:c + 1], mod[:, 40 + c, s.i:s.i + 1],
                            ALU.mult, ALU.mult), r=['tmp', 'ng', 'mod'], w=['tmp'])
                        cx.op('dve', lambda e, c=c: e.tensor_tensor(xt[:, sl, c, :n], xt[:, sl, c, :n], tmp[:, :n],
                                                                    ALU.add), r=['tmp', XT], w=[XT])
                    dst = s.yout if l == depth - 1 else s.xT
                    cx.dma('pool', dst.rearrange("(c p) t -> p c t", p=128)[:, :, t0:t0 + n], xt[:, sl, :, :n],
                           r=[XT], w=['xT%d' % s.i])
            cx.barrier()


def emit_mixer(nc, cx, ps, PS, l, seqs, mod, gsc, ng, ones_f, rms_stats, W):
    ab = (l % 2 == 0)
    jl = l // 2
    with ExitStack() as st:
        def T(name, shape, dt):
            return st.enter_context(SB(nc, name, shape, dt))
        xt = T("xt", [128, 2, 8, 512], F32)
        sq = T("sq", [128, 8, 512], F32)
        rstd = T("rstd", [128, 512], F32)
        tmp = T("tmp", [128, 512], F32)
        h = T("h", [128, 8, 512], BF16)
        ot = T("otile", [128, 2, 8, 512], BF16)
        if ab:
            wst = T("wst", [128, 2, 1536], F32)
            win = T("win", [128, 8, 3072], BF16)
            P4 = T("P4", [128, 4, 512], F32)
            sig = T("sig", [128, 512], F32)
            kk = T("kk", [128, 512], F32)
            qq = T("qq", [128, 512], F32)
            lf = T("lf", [128, 512], F32)
            bb = T("bb", [128, 513], F32)
            dd = T("dd", [128, 512], F32)
            ee = T("ee", [128, 512], F32)
            qt = T("qt", [128, 512], BF16)
            kt = T("kt", [128, 512], BF16)
            qe = T("qe", [128, 512], BF16)
            kl = T("kl", [128, 512], BF16)
            vb = T("vb", [128, 512], BF16)
            edl = T("edl", [128, 8], F32)
            ob = T("ob", [128, 512], F32)
            attm = T("attm", [128, 128], BF16)
            attf = T("attf", [128, 128], F32)
            vtok = T("vtok", [128, 128], BF16)
            kltok = T("kltok", [128, 128], BF16)
            S = T("S", [128, 4, 128], F32)
            Sb = T("Sb", [128, 4, 128], BF16)
            onesr = T("onesr", [128, 512], F32)
            cst = T("cst", [128, 3, 128], F32)
            identb = T("identb", [128, 128], BF16)
            lbl = T("lbl", [128, 2, 4], F32)
            lbv = T("lbv", [128, 4], F32)
            oml = T("oml", [128, 4], F32)
            noml = T("noml", [128, 4], F32)
            hgg = T("hgg", [128, 1], F32)
            lxb = T("lxb", [128, 515], F32)
            lhalo = T("lhalo", [128, 4, 3], F32)
            lcw = T("lcw", [128, 4, 4], F32)
            lvec = T("lvec", [128, 4, 4], F32)
            m8sp = T("m8sp", [128, 4], F32)
            gst = T("gst", [128, 2, 4, 128], F32)
            gw = T("gw", [128, 2, 4, 128], BF16)
            hst = T("hst", [128, 4], F32)
            xc = T("xc", [128, 512], F32)
            xcb = T("xcb", [128, 512], BF16)
            rr = sig
            ii = kk
            aa = qq
            uu = lf
            hh = dd
            ly = ee
            gl = ob
            for c2 in range(16):
                c, hf = c2 // 2, c2 % 2
                sl = hf
                cx.dma('sp', wst[:, sl], W['ab_w_in'][jl][c * 128:(c + 1) * 128, hf * 1536:(hf + 1) * 1536],
                       w=['wst%d' % sl])
                cx.op('act' if sl else 'dve',
                      (lambda e: e.activation(out=win[:, c, hf * 1536:(hf + 1) * 1536], in_=wst[:, sl], func=AF.Copy))
                      if sl else (lambda e: e.tensor_copy(win[:, c, hf * 1536:(hf + 1) * 1536], wst[:, sl])),
                      r=['wst%d' % sl], w=['win'])
            cx.dma('sp', cst[:], W['consts'][:, 0:3, :], w=['cst'])
            cx.op('dve', lambda e: e.tensor_copy(identb[:], cst[:, 0, :]), r=['cst'], w=['identb'])
            cx.op('dve', lambda e: e.memset(onesr[:], 1.0), w=['onesr'])
            cx.dma('sp', lbl[:], W['hg_lb'][:, :, :], w=['lbl'])
            cx.dma('sp', hgg[:], W['hg_g'][jl], w=['hgg'])
            if jl == 0:
                cx.op('dve', lambda e: e.memset(lbv[:], 0.0), w=['lbv'])
            else:
                cx.op('dve', lambda e: e.tensor_tensor(lbv[:], lbl[:, 1, :], lbl[:, 0, :], ALU.subtract),
                      r=['lbl'], w=['lbv'])
                cx.op('act', lambda e: e.activation(out=lbv[:], in_=lbv[:], func=AF.Sigmoid), r=['lbv'], w=['lbv'])
            cx.op('dve', lambda e: e.tensor_scalar(oml[:], lbv[:], -1.0, 1.0, ALU.mult, ALU.add), r=['lbv'], w=['oml'])
            cx.op('dve', lambda e: e.tensor_scalar(noml[:], oml[:], -1.0, None, ALU.mult), r=['oml'], w=['noml'])
            cx.dma('sp', lcw[:], W['lru_cw'][jl], w=['lcw'])
            cx.dma('sp', lvec[:], W['lru_vec'][jl], w=['lvec'])
            cx.dma('sp', gst[:], W['lru_gw'][jl], w=['gst'])
            cx.op('dve', lambda e: e.tensor_copy(gw[:], gst[:]), r=['gst'], w=['gw'])
            cx.op('act', lambda e: e.activation(out=m8sp[:], in_=lvec[:, :, 3], func=AF.Exp, scale=-1.0),
                  r=['lvec'], w=['m8sp'])
            cx.op('dve', lambda e: e.tensor_scalar(m8sp[:], m8sp[:], 1.0, None, ALU.add), r=['m8sp'], w=['m8sp'])
            cx.op('act', lambda e: e.activation(out=m8sp[:], in_=m8sp[:], func=AF.Ln), r=['m8sp'], w=['m8sp'])
            cx.op('dve', lambda e: e.tensor_scalar(m8sp[:], m8sp[:], -8.0, None, ALU.mult), r=['m8sp'], w=['m8sp'])
        k = 0
        for s in seqs:
            L = 64 if s.T % 64 == 0 else s.T
            if ab:
                if s.i == 0:
                    cx.op('dve', lambda e: e.memset(S[:], 0.0), w=['S'])
                    cx.op('dve', lambda e: e.memset(hst[:], 0.0), w=['hst'])
                    cx.op('dve', lambda e: e.memset(lhalo[:], 0.0), w=['lhalo'])
                else:
                    cx.dma('sp', S[:], s.hg_in[jl], w=['S'])
                    cx.dma('sp', hst[:], s.lru_in[jl], w=['hst'])
                    cx.dma('sp', lhalo[:], s.lruc_in[jl], w=['lhalo'])
                cx.op('act', lambda e: e.activation(out=Sb[:], in_=S[:], func=AF.Copy), r=['S'], w=['Sb'])
            for (t0, n) in s.tiles:
                sl = k % 2
                k += 1
                XT, OT = 'xt%d' % sl, 'ot%d' % sl
                cx.dma('sp', xt[:, sl, :, :n], s.xT.rearrange("(c p) t -> p c t", p=128)[:, :, t0:t0 + n],
                       r=['xT%d' % s.i], w=[XT])
                rms_stats(lambda c: xt[:, sl, c, :n], n, sq, rstd, 6, [XT])
                for c in range(8):
                    cx.op('dve', lambda e, c=c: e.tensor_tensor(tmp[:, :n], xt[:, sl, c, :n], rstd[:, :n], ALU.mult),
                          r=[XT, 'rstd'], w=['tmp'])
                    cx.op('act', lambda e, c=c: e.activation(
                        out=(h[:, c, :n] if ab else ot[:, sl, c, :n]), in_=tmp[:, :n], func=AF.Identity,
                        scale=gsc[:, 0, c, s.i:s.i + 1], bias=mod[:, 0 + c, s.i:s.i + 1]),
                        r=['tmp', 'gsc', 'mod'], w=(['h'] if ab else [OT]))
                if ab:
                    def proj(ch, pn):
                        for c in range(8):
                            cx.op('pe', lambda e, c=c: e.matmul(ps[pn][:, :n], win[:, c, ch * 128:(ch + 1) * 128],
                                                                 h[:, c, :n], start=(c == 0), stop=(c == 7)),
                                  r=['win', 'h'], w=[PS[pn]])
                    nch = n // L
                    G = min(128, n)
                    mi = 1 if L == 64 else 2
                    for hd in range(0 if 'H' in AB_SKIP else 4):
                        for j4 in range(4):
                            pn = j4 % 2
                            proj(j4 * 4 + hd, pn)
                            cx.op('act', lambda e, j4=j4, pn=pn: e.activation(out=P4[:, j4, :n], in_=ps[pn][:, :n],
                                                                              func=AF.Copy), r=[PS[pn]], w=['P4'])
                        cx.op('act', lambda e: e.activation(out=sig[:, :n], in_=P4[:, 1, :n], func=AF.Sigmoid),
                              r=['P4'], w=['sig'])
                        cx.op('dve', lambda e: e.tensor_scalar(lf[:, :n], sig[:, :n], oml[:, hd:hd + 1], lbv[:, hd:hd + 1],
                                                               ALU.mult, ALU.add), r=['sig', 'oml', 'lbv'], w=['lf'])
                        cx.op('act', lambda e: e.activation(out=lf[:, :n], in_=lf[:, :n], func=AF.Ln), r=['lf'], w=['lf'])
                        cx.op('dve', lambda e: e.tensor_scalar(kk[:, :n], sig[:, :n], noml[:, hd:hd + 1], oml[:, hd:hd + 1],
                                                               ALU.mult, ALU.add), r=['sig', 'oml', 'noml'], w=['kk'])
                        cx.op('act', lambda e: e.activation(out=qq[:, :n], in_=P4[:, 0, :n], func=AF.Silu),
                              r=['P4'], w=['qq'])
                        cx.op('act', lambda e: e.activation(out=vb[:, :n], in_=P4[:, 2, :n], func=AF.Copy),
                              r=['P4'], w=['vb'])
                        cx.op('dve', lambda e: e.memset(bb[:, 0:1], 0.0), w=['bb'])
                        cx.op('dve', lambda e: e.tensor_tensor_scan(bb[:, 1:1 + n], onesr[:, :n], lf[:, :n], 0.0,
                                                                    ALU.mult, ALU.add), r=['onesr', 'lf'], w=['bb'])
                        b3 = bb[:, 1:1 + n].rearrange("p (c l) -> p c l", l=L)
                        mid3 = b3[:, :, L // 2:L // 2 + 1].to_broadcast([128, nch, L])
                        last3 = b3[:, :, L - 1:L].to_broadcast([128, nch, L])
                        prev3 = bb[:, 0:n].rearrange("p (c l) -> p c l", l=L)[:, :, 0:1].to_broadcast([128, nch, L])
                        d3 = dd[:, :n].rearrange("p (c l) -> p c l", l=L)

                        def expmul(ref3, scale, src, dst, DST):
                            cx.op('dve', lambda e: e.tensor_tensor(d3, b3, ref3, ALU.subtract), r=['bb'], w=['dd'])
                            cx.op('act', lambda e: e.activation(out=ee[:, :n], in_=dd[:, :n], func=AF.Exp, scale=scale),
                                  r=['dd'], w=['ee'])
                            cx.op('dve', lambda e: e.tensor_tensor(dst[:, :n], src[:, :n], ee[:, :n], ALU.mult),
                                  r=['ee', 'qq', 'kk'], w=[DST])
                        expmul(mid3, 1.0, qq, qt, 'qt')
                        expmul(mid3, -1.0, kk, kt, 'kt')
                        expmul(prev3, 1.0, qq, qe, 'qe')
                        expmul(last3, -1.0, kk, kl, 'kl')
                        cx.op('dve', lambda e: e.tensor_tensor(
                            edl[:, :nch], bb[:, 1:1 + n].rearrange("p (c l) -> p c l", l=L)[:, :, L - 1],
                            bb[:, 0:n].rearrange("p (c l) -> p c l", l=L)[:, :, 0], ALU.subtract), r=['bb'], w=['edl'])
                        cx.op('act', lambda e: e.activation(out=edl[:, :nch], in_=edl[:, :nch], func=AF.Exp),
                              r=['edl'], w=['edl'])
                        psb = ps[5].bitcast(BF16)
                        for g0 in range(0, n, G):
                            cx.op('pe', lambda e, g0=g0: e.matmul(ps[2][:G, :G], kt[:, g0:g0 + G], qt[:, g0:g0 + G],
                                                                   start=True, stop=True), r=['kt', 'qt'], w=[PS[2]])
                            cx.op('dve', lambda e: e.tensor_scalar(attf[:G, :G], ps[2][:G, :G], -1e30, 1e30, ALU.max, ALU.min),
                                  r=[PS[2]], w=['attf'])
                            cx.op('dve', lambda e: e.tensor_tensor(attm[:G, :G], attf[:G, :G], cst[:G, mi, :G], ALU.mult),
                                  r=['attf', 'cst'], w=['attm'])
                            cx.op('pe', lambda e, g0=g0: e.transpose(psb[:G, 0:128], vb[:, g0:g0 + G], identb[:]),
                                  r=['vb', 'identb'], w=[PS[5]])
                            cx.op('act', lambda e: e.activation(out=vtok[:G, :], in_=psb[:G, 0:128], func=AF.Copy),
                                  r=[PS[5]], w=['vtok'])
                            cx.op('pe', lambda e, g0=g0: e.transpose(psb[:G, 0:128], kl[:, g0:g0 + G], identb[:]),
                                  r=['kl', 'identb'], w=[PS[5]])
                            cx.op('act', lambda e: e.activation(out=kltok[:G, :], in_=psb[:G, 0:128], func=AF.Copy),
                                  r=[PS[5]], w=['kltok'])
                            for ci in range(G // L):
                                c0 = ci * L
                                cidx = (g0 + c0) // L
                                cx.op('pe', lambda e, c0=c0: e.matmul(ps[3][:, c0:c0 + L], vtok[c0:c0 + L, :],
                                                                       attm[c0:c0 + L, c0:c0 + L], start=True, stop=False),
                                      r=['vtok', 'attm'], w=[PS[3]])
                                cx.op('pe', lambda e, c0=c0, g0=g0: e.matmul(ps[3][:, c0:c0 + L], Sb[:, hd, :],
                                                                              qe[:, g0 + c0:g0 + c0 + L], start=False, stop=True),
                                      r=['Sb', 'qe'], w=[PS[3]])
                                cx.op('pe', lambda e, c0=c0: e.matmul(ps[4][:, :128], kltok[c0:c0 + L, :], vtok[c0:c0 + L, :],
                                                                       start=True, stop=True), r=['kltok', 'vtok'], w=[PS[4]])
                                cx.op('dve', lambda e, cidx=cidx: e.scalar_tensor_tensor(
                                    S[:, hd, :], S[:, hd, :], edl[:, cidx:cidx + 1], ps[4][:, :128], ALU.mult, ALU.add),
                                    r=['S', 'edl', PS[4]], w=['S'])
                                cx.op('act', lambda e: e.activation(out=Sb[:, hd, :], in_=S[:, hd, :], func=AF.Copy),
                                      r=['S'], w=['Sb'])
                            cx.op('act', lambda e, g0=g0: e.activation(out=ob[:, g0:g0 + G], in_=ps[3][:, :G], func=AF.Copy),
                                  r=[PS[3]], w=['ob'])
                        cx.op('act', lambda e: e.activation(out=dd[:, :n], in_=ob[:, :n], func=AF.Square), r=['ob'], w=['dd'])
                        cx.op('pe', lambda e: e.matmul(ps[7][:, :n], ones_f[:], dd[:, :n], start=True, stop=True),
                              r=['dd', 'ones_f'], w=[PS[7]])
                        cx.op('dve', lambda e: e.tensor_scalar(ee[:, :n], ps[7][:, :n], 1.0 / 128, EPS, ALU.mult, ALU.add),
                              r=[PS[7]], w=['ee'])
                        cx.op('dve', lambda e: e.reciprocal(ee[:, :n], ee[:, :n]), r=['ee'], w=['ee'])
                        cx.op('act', lambda e: e.activation(out=ee[:, :n], in_=ee[:, :n], func=AF.Sqrt), r=['ee'], w=['ee'])
                        cx.op('dve', lambda e: e.tensor_tensor(ob[:, :n], ob[:, :n], ee[:, :n], ALU.mult),
                              r=['ob', 'ee'], w=['ob'])
                        cx.op('act', lambda e: e.activation(out=dd[:, :n], in_=P4[:, 3, :n], func=AF.Silu), r=['P4'], w=['dd'])
                        cx.op('dve', lambda e: e.scalar_tensor_tensor(ot[:, sl, hd, :n], ob[:, :n], hgg[:, 0:1], dd[:, :n],
                                                                      ALU.mult, ALU.mult), r=['ob', 'hgg', 'dd'], w=[OT])
                    for ch in range(0 if 'L' in AB_SKIP else 4):
                        proj(16 + ch, 0)
                        cx.op('act', lambda e: e.activation(out=lxb[:, 0:3], in_=lhalo[:, ch, :], func=AF.Copy),
                              r=['lhalo'], w=['lxb'])
                        cx.op('act', lambda e: e.activation(out=lxb[:, 3:3 + n], in_=ps[0][:, :n], func=AF.Copy),
                              r=[PS[0]], w=['lxb'])
                        cx.op('act', lambda e: e.activation(out=lhalo[:, ch, :], in_=lxb[:, n:n + 3], func=AF.Copy),
                              r=['lxb'], w=['lhalo'])
                        proj(20 + ch, 1)
                        cx.op('act', lambda e: e.activation(out=ly[:, :n], in_=ps[1][:, :n], func=AF.Copy),
                              r=[PS[1]], w=['ee'])
                        cx.op('dve', lambda e: e.tensor_scalar(xc[:, :n], lxb[:, 0:n], lcw[:, ch, 0:1], lvec[:, ch, 0:1],
                                                               ALU.mult, ALU.add), r=['lxb', 'lcw', 'lvec'], w=['xc'])
                        for tp in range(1, 4):
                            cx.op('dve', lambda e, tp=tp: e.scalar_tensor_tensor(
                                xc[:, :n], lxb[:, tp:tp + n], lcw[:, ch, tp:tp + 1], xc[:, :n], ALU.mult, ALU.add),
                                r=['lxb', 'lcw', 'xc'], w=['xc'])
                        cx.op('act', lambda e: e.activation(out=xcb[:, :n], in_=xc[:, :n], func=AF.Copy), r=['xc'], w=['xcb'])
                        cx.op('pe', lambda e: e.matmul(ps[2][:, :n], gw[:, 0, ch, :], xcb[:, :n], start=True, stop=True),
                              r=['gw', 'xcb'], w=[PS[2]])
                        cx.op('act', lambda e: e.activation(out=rr[:, :n], in_=ps[2][:, :n], func=AF.Sigmoid,
                                                            bias=lvec[:, ch, 1:2]), r=[PS[2], 'lvec'], w=['sig'])
                        cx.op('pe', lambda e: e.matmul(ps[3][:, :n], gw[:, 1, ch, :], xcb[:, :n], start=True, stop=True),
                              r=['gw', 'xcb'], w=[PS[3]])
                        cx.op('act', lambda e: e.activation(out=ii[:, :n], in_=ps[3][:, :n], func=AF.Sigmoid,
                                                            bias=lvec[:, ch, 2:3]), r=[PS[3], 'lvec'], w=['kk'])
                        cx.op('dve', lambda e: e.tensor_scalar(rr[:, :n], rr[:, :n], m8sp[:, ch:ch + 1], None, ALU.mult),
                              r=['sig', 'm8sp'], w=['sig'])
                        cx.op('act', lambda e: e.activation(out=aa[:, :n], in_=rr[:, :n], func=AF.Exp), r=['sig'], w=['qq'])
                        cx.op('act', lambda e: e.activation(out=uu[:, :n], in_=rr[:, :n], func=AF.Exp, scale=2.0),
                              r=['sig'], w=['lf'])
                        cx.op('dve', lambda e: e.tensor_scalar(uu[:, :n], uu[:, :n], -1.0, 1.0, ALU.mult, ALU.add),
                              r=['lf'], w=['lf'])
                        cx.op('dve', lambda e: e.tensor_scalar(uu[:, :n], uu[:, :n], 1e-12, None, ALU.max),
                              r=['lf'], w=['lf'])
                        cx.op('act', lambda e: e.activation(out=uu[:, :n], in_=uu[:, :n], func=AF.Sqrt), r=['lf'], w=['lf'])
                        cx.op('dve', lambda e: e.tensor_tensor(uu[:, :n], uu[:, :n], ii[:, :n], ALU.mult),
                              r=['lf', 'kk'], w=['lf'])
                        cx.op('dve', lambda e: e.tensor_tensor(uu[:, :n], uu[:, :n], xc[:, :n], ALU.mult),
                              r=['lf', 'xc'], w=['lf'])
                        cx.op('dve', lambda e: e.tensor_tensor_scan(hh[:, :n], aa[:, :n], uu[:, :n], hst[:, ch:ch + 1],
                                                                    ALU.mult, ALU.add), r=['qq', 'lf', 'hst'], w=['dd'])
                        cx.op('act', lambda e: e.activation(out=hst[:, ch:ch + 1], in_=hh[:, n - 1:n], func=AF.Copy),
                              r=['dd'], w=['hst'])
                        cx.op('dve', lambda e: e.tensor_tensor(gl[:, :n], ly[:, :n], ly[:, :n], ALU.mult), r=['ee'], w=['ob'])
                        cx.op('dve', lambda e: e.tensor_scalar(gl[:, :n], gl[:, :n], 0.044715, 1.0, ALU.mult, ALU.add),
                              r=['ob'], w=['ob'])
                        cx.op('dve', lambda e: e.tensor_tensor(gl[:, :n], gl[:, :n], ly[:, :n], ALU.mult),
                              r=['ob', 'ee'], w=['ob'])
                        cx.op('act', lambda e: e.activation(out=gl[:, :n], in_=gl[:, :n], func=AF.Sigmoid,
                                                            scale=1.5957691216057308), r=['ob'], w=['ob'])
                        cx.op('dve', lambda e: e.tensor_tensor(gl[:, :n], gl[:, :n], ly[:, :n], ALU.mult),
                              r=['ob', 'ee'], w=['ob'])
                        cx.op('dve', lambda e: e.tensor_tensor(ot[:, sl, 4 + ch, :n], hh[:, :n], gl[:, :n], ALU.mult),
                              r=['dd', 'ob'], w=[OT])
                cx.dma('pool', s.oT.rearrange("(c p) t -> p c t", p=128)[:, :, t0:t0 + n], ot[:, sl, :, :n],
                       r=[OT], w=['oT%d' % s.i])
            if ab:
                cx.dma('pool', s.hg_out[jl], S[:], r=['S'], w=['hg_out'])
                cx.dma('pool', s.lru_out[jl], hst[:], r=['hst'], w=['lru_out'])
                cx.dma('pool', s.lruc_out[jl], lhalo[:], r=['lhalo'], w=['lruc_out'])
        cx.barrier()


MLA_SCALE = (128 + 64) ** -0.5
CD_STAGE = int(os.environ.get('CD_STAGE', '99'))
CD_SUB = int(os.environ.get('CD_SUB', '99'))
M2_SUB = int(os.environ.get('M2_SUB', '99'))
AB_SKIP = os.environ.get('AB_SKIP', '')


def emit_cd(nc, cx, ps, PS, l, seqs, mod, gsc, ng, ones_f, rms_stats, W):
    jl = l // 2
    RT = lambda a: a.rearrange("(c p) t -> p c t", p=128)
    with ExitStack() as st:
        def T(name, shape, dt):
            return st.enter_context(SB(nc, name, shape, dt))
        xt = T("xt", [128, 8, 512], F32)
        sq = T("sq", [128, 8, 512], F32)
        rstd = T("rstd", [128, 512], F32)
        tmp = T("tmp", [128, 512], F32)
        h = T("h", [128, 8, 512], BF16)
        ot = T("otile", [128, 2, 4, 512], BF16)
        wst = T("wst", [128, 2, 1536], F32)
        win = T("win", [128, 8, 3072], BF16)
        cst = T("cst", [128, 13, 128], F32)
        identb = T("identb", [128, 128], BF16)
        onesr = T("onesr", [128, 512], F32)
        pre = T("pre", [128, 3, 515], F32)
        cvx = T("cvx", [128, 3, 512], F32)
        ghalo = T("ghalo", [128, 12, 3], F32)
        gcw = T("gcw", [128, 12, 4], F32)
        gvec = T("gvec", [128, 4, 2], F32)
        nexpa = T("nexpa", [128, 4], F32)
        gdg = T("gdg", [128, 1], F32)
        c21 = T("c21", [128, 512], F32)
        c22 = T("c22", [128, 512], F32)
        c23 = T("c23", [128, 512], F32)
        gz = T("gz", [128, 512], F32)
        betaB = T("betaB", [128, 512], F32)
        gg = T("gg", [128, 513], F32)
        egr = T("egr", [128, 512], F32)
        ela = T("ela", [128, 512], F32)
        edl = T("edl", [128, 8], F32)
        w1 = T("w1", [128, 512], F32)
        w2 = T("w2", [128, 512], F32)
        vbet = T("vbet", [128, 512], F32)
        kbe = T("kbe", [128, 512], F32)
        kd = T("kd", [128, 512], F32)
        qbf = T("qbf", [128, 512], BF16)
        kbf = T("kbf", [128, 512], BF16)
        qg = T("qg", [128, 512], BF16)
        ob = T("ob", [128, 512], F32)
        gcol = T("gcol", [128, 1], F32)
        DBe = T("DBe", [128, 128], F32)
        XY = T("XY", [128, 2, 2, 128], F32)
        Rm = T("Rm", [128, 256], F32)
        qkT = T("qkT", [128, 128], BF16)
        kdtok = T("kdtok", [128, 128], BF16)
        wkT = T("wkT", [128, 128], BF16)
        utok = T("utok", [128, 128], BF16)
        S = T("S", [128, 4, 128], F32)
        Sb = T("Sb", [128, 4, 128], BF16)
        PQ = T("PQ", [128, 3, 512], F32)
        qng = T("qng", [128, 3], F32)
        kvg = T("kvg", [128, 2], F32)
        qan = T("qan", [128, 3, 512], BF16)
        ckv = T("ckv", [128, 2, 512], F32)
        tokst = T("tokst", [128, 4, 320], F32)
        rope = T("rope", [64, 2, 512], F32)
        krr = T("krr", [64, 512], F32)
        for c2 in range(16):
            c, hf = c2 // 2, c2 % 2
            sl = hf
            cx.dma('sp', wst[:, sl], W['cd_w_in'][jl][c * 128:(c + 1) * 128, hf * 1536:(hf + 1) * 1536],
                   w=['wst%d' % sl])
            cx.op('act' if sl else 'dve',
                  (lambda e: e.activation(out=win[:, c, hf * 1536:(hf + 1) * 1536], in_=wst[:, sl], func=AF.Copy))
                  if sl else (lambda e: e.tensor_copy(win[:, c, hf * 1536:(hf + 1) * 1536], wst[:, sl])),
                  r=['wst%d' % sl], w=['win'])
        cx.dma('sp', cst[:], W['consts'][:, :, :], w=['cst'])
        cx.op('dve', lambda e: e.tensor_copy(identb[:], cst[:, 0, :]), r=['cst'], w=['identb'])
        cx.op('dve', lambda e: e.memset(onesr[:], 1.0), w=['onesr'])
        cx.dma('sp', gcw[:], W['gd_cw'][jl], w=['gcw'])
        cx.dma('sp', gvec[:], W['gd_vec'][jl], w=['gvec'])
        cx.dma('sp', gdg[:], W['gd_g'][jl], w=['gdg'])
        cx.dma('sp', qng[:], W['q_g'][jl], w=['qng'])
        cx.dma('sp', kvg[:], W['kv_g'][jl], w=['kvg'])
        cx.op('act', lambda e: e.activation(out=nexpa[:], in_=gvec[:, :, 0], func=AF.Exp), r=['gvec'], w=['nexpa'])
        cx.op('dve', lambda e: e.tensor_scalar(nexpa[:], nexpa[:], -1.0, None, ALU.mult), r=['nexpa'], w=['nexpa'])
        k = 0
        for s in seqs:
            L = 64 if s.T % 64 == 0 else s.T
            nsteps = int(round(math.log2(L)))
            past = s.past
            if s.i == 0:
                cx.op('dve', lambda e: e.memset(S[:], 0.0), w=['S'])
                cx.op('dve', lambda e: e.memset(ghalo[:], 0.0), w=['ghalo'])
            else:
                cx.dma('sp', S[:], s.gd_in[jl], w=['S'])
                cx.dma('sp', ghalo[:], s.gdc_in[jl], w=['ghalo'])
                cx.dma('pool', s.ckvT[:, 0:past], s.lat_inT[jl], w=['ckvT%d' % s.i])
                cx.dma('pool', s.krT[:, 0:past], s.kr_inT[jl], w=['krT%d' % s.i])
            cx.op('act', lambda e: e.activation(out=Sb[:], in_=S[:], func=AF.Copy), r=['S'], w=['Sb'])
            for (t0, n) in s.tiles:
                sl = k % 2
                k += 1
                OT = 'ot%d' % sl
                cx.dma('sp', xt[:, :, :n], RT(s.xT)[:, :, t0:t0 + n], r=['xT%d' % s.i], w=['xt'])
                cx.dma('sp', rope[:, 0, :n], s.ropeC[:, t0:t0 + n], w=['rope'])
                cx.dma('sp', rope[:, 1, :n], s.ropeS[:, t0:t0 + n], w=['rope'])
                rms_stats(lambda c: xt[:, c, :n], n, sq, rstd, 6, ['xt'])
                for c in range(8):
                    cx.op('dve', lambda e: e.tensor_tensor(tmp[:, :n], xt[:, c, :n], rstd[:, :n], ALU.mult),
                          r=['xt', 'rstd'], w=['tmp'])
                    cx.op('act', lambda e: e.activation(out=h[:, c, :n], in_=tmp[:, :n], func=AF.Identity,
                                                        scale=gsc[:, 0, c, s.i:s.i + 1], bias=mod[:, c, s.i:s.i + 1]),
                          r=['tmp', 'gsc', 'mod'], w=['h'])

                def proj(ch, pn):
                    for c in range(8):
                        cx.op('pe', lambda e, c=c: e.matmul(ps[pn][:, :n], win[:, c, ch * 128:(ch + 1) * 128],
                                                             h[:, c, :n], start=(c == 0), stop=(c == 7)),
                              r=['win', 'h'], w=[PS[pn]])

                def pcopy(ch, pn, dst, DST):
                    proj(ch, pn)
                    cx.op('act', lambda e: e.activation(out=dst, in_=ps[pn][:, :n], func=AF.Copy), r=[PS[pn]], w=[DST])

                def stat128(src, SRC, dst, DST, scl, bias_eps):
                    cx.op('act', lambda e: e.activation(out=w2[:, :n], in_=src, func=AF.Square), r=[SRC], w=['w2'])
                    cx.op('pe', lambda e: e.matmul(ps[6][:, :n], ones_f[:], w2[:, :n], start=True, stop=True),
                          r=['w2', 'ones_f'], w=[PS[6]])
                    cx.op('dve', lambda e: e.tensor_scalar(dst, ps[6][:, :n], scl, bias_eps, ALU.mult, ALU.add),
                          r=[PS[6]], w=[DST])
                    cx.op('dve', lambda e: e.reciprocal(dst, dst), r=[DST], w=[DST])
                    cx.op('act', lambda e: e.activation(out=dst, in_=dst, func=AF.Sqrt), r=[DST], w=[DST])

                pcopy(21, 0, c21[:, :n], 'c21')
                pcopy(22, 1, c22[:, :n], 'c22')
                pcopy(23, 0, c23[:, :n], 'c23')
                for c in range(3):
                    pcopy(16 + c, c % 2, PQ[:, c, :n], 'PQ')
                for c in range(3):
                    cx.op('act', lambda e: e.activation(out=sq[:, c, :n], in_=PQ[:, c, :n], func=AF.Square), r=['PQ'], w=['sq'])
                for c in range(3):
                    cx.op('pe', lambda e: e.matmul(ps[6][:, :n], ones_f[:], sq[:, c, :n], start=(c == 0), stop=(c == 2)),
                          r=['sq', 'ones_f'], w=[PS[6]])
                cx.op('dve', lambda e: e.tensor_scalar(w1[:, :n], ps[6][:, :n], 1.0 / 384, EPS, ALU.mult, ALU.add),
                      r=[PS[6]], w=['w1'])
                cx.op('dve', lambda e: e.reciprocal(w1[:, :n], w1[:, :n]), r=['w1'], w=['w1'])
                cx.op('act', lambda e: e.activation(out=w1[:, :n], in_=w1[:, :n], func=AF.Sqrt), r=['w1'], w=['w1'])
                for c in range(3):
                    cx.op('dve', lambda e: e.tensor_tensor(PQ[:, c, :n], PQ[:, c, :n], w1[:, :n], ALU.mult),
                          r=['PQ', 'w1'], w=['PQ'])
                    cx.op('dve', lambda e: e.tensor_scalar(qan[:, c, :n], PQ[:, c, :n], qng[:, c:c + 1], None, ALU.mult),
                          r=['PQ', 'qng'], w=['qan'])
                cx.dma('pool', RT(s.qanT)[:, :, t0:t0 + n], qan[:, :, :n], r=['qan'], w=['qanT%d' % s.i])
                for c in range(2):
                    pcopy(19 + c, c % 2, PQ[:, c, :n], 'PQ')
                for c in range(2):
                    cx.op('act', lambda e: e.activation(out=sq[:, c, :n], in_=PQ[:, c, :n], func=AF.Square), r=['PQ'], w=['sq'])
                for c in range(2):
                    cx.op('pe', lambda e: e.matmul(ps[6][:, :n], ones_f[:], sq[:, c, :n], start=(c == 0), stop=(c == 1)),
                          r=['sq', 'ones_f'], w=[PS[6]])
                cx.op('dve', lambda e: e.tensor_scalar(w1[:, :n], ps[6][:, :n], 1.0 / 256, EPS, ALU.mult, ALU.add),
                      r=[PS[6]], w=['w1'])
                cx.op('dve', lambda e: e.reciprocal(w1[:, :n], w1[:, :n]), r=['w1'], w=['w1'])
                cx.op('act', lambda e: e.activation(out=w1[:, :n], in_=w1[:, :n], func=AF.Sqrt), r=['w1'], w=['w1'])
                for c in range(2):
                    cx.op('dve', lambda e: e.tensor_tensor(PQ[:, c, :n], PQ[:, c, :n], w1[:, :n], ALU.mult),
                          r=['PQ', 'w1'], w=['PQ'])
                    cx.op('dve', lambda e: e.tensor_scalar(ckv[:, c, :n], PQ[:, c, :n], kvg[:, c:c + 1], None, ALU.mult),
                          r=['PQ', 'kvg'], w=['ckv'])
                cx.dma('pool', RT(s.ckvT)[:, :, past + t0:past + t0 + n], ckv[:, :, :n], r=['ckv'], w=['ckvT%d' % s.i])
                cx.op('dve', lambda e: e.tensor_tensor(krr[:, :n], c21[0:64, :n], rope[:, 0, :n], ALU.mult),
                      r=['c21', 'rope'], w=['krr'])
                cx.op('dve', lambda e: e.tensor_tensor(w1[0:64, :n], c23[0:64, :n], rope[:, 1, :n], ALU.mult),
                      r=['c23', 'rope'], w=['w1'])
                cx.op('dve', lambda e: e.tensor_tensor(krr[:, :n], krr[:, :n], w1[0:64, :n], ALU.add),
                      r=['krr', 'w1'], w=['krr'])
                cx.dma('pool', s.krT[:, past + t0:past + t0 + n], krr[:, :n], r=['krr'], w=['krT%d' % s.i])
                nsub = (n + 127) // 128
                for sb_ in range(nsub):
                    a0 = sb_ * 128
                    an = min(128, n - a0)
                    for c in range(2):
                        cx.op('pe', lambda e: e.transpose(ps[4][:an, c * 128:(c + 1) * 128], ckv[:, c, a0:a0 + an], cst[:, 0, :]),
                              r=['ckv', 'cst'], w=[PS[4]])
                    cx.op('pe', lambda e: e.transpose(ps[4][:an, 256:320], krr[:, a0:a0 + an], cst[0:64, 0, 0:64]),
                          r=['krr', 'cst'], w=[PS[4]])
                    cx.op('act', lambda e: e.activation(out=tokst[:an, sb_, :], in_=ps[4][:an, 0:320], func=AF.Copy),
                          r=[PS[4]], w=['tokst'])
                    cx.dma('pool', s.lat_out[jl][t0 + a0:t0 + a0 + an, :], tokst[:an, sb_, 0:256], r=['tokst'], w=['lat_out'])
                    cx.dma('pool', s.kr_out[jl][t0 + a0:t0 + a0 + an, :], tokst[:an, sb_, 256:320], r=['tokst'], w=['kr_out'])
                nch = n // L
                G = min(128, n)
                mI, mS = (1, 4) if L == 64 else (2, 3)
                for hd in range(4 if CD_STAGE >= 2 else 0):
                    for idx, ch in enumerate((hd, 4 + hd, 8 + hd)):
                        proj(ch, idx % 2)
                        cx.op('act', lambda e: e.activation(out=pre[:, idx, 0:3], in_=ghalo[:, ch, :], func=AF.Copy),
                              r=['ghalo'], w=['pre'])
                        cx.op('act', lambda e: e.activation(out=pre[:, idx, 3:3 + n], in_=ps[idx % 2][:, :n], func=AF.Copy),
                              r=[PS[idx % 2]], w=['pre'])
                        cx.op('act', lambda e: e.activation(out=ghalo[:, ch, :], in_=pre[:, idx, n:n + 3], func=AF.Copy),
                              r=['pre'], w=['ghalo'])
                        cx.op('dve', lambda e: e.tensor_scalar(cvx[:, idx, :n], pre[:, idx, 0:n], gcw[:, ch, 0:1], None, ALU.mult),
                              r=['pre', 'gcw'], w=['cvx'])
                        for tp in range(1, 4):
                            cx.op('dve', lambda e: e.scalar_tensor_tensor(cvx[:, idx, :n], pre[:, idx, tp:tp + n],
                                                                          gcw[:, ch, tp:tp + 1], cvx[:, idx, :n], ALU.mult, ALU.add),
                                  r=['pre', 'gcw', 'cvx'], w=['cvx'])
                        cx.op('act', lambda e: e.activation(out=cvx[:, idx, :n], in_=cvx[:, idx, :n], func=AF.Silu),
                              r=['cvx'], w=['cvx'])
                    pcopy(12 + hd, 0, gz[:, :n], 'gz')
                    stat128(cvx[:, 0, :n], 'cvx', w1[:, :n], 'w1', 1.0, EPS)
                    cx.op('dve', lambda e: e.scalar_tensor_tensor(cvx[:, 0, :n], cvx[:, 0, :n], 128 ** -0.5, w1[:, :n],
                                                                  ALU.mult, ALU.mult), r=['cvx', 'w1'], w=['cvx'])
                    stat128(cvx[:, 1, :n], 'cvx', w1[:, :n], 'w1', 1.0, EPS)
                    cx.op('dve', lambda e: e.tensor_tensor(cvx[:, 1, :n], cvx[:, 1, :n], w1[:, :n], ALU.mult),
                          r=['cvx', 'w1'], w=['cvx'])
                    cx.op('act', lambda e: e.activation(out=qbf[:, :n], in_=cvx[:, 0, :n], func=AF.Copy), r=['cvx'], w=['qbf'])
                    cx.op('act', lambda e: e.activation(out=kbf[:, :n], in_=cvx[:, 1, :n], func=AF.Copy), r=['cvx'], w=['kbf'])
                    cx.op('pe', lambda e: e.matmul(ps[2][:, :n], cst[:, 5 + hd, :], c21[:, :n], start=True, stop=True),
                          r=['cst', 'c21'], w=[PS[2]])
                    cx.op('act', lambda e: e.activation(out=betaB[:, :n], in_=ps[2][:, :n], func=AF.Sigmoid),
                          r=[PS[2]], w=['betaB'])
                    cx.op('pe', lambda e: e.matmul(ps[3][:, :n], cst[:, 9 + hd, :], c22[:, :n], start=True, stop=True),
                          r=['cst', 'c22'], w=[PS[3]])
                    cx.op('dve', lambda e: e.tensor_scalar(w1[:, :n], ps[3][:, :n], gvec[:, hd, 1:2], None, ALU.add),
                          r=[PS[3], 'gvec'], w=['w1'])
                    cx.op('act', lambda e: e.activation(out=w2[:, :n], in_=w1[:, :n], func=AF.Abs), r=['w1'], w=['w2'])
                    cx.op('act', lambda e: e.activation(out=w2[:, :n], in_=w2[:, :n], func=AF.Exp, scale=-1.0), r=['w2'], w=['w2'])
                    cx.op('dve', lambda e: e.tensor_scalar(w2[:, :n], w2[:, :n], 1.0, None, ALU.add), r=['w2'], w=['w2'])
                    cx.op('act', lambda e: e.activation(out=w2[:, :n], in_=w2[:, :n], func=AF.Ln), r=['w2'], w=['w2'])
                    cx.op('dve', lambda e: e.scalar_tensor_tensor(w1[:, :n], w1[:, :n], 0.0, w2[:, :n], ALU.max, ALU.add),
                          r=['w1', 'w2'], w=['w1'])
                    cx.op('dve', lambda e: e.tensor_scalar(w1[:, :n], w1[:, :n], nexpa[:, hd:hd + 1], None, ALU.mult),
                          r=['w1', 'nexpa'], w=['w1'])
                    cx.op('dve', lambda e: e.memset(gg[:, 0:1], 0.0), w=['gg'])
                    cx.op('dve', lambda e: e.tensor_tensor_scan(gg[:, 1:1 + n], onesr[:, :n], w1[:, :n], 0.0, ALU.mult, ALU.add),
                          r=['onesr', 'w1'], w=['gg'])
                    b3 = gg[:, 1:1 + n].rearrange("p (c l) -> p c l", l=L)
                    last3 = b3[:, :, L - 1:L].to_broadcast([128, nch, L])
                    prev3 = gg[:, 0:n].rearrange("p (c l) -> p c l", l=L)[:, :, 0:1].to_broadcast([128, nch, L])
                    cx.op('dve', lambda e: e.tensor_tensor(egr[:, :n].rearrange("p (c l) -> p c l", l=L), b3, prev3, ALU.subtract),
                          r=['gg'], w=['egr'])
                    cx.op('act', lambda e: e.activation(out=egr[:, :n], in_=egr[:, :n], func=AF.Exp), r=['egr'], w=['egr'])
                    cx.op('dve', lambda e: e.tensor_tensor(ela[:, :n].rearrange("p (c l) -> p c l", l=L), b3, last3, ALU.subtract),
                          r=['gg'], w=['ela'])
                    cx.op('act', lambda e: e.activation(out=ela[:, :n], in_=ela[:, :n], func=AF.Exp, scale=-1.0), r=['ela'], w=['ela'])
                    cx.op('dve', lambda e: e.tensor_tensor(
                        edl[:, :nch], gg[:, 1:1 + n].rearrange("p (c l) -> p c l", l=L)[:, :, L - 1],
                        gg[:, 0:n].rearrange("p (c l) -> p c l", l=L)[:, :, 0], ALU.subtract), r=['gg'], w=['edl'])
                    cx.op('act', lambda e: e.activation(out=edl[:, :nch], in_=edl[:, :nch], func=AF.Exp), r=['edl'], w=['edl'])
                    cx.op('dve', lambda e: e.tensor_tensor(vbet[:, :n], cvx[:, 2, :n], betaB[:, :n], ALU.mult),
                          r=['cvx', 'betaB'], w=['vbet'])
                    cx.op('dve', lambda e: e.tensor_tensor(kbe[:, :n], cvx[:, 1, :n], betaB[:, :n], ALU.mult),
                          r=['cvx', 'betaB'], w=['kbe'])
                    cx.op('dve', lambda e: e.tensor_tensor(kbe[:, :n], kbe[:, :n], egr[:, :n], ALU.mult), r=['kbe', 'egr'], w=['kbe'])
                    cx.op('dve', lambda e: e.tensor_tensor(kd[:, :n], cvx[:, 1, :n], ela[:, :n], ALU.mult), r=['cvx', 'ela'], w=['kd'])
                    cx.op('dve', lambda e: e.tensor_tensor(qg[:, :n], cvx[:, 0, :n], egr[:, :n], ALU.mult), r=['cvx', 'egr'], w=['qg'])
                    for g0 in range(0, n if CD_SUB >= 2 else 0, G):
                        gs = slice(g0, g0 + G)
                        cx.op('pe', lambda e: e.transpose(ps[4][:G, 0:128], gg[:, 1 + g0:1 + g0 + G], cst[:, 0, :]),
                              r=['gg', 'cst'], w=[PS[4]])
                        cx.op('act', lambda e: e.activation(out=gcol[:G, :], in_=ps[4][:G, 0:1], func=AF.Copy), r=[PS[4]], w=['gcol'])
                        cx.op('dve', lambda e: e.tensor_scalar(DBe[:G, :G], gg[:G, 1 + g0:1 + g0 + G], gcol[:G, 0:1], 0.0,
                                                               ALU.subtract, ALU.min), r=['gg', 'gcol'], w=['DBe'])
                        cx.op('act', lambda e: e.activation(out=DBe[:G, :G], in_=DBe[:G, :G], func=AF.Exp), r=['DBe'], w=['DBe'])
                        cx.op('pe', lambda e: e.matmul(ps[2][:G, :G], kbf[:, gs], kbf[:, gs], start=True, stop=True), r=['kbf'], w=[PS[2]])
                        X0, Y0 = XY[:G, 0, 0, :G], XY[:G, 0, 1, :G]
                        cx.op('dve', lambda e: e.scalar_tensor_tensor(X0, ps[2][:G, :G], -1.0, DBe[:G, :G], ALU.mult, ALU.mult),
                              r=[PS[2], 'DBe'], w=['XY0'])
                        cx.op('dve', lambda e: e.tensor_tensor(X0, X0, betaB[:G, gs], ALU.mult), r=['XY0', 'betaB'], w=['XY0'])
                        cx.op('dve', lambda e: e.tensor_tensor(X0, X0, cst[:G, mS, :G], ALU.mult), r=['XY0', 'cst'], w=['XY0'])
                        cx.op('pe', lambda e: e.transpose(ps[5][:G, :G], X0, cst[:G, 0, :G]), r=['XY0', 'cst'], w=[PS[5]])
                        cx.op('act', lambda e: e.activation(out=Y0, in_=ps[5][:G, :G], func=AF.Copy), r=[PS[5]], w=['XY0'])
                        cx.op('pe', lambda e: e.matmul(ps[3][:G, :G], kbf[:, gs], qbf[:, gs], start=True, stop=True),
                              r=['kbf', 'qbf'], w=[PS[3]])
                        cx.op('dve', lambda e: e.tensor_tensor(DBe[:G, :G], DBe[:G, :G], cst[:G, mI, :G], ALU.mult), r=['DBe', 'cst'], w=['DBe'])
                        cx.op('dve', lambda e: e.tensor_tensor(qkT[:G, :G], ps[3][:G, :G], DBe[:G, :G], ALU.mult), r=[PS[3], 'DBe'], w=['qkT'])
                        cx.op('pe', lambda e: e.transpose(ps[4][:G, 0:128], vbet[:, gs], cst[:, 0, :]), r=['vbet', 'cst'], w=[PS[4]])
                        cx.op('pe', lambda e: e.transpose(ps[4][:G, 128:256], kbe[:, gs], cst[:, 0, :]), r=['kbe', 'cst'], w=[PS[4]])
                        cx.op('pe', lambda e: e.transpose(ps[4][:G, 256:384], kd[:, gs], cst[:, 0, :]), r=['kd', 'cst'], w=[PS[4]])
                        cx.op('act', lambda e: e.activation(out=Rm[:G, :], in_=ps[4][:G, 0:256], func=AF.Copy), r=[PS[4]], w=['Rm'])
                        cx.op('act', lambda e: e.activation(out=kdtok[:G, :], in_=ps[4][:G, 256:384], func=AF.Copy), r=[PS[4]], w=['kdtok'])
                        for kk_ in range(nsteps if CD_SUB >= 3 else 0):
                            pg = kk_ % 2
                            Xc, Yc = XY[:G, pg, 0, :G], XY[:G, pg, 1, :G]
                            cx.op('pe', lambda e: e.matmul(ps[2][:G, 0:256], Xc, Rm[:G, :], start=True, stop=True),
                                  r=['XY%d' % pg, 'Rm'], w=[PS[2]])
                            cx.op('dve', lambda e: e.tensor_tensor(Rm[:G, :], Rm[:G, :], ps[2][:G, 0:256], ALU.add),
                                  r=['Rm', PS[2]], w=['Rm'])
                            if kk_ < nsteps - 1:
                                Xn, Yn = XY[:G, 1 - pg, 0, :G], XY[:G, 1 - pg, 1, :G]
                                cx.op('pe', lambda e: e.matmul(ps[3][:G, 0:G], Yc, Xc, start=True, stop=True), r=['XY%d' % pg], w=[PS[3]])
                                cx.op('pe', lambda e: e.matmul(ps[5][:G, 0:G], Xc, Yc, start=True, stop=True), r=['XY%d' % pg], w=[PS[5]])
                                cx.op('act', lambda e: e.activation(out=Xn, in_=ps[3][:G, 0:G], func=AF.Copy), r=[PS[3]], w=['XY%d' % (1 - pg)])
                                cx.op('dve', lambda e: e.tensor_copy(Yn, ps[5][:G, 0:G]), r=[PS[5]], w=['XY%d' % (1 - pg)])
                        cx.op('pe', lambda e: e.transpose(ps[4][:, 0:G], Rm[:G, 128:256], cst[:G, 0, :G]), r=['Rm', 'cst'], w=[PS[4]])
                        cx.op('act', lambda e: e.activation(out=wkT[:, :G], in_=ps[4][:, 0:G], func=AF.Copy), r=[PS[4]], w=['wkT'])
                        for ci in range(G // L if CD_SUB >= 4 else 0):
                            c0 = ci * L
                            cs_ = slice(c0, c0 + L)
                            cidx = (g0 + c0) // L
                            cx.op('pe', lambda e: e.matmul(ps[5][:G, 0:128], wkT[:, :G], Sb[:, hd, :], start=True, stop=True),
                                  r=['wkT', 'Sb'], w=[PS[5]])
                            cx.op('dve', lambda e: e.tensor_tensor(utok[cs_, :], Rm[cs_, 0:128], ps[5][cs_, 0:128], ALU.subtract),
                                  r=['Rm', PS[5]], w=['utok'])
                            if CD_SUB < 5:
                                continue
                            cx.op('pe', lambda e: e.matmul(ps[7][:, cs_], utok[cs_, :], qkT[cs_, cs_], start=True, stop=False),
                                  r=['utok', 'qkT'], w=[PS[7]])
                            cx.op('pe', lambda e: e.matmul(ps[7][:, cs_], Sb[:, hd, :], qg[:, g0 + c0:g0 + c0 + L], start=False, stop=True),
                                  r=['Sb', 'qg'], w=[PS[7]])
                            if CD_SUB < 6:
                                continue
                            cx.op('pe', lambda e: e.matmul(ps[3][:, 0:128], kdtok[cs_, :], utok[cs_, :], start=True, stop=True),
                                  r=['kdtok', 'utok'], w=[PS[3]])
                            cx.op('dve', lambda e: e.scalar_tensor_tensor(S[:, hd, :], S[:, hd, :], edl[:, cidx:cidx + 1], ps[3][:, 0:128],
                                                                          ALU.mult, ALU.add), r=['S', 'edl', PS[3]], w=['S'])
                            cx.op('act', lambda e: e.activation(out=Sb[:, hd, :], in_=S[:, hd, :], func=AF.Copy), r=['S'], w=['Sb'])
                        cx.op('act', lambda e: e.activation(out=ob[:, gs], in_=ps[7][:, :G], func=AF.Copy), r=[PS[7]], w=['ob'])
                    stat128(ob[:, :n], 'ob', w1[:, :n], 'w1', 1.0 / 128, EPS)
                    cx.op('dve', lambda e: e.tensor_tensor(ob[:, :n], ob[:, :n], w1[:, :n], ALU.mult), r=['ob', 'w1'], w=['ob'])
                    cx.op('act', lambda e: e.activation(out=gz[:, :n], in_=gz[:, :n], func=AF.Silu), r=['gz'], w=['gz'])
                    cx.op('dve', lambda e: e.scalar_tensor_tensor(ot[:, sl, hd, :n], ob[:, :n], gdg[:, 0:1], gz[:, :n], ALU.mult, ALU.mult),
                          r=['ob', 'gdg', 'gz'], w=[OT])
                cx.dma('pool', RT(s.oT)[:, 0:4, t0:t0 + n], ot[:, sl, :, :n], r=[OT], w=['oT%d' % s.i])
            cx.dma('pool', s.gd_out[jl], S[:], r=['S'], w=['gd_out'])
            cx.dma('pool', s.gdc_out[jl], ghalo[:], r=['ghalo'], w=['gdc_out'])
        cx.barrier()

    if CD_STAGE < 3:
        return
    with ExitStack() as st:
        def T(name, shape, dt):
            return st.enter_context(SB(nc, name, shape, dt))
        TA = max(s.past + s.T for s in seqs)
        NT128 = (TA + 127) // 128
        wst = T("wst", [128, 3, 1024], F32)
        wqb = T("wqb", [128, 3, 1024], BF16)
        wkvb = T("wkvb", [128, 2, 1024], BF16)
        onesb = T("onesb", [128, 128], BF16)
        KT = T("KT", [128, TA], BF16)
        VT = T("VT", [128, NT128, 128], BF16)
        KR = T("KR", [65, TA], BF16)
        negm = T("negm", [1, 512], BF16)
        cst_ = T("cstf", [128, 512], F32)
        ckb = T("ckb", [128, 2, 512], BF16)
        ksq = T("ksq", [128, 512], F32)
        kmax = T("kmax", [128, 2], F32)
        qan = T("qan", [128, 2, 3, 512], BF16)
        rope = T("rope", [64, 2, 512], F32)
        Qn = T("Qn", [128, 512], BF16)
        Qf = T("Qf", [128, 512], F32)
        Qr = T("Qr", [65, 512], BF16)
        qr1 = T("qr1", [64, 512], F32)
        qr2 = T("qr2", [64, 512], F32)
        Pt = T("Pt", [128, 3, 512], BF16)
        rs = T("rs", [128, 512], F32)
        od = T("od", [128, 2, 512], BF16)
        for c in range(3):
            cx.dma('sp', wst[:, c, :], W['w_qb'][jl][c * 128:(c + 1) * 128, :], w=['wst'])
        cx.op('dve', lambda e: e.tensor_copy(wqb[:], wst[:]), r=['wst'], w=['wqb'])
        for c in range(2):
            cx.dma('sp', wst[:, c, :], W['w_kvb'][jl][c * 128:(c + 1) * 128, :], w=['wst'])
        cx.op('dve', lambda e: e.tensor_copy(wkvb[:], wst[:, 0:2, :]), r=['wst'], w=['wkvb'])
        cx.op('dve', lambda e: e.memset(onesb[:], 1.0), w=['onesb'])
        kq = 0
        for s in seqs:
            past = s.past
            Tall = past + s.T
            chunked = (s.T % 64 == 0)
            ktiles = tiles_of(Tall, 512)
            for (k0, nk) in ktiles:
                cx.dma('sp', cst_[0:64, :nk], s.krT[:, k0:k0 + nk], r=['krT%d' % s.i], w=['cstf'])
                cx.op('act', lambda e: e.activation(out=KR[0:64, k0:k0 + nk], in_=cst_[0:64, :nk], func=AF.Copy), r=['cstf'], w=['KR'])
            cx.op('dve', lambda e: e.memset(KR[64:65, 0:Tall], 1.0), w=['KR'])
            for hd in range(4 if M2_SUB >= 2 else 0):
                cx.op('dve', lambda e: e.memset(kmax[:], 0.0), w=['kmax'])
                for (k0, nk) in ktiles:
                    cx.dma('sp', cst_[:, :nk], s.ckvT[0:128, k0:k0 + nk], r=['ckvT%d' % s.i], w=['cstf'])
                    cx.op('act', lambda e: e.activation(out=ckb[:, 0, :nk], in_=cst_[:, :nk], func=AF.Copy), r=['cstf'], w=['ckb'])
                    cx.dma('sp', cst_[:, :nk], s.ckvT[128:256, k0:k0 + nk], r=['ckvT%d' % s.i], w=['cstf'])
                    cx.op('act', lambda e: e.activation(out=ckb[:, 1, :nk], in_=cst_[:, :nk], func=AF.Copy), r=['cstf'], w=['ckb'])
                    for c in range(2):
                        cx.op('pe', lambda e: e.matmul(ps[4][:, :nk], wkvb[:, c, hd * 256:hd * 256 + 128], ckb[:, c, :nk],
                                                       start=(c == 0), stop=(c == 1)), r=['wkvb', 'ckb'], w=[PS[4]])
                    cx.op('act', lambda e: e.activation(out=KT[:, k0:k0 + nk], in_=ps[4][:, :nk], func=AF.Copy), r=[PS[4]], w=['KT'])
                    cx.op('act', lambda e: e.activation(out=ksq[:, :nk], in_=ps[4][:, :nk], func=AF.Square), r=[PS[4]], w=['ksq'])
                    cx.op('pe', lambda e: e.matmul(ps[5][:, :nk], ones_f[:], ksq[:, :nk], start=True, stop=False),
                          r=['ksq', 'ones_f'], w=[PS[5]])
                    cx.op('act', lambda e: e.activation(out=ksq[0:64, :nk], in_=KR[0:64, k0:k0 + nk], func=AF.Square), r=['KR', 'ksq'], w=['ksq'])
                    cx.op('pe', lambda e: e.matmul(ps[5][:, :nk], ones_f[0:64, :], ksq[0:64, :nk], start=False, stop=True),
                          r=['ksq', 'ones_f'], w=[PS[5]])
                    cx.op('dve', lambda e: e.tensor_reduce(kmax[:, 1:2], ps[5][:, :nk], AX.X, ALU.max), r=[PS[5]], w=['kmax'])
                    cx.op('dve', lambda e: e.tensor_tensor(kmax[:, 0:1], kmax[:, 0:1], kmax[:, 1:2], ALU.max), r=['kmax'], w=['kmax'])
                    for a0 in range(0, nk, 128):
                        an = min(128, nk - a0)
                        ti = (k0 + a0) // 128
                        for c in range(2):
                            cx.op('pe', lambda e: e.matmul(ps[6][:an, 0:128], ckb[:, c, a0:a0 + an],
                                                           wkvb[:, c, hd * 256 + 128:hd * 256 + 256], start=(c == 0), stop=(c == 1)),
                                  r=['ckb', 'wkvb'], w=[PS[6]])
                        cx.op('dve', lambda e: e.tensor_copy(VT[:an, ti, :], ps[6][:an, 0:128]), r=[PS[6]], w=['VT'])
                for (q0, nq) in (s.tiles if M2_SUB >= 3 else []):
                    sl = kq % 2
                    kq += 1
                    QA, OD, PTn = 'qan%d' % sl, 'od%d' % sl, None
                    cx.dma('sp', qan[:, sl, :, :nq], RT(s.qanT)[:, :, q0:q0 + nq], r=['qanT%d' % s.i], w=[QA])
                    cx.dma('sp', rope[:, 0, :nq], s.ropeC[:, q0:q0 + nq], w=['rope'])
                    cx.dma('sp', rope[:, 1, :nq], s.ropeS[:, q0:q0 + nq], w=['rope'])
                    for c in range(3):
                        cx.op('pe', lambda e: e.matmul(ps[4][:, :nq], wqb[:, c, hd * 256:hd * 256 + 128], qan[:, sl, c, :nq],
                                                       start=(c == 0), stop=(c == 2)), r=['wqb', QA], w=[PS[4]])
                    cx.op('act', lambda e: e.activation(out=Qf[:, :nq], in_=ps[4][:, :nq], func=AF.Copy, scale=MLA_SCALE), r=[PS[4]], w=['Qf'])
                    cx.op('dve', lambda e: e.tensor_copy(Qn[:, :nq], Qf[:, :nq]), r=['Qf'], w=['Qn'])
                    for c in range(3):
                        cx.op('pe', lambda e: e.matmul(ps[5][0:64, :nq], wqb[:, c, hd * 256 + 128:hd * 256 + 192], qan[:, sl, c, :nq],
                                                       start=(c == 0), stop=(c == 2)), r=['wqb', QA], w=[PS[5]])
                    for c in range(3):
                        cx.op('pe', lambda e: e.matmul(ps[6][0:64, :nq], wqb[:, c, hd * 256 + 192:hd * 256 + 256], qan[:, sl, c, :nq],
                                                       start=(c == 0), stop=(c == 2)), r=['wqb', QA], w=[PS[6]])
                    cx.op('dve', lambda e: e.tensor_tensor(qr1[:, :nq], ps[5][0:64, :nq], rope[:, 0, :nq], ALU.mult), r=[PS[5], 'rope'], w=['qr1'])
                    cx.op('dve', lambda e: e.tensor_tensor(qr2[:, :nq], ps[6][0:64, :nq], rope[:, 1, :nq], ALU.mult), r=[PS[6], 'rope'], w=['qr2'])
                    cx.op('dve', lambda e: e.scalar_tensor_tensor(qr1[:, :nq], qr1[:, :nq], 1.0, qr2[:, :nq], ALU.mult, ALU.add),
                          r=['qr1', 'qr2'], w=['qr1'])
                    cx.op('act', lambda e: e.activation(out=qr1[:, :nq], in_=qr1[:, :nq], func=AF.Copy, scale=MLA_SCALE), r=['qr1'], w=['qr1'])
                    cx.op('dve', lambda e: e.tensor_copy(Qr[0:64, :nq], qr1[:, :nq]), r=['qr1'], w=['Qr'])
                    if M2_SUB == 30:
                        continue
                    cx.op('act', lambda e: e.activation(out=Qf[:, :nq], in_=Qf[:, :nq], func=AF.Square), r=['Qf'], w=['Qf'])
                    cx.op('act', lambda e: e.activation(out=qr2[:, :nq], in_=qr1[:, :nq], func=AF.Square), r=['qr1'], w=['qr2'])
                    cx.op('pe', lambda e: e.matmul(ps[7][0:1, :nq], ones_f[:, 0:1], Qf[:, :nq], start=True, stop=False), r=['Qf', 'ones_f'], w=[PS[7]])
                    cx.op('pe', lambda e: e.matmul(ps[7][0:1, :nq], ones_f[0:64, 0:1], qr2[:, :nq], start=False, stop=True), r=['qr2', 'ones_f'], w=[PS[7]])
                    cx.op('dve', lambda e: e.tensor_scalar(rs[0:1, :nq], ps[7][0:1, :nq], kmax[0:1, 0:1], None, ALU.mult),
                          r=[PS[7], 'kmax'], w=['rs'])
                    cx.op('act', lambda e: e.activation(out=rs[0:1, :nq], in_=rs[0:1, :nq], func=AF.Sqrt), r=['rs'], w=['rs'])
                    cx.op('dve', lambda e: e.tensor_scalar(negm[0:1, :nq], rs[0:1, :nq], -1.0, None, ALU.mult), r=['rs'], w=['negm'])
                    cx.dma('sp', Qr[64:65, :nq], negm[0:1, :nq], r=['negm'], w=['Qr'])
                    if M2_SUB == 31:
                        continue
                    if chunked:
                        qb = q0 // 512
                        klist = [(kt, 0, False) for kt in range(4 * qb)] + [(4 * qb + j, 128 * j, True) for j in range((nq + 127) // 128)]
                    else:
                        klist = [(kt, 0, False) for kt in range((Tall + 127) // 128)]
                    if M2_SUB < 4 or M2_SUB in (32, 33, 34, 35):
                        klist = klist[:1]
                    nkl = len(klist)
                    def emit_scores(ki):
                        kt, qlo, diag = klist[ki]
                        kn = min(128, Tall - kt * 128)
                        pn = (0, 1, 4)[ki % 3]
                        ksl = slice(kt * 128, kt * 128 + kn)
                        cx.op('pe', lambda e: e.matmul(ps[pn][:kn, qlo:nq], KT[:, ksl], Qn[:, qlo:nq], start=True, stop=False),
                              r=['KT', 'Qn'], w=[PS[pn]])
                        cx.op('pe', lambda e: e.matmul(ps[pn][:kn, qlo:nq], KR[0:65, ksl], Qr[0:65, qlo:nq], start=False, stop=True),
                              r=['KR', 'Qr'], w=[PS[pn]])

                    emit_scores(0)
                    if nkl > 1:
                        emit_scores(1)
                    for ki, (kt, qlo, diag) in enumerate(klist):
                        kn = min(128, Tall - kt * 128)
                        pn = (0, 1, 4)[ki % 3]
                        pt = ki % 3
                        PTn = 'Pt%d' % pt
                        if ki + 2 < nkl:
                            emit_scores(ki + 2)
                        cx.op('act', lambda e: e.activation(out=Pt[:kn, pt, qlo:nq], in_=ps[pn][:kn, qlo:nq], func=AF.Exp),
                              r=[PS[pn]], w=[PTn])
                        first, last = (ki == 0), (ki == nkl - 1)
                        if not diag:
                            parts = [(0, kn, qlo, nq)]
                        else:
                            parts = [(0, 64, qlo, min(qlo + 64, nq))]
                            if qlo + 64 < nq:
                                parts.append((0, 128, qlo + 64, nq))
                        for pi, (r0, r1, ca, cb) in enumerate(parts):
                            lastp = last and (pi == len(parts) - 1)
                            cx.op('pe', lambda e: e.matmul(ps[2][:, ca:cb], VT[r0:r1, kt, :], Pt[r0:r1, pt, ca:cb],
                                                           start=first, stop=lastp), r=['VT', PTn], w=[PS[2]])
                            cx.op('pe', lambda e: e.matmul(ps[3][:, ca:cb], onesb[r0:r1, :], Pt[r0:r1, pt, ca:cb],
                                                           start=first, stop=lastp), r=['onesb', PTn], w=[PS[3]])
                    if M2_SUB in (32, 33, 34, 35):
                        continue
                    cx.op('dve', lambda e: e.reciprocal(rs[:, :nq], ps[3][:, :nq]), r=[PS[3]], w=['rs'])
                    cx.op('dve', lambda e: e.tensor_tensor(od[:, sl, :nq], ps[2][:, :nq], rs[:, :nq], ALU.mult), r=[PS[2], 'rs'], w=[OD])
                    cx.dma('pool', s.oT[512 + hd * 128:512 + (hd + 1) * 128, q0:q0 + nq], od[:, sl, :nq], r=[OD], w=['oT%d' % s.i])
        cx.barrier()


def _pc(v, nchunk):
    sh = v.shape[:-1]
    return np.ascontiguousarray(np.moveaxis(v.reshape(sh + (nchunk, 128)), -1, -2))


def make_consts():
    c = np.zeros((128, 13, 128), np.float32)
    c[:, 0, :] = np.eye(128, dtype=np.float32)
    s_ = np.arange(128)[:, None]
    t_ = np.arange(128)[None, :]
    c[:, 1, :] = ((s_ // 64 == t_ // 64) & (s_ <= t_)).astype(np.float32)
    c[:, 2, :] = (s_ <= t_).astype(np.float32)
    c[:, 3, :] = (s_ < t_).astype(np.float32)
    c[:, 4, :] = ((s_ // 64 == t_ // 64) & (s_ < t_)).astype(np.float32)
    for h in range(4):
        c[64 + h, 5 + h, :] = 1.0
        c[h, 9 + h, :] = 1.0
    return c


def rope_tables(past, T):
    half = 32
    freqs = np.exp(np.float32(-math.log(10000.0)) * np.arange(half, dtype=np.float32) / np.float32(half)).astype(np.float32)
    pos = (past + np.arange(T)).astype(np.float32)
    ang = (pos[:, None] * freqs[None, :]).astype(np.float32)
    cos = np.cos(ang).astype(np.float32).T
    sin = np.sin(ang).astype(np.float32).T
    return (np.ascontiguousarray(np.concatenate([cos, cos], 0)),
            np.ascontiguousarray(np.concatenate([-sin, sin], 0)))


def make_core_inputs(inp, xs, cs, sidx, depth):
    nab = (depth + 1) // 2
    im = {}
    nseq = len(xs)
    for i in range(nseq):
        im["xT%d" % i] = np.ascontiguousarray(xs[i].T)
        if sidx[i] is not None:
            b = sidx[i]
            im["ffnc_i%d" % i] = np.ascontiguousarray(
                inp['state_ffn_conv'][:depth, b].reshape(depth, 2, 44, 128).transpose(0, 3, 2, 1))
            im["hg_i%d" % i] = np.ascontiguousarray(inp['state_hgrn'][:nab, b].transpose(0, 2, 1, 3))
            im["lru_i%d" % i] = _pc(inp['state_rglru'][:nab, b], 4)
            im["lruc_i%d" % i] = np.ascontiguousarray(
                inp['state_rglru_conv'][:nab, b].reshape(nab, 3, 4, 128).transpose(0, 3, 2, 1))
    im["cT"] = np.ascontiguousarray(np.asarray(cs).reshape(nseq, 8, 128).transpose(2, 1, 0))
    im["ada_w"] = np.ascontiguousarray(inp['ada_w'][:depth])
    im["ada_bT"] = _pc(inp['ada_b'][:depth], 48)
    im["norm_gT"] = np.ascontiguousarray(inp['norm_g'][:depth].reshape(depth, 4, 8, 128).transpose(0, 3, 1, 2))
    wo = np.zeros((depth, D, D), np.float32)
    for l in range(depth):
        wo[l] = inp['ab_w_out'][l // 2] if l % 2 == 0 else inp['cd_w_out'][l // 2]
    im["w_out"] = wo
    im["ffn_up"] = np.ascontiguousarray(inp['ffn_w_up'][:depth])
    im["ffn_cw"] = np.ascontiguousarray(inp['ffn_conv_w'][:depth].reshape(depth, 3, 44, 128).transpose(0, 3, 2, 1))
    im["ffn_dn"] = np.ascontiguousarray(inp['ffn_w_down'][:depth])
    im["consts"] = make_consts()
    im["ab_w_in"] = np.ascontiguousarray(inp['ab_w_in'][:nab])
    im["hg_lb"] = np.ascontiguousarray(inp['hgrn_lb_logits'].reshape(2, 4, 128).transpose(2, 0, 1))
    im["hg_g"] = np.ascontiguousarray(inp['hgrn_norm_g'][:nab].reshape(nab, 128, 1))
    im["lru_cw"] = np.ascontiguousarray(inp['lru_conv_w'][:nab].reshape(nab, 4, 4, 128).transpose(0, 3, 2, 1))
    vec = np.stack([inp['lru_conv_b'][:nab], inp['lru_b_a'][:nab], inp['lru_b_x'][:nab], inp['lru_lambda'][:nab]], -1)
    im["lru_vec"] = np.ascontiguousarray(vec.reshape(nab, 4, 128, 4).transpose(0, 2, 1, 3))
    gw = np.zeros((nab, 128, 2, 4, 128), np.float32)
    for k, nm in enumerate(('lru_w_a', 'lru_w_x')):
        w = inp[nm][:nab]
        for ch in range(4):
            for bb in range(2):
                gw[:, bb * 64:(bb + 1) * 64, k, ch, bb * 64:(bb + 1) * 64] = w[:, 2 * ch + bb]
    im["lru_gw"] = gw
    ncd = depth // 2
    if ncd > 0:
        src = inp['cd_w_in'][:ncd]
        w = np.zeros((ncd, D, 3072), np.float32)
        w[:, :, 0:2048] = src[:, :, 0:2048]
        w[:, :, 2048:2432] = src[:, :, 2056:2440]
        w[:, :, 2432:2688] = src[:, :, 2440:2696]
        w[:, :, 2688:2752] = src[:, :, 2696:2760]
        w[:, :, 2752:2756] = src[:, :, 2048:2052]
        w[:, :, 2816:2820] = src[:, :, 2052:2056]
        w[:, :, 2944:2976] = src[:, :, 2728:2760]
        w[:, :, 2976:3008] = src[:, :, 2696:2728]
        im["cd_w_in"] = w
        im["gd_cw"] = np.ascontiguousarray(inp['gdn_conv_w'][:ncd].reshape(ncd, 4, 12, 128).transpose(0, 3, 2, 1))
        gv = np.stack([inp['gdn_a_log'][:ncd], inp['gdn_dt_bias'][:ncd]], -1)
        im["gd_vec"] = np.ascontiguousarray(np.broadcast_to(gv[:, None], (ncd, 128, 4, 2)))
        im["gd_g"] = np.ascontiguousarray(inp['gdn_norm_g'][:ncd].reshape(ncd, 128, 1))
        im["q_g"] = _pc(inp['mla_q_norm_g'][:ncd], 3)
        im["kv_g"] = _pc(inp['mla_kv_norm_g'][:ncd], 2)
        wq = inp['mla_w_qb'][:ncd].reshape(ncd, 384, 4, 192)
        wqe = np.zeros((ncd, 384, 4, 256), np.float32)
        wqe[..., 0:192] = wq
        wqe[..., 192:224] = wq[..., 160:192]
        wqe[..., 224:256] = wq[..., 128:160]
        im["w_qb"] = np.ascontiguousarray(wqe.reshape(ncd, 384, 1024))
        im["w_kvb"] = np.ascontiguousarray(inp['mla_w_kvb'][:ncd])
        for i in range(nseq):
            T_ = xs[i].shape[0]
            past = 0 if sidx[i] is None else PAST
            im["ropeC%d" % i], im["ropeS%d" % i] = rope_tables(past, T_)
            if sidx[i] is not None:
                b = sidx[i]
                im["gd_i%d" % i] = np.ascontiguousarray(inp['state_gdn'][:ncd, b].transpose(0, 2, 1, 3))
                im["gdc_i%d" % i] = np.ascontiguousarray(
                    inp['state_gdn_conv'][:ncd, b].reshape(ncd, 3, 12, 128).transpose(0, 3, 2, 1))
                im["lat_iT%d" % i] = np.ascontiguousarray(inp['cache_mla_latent'][:ncd, b].transpose(0, 2, 1))
                im["kr_iT%d" % i] = np.ascontiguousarray(inp['cache_mla_krope'][:ncd, b].transpose(0, 2, 1))
    return im


_PROG = {}


def kernel(**inputs):
    inp = {k: np.asarray(v) for k, v in inputs.items()}
    xp, xsm = inp['x_prompt'], inp['x_sample']
    B, SEQ, _ = xp.shape
    NB, DT, _ = xsm.shape
    NS = NB // NCORES
    depth = DEPTH
    nab, ncd = (depth + 1) // 2, depth // 2
    key = (SEQ, DT, NS, depth)
    if key not in _PROG:
        _PROG[key] = build_program(SEQ, DT, NS, depth=depth)
    nc = _PROG[key]
    in_maps = []
    shared = None
    for c in range(NCORES):
        b = c * B // NCORES
        sidx = [None] + [c * NS + j for j in range(NS)]
        xs = [xp[b]] + [xsm[i] for i in sidx[1:]]
        cs = np.stack([inp['c_prompt'][b]] + [inp['c_sample'][i] for i in sidx[1:]])
        im = make_core_inputs(inp, xs, cs, sidx, depth)
        if shared is None:
            shared = im
        else:
            for k in ('ada_w', 'w_out', 'ffn_up', 'ffn_dn', 'ab_w_in', 'consts', 'cd_w_in', 'w_qb', 'w_kvb', 'ropeC0', 'ropeS0'):
                im[k] = shared[k]
        in_maps.append(im)
    res = run_bass_kernel_spmd(nc, in_maps, core_ids=list(range(NCORES))).results
    pc = [b * NCORES // B for b in range(B)]

    def un_ffnc(a):
        return a.transpose(0, 3, 2, 1).reshape(depth, 2, 2 * DFF)

    def un_hg(a):
        return a.transpose(0, 2, 1, 3)

    def un_lru(a):
        return a.transpose(0, 2, 1).reshape(nab, HALF)

    def un_gdc(a):
        return a.transpose(0, 3, 2, 1).reshape(ncd, 3, 3 * HALF)

    def un_lruc(a):
        return a.transpose(0, 3, 2, 1).reshape(nab, 3, HALF)

    def gather(fn, name, prompt):
        if prompt:
            return np.ascontiguousarray(np.stack([fn(res[pc[b]][name + "0"]) for b in range(B)], axis=1))
        return np.ascontiguousarray(np.stack(
            [fn(res[i // NS][name + str(1 + i % NS)]) for i in range(NB)], axis=1))

    y_prompt = np.ascontiguousarray(np.stack([res[pc[b]]["yT0"].T for b in range(B)]))
    y_sample = np.ascontiguousarray(np.stack([res[i // NS]["yT%d" % (1 + i % NS)].T for i in range(NB)]))
    outs = [y_prompt, y_sample]
    for prompt in (True, False):
        nb = B if prompt else NB
        tt = SEQ if prompt else DT
        outs += [
            gather(un_hg, "hg_o", prompt), gather(un_lru, "lru_o", prompt), gather(un_lruc, "lruc_o", prompt),
            gather(un_hg, "gd_o", prompt), gather(un_gdc, "gdc_o", prompt),
            gather(lambda a: a, "lat_o", prompt), gather(lambda a: a, "kr_o", prompt),
            gather(un_ffnc, "ffnc_o", prompt),
        ]
    return tuple(outs)
```

```python
import math
import os
from contextlib import ExitStack
import numpy as np
import concourse.bass as bass
import concourse.mybir as mybir
from concourse.bass_utils import run_bass_kernel_spmd

F32 = mybir.dt.float32
BF16 = mybir.dt.bfloat16
AF = mybir.ActivationFunctionType
ALU = mybir.AluOpType
AX = mybir.AxisListType

D = 1024
DEPTH = 4
HALF = 512
DFF = 2816
EPS = 1e-6
NCORES = 8
PAST = 2048
SAME_ENGINE_SYNC = os.environ.get('SES', '1') == '1'


class Ctx:
    def __init__(self, nc, es):
        self.nc = nc
        self.es = es
        self.eng = {'pe': nc.tensor, 'dve': nc.vector, 'act': nc.scalar, 'pool': nc.gpsimd, 'sp': nc.sync}
        self.sem = {}
        self.cnt = {}
        for e in self.eng:
            self.sem[e] = es.enter_context(nc.semaphore("sem_" + e))
            self.cnt[e] = 0
        self.ndma = 20
        self.dslots = {}
        for q in ('sp', 'pool'):
            self.dslots[q] = []
            for i in range(self.ndma):
                k = "d_%s_%d" % (q, i)
                self.sem[k] = es.enter_context(nc.semaphore(k))
                self.cnt[k] = 0
                self.dslots[q].append(k)
        self.dnext = {'sp': 0, 'pool': 0}
        self.waited = {e: {} for e in self.eng}
        self.lastw = {}
        self.readers = {}
        self.nins = 0
        self._il = None
        self._noyield = 0

    def _wait(self, e, deps):
        best = {}
        for (k, v) in deps:
            if v > best.get(k, 0):
                best[k] = v
        for k, v in best.items():
            if k == e and (e == 'pe' or not SAME_ENGINE_SYNC):
                continue
            if self.waited[e].get(k, 0) >= v:
                continue
            self.eng[e].wait_ge(self.sem[k], v)
            self.waited[e][k] = v

    def _deps(self, r, w):
        deps = []
        for x in r:
            if x in self.lastw:
                deps.append(self.lastw[x])
        for x in w:
            if x in self.lastw:
                deps.append(self.lastw[x])
            rd = self.readers.get(x)
            if rd:
                deps.extend(rd.items())
        return deps

    def _record(self, tok, r, w):
        for x in r:
            rd = self.readers.setdefault(x, {})
            if tok[1] > rd.get(tok[0], 0):
                rd[tok[0]] = tok[1]
        for x in w:
            self.lastw[x] = tok
            self.readers[x] = {}

    def op(self, e, fn, r=(), w=()):
        self._wait(e, self._deps(r, w))
        ins = fn(self.eng[e])
        self.cnt[e] += 1
        ins.then_inc(self.sem[e], 1)
        self._record((e, self.cnt[e]), r, w)
        self.nins += 1
        self._yield()

    def dma(self, q, out, in_, r=(), w=(), **kw):
        k = self.dslots[q][self.dnext[q]]
        self.dnext[q] = (self.dnext[q] + 1) % self.ndma
        deps = self._deps(r, w)
        if self.cnt[k] > 0:
            deps.append((k, self.cnt[k]))
        self._wait(q, deps)
        ins = self.eng[q].dma_start(out=out, in_=in_, **kw)
        self.cnt[k] += 16
        ins.then_inc(self.sem[k], 16)
        self._record((k, self.cnt[k]), r, w)
        self.nins += 1
        self._yield()

    def _yield(self):
        il = self._il
        if il is None or self._noyield:
            return
        me = il['cur']
        n = len(il['sems'])
        nxt = None
        for d in range(1, n + 1):
            c = (me + d) % n
            if il['alive'][c]:
                nxt = c
                break
        if nxt is None or nxt == me:
            return
        il['cur'] = nxt
        il['sems'][nxt].release()
        il['sems'][me].acquire()
        if il['err']:
            raise RuntimeError("interleave aborted")

    def wait_until(self, cond):
        spins = 0
        while not cond():
            if self._il is None or sum(self._il['alive']) <= 1:
                raise RuntimeError("wait_until would deadlock")
            self._yield()
            spins += 1
            if spins > 100000000:
                raise RuntimeError("wait_until spin limit")

    def interleave(self, fns):
        if len(fns) == 1:
            fns[0]()
            return
        import threading
        n = len(fns)
        il = {'sems': [threading.Semaphore(0) for _ in range(n)], 'alive': [True] * n, 'cur': 0, 'err': []}
        done = threading.Semaphore(0)

        def runner(i):
            il['sems'][i].acquire()
            try:
                if not il['err']:
                    fns[i]()
            except BaseException as ex:
                il['err'].append(ex)
            il['alive'][i] = False
            nxt = None
            for d in range(1, n + 1):
                c = (i + d) % n
                if il['alive'][c]:
                    nxt = c
                    break
            if nxt is None:
                done.release()
            else:
                il['cur'] = nxt
                il['sems'][nxt].release()

        self._il = il
        ths = [threading.Thread(target=runner, args=(i,)) for i in range(n)]
        for t in ths:
            t.start()
        il['sems'][0].release()
        done.acquire()
        for t in ths:
            t.join()
        self._il = None
        if il['err']:
            raise il['err'][0]

    def barrier(self):
        allk = [(k, v) for k, v in self.cnt.items() if v > 0]
        for e in self.eng:
            self._wait(e, [kv for kv in allk if kv[0] != e or e in ('dve', 'act', 'pool')])
        self.lastw = {}
        self.readers = {}

    def final_wait(self):
        allk = [(k, v) for k, v in self.cnt.items() if v > 0 and k != 'sp']
        self._wait('sp', allk)


def tiles_of(T, n=512):
    out = []
    t = 0
    while t < T:
        out.append((t, min(n, T - t)))
        t += n
    return out


class Seq:
    pass


_uid = [0]


def SB(nc, name, shape, dt):
    _uid[0] += 1
    return nc.sbuf_tensor("%s_u%d" % (name, _uid[0]), shape, dt)


def build_program(SEQ, DT, NS, depth=DEPTH, dbg=None):
    nc = bass.Bass("TRN2", target_bir_lowering=False)
    es = ExitStack()
    with es:
        cx = Ctx(nc, es)
        _emit(nc, es, cx, SEQ, DT, NS, depth, dbg)
        cx.final_wait()
        print("instructions:", cx.nins)
    return nc


def _emit(nc, es, cx, SEQ, DT, NS, depth, dbg):
    NSEQ = 1 + NS
    seqs = []
    for i in range(NSEQ):
        s = Seq()
        s.i = i
        s.T = SEQ if i == 0 else DT
        s.tiles = tiles_of(s.T)
        s.xin = nc.dram_tensor("xT%d" % i, [D, s.T], F32, kind="ExternalInput").ap()
        s.yout = nc.dram_tensor("yT%d" % i, [D, s.T], F32, kind="ExternalOutput").ap()
        s.ffnc_out = nc.dram_tensor("ffnc_o%d" % i, [depth, 128, 44, 2], F32, kind="ExternalOutput").ap()
        s.xT = nc.dram_tensor("s_xT%d" % i, [D, s.T], F32).ap()
        s.h2T = nc.dram_tensor("s_h2T%d" % i, [D, s.T], BF16).ap()
        s.oT = nc.dram_tensor("s_oT%d" % i, [D, s.T], BF16).ap()
        s.aT = nc.dram_tensor("s_aT%d" % i, [DFF, s.T], BF16).ap()
        nab = (depth + 1) // 2
        s.hg_out = nc.dram_tensor("hg_o%d" % i, [nab, 128, 4, 128], F32, kind="ExternalOutput").ap()
        s.lru_out = nc.dram_tensor("lru_o%d" % i, [nab, 128, 4], F32, kind="ExternalOutput").ap()
        s.lruc_out = nc.dram_tensor("lruc_o%d" % i, [nab, 128, 4, 3], F32, kind="ExternalOutput").ap()
        ncd = depth // 2
        s.past = 0 if i == 0 else PAST
        if ncd > 0:
            s.gd_out = nc.dram_tensor("gd_o%d" % i, [ncd, 128, 4, 128], F32, kind="ExternalOutput").ap()
            s.gdc_out = nc.dram_tensor("gdc_o%d" % i, [ncd, 128, 12, 3], F32, kind="ExternalOutput").ap()
            s.lat_out = nc.dram_tensor("lat_o%d" % i, [ncd, s.T, 256], F32, kind="ExternalOutput").ap()
            s.kr_out = nc.dram_tensor("kr_o%d" % i, [ncd, s.T, 64], F32, kind="ExternalOutput").ap()
            s.ropeC = nc.dram_tensor("ropeC%d" % i, [64, s.T], F32, kind="ExternalInput").ap()
            s.ropeS = nc.dram_tensor("ropeS%d" % i, [64, s.T], F32, kind="ExternalInput").ap()
            s.qanT = nc.dram_tensor("s_qanT%d" % i, [384, s.T], BF16).ap()
            s.ckvT = nc.dram_tensor("s_ckvT%d" % i, [256, s.past + s.T], F32).ap()
            s.krT = nc.dram_tensor("s_krT%d" % i, [64, s.past + s.T], F32).ap()
            if i > 0:
                s.gd_in = nc.dram_tensor("gd_i%d" % i, [ncd, 128, 4, 128], F32, kind="ExternalInput").ap()
                s.gdc_in = nc.dram_tensor("gdc_i%d" % i, [ncd, 128, 12, 3], F32, kind="ExternalInput").ap()
                s.lat_inT = nc.dram_tensor("lat_iT%d" % i, [ncd, 256, PAST], F32, kind="ExternalInput").ap()
                s.kr_inT = nc.dram_tensor("kr_iT%d" % i, [ncd, 64, PAST], F32, kind="ExternalInput").ap()
        if i > 0:
            s.ffnc_in = nc.dram_tensor("ffnc_i%d" % i, [depth, 128, 44, 2], F32, kind="ExternalInput").ap()
            s.hg_in = nc.dram_tensor("hg_i%d" % i, [nab, 128, 4, 128], F32, kind="ExternalInput").ap()
            s.lru_in = nc.dram_tensor("lru_i%d" % i, [nab, 128, 4], F32, kind="ExternalInput").ap()
            s.lruc_in = nc.dram_tensor("lruc_i%d" % i, [nab, 128, 4, 3], F32, kind="ExternalInput").ap()
        seqs.append(s)
    cT = nc.dram_tensor("cT", [128, 8, NSEQ], F32, kind="ExternalInput").ap()
    ada_w = nc.dram_tensor("ada_w", [depth, D, 6 * D], F32, kind="ExternalInput").ap()
    ada_bT = nc.dram_tensor("ada_bT", [depth, 128, 48], F32, kind="ExternalInput").ap()
    norm_gT = nc.dram_tensor("norm_gT", [depth, 128, 4, 8], F32, kind="ExternalInput").ap()
    w_out = nc.dram_tensor("w_out", [depth, D, D], F32, kind="ExternalInput").ap()
    ffn_up = nc.dram_tensor("ffn_up", [depth, D, 2 * DFF], F32, kind="ExternalInput").ap()
    ffn_cw = nc.dram_tensor("ffn_cw", [depth, 128, 44, 3], F32, kind="ExternalInput").ap()
    ffn_dn = nc.dram_tensor("ffn_dn", [depth, DFF, D], F32, kind="ExternalInput").ap()

    nab = (depth + 1) // 2
    W = {}
    W['consts'] = nc.dram_tensor("consts", [128, 13, 128], F32, kind="ExternalInput").ap()
    ncd = depth // 2
    if ncd > 0:
        W['cd_w_in'] = nc.dram_tensor("cd_w_in", [ncd, D, 3072], F32, kind="ExternalInput").ap()
        W['gd_cw'] = nc.dram_tensor("gd_cw", [ncd, 128, 12, 4], F32, kind="ExternalInput").ap()
        W['gd_vec'] = nc.dram_tensor("gd_vec", [ncd, 128, 4, 2], F32, kind="ExternalInput").ap()
        W['gd_g'] = nc.dram_tensor("gd_g", [ncd, 128, 1], F32, kind="ExternalInput").ap()
        W['q_g'] = nc.dram_tensor("q_g", [ncd, 128, 3], F32, kind="ExternalInput").ap()
        W['kv_g'] = nc.dram_tensor("kv_g", [ncd, 128, 2], F32, kind="ExternalInput").ap()
        W['w_qb'] = nc.dram_tensor("w_qb", [ncd, 384, 1024], F32, kind="ExternalInput").ap()
        W['w_kvb'] = nc.dram_tensor("w_kvb", [ncd, 256, 1024], F32, kind="ExternalInput").ap()
    W['ab_w_in'] = nc.dram_tensor("ab_w_in", [nab, D, 3072], F32, kind="ExternalInput").ap()
    W['hg_lb'] = nc.dram_tensor("hg_lb", [128, 2, 4], F32, kind="ExternalInput").ap()
    W['hg_g'] = nc.dram_tensor("hg_g", [nab, 128, 1], F32, kind="ExternalInput").ap()
    W['lru_cw'] = nc.dram_tensor("lru_cw", [nab, 128, 4, 4], F32, kind="ExternalInput").ap()
    W['lru_vec'] = nc.dram_tensor("lru_vec", [nab, 128, 4, 4], F32, kind="ExternalInput").ap()
    W['lru_gw'] = nc.dram_tensor("lru_gw", [nab, 128, 2, 4, 128], F32, kind="ExternalInput").ap()
    ps = [es.enter_context(nc.psum_tensor("ps%d" % i, [128, 512], F32)) for i in range(8)]
    PS = ["ps%d" % i for i in range(8)]
    ones_f = es.enter_context(nc.sbuf_tensor("ones_f", [128, 128], F32))
    mod = es.enter_context(nc.sbuf_tensor("mod", [128, 48, NSEQ], F32))
    ng = es.enter_context(nc.sbuf_tensor("ng", [128, 4, 8], F32))
    gsc = es.enter_context(nc.sbuf_tensor("gsc", [128, 2, 8, NSEQ], F32))
    csil = es.enter_context(nc.sbuf_tensor("csil", [128, 8, NSEQ], F32))
    cx.op('dve', lambda e: e.memset(ones_f[:], 1.0), w=['ones_f'])
    cx.dma('sp', csil[:], cT[:, :, :], w=['csil'])
    cx.op('act', lambda e: e.activation(out=csil[:], in_=csil[:], func=AF.Silu), r=['csil'], w=['csil'])

    with SB(nc, "cp", [128, 2, 8, 512], F32) as cp:
        k = 0
        for s in seqs:
            for (t0, n) in s.tiles:
                sl = k % 2
                k += 1
                cx.dma('sp', cp[:, sl, :, :n], s.xin.rearrange("(c p) t -> p c t", p=128)[:, :, t0:t0 + n],
                       w=['cp%d' % sl])
                cx.dma('pool', s.xT.rearrange("(c p) t -> p c t", p=128)[:, :, t0:t0 + n], cp[:, sl, :, :n],
                       r=['cp%d' % sl], w=['xT%d' % s.i])
        cx.barrier()

    def rms_stats(src, n, sq, rstd, psn, srcres, nfeat_chunks=8):
        for c in range(nfeat_chunks):
            cx.op('act', lambda e, c=c: e.activation(out=sq[:, c, :n], in_=src(c), func=AF.Square),
                  r=srcres, w=['sq'])
        for c in range(nfeat_chunks):
            cx.op('pe', lambda e, c=c: e.matmul(ps[psn][:, :n], ones_f[:], sq[:, c, :n],
                                                 start=(c == 0), stop=(c == nfeat_chunks - 1)),
                  r=['sq', 'ones_f'], w=[PS[psn]])
        cx.op('dve', lambda e: e.tensor_scalar(rstd[:, :n], ps[psn][:, :n], 1.0 / (128 * nfeat_chunks), EPS,
                                               ALU.mult, ALU.add), r=[PS[psn]], w=['rstd'])
        cx.op('dve', lambda e: e.reciprocal(rstd[:, :n], rstd[:, :n]), r=['rstd'], w=['rstd'])
        cx.op('act', lambda e: e.activation(out=rstd[:, :n], in_=rstd[:, :n], func=AF.Sqrt), r=['rstd'], w=['rstd'])

    for l in range(depth):
        with SB(nc, "aw", [128, 2, 8, 768], F32) as aw, SB(nc, "adb", [128, 48], F32) as adb:
            cx.dma('sp', adb[:], ada_bT[l], w=['adb'])
            cx.dma('sp', ng[:], norm_gT[l], w=['ng'])
            for g in range(8):
                sl = g % 2
                cx.dma('sp', aw[:, sl], ada_w[l].rearrange("(c p) f -> p c f", p=128)[:, :, g * 768:(g + 1) * 768],
                       w=['aw%d' % sl])
                for j in range(6):
                    fc = g * 6 + j
                    pn = fc % 2
                    for c in range(8):
                        cx.op('pe', lambda e, c=c, j=j, sl=sl, pn=pn: e.matmul(
                            ps[pn][:, :NSEQ], aw[:, sl, c, j * 128:(j + 1) * 128], csil[:, c, :],
                            start=(c == 0), stop=(c == 7)), r=['aw%d' % sl, 'csil'], w=[PS[pn]])
                    cx.op('dve', lambda e, fc=fc, pn=pn: e.tensor_scalar(
                        mod[:, fc, :], ps[pn][:, :NSEQ], adb[:, fc:fc + 1], None, ALU.add),
                        r=[PS[pn], 'adb'], w=['mod'])
            for k2, (gi, sc0) in enumerate(((0, 8), (2, 32))):
                for c in range(8):
                    cx.op('dve', lambda e, k2=k2, gi=gi, sc0=sc0, c=c: e.tensor_scalar(
                        gsc[:, k2, c, :], mod[:, sc0 + c, :], 1.0, ng[:, gi, c:c + 1], ALU.add, ALU.mult),
                        r=['mod', 'ng'], w=['gsc'])
            cx.barrier()

        if l % 2 == 0:
            emit_mixer(nc, cx, ps, PS, l, seqs, mod, gsc, ng, ones_f, rms_stats, W)
        else:
            emit_cd(nc, cx, ps, PS, l, seqs, mod, gsc, ng, ones_f, rms_stats, W)

        with (SB(nc, "wst", [128, 2, 8, 512], F32) as wst,
              SB(nc, "wo", [128, 8, D], BF16) as wo,
              SB(nc, "ot", [128, 2, 8, 512], BF16) as ot,
              SB(nc, "xt", [128, 2, 8, 512], F32) as xt,
              SB(nc, "yt", [128, 8, 512], F32) as yt,
              SB(nc, "sq", [128, 8, 512], F32) as sq,
              SB(nc, "rstd", [128, 512], F32) as rstd,
              SB(nc, "tmp", [128, 512], F32) as tmp,
              SB(nc, "h2", [128, 2, 8, 512], BF16) as h2):
            for g in range(2):
                cx.dma('sp', wst[:, g], w_out[l].rearrange("(c p) f -> p c f", p=128)[:, :, g * 512:(g + 1) * 512],
                       w=['wst%d' % g])
                cx.op('act', lambda e, g=g: e.activation(out=wo[:, :, g * 512:(g + 1) * 512], in_=wst[:, g],
                                                         func=AF.Copy), r=['wst%d' % g], w=['wo'])
            k = 0
            for s in seqs:
                for (t0, n) in s.tiles:
                    sl = k % 2
                    k += 1
                    XT, OT, H2 = 'xt%d' % sl, 'ot%d' % sl, 'h2%d' % sl
                    cx.dma('sp', ot[:, sl, :, :n], s.oT.rearrange("(c p) t -> p c t", p=128)[:, :, t0:t0 + n],
                           r=['oT%d' % s.i], w=[OT])
                    cx.dma('sp', xt[:, sl, :, :n], s.xT.rearrange("(c p) t -> p c t", p=128)[:, :, t0:t0 + n],
                           r=['xT%d' % s.i], w=[XT])
                    for m in range(8):
                        pn = m % 4
                        for c in range(8):
                            cx.op('pe', lambda e, m=m, c=c, pn=pn: e.matmul(
                                ps[pn][:, :n], wo[:, c, m * 128:(m + 1) * 128], ot[:, sl, c, :n],
                                start=(c == 0), stop=(c == 7)), r=['wo', OT], w=[PS[pn]])
                        cx.op('act', lambda e, m=m, pn=pn: e.activation(out=yt[:, m, :n], in_=ps[pn][:, :n],
                                                                        func=AF.Copy), r=[PS[pn]], w=['yt'])
                    rms_stats(lambda c: yt[:, c, :n], n, sq, rstd, 4, ['yt'])
                    for c in range(8):
                        cx.op('dve', lambda e, c=c: e.tensor_tensor(tmp[:, :n], yt[:, c, :n], rstd[:, :n], ALU.mult),
                              r=['yt', 'rstd'], w=['tmp'])
                        cx.op('dve', lambda e, c=c: e.tensor_scalar(
                            tmp[:, :n], tmp[:, :n], ng[:, 1, c:c + 1], mod[:, 16 + c, s.i:s.i + 1],
                            ALU.mult, ALU.mult), r=['tmp', 'ng', 'mod'], w=['tmp'])
                        cx.op('dve', lambda e, c=c: e.tensor_tensor(xt[:, sl, c, :n], xt[:, sl, c, :n], tmp[:, :n],
                                                                    ALU.add), r=['tmp', XT], w=[XT])
                    cx.dma('pool', s.xT.rearrange("(c p) t -> p c t", p=128)[:, :, t0:t0 + n], xt[:, sl, :, :n],
                           r=[XT], w=['xT%d' % s.i])
                    rms_stats(lambda c: xt[:, sl, c, :n], n, sq, rstd, 5, [XT])
                    for c in range(8):
                        cx.op('dve', lambda e, c=c: e.tensor_tensor(tmp[:, :n], xt[:, sl, c, :n], rstd[:, :n], ALU.mult),
                              r=[XT, 'rstd'], w=['tmp'])
                        cx.op('act', lambda e, c=c: e.activation(
                            out=h2[:, sl, c, :n], in_=tmp[:, :n], func=AF.Identity,
                            scale=gsc[:, 1, c, s.i:s.i + 1], bias=mod[:, 24 + c, s.i:s.i + 1]),
                            r=['tmp', 'gsc', 'mod'], w=[H2])
                    cx.dma('pool', s.h2T.rearrange("(c p) t -> p c t", p=128)[:, :, t0:t0 + n], h2[:, sl, :, :n],
                           r=[H2], w=['h2T%d' % s.i])
            cx.barrier()

        with (SB(nc, "wst", [128, 2, 2816], F32) as wst,
              SB(nc, "wu", [128, 8, 2 * DFF], BF16) as wu,
              SB(nc, "cw", [128, 44, 3], F32) as cw,
              SB(nc, "h2", [128, 2, 8, 512], BF16) as h2,
              SB(nc, "halo", [128, 44, 2], F32) as halo,
              SB(nc, "u", [128, 4, 514], F32) as u,
              SB(nc, "cv", [128, 4, 512], F32) as cv,
              SB(nc, "at", [128, 2, 22, 512], BF16) as at):
            cx.dma('sp', cw[:], ffn_cw[l], w=['cw'])
            k = 0
            for c in range(8):
                for g in range(2):
                    sl = k % 2
                    k += 1
                    cx.dma('sp', wst[:, sl], ffn_up[l][c * 128:(c + 1) * 128, g * 2816:(g + 1) * 2816],
                           w=['wst%d' % sl])
                    cx.op('act' if g else 'dve', (lambda e, c=c, g=g, sl=sl: e.activation(
                        out=wu[:, c, g * 2816:(g + 1) * 2816], in_=wst[:, sl], func=AF.Copy)) if g else
                        (lambda e, c=c, g=g, sl=sl: e.tensor_copy(wu[:, c, g * 2816:(g + 1) * 2816], wst[:, sl])),
                        r=['wst%d' % sl], w=['wu'])
            k = 0
            for s in seqs:
                if s.i == 0:
                    cx.op('dve', lambda e: e.memset(halo[:], 0.0), w=['halo'])
                else:
                    cx.dma('sp', halo[:], s.ffnc_in[l], w=['halo'])
                for (t0, n) in s.tiles:
                    sl = k % 2
                    k += 1
                    H2, AT = 'h2%d' % sl, 'at%d' % sl
                    cx.dma('sp', h2[:, sl, :, :n], s.h2T.rearrange("(c p) t -> p c t", p=128)[:, :, t0:t0 + n],
                           r=['h2T%d' % s.i], w=[H2])
                    for j in range(22):
                        for gv in range(2):
                            ch = gv * 22 + j
                            ui = (j % 2) * 2 + gv
                            pn = ui
                            U, CV = 'u%d' % ui, 'cv%d' % ui
                            for c in range(8):
                                cx.op('pe', lambda e, c=c, ch=ch, pn=pn: e.matmul(
                                    ps[pn][:, :n], wu[:, c, ch * 128:(ch + 1) * 128], h2[:, sl, c, :n],
                                    start=(c == 0), stop=(c == 7)), r=['wu', H2], w=[PS[pn]])
                            cx.op('act', lambda e, ui=ui, ch=ch: e.activation(out=u[:, ui, 0:2], in_=halo[:, ch, :],
                                                                            func=AF.Copy), r=['halo'], w=[U])
                            cx.op('act', lambda e, ui=ui, pn=pn: e.activation(out=u[:, ui, 2:2 + n], in_=ps[pn][:, :n],
                                                                            func=AF.Copy), r=[PS[pn]], w=[U])
                            cx.op('act', lambda e, ui=ui, ch=ch: e.activation(out=halo[:, ch, :], in_=u[:, ui, n:n + 2],
                                                                            func=AF.Copy), r=[U], w=['halo'])
                            cx.op('dve', lambda e, ui=ui, ch=ch: e.tensor_scalar(
                                cv[:, ui, :n], u[:, ui, 0:n], cw[:, ch, 0:1], None, ALU.mult),
                                r=[U, 'cw'], w=[CV])
                            cx.op('dve', lambda e, ui=ui, ch=ch: e.scalar_tensor_tensor(
                                cv[:, ui, :n], u[:, ui, 1:1 + n], cw[:, ch, 1:2], cv[:, ui, :n], ALU.mult, ALU.add),
                                r=[U, 'cw', CV], w=[CV])
                            cx.op('dve', lambda e, ui=ui, ch=ch: e.scalar_tensor_tensor(
                                cv[:, ui, :n], u[:, ui, 2:2 + n], cw[:, ch, 2:3], cv[:, ui, :n], ALU.mult, ALU.add),
                                r=[U, 'cw', CV], w=[CV])
                        gi = (j % 2) * 2
                        cx.op('act', lambda e, gi=gi: e.activation(out=cv[:, gi, :n], in_=cv[:, gi, :n], func=AF.Silu),
                              r=['cv%d' % gi], w=['cv%d' % gi])
                        cx.op('dve', lambda e, gi=gi, j=j: e.tensor_tensor(at[:, sl, j, :n], cv[:, gi, :n],
                                                                         cv[:, gi + 1, :n], ALU.mult),
                              r=['cv%d' % gi, 'cv%d' % (gi + 1)], w=[AT])
                    cx.dma('pool', s.aT.rearrange("(c p) t -> p c t", p=128)[:, :, t0:t0 + n], at[:, sl, :, :n],
                           r=[AT], w=['aT%d' % s.i])
                cx.dma('pool', s.ffnc_out[l], halo[:], r=['halo'], w=['ffnc_out'])
            cx.barrier()

        with (SB(nc, "wst", [128, 2, 11, 512], F32) as wst,
              SB(nc, "wd", [128, 22, D], BF16) as wd,
              SB(nc, "at", [128, 2, 22, 512], BF16) as at,
              SB(nc, "xt", [128, 2, 8, 512], F32) as xt,
              SB(nc, "yt", [128, 8, 512], F32) as yt,
              SB(nc, "sq", [128, 8, 512], F32) as sq,
              SB(nc, "rstd", [128, 512], F32) as rstd,
              SB(nc, "tmp", [128, 512], F32) as tmp):
            k = 0
            for g in range(2):
                for hh in range(2):
                    sl = k % 2
                    k += 1
                    cx.dma('sp', wst[:, sl], ffn_dn[l].rearrange("(c p) f -> p c f", p=128)[
                        :, hh * 11:(hh + 1) * 11, g * 512:(g + 1) * 512], w=['wst%d' % sl])
                    cx.op('act', lambda e, g=g, hh=hh, sl=sl: e.activation(
                        out=wd[:, hh * 11:(hh + 1) * 11, g * 512:(g + 1) * 512], in_=wst[:, sl], func=AF.Copy),
                        r=['wst%d' % sl], w=['wd'])
            k = 0
            for s in seqs:
                for (t0, n) in s.tiles:
                    sl = k % 2
                    k += 1
                    XT, AT = 'xt%d' % sl, 'at%d' % sl
                    cx.dma('sp', at[:, sl, :, :n], s.aT.rearrange("(c p) t -> p c t", p=128)[:, :, t0:t0 + n],
                           r=['aT%d' % s.i], w=[AT])
                    cx.dma('sp', xt[:, sl, :, :n], s.xT.rearrange("(c p) t -> p c t", p=128)[:, :, t0:t0 + n],
                           r=['xT%d' % s.i], w=[XT])
                    for m in range(8):
                        pn = m % 4
                        for c in range(22):
                            cx.op('pe', lambda e, m=m, c=c, pn=pn: e.matmul(
                                ps[pn][:, :n], wd[:, c, m * 128:(m + 1) * 128], at[:, sl, c, :n],
                                start=(c == 0), stop=(c == 21)), r=['wd', AT], w=[PS[pn]])
                        cx.op('act', lambda e, m=m, pn=pn: e.activation(out=yt[:, m, :n], in_=ps[pn][:, :n],
                                                                        func=AF.Copy), r=[PS[pn]], w=['yt'])
                    rms_stats(lambda c: yt[:, c, :n], n, sq, rstd, 4, ['yt'])
                    for c in range(8):
                        cx.op('dve', lambda e, c=c: e.tensor_tensor(tmp[:, :n], yt[:, c, :n], rstd[:, :n], ALU.mult),
                              r=['yt', 'rstd'], w=['tmp'])
                        cx.op('dve', lambda e, c=c: e.tensor_scalar(
                            tmp[:, :n], tmp[:, :n], ng[:, 3, c:c + 1], mod[:, 40 + c, s.i:s.i + 1],
                            ALU.mult, ALU.mult), r=['tmp', 'ng', 'mod'], w=['tmp'])
                        cx.op('dve', lambda e, c=c: e.tensor_tensor(xt[:, sl, c, :n], xt[:, sl, c, :n], tmp[:, :n],
                                                                    ALU.add), r=['tmp', XT], w=[XT])
                    dst = s.yout if l == depth - 1 else s.xT
                    cx.dma('pool', dst.rearrange("(c p) t -> p c t", p=128)[:, :, t0:t0 + n], xt[:, sl, :, :n],
                           r=[XT], w=['xT%d' % s.i])
            cx.barrier()


def emit_mixer(nc, cx, ps, PS, l, seqs, mod, gsc, ng, ones_f, rms_stats, W):
    ab = (l % 2 == 0)
    jl = l // 2
    with ExitStack() as st:
        def T(name, shape, dt):
            return st.enter_context(SB(nc, name, shape, dt))
        xt = T("xt", [128, 2, 8, 512], F32)
        sq = T("sq", [128, 8, 512], F32)
        rstd = T("rstd", [128, 512], F32)
        tmp = T("tmp", [128, 512], F32)
        h = T("h", [128, 8, 512], BF16)
        ot = T("otile", [128, 2, 8, 512], BF16)
        if ab:
            wst = T("wst", [128, 2, 1536], F32)
            win = T("win", [128, 8, 3072], BF16)
            P4 = T("P4", [128, 4, 512], F32)
            sig = T("sig", [128, 512], F32)
            kk = T("kk", [128, 512], F32)
            qq = T("qq", [128, 512], F32)
            lf = T("lf", [128, 512], F32)
            bb = T("bb", [128, 513], F32)
            dd = T("dd", [128, 512], F32)
            ee = T("ee", [128, 512], F32)
            qt = T("qt", [128, 512], BF16)
            kt = T("kt", [128, 512], BF16)
            qe = T("qe", [128, 512], BF16)
            kl = T("kl", [128, 512], BF16)
            vb = T("vb", [128, 512], BF16)
            edl = T("edl", [128, 8], F32)
            ob = T("ob", [128, 512], F32)
            attm = T("attm", [128, 128], BF16)
            attf = T("attf", [128, 128], F32)
            vtok = T("vtok", [128, 128], BF16)
            kltok = T("kltok", [128, 128], BF16)
            S = T("S", [128, 4, 128], F32)
            Sb = T("Sb", [128, 4, 128], BF16)
            onesr = T("onesr", [128, 512], F32)
            cst = T("cst", [128, 3, 128], F32)
            identb = T("identb", [128, 128], BF16)
            lbl = T("lbl", [128, 2, 4], F32)
            lbv = T("lbv", [128, 4], F32)
            oml = T("oml", [128, 4], F32)
            noml = T("noml", [128, 4], F32)
            hgg = T("hgg", [128, 1], F32)
            lxb = T("lxb", [128, 515], F32)
            lhalo = T("lhalo", [128, 4, 3], F32)
            lcw = T("lcw", [128, 4, 4], F32)
            lvec = T("lvec", [128, 4, 4], F32)
            m8sp = T("m8sp", [128, 4], F32)
            gst = T("gst", [128, 2, 4, 128], F32)
            gw = T("gw", [128, 2, 4, 128], BF16)
            hst = T("hst", [128, 4], F32)
            xc = T("xc", [128, 512], F32)
            xcb = T("xcb", [128, 512], BF16)
            rr = sig
            ii = kk
            aa = qq
            uu = lf
            hh = dd
            ly = ee
            gl = ob
            for c2 in range(16):
                c, hf = c2 // 2, c2 % 2
                sl = hf
                cx.dma('sp', wst[:, sl], W['ab_w_in'][jl][c * 128:(c + 1) * 128, hf * 1536:(hf + 1) * 1536],
                       w=['wst%d' % sl])
                cx.op('act' if sl else 'dve',
                      (lambda e: e.activation(out=win[:, c, hf * 1536:(hf + 1) * 1536], in_=wst[:, sl], func=AF.Copy))
                      if sl else (lambda e: e.tensor_copy(win[:, c, hf * 1536:(hf + 1) * 1536], wst[:, sl])),
                      r=['wst%d' % sl], w=['win'])
            cx.dma('sp', cst[:], W['consts'][:, 0:3, :], w=['cst'])
            cx.op('dve', lambda e: e.tensor_copy(identb[:], cst[:, 0, :]), r=['cst'], w=['identb'])
            cx.op('dve', lambda e: e.memset(onesr[:], 1.0), w=['onesr'])
            cx.dma('sp', lbl[:], W['hg_lb'][:, :, :], w=['lbl'])
            cx.dma('sp', hgg[:], W['hg_g'][jl], w=['hgg'])
            if jl == 0:
                cx.op('dve', lambda e: e.memset(lbv[:], 0.0), w=['lbv'])
            else:
                cx.op('dve', lambda e: e.tensor_tensor(lbv[:], lbl[:, 1, :], lbl[:, 0, :], ALU.subtract),
                      r=['lbl'], w=['lbv'])
                cx.op('act', lambda e: e.activation(out=lbv[:], in_=lbv[:], func=AF.Sigmoid), r=['lbv'], w=['lbv'])
            cx.op('dve', lambda e: e.tensor_scalar(oml[:], lbv[:], -1.0, 1.0, ALU.mult, ALU.add), r=['lbv'], w=['oml'])
            cx.op('dve', lambda e: e.tensor_scalar(noml[:], oml[:], -1.0, None, ALU.mult), r=['oml'], w=['noml'])
            cx.dma('sp', lcw[:], W['lru_cw'][jl], w=['lcw'])
            cx.dma('sp', lvec[:], W['lru_vec'][jl], w=['lvec'])
            cx.dma('sp', gst[:], W['lru_gw'][jl], w=['gst'])
            cx.op('dve', lambda e: e.tensor_copy(gw[:], gst[:]), r=['gst'], w=['gw'])
            cx.op('act', lambda e: e.activation(out=m8sp[:], in_=lvec[:, :, 3], func=AF.Exp, scale=-1.0),
                  r=['lvec'], w=['m8sp'])
            cx.op('dve', lambda e: e.tensor_scalar(m8sp[:], m8sp[:], 1.0, None, ALU.add), r=['m8sp'], w=['m8sp'])
            cx.op('act', lambda e: e.activation(out=m8sp[:], in_=m8sp[:], func=AF.Ln), r=['m8sp'], w=['m8sp'])
            cx.op('dve', lambda e: e.tensor_scalar(m8sp[:], m8sp[:], -8.0, None, ALU.mult), r=['m8sp'], w=['m8sp'])
        k = 0
        for s in seqs:
            L = 64 if s.T % 64 == 0 else s.T
            if ab:
                if s.i == 0:
                    cx.op('dve', lambda e: e.memset(S[:], 0.0), w=['S'])
                    cx.op('dve', lambda e: e.memset(hst[:], 0.0), w=['hst'])
                    cx.op('dve', lambda e: e.memset(lhalo[:], 0.0), w=['lhalo'])
                else:
                    cx.dma('sp', S[:], s.hg_in[jl], w=['S'])
                    cx.dma('sp', hst[:], s.lru_in[jl], w=['hst'])
                    cx.dma('sp', lhalo[:], s.lruc_in[jl], w=['lhalo'])
                cx.op('act', lambda e: e.activation(out=Sb[:], in_=S[:], func=AF.Copy), r=['S'], w=['Sb'])
            for (t0, n) in s.tiles:
                sl = k % 2
                k += 1
                XT, OT = 'xt%d' % sl, 'ot%d' % sl
                cx.dma('sp', xt[:, sl, :, :n], s.xT.rearrange("(c p) t -> p c t", p=128)[:, :, t0:t0 + n],
                       r=['xT%d' % s.i], w=[XT])
                rms_stats(lambda c: xt[:, sl, c, :n], n, sq, rstd, 6, [XT])
                for c in range(8):
                    cx.op('dve', lambda e, c=c: e.tensor_tensor(tmp[:, :n], xt[:, sl, c, :n], rstd[:, :n], ALU.mult),
                          r=[XT, 'rstd'], w=['tmp'])
                    cx.op('act', lambda e, c=c: e.activation(
                        out=(h[:, c, :n] if ab else ot[:, sl, c, :n]), in_=tmp[:, :n], func=AF.Identity,
                        scale=gsc[:, 0, c, s.i:s.i + 1], bias=mod[:, 0 + c, s.i:s.i + 1]),
                        r=['tmp', 'gsc', 'mod'], w=(['h'] if ab else [OT]))
                if ab:
                    def proj(ch, pn):
                        for c in range(8):
                            cx.op('pe', lambda e, c=c: e.matmul(ps[pn][:, :n], win[:, c, ch * 128:(ch + 1) * 128],
                                                                 h[:, c, :n], start=(c == 0), stop=(c == 7)),
                                  r=['win', 'h'], w=[PS[pn]])
                    nch = n // L
                    G = min(128, n)
                    mi = 1 if L == 64 else 2
                    for hd in range(0 if 'H' in AB_SKIP else 4):
                        for j4 in range(4):
                            pn = j4 % 2
                            proj(j4 * 4 + hd, pn)
                            cx.op('act', lambda e, j4=j4, pn=pn: e.activation(out=P4[:, j4, :n], in_=ps[pn][:, :n],
                                                                              func=AF.Copy), r=[PS[pn]], w=['P4'])
                        cx.op('act', lambda e: e.activation(out=sig[:, :n], in_=P4[:, 1, :n], func=AF.Sigmoid),
                              r=['P4'], w=['sig'])
                        cx.op('dve', lambda e: e.tensor_scalar(lf[:, :n], sig[:, :n], oml[:, hd:hd + 1], lbv[:, hd:hd + 1],
                                                               ALU.mult, ALU.add), r=['sig', 'oml', 'lbv'], w=['lf'])
                        cx.op('act', lambda e: e.activation(out=lf[:, :n], in_=lf[:, :n], func=AF.Ln), r=['lf'], w=['lf'])
                        cx.op('dve', lambda e: e.tensor_scalar(kk[:, :n], sig[:, :n], noml[:, hd:hd + 1], oml[:, hd:hd + 1],
                                                               ALU.mult, ALU.add), r=['sig', 'oml', 'noml'], w=['kk'])
                        cx.op('act', lambda e: e.activation(out=qq[:, :n], in_=P4[:, 0, :n], func=AF.Silu),
                              r=['P4'], w=['qq'])
                        cx.op('act', lambda e: e.activation(out=vb[:, :n], in_=P4[:, 2, :n], func=AF.Copy),
                              r=['P4'], w=['vb'])
                        cx.op('dve', lambda e: e.memset(bb[:, 0:1], 0.0), w=['bb'])
                        cx.op('dve', lambda e: e.tensor_tensor_scan(bb[:, 1:1 + n], onesr[:, :n], lf[:, :n], 0.0,
                                                                    ALU.mult, ALU.add), r=['onesr', 'lf'], w=['bb'])
                        b3 = bb[:, 1:1 + n].rearrange("p (c l) -> p c l", l=L)
                        mid3 = b3[:, :, L // 2:L // 2 + 1].to_broadcast([128, nch, L])
                        last3 = b3[:, :, L - 1:L].to_broadcast([128, nch, L])
                        prev3 = bb[:, 0:n].rearrange("p (c l) -> p c l", l=L)[:, :, 0:1].to_broadcast([128, nch, L])
                        d3 = dd[:, :n].rearrange("p (c l) -> p c l", l=L)

                        def expmul(ref3, scale, src, dst, DST):
                            cx.op('dve', lambda e: e.tensor_tensor(d3, b3, ref3, ALU.subtract), r=['bb'], w=['dd'])
                            cx.op('act', lambda e: e.activation(out=ee[:, :n], in_=dd[:, :n], func=AF.Exp, scale=scale),
                                  r=['dd'], w=['ee'])
                            cx.op('dve', lambda e: e.tensor_tensor(dst[:, :n], src[:, :n], ee[:, :n], ALU.mult),
                                  r=['ee', 'qq', 'kk'], w=[DST])
                        expmul(mid3, 1.0, qq, qt, 'qt')
                        expmul(mid3, -1.0, kk, kt, 'kt')
                        expmul(prev3, 1.0, qq, qe, 'qe')
                        expmul(last3, -1.0, kk, kl, 'kl')
                        cx.op('dve', lambda e: e.tensor_tensor(
                            edl[:, :nch], bb[:, 1:1 + n].rearrange("p (c l) -> p c l", l=L)[:, :, L - 1],
                            bb[:, 0:n].rearrange("p (c l) -> p c l", l=L)[:, :, 0], ALU.subtract), r=['bb'], w=['edl'])
                        cx.op('act', lambda e: e.activation(out=edl[:, :nch], in_=edl[:, :nch], func=AF.Exp),
                              r=['edl'], w=['edl'])
                        psb = ps[5].bitcast(BF16)
                        for g0 in range(0, n, G):
                            cx.op('pe', lambda e, g0=g0: e.matmul(ps[2][:G, :G], kt[:, g0:g0 + G], qt[:, g0:g0 + G],
                                                                   start=True, stop=True), r=['kt', 'qt'], w=[PS[2]])
                            cx.op('dve', lambda e: e.tensor_scalar(attf[:G, :G], ps[2][:G, :G], -1e30, 1e30, ALU.max, ALU.min),
                                  r=[PS[2]], w=['attf'])
                            cx.op('dve', lambda e: e.tensor_tensor(attm[:G, :G], attf[:G, :G], cst[:G, mi, :G], ALU.mult),
                                  r=['attf', 'cst'], w=['attm'])
                            cx.op('pe', lambda e, g0=g0: e.transpose(psb[:G, 0:128], vb[:, g0:g0 + G], identb[:]),
                                  r=['vb', 'identb'], w=[PS[5]])
                            cx.op('act', lambda e: e.activation(out=vtok[:G, :], in_=psb[:G, 0:128], func=AF.Copy),
                                  r=[PS[5]], w=['vtok'])
                            cx.op('pe', lambda e, g0=g0: e.transpose(psb[:G, 0:128], kl[:, g0:g0 + G], identb[:]),
                                  r=['kl', 'identb'], w=[PS[5]])
                            cx.op('act', lambda e: e.activation(out=kltok[:G, :], in_=psb[:G, 0:128], func=AF.Copy),
                                  r=[PS[5]], w=['kltok'])
                            for ci in range(G // L):
                                c0 = ci * L
                                cidx = (g0 + c0) // L
                                cx.op('pe', lambda e, c0=c0: e.matmul(ps[3][:, c0:c0 + L], vtok[c0:c0 + L, :],
                                                                       attm[c0:c0 + L, c0:c0 + L], start=True, stop=False),
                                      r=['vtok', 'attm'], w=[PS[3]])
                                cx.op('pe', lambda e, c0=c0, g0=g0: e.matmul(ps[3][:, c0:c0 + L], Sb[:, hd, :],
                                                                              qe[:, g0 + c0:g0 + c0 + L], start=False, stop=True),
                                      r=['Sb', 'qe'], w=[PS[3]])
                                cx.op('pe', lambda e, c0=c0: e.matmul(ps[4][:, :128], kltok[c0:c0 + L, :], vtok[c0:c0 + L, :],
                                                                       start=True, stop=True), r=['kltok', 'vtok'], w=[PS[4]])
                                cx.op('dve', lambda e, cidx=cidx: e.scalar_tensor_tensor(
                                    S[:, hd, :], S[:, hd, :], edl[:, cidx:cidx + 1], ps[4][:, :128], ALU.mult, ALU.add),
                                    r=['S', 'edl', PS[4]], w=['S'])
                                cx.op('act', lambda e: e.activation(out=Sb[:, hd, :], in_=S[:, hd, :], func=AF.Copy),
                                      r=['S'], w=['Sb'])
                            cx.op('act', lambda e, g0=g0: e.activation(out=ob[:, g0:g0 + G], in_=ps[3][:, :G], func=AF.Copy),
                                  r=[PS[3]], w=['ob'])
                        cx.op('act', lambda e: e.activation(out=dd[:, :n], in_=ob[:, :n], func=AF.Square), r=['ob'], w=['dd'])
                        cx.op('pe', lambda e: e.matmul(ps[7][:, :n], ones_f[:], dd[:, :n], start=True, stop=True),
                              r=['dd', 'ones_f'], w=[PS[7]])
                        cx.op('dve', lambda e: e.tensor_scalar(ee[:, :n], ps[7][:, :n], 1.0 / 128, EPS, ALU.mult, ALU.add),
                              r=[PS[7]], w=['ee'])
                        cx.op('dve', lambda e: e.reciprocal(ee[:, :n], ee[:, :n]), r=['ee'], w=['ee'])
                        cx.op('act', lambda e: e.activation(out=ee[:, :n], in_=ee[:, :n], func=AF.Sqrt), r=['ee'], w=['ee'])
                        cx.op('dve', lambda e: e.tensor_tensor(ob[:, :n], ob[:, :n], ee[:, :n], ALU.mult),
                              r=['ob', 'ee'], w=['ob'])
                        cx.op('act', lambda e: e.activation(out=dd[:, :n], in_=P4[:, 3, :n], func=AF.Silu), r=['P4'], w=['dd'])
                        cx.op('dve', lambda e: e.scalar_tensor_tensor(ot[:, sl, hd, :n], ob[:, :n], hgg[:, 0:1], dd[:, :n],
                                                                      ALU.mult, ALU.mult), r=['ob', 'hgg', 'dd'], w=[OT])
                    for ch in range(0 if 'L' in AB_SKIP else 4):
                        proj(16 + ch, 0)
                        cx.op('act', lambda e: e.activation(out=lxb[:, 0:3], in_=lhalo[:, ch, :], func=AF.Copy),
                              r=['lhalo'], w=['lxb'])
                        cx.op('act', lambda e: e.activation(out=lxb[:, 3:3 + n], in_=ps[0][:, :n], func=AF.Copy),
                              r=[PS[0]], w=['lxb'])
                        cx.op('act', lambda e: e.activation(out=lhalo[:, ch, :], in_=lxb[:, n:n + 3], func=AF.Copy),
                              r=['lxb'], w=['lhalo'])
                        proj(20 + ch, 1)
                        cx.op('act', lambda e: e.activation(out=ly[:, :n], in_=ps[1][:, :n], func=AF.Copy),
                              r=[PS[1]], w=['ee'])
                        cx.op('dve', lambda e: e.tensor_scalar(xc[:, :n], lxb[:, 0:n], lcw[:, ch, 0:1], lvec[:, ch, 0:1],
                                                               ALU.mult, ALU.add), r=['lxb', 'lcw', 'lvec'], w=['xc'])
                        for tp in range(1, 4):
                            cx.op('dve', lambda e, tp=tp: e.scalar_tensor_tensor(
                                xc[:, :n], lxb[:, tp:tp + n], lcw[:, ch, tp:tp + 1], xc[:, :n], ALU.mult, ALU.add),
                                r=['lxb', 'lcw', 'xc'], w=['xc'])
                        cx.op('act', lambda e: e.activation(out=xcb[:, :n], in_=xc[:, :n], func=AF.Copy), r=['xc'], w=['xcb'])
                        cx.op('pe', lambda e: e.matmul(ps[2][:, :n], gw[:, 0, ch, :], xcb[:, :n], start=True, stop=True),
                              r=['gw', 'xcb'], w=[PS[2]])
                        cx.op('act', lambda e: e.activation(out=rr[:, :n], in_=ps[2][:, :n], func=AF.Sigmoid,
                                                            bias=lvec[:, ch, 1:2]), r=[PS[2], 'lvec'], w=['sig'])
                        cx.op('pe', lambda e: e.matmul(ps[3][:, :n], gw[:, 1, ch, :], xcb[:, :n], start=True, stop=True),
                              r=['gw', 'xcb'], w=[PS[3]])
                        cx.op('act', lambda e: e.activation(out=ii[:, :n], in_=ps[3][:, :n], func=AF.Sigmoid,
                                                            bias=lvec[:, ch, 2:3]), r=[PS[3], 'lvec'], w=['kk'])
                        cx.op('dve', lambda e: e.tensor_scalar(rr[:, :n], rr[:, :n], m8sp[:, ch:ch + 1], None, ALU.mult),
                              r=['sig', 'm8sp'], w=['sig'])
                        cx.op('act', lambda e: e.activation(out=aa[:, :n], in_=rr[:, :n], func=AF.Exp), r=['sig'], w=['qq'])
                        cx.op('act', lambda e: e.activation(out=uu[:, :n], in_=rr[:, :n], func=AF.Exp, scale=2.0),
                              r=['sig'], w=['lf'])
                        cx.op('dve', lambda e: e.tensor_scalar(uu[:, :n], uu[:, :n], -1.0, 1.0, ALU.mult, ALU.add),
                              r=['lf'], w=['lf'])
                        cx.op('dve', lambda e: e.tensor_scalar(uu[:, :n], uu[:, :n], 1e-12, None, ALU.max),
                              r=['lf'], w=['lf'])
                        cx.op('act', lambda e: e.activation(out=uu[:, :n], in_=uu[:, :n], func=AF.Sqrt), r=['lf'], w=['lf'])
                        cx.op('dve', lambda e: e.tensor_tensor(uu[:, :n], uu[:, :n], ii[:, :n], ALU.mult),
                              r=['lf', 'kk'], w=['lf'])
                        cx.op('dve', lambda e: e.tensor_tensor(uu[:, :n], uu[:, :n], xc[:, :n], ALU.mult),
                              r=['lf', 'xc'], w=['lf'])
                        cx.op('dve', lambda e: e.tensor_tensor_scan(hh[:, :n], aa[:, :n], uu[:, :n], hst[:, ch:ch + 1],
                                                                    ALU.mult, ALU.add), r=['qq', 'lf', 'hst'], w=['dd'])
                        cx.op('act', lambda e: e.activation(out=hst[:, ch:ch + 1], in_=hh[:, n - 1:n], func=AF.Copy),
                              r=['dd'], w=['hst'])
                        cx.op('dve', lambda e: e.tensor_tensor(gl[:, :n], ly[:, :n], ly[:, :n], ALU.mult), r=['ee'], w=['ob'])
                        cx.op('dve', lambda e: e.tensor_scalar(gl[:, :n], gl[:, :n], 0.044715, 1.0, ALU.mult, ALU.add),
                              r=['ob'], w=['ob'])
                        cx.op('dve', lambda e: e.tensor_tensor(gl[:, :n], gl[:, :n], ly[:, :n], ALU.mult),
                              r=['ob', 'ee'], w=['ob'])
                        cx.op('act', lambda e: e.activation(out=gl[:, :n], in_=gl[:, :n], func=AF.Sigmoid,
                                                            scale=1.5957691216057308), r=['ob'], w=['ob'])
                        cx.op('dve', lambda e: e.tensor_tensor(gl[:, :n], gl[:, :n], ly[:, :n], ALU.mult),
                              r=['ob', 'ee'], w=['ob'])
                        cx.op('dve', lambda e: e.tensor_tensor(ot[:, sl, 4 + ch, :n], hh[:, :n], gl[:, :n], ALU.mult),
                              r=['dd', 'ob'], w=[OT])
                cx.dma('pool', s.oT.rearrange("(c p) t -> p c t", p=128)[:, :, t0:t0 + n], ot[:, sl, :, :n],
                       r=[OT], w=['oT%d' % s.i])
            if ab:
                cx.dma('pool', s.hg_out[jl], S[:], r=['S'], w=['hg_out'])
                cx.dma('pool', s.lru_out[jl], hst[:], r=['hst'], w=['lru_out'])
                cx.dma('pool', s.lruc_out[jl], lhalo[:], r=['lhalo'], w=['lruc_out'])
        cx.barrier()


MLA_SCALE = (128 + 64) ** -0.5
CD_STAGE = int(os.environ.get('CD_STAGE', '99'))
CD_SUB = int(os.environ.get('CD_SUB', '99'))
M2_SUB = int(os.environ.get('M2_SUB', '99'))
AB_SKIP = os.environ.get('AB_SKIP', '')


def emit_cd(nc, cx, ps, PS, l, seqs, mod, gsc, ng, ones_f, rms_stats, W):
    jl = l // 2
    RT = lambda a: a.rearrange("(c p) t -> p c t", p=128)
    with ExitStack() as st:
        def T(name, shape, dt):
            return st.enter_context(SB(nc, name, shape, dt))
        xt = T("xt", [128, 8, 512], F32)
        sq = T("sq", [128, 8, 512], F32)
        rstd = T("rstd", [128, 512], F32)
        tmp = T("tmp", [128, 512], F32)
        h = T("h", [128, 8, 512], BF16)
        ot = T("otile", [128, 2, 4, 512], BF16)
        wst = T("wst", [128, 2, 1536], F32)
        win = T("win", [128, 8, 3072], BF16)
        cst = T("cst", [128, 13, 128], F32)
        identb = T("identb", [128, 128], BF16)
        onesr = T("onesr", [128, 512], F32)
        pre = T("pre", [128, 3, 515], F32)
        cvx = T("cvx", [128, 3, 512], F32)
        ghalo = T("ghalo", [128, 12, 3], F32)
        gcw = T("gcw", [128, 12, 4], F32)
        gvec = T("gvec", [128, 4, 2], F32)
        nexpa = T("nexpa", [128, 4], F32)
        gdg = T("gdg", [128, 1], F32)
        c21 = T("c21", [128, 512], F32)
        c22 = T("c22", [128, 512], F32)
        c23 = T("c23", [128, 512], F32)
        gz = T("gz", [128, 512], F32)
        betaB = T("betaB", [128, 512], F32)
        gg = T("gg", [128, 513], F32)
        egr = T("egr", [128, 512], F32)
        ela = T("ela", [128, 512], F32)
        edl = T("edl", [128, 8], F32)
        w1 = T("w1", [128, 512], F32)
        w2 = T("w2", [128, 512], F32)
        vbet = T("vbet", [128, 512], F32)
        kbe = T("kbe", [128, 512], F32)
        kd = T("kd", [128, 512], F32)
        qbf = T("qbf", [128, 512], BF16)
        kbf = T("kbf", [128, 512], BF16)
        qg = T("qg", [128, 512], BF16)
        ob = T("ob", [128, 512], F32)
        gcolS = [T("gcol%d" % i_, [128, 1], F32) for i_ in range(2)]
        DBeS = [T("DBe%d" % i_, [128, 128], F32) for i_ in range(2)]
        XYS = [T("XY%d" % i_, [128, 2, 2, 128], F32) for i_ in range(2)]
        RmS = [T("Rm%d" % i_, [128, 256], F32) for i_ in range(2)]
        qkTS = [T("qkT%d" % i_, [128, 128], BF16) for i_ in range(2)]
        kdtokS = [T("kdtok%d" % i_, [128, 128], BF16) for i_ in range(2)]
        wkTS = [T("wkT%d" % i_, [128, 128], BF16) for i_ in range(2)]
        utokS = [T("utok%d" % i_, [128, 128], BF16) for i_ in range(2)]
        S = T("S", [128, 4, 128], F32)
        Sb = T("Sb", [128, 4, 128], BF16)
        PQ = T("PQ", [128, 3, 512], F32)
        qng = T("qng", [128, 3], F32)
        kvg = T("kvg", [128, 2], F32)
        qan = T("qan", [128, 3, 512], BF16)
        ckv = T("ckv", [128, 2, 512], F32)
        tokst = T("tokst", [128, 4, 320], F32)
        rope = T("rope", [64, 2, 512], F32)
        krr = T("krr", [64, 512], F32)
        for c2 in range(16):
            c, hf = c2 // 2, c2 % 2
            sl = hf
            cx.dma('sp', wst[:, sl], W['cd_w_in'][jl][c * 128:(c + 1) * 128, hf * 1536:(hf + 1) * 1536],
                   w=['wst%d' % sl])
            cx.op('act' if sl else 'dve',
                  (lambda e: e.activation(out=win[:, c, hf * 1536:(hf + 1) * 1536], in_=wst[:, sl], func=AF.Copy))
                  if sl else (lambda e: e.tensor_copy(win[:, c, hf * 1536:(hf + 1) * 1536], wst[:, sl])),
                  r=['wst%d' % sl], w=['win'])
        cx.dma('sp', cst[:], W['consts'][:, :, :], w=['cst'])
        cx.op('dve', lambda e: e.tensor_copy(identb[:], cst[:, 0, :]), r=['cst'], w=['identb'])
        cx.op('dve', lambda e: e.memset(onesr[:], 1.0), w=['onesr'])
        cx.dma('sp', gcw[:], W['gd_cw'][jl], w=['gcw'])
        cx.dma('sp', gvec[:], W['gd_vec'][jl], w=['gvec'])
        cx.dma('sp', gdg[:], W['gd_g'][jl], w=['gdg'])
        cx.dma('sp', qng[:], W['q_g'][jl], w=['qng'])
        cx.dma('sp', kvg[:], W['kv_g'][jl], w=['kvg'])
        cx.op('act', lambda e: e.activation(out=nexpa[:], in_=gvec[:, :, 0], func=AF.Exp), r=['gvec'], w=['nexpa'])
        cx.op('dve', lambda e: e.tensor_scalar(nexpa[:], nexpa[:], -1.0, None, ALU.mult), r=['nexpa'], w=['nexpa'])
        k = 0
        for s in seqs:
            L = 64 if s.T % 64 == 0 else s.T
            nsteps = int(round(math.log2(L)))
            past = s.past
            if s.i == 0:
                cx.op('dve', lambda e: e.memset(S[:], 0.0), w=['S'])
                cx.op('dve', lambda e: e.memset(ghalo[:], 0.0), w=['ghalo'])
            else:
                cx.dma('sp', S[:], s.gd_in[jl], w=['S'])
                cx.dma('sp', ghalo[:], s.gdc_in[jl], w=['ghalo'])
                cx.dma('pool', s.ckvT[:, 0:past], s.lat_inT[jl], w=['ckvT%d' % s.i])
                cx.dma('pool', s.krT[:, 0:past], s.kr_inT[jl], w=['krT%d' % s.i])
            cx.op('act', lambda e: e.activation(out=Sb[:], in_=S[:], func=AF.Copy), r=['S'], w=['Sb'])
            for (t0, n) in s.tiles:
                sl = k % 2
                k += 1
                OT = 'ot%d' % sl
                cx.dma('sp', xt[:, :, :n], RT(s.xT)[:, :, t0:t0 + n], r=['xT%d' % s.i], w=['xt'])
                cx.dma('sp', rope[:, 0, :n], s.ropeC[:, t0:t0 + n], w=['rope'])
                cx.dma('sp', rope[:, 1, :n], s.ropeS[:, t0:t0 + n], w=['rope'])
                rms_stats(lambda c: xt[:, c, :n], n, sq, rstd, 6, ['xt'])
                for c in range(8):
                    cx.op('dve', lambda e: e.tensor_tensor(tmp[:, :n], xt[:, c, :n], rstd[:, :n], ALU.mult),
                          r=['xt', 'rstd'], w=['tmp'])
                    cx.op('act', lambda e: e.activation(out=h[:, c, :n], in_=tmp[:, :n], func=AF.Identity,
                                                        scale=gsc[:, 0, c, s.i:s.i + 1], bias=mod[:, c, s.i:s.i + 1]),
                          r=['tmp', 'gsc', 'mod'], w=['h'])

                def proj(ch, pn):
                    for c in range(8):
                        cx.op('pe', lambda e, c=c: e.matmul(ps[pn][:, :n], win[:, c, ch * 128:(ch + 1) * 128],
                                                             h[:, c, :n], start=(c == 0), stop=(c == 7)),
                              r=['win', 'h'], w=[PS[pn]])

                def pcopy(ch, pn, dst, DST):
                    proj(ch, pn)
                    cx.op('act', lambda e: e.activation(out=dst, in_=ps[pn][:, :n], func=AF.Copy), r=[PS[pn]], w=[DST])

                def stat128(src, SRC, dst, DST, scl, bias_eps):
                    cx.op('act', lambda e: e.activation(out=w2[:, :n], in_=src, func=AF.Square), r=[SRC], w=['w2'])
                    cx.op('pe', lambda e: e.matmul(ps[6][:, :n], ones_f[:], w2[:, :n], start=True, stop=True),
                          r=['w2', 'ones_f'], w=[PS[6]])
                    cx.op('dve', lambda e: e.tensor_scalar(dst, ps[6][:, :n], scl, bias_eps, ALU.mult, ALU.add),
                          r=[PS[6]], w=[DST])
                    cx.op('dve', lambda e: e.reciprocal(dst, dst), r=[DST], w=[DST])
                    cx.op('act', lambda e: e.activation(out=dst, in_=dst, func=AF.Sqrt), r=[DST], w=[DST])

                pcopy(21, 0, c21[:, :n], 'c21')
                pcopy(22, 1, c22[:, :n], 'c22')
                pcopy(23, 0, c23[:, :n], 'c23')
                for c in range(3):
                    pcopy(16 + c, c % 2, PQ[:, c, :n], 'PQ')
                for c in range(3):
                    cx.op('act', lambda e: e.activation(out=sq[:, c, :n], in_=PQ[:, c, :n], func=AF.Square), r=['PQ'], w=['sq'])
                for c in range(3):
                    cx.op('pe', lambda e: e.matmul(ps[6][:, :n], ones_f[:], sq[:, c, :n], start=(c == 0), stop=(c == 2)),
                          r=['sq', 'ones_f'], w=[PS[6]])
                cx.op('dve', lambda e: e.tensor_scalar(w1[:, :n], ps[6][:, :n], 1.0 / 384, EPS, ALU.mult, ALU.add),
                      r=[PS[6]], w=['w1'])
                cx.op('dve', lambda e: e.reciprocal(w1[:, :n], w1[:, :n]), r=['w1'], w=['w1'])
                cx.op('act', lambda e: e.activation(out=w1[:, :n], in_=w1[:, :n], func=AF.Sqrt), r=['w1'], w=['w1'])
                for c in range(3):
                    cx.op('dve', lambda e: e.tensor_tensor(PQ[:, c, :n], PQ[:, c, :n], w1[:, :n], ALU.mult),
                          r=['PQ', 'w1'], w=['PQ'])
                    cx.op('dve', lambda e: e.tensor_scalar(qan[:, c, :n], PQ[:, c, :n], qng[:, c:c + 1], None, ALU.mult),
                          r=['PQ', 'qng'], w=['qan'])
                cx.dma('pool', RT(s.qanT)[:, :, t0:t0 + n], qan[:, :, :n], r=['qan'], w=['qanT%d' % s.i])
                for c in range(2):
                    pcopy(19 + c, c % 2, PQ[:, c, :n], 'PQ')
                for c in range(2):
                    cx.op('act', lambda e: e.activation(out=sq[:, c, :n], in_=PQ[:, c, :n], func=AF.Square), r=['PQ'], w=['sq'])
                for c in range(2):
                    cx.op('pe', lambda e: e.matmul(ps[6][:, :n], ones_f[:], sq[:, c, :n], start=(c == 0), stop=(c == 1)),
                          r=['sq', 'ones_f'], w=[PS[6]])
                cx.op('dve', lambda e: e.tensor_scalar(w1[:, :n], ps[6][:, :n], 1.0 / 256, EPS, ALU.mult, ALU.add),
                      r=[PS[6]], w=['w1'])
                cx.op('dve', lambda e: e.reciprocal(w1[:, :n], w1[:, :n]), r=['w1'], w=['w1'])
                cx.op('act', lambda e: e.activation(out=w1[:, :n], in_=w1[:, :n], func=AF.Sqrt), r=['w1'], w=['w1'])
                for c in range(2):
                    cx.op('dve', lambda e: e.tensor_tensor(PQ[:, c, :n], PQ[:, c, :n], w1[:, :n], ALU.mult),
                          r=['PQ', 'w1'], w=['PQ'])
                    cx.op('dve', lambda e: e.tensor_scalar(ckv[:, c, :n], PQ[:, c, :n], kvg[:, c:c + 1], None, ALU.mult),
                          r=['PQ', 'kvg'], w=['ckv'])
                cx.dma('pool', RT(s.ckvT)[:, :, past + t0:past + t0 + n], ckv[:, :, :n], r=['ckv'], w=['ckvT%d' % s.i])
                cx.op('dve', lambda e: e.tensor_tensor(krr[:, :n], c21[0:64, :n], rope[:, 0, :n], ALU.mult),
                      r=['c21', 'rope'], w=['krr'])
                cx.op('dve', lambda e: e.tensor_tensor(w1[0:64, :n], c23[0:64, :n], rope[:, 1, :n], ALU.mult),
                      r=['c23', 'rope'], w=['w1'])
                cx.op('dve', lambda e: e.tensor_tensor(krr[:, :n], krr[:, :n], w1[0:64, :n], ALU.add),
                      r=['krr', 'w1'], w=['krr'])
                cx.dma('pool', s.krT[:, past + t0:past + t0 + n], krr[:, :n], r=['krr'], w=['krT%d' % s.i])
                nsub = (n + 127) // 128
                for sb_ in range(nsub):
                    a0 = sb_ * 128
                    an = min(128, n - a0)
                    for c in range(2):
                        cx.op('pe', lambda e: e.transpose(ps[4][:an, c * 128:(c + 1) * 128], ckv[:, c, a0:a0 + an], cst[:, 0, :]),
                              r=['ckv', 'cst'], w=[PS[4]])
                    cx.op('pe', lambda e: e.transpose(ps[4][:an, 256:320], krr[:, a0:a0 + an], cst[0:64, 0, 0:64]),
                          r=['krr', 'cst'], w=[PS[4]])
                    cx.op('act', lambda e: e.activation(out=tokst[:an, sb_, :], in_=ps[4][:an, 0:320], func=AF.Copy),
                          r=[PS[4]], w=['tokst'])
                    cx.dma('pool', s.lat_out[jl][t0 + a0:t0 + a0 + an, :], tokst[:an, sb_, 0:256], r=['tokst'], w=['lat_out'])
                    cx.dma('pool', s.kr_out[jl][t0 + a0:t0 + a0 + an, :], tokst[:an, sb_, 256:320], r=['tokst'], w=['kr_out'])
                nch = n // L
                G = min(128, n)
                mI, mS = (1, 4) if L == 64 else (2, 3)
                for hd in range(4 if CD_STAGE >= 2 else 0):
                    for idx, ch in enumerate((hd, 4 + hd, 8 + hd)):
                        proj(ch, idx % 2)
                        cx.op('act', lambda e: e.activation(out=pre[:, idx, 0:3], in_=ghalo[:, ch, :], func=AF.Copy),
                              r=['ghalo'], w=['pre'])
                        cx.op('act', lambda e: e.activation(out=pre[:, idx, 3:3 + n], in_=ps[idx % 2][:, :n], func=AF.Copy),
                              r=[PS[idx % 2]], w=['pre'])
                        cx.op('act', lambda e: e.activation(out=ghalo[:, ch, :], in_=pre[:, idx, n:n + 3], func=AF.Copy),
                              r=['pre'], w=['ghalo'])
                        cx.op('dve', lambda e: e.tensor_scalar(cvx[:, idx, :n], pre[:, idx, 0:n], gcw[:, ch, 0:1], None, ALU.mult),
                              r=['pre', 'gcw'], w=['cvx'])
                        for tp in range(1, 4):
                            cx.op('dve', lambda e: e.scalar_tensor_tensor(cvx[:, idx, :n], pre[:, idx, tp:tp + n],
                                                                          gcw[:, ch, tp:tp + 1], cvx[:, idx, :n], ALU.mult, ALU.add),
                                  r=['pre', 'gcw', 'cvx'], w=['cvx'])
                        cx.op('act', lambda e: e.activation(out=cvx[:, idx, :n], in_=cvx[:, idx, :n], func=AF.Silu),
                              r=['cvx'], w=['cvx'])
                    pcopy(12 + hd, 0, gz[:, :n], 'gz')
                    stat128(cvx[:, 0, :n], 'cvx', w1[:, :n], 'w1', 1.0, EPS)
                    cx.op('dve', lambda e: e.scalar_tensor_tensor(cvx[:, 0, :n], cvx[:, 0, :n], 128 ** -0.5, w1[:, :n],
                                                                  ALU.mult, ALU.mult), r=['cvx', 'w1'], w=['cvx'])
                    stat128(cvx[:, 1, :n], 'cvx', w1[:, :n], 'w1', 1.0, EPS)
                    cx.op('dve', lambda e: e.tensor_tensor(cvx[:, 1, :n], cvx[:, 1, :n], w1[:, :n], ALU.mult),
                          r=['cvx', 'w1'], w=['cvx'])
                    cx.op('act', lambda e: e.activation(out=qbf[:, :n], in_=cvx[:, 0, :n], func=AF.Copy), r=['cvx'], w=['qbf'])
                    cx.op('act', lambda e: e.activation(out=kbf[:, :n], in_=cvx[:, 1, :n], func=AF.Copy), r=['cvx'], w=['kbf'])
                    cx.op('pe', lambda e: e.matmul(ps[2][:, :n], cst[:, 5 + hd, :], c21[:, :n], start=True, stop=True),
                          r=['cst', 'c21'], w=[PS[2]])
                    cx.op('act', lambda e: e.activation(out=betaB[:, :n], in_=ps[2][:, :n], func=AF.Sigmoid),
                          r=[PS[2]], w=['betaB'])
                    cx.op('pe', lambda e: e.matmul(ps[3][:, :n], cst[:, 9 + hd, :], c22[:, :n], start=True, stop=True),
                          r=['cst', 'c22'], w=[PS[3]])
                    cx.op('dve', lambda e: e.tensor_scalar(w1[:, :n], ps[3][:, :n], gvec[:, hd, 1:2], None, ALU.add),
                          r=[PS[3], 'gvec'], w=['w1'])
                    cx.op('act', lambda e: e.activation(out=w2[:, :n], in_=w1[:, :n], func=AF.Abs), r=['w1'], w=['w2'])
                    cx.op('act', lambda e: e.activation(out=w2[:, :n], in_=w2[:, :n], func=AF.Exp, scale=-1.0), r=['w2'], w=['w2'])
                    cx.op('dve', lambda e: e.tensor_scalar(w2[:, :n], w2[:, :n], 1.0, None, ALU.add), r=['w2'], w=['w2'])
                    cx.op('act', lambda e: e.activation(out=w2[:, :n], in_=w2[:, :n], func=AF.Ln), r=['w2'], w=['w2'])
                    cx.op('dve', lambda e: e.scalar_tensor_tensor(w1[:, :n], w1[:, :n], 0.0, w2[:, :n], ALU.max, ALU.add),
                          r=['w1', 'w2'], w=['w1'])
                    cx.op('dve', lambda e: e.tensor_scalar(w1[:, :n], w1[:, :n], nexpa[:, hd:hd + 1], None, ALU.mult),
                          r=['w1', 'nexpa'], w=['w1'])
                    cx.op('dve', lambda e: e.memset(gg[:, 0:1], 0.0), w=['gg'])
                    cx.op('dve', lambda e: e.tensor_tensor_scan(gg[:, 1:1 + n], onesr[:, :n], w1[:, :n], 0.0, ALU.mult, ALU.add),
                          r=['onesr', 'w1'], w=['gg'])
                    b3 = gg[:, 1:1 + n].rearrange("p (c l) -> p c l", l=L)
                    last3 = b3[:, :, L - 1:L].to_broadcast([128, nch, L])
                    prev3 = gg[:, 0:n].rearrange("p (c l) -> p c l", l=L)[:, :, 0:1].to_broadcast([128, nch, L])
                    cx.op('dve', lambda e: e.tensor_tensor(egr[:, :n].rearrange("p (c l) -> p c l", l=L), b3, prev3, ALU.subtract),
                          r=['gg'], w=['egr'])
                    cx.op('act', lambda e: e.activation(out=egr[:, :n], in_=egr[:, :n], func=AF.Exp), r=['egr'], w=['egr'])
                    cx.op('dve', lambda e: e.tensor_tensor(ela[:, :n].rearrange("p (c l) -> p c l", l=L), b3, last3, ALU.subtract),
                          r=['gg'], w=['ela'])
                    cx.op('act', lambda e: e.activation(out=ela[:, :n], in_=ela[:, :n], func=AF.Exp, scale=-1.0), r=['ela'], w=['ela'])
                    cx.op('dve', lambda e: e.tensor_tensor(
                        edl[:, :nch], gg[:, 1:1 + n].rearrange("p (c l) -> p c l", l=L)[:, :, L - 1],
                        gg[:, 0:n].rearrange("p (c l) -> p c l", l=L)[:, :, 0], ALU.subtract), r=['gg'], w=['edl'])
                    cx.op('act', lambda e: e.activation(out=edl[:, :nch], in_=edl[:, :nch], func=AF.Exp), r=['edl'], w=['edl'])
                    cx.op('dve', lambda e: e.tensor_tensor(vbet[:, :n], cvx[:, 2, :n], betaB[:, :n], ALU.mult),
                          r=['cvx', 'betaB'], w=['vbet'])
                    cx.op('dve', lambda e: e.tensor_tensor(kbe[:, :n], cvx[:, 1, :n], betaB[:, :n], ALU.mult),
                          r=['cvx', 'betaB'], w=['kbe'])
                    cx.op('dve', lambda e: e.tensor_tensor(kbe[:, :n], kbe[:, :n], egr[:, :n], ALU.mult), r=['kbe', 'egr'], w=['kbe'])
                    cx.op('dve', lambda e: e.tensor_tensor(kd[:, :n], cvx[:, 1, :n], ela[:, :n], ALU.mult), r=['cvx', 'ela'], w=['kd'])
                    cx.op('dve', lambda e: e.tensor_tensor(qg[:, :n], cvx[:, 0, :n], egr[:, :n], ALU.mult), r=['cvx', 'egr'], w=['qg'])
                    groups = list(range(0, n, G))
                    nthr = min(int(os.environ.get('GDN_THR', '2')), len(groups))
                    if os.environ.get('GDN_BAR', '0') == '1':
                        cx.barrier()
                    tick = {'done': 0}

                    def gdn_thread(p):
                        pa, pb = ps[2 + 2 * p], ps[3 + 2 * p]
                        A0 = A1 = A2 = PS[2 + 2 * p]
                        B0 = B1 = PS[3 + 2 * p]
                        gcol_, DBe_, XY_, Rm_, qkT_, kdtok_, wkT_, utok_ = [b_[p] for b_ in
                                                                           (gcolS, DBeS, XYS, RmS, qkTS, kdtokS, wkTS, utokS)]
                        N_ = lambda nm: '%s_%d' % (nm, p)
                        for gi in range(p, len(groups), nthr):
                            g0 = groups[gi]
                            gs = slice(g0, g0 + G)
                            cx.op('pe', lambda e: e.transpose(pb[:G, 0:128], gg[:, 1 + g0:1 + g0 + G], cst[:, 0, :]),
                                  r=['gg', 'cst'], w=[B0])
                            cx.op('act', lambda e: e.activation(out=gcol_[:G, :], in_=pb[:G, 0:1], func=AF.Copy), r=[B0], w=[N_('gcol')])
                            cx.op('dve', lambda e: e.tensor_scalar(DBe_[:G, :G], gg[:G, 1 + g0:1 + g0 + G], gcol_[:G, 0:1], 0.0,
                                                                   ALU.subtract, ALU.min), r=['gg', N_('gcol')], w=[N_('DBe')])
                            cx.op('act', lambda e: e.activation(out=DBe_[:G, :G], in_=DBe_[:G, :G], func=AF.Exp), r=[N_('DBe')], w=[N_('DBe')])
                            cx.op('pe', lambda e: e.matmul(pa[:G, 0:G], kbf[:, gs], kbf[:, gs], start=True, stop=True), r=['kbf'], w=[A0])
                            X0, Y0 = XY_[:G, 0, 0, :G], XY_[:G, 0, 1, :G]
                            cx.op('dve', lambda e: e.scalar_tensor_tensor(X0, pa[:G, 0:G], -1.0, DBe_[:G, :G], ALU.mult, ALU.mult),
                                  r=[A0, N_('DBe')], w=[N_('XY0')])
                            cx.op('dve', lambda e: e.tensor_tensor(X0, X0, betaB[:G, gs], ALU.mult), r=[N_('XY0'), 'betaB'], w=[N_('XY0')])
                            cx.op('dve', lambda e: e.tensor_tensor(X0, X0, cst[:G, mS, :G], ALU.mult), r=[N_('XY0'), 'cst'], w=[N_('XY0')])
                            cx.op('pe', lambda e: e.transpose(pb[:G, 384:384 + G], X0, cst[:G, 0, :G]), r=[N_('XY0'), 'cst'], w=[B1])
                            cx.op('act', lambda e: e.activation(out=Y0, in_=pb[:G, 384:384 + G], func=AF.Copy), r=[B1], w=[N_('XY0')])
                            cx.op('pe', lambda e: e.matmul(pa[:G, 128:128 + G], kbf[:, gs], qbf[:, gs], start=True, stop=True),
                                  r=['kbf', 'qbf'], w=[A1])
                            cx.op('dve', lambda e: e.tensor_tensor(DBe_[:G, :G], DBe_[:G, :G], cst[:G, mI, :G], ALU.mult),
                                  r=[N_('DBe'), 'cst'], w=[N_('DBe')])
                            cx.op('dve', lambda e: e.tensor_tensor(qkT_[:G, :G], pa[:G, 128:128 + G], DBe_[:G, :G], ALU.mult),
                                  r=[A1, N_('DBe')], w=[N_('qkT')])
                            cx.op('pe', lambda e: e.transpose(pb[:G, 0:128], vbet[:, gs], cst[:, 0, :]), r=['vbet', 'cst'], w=[B0])
                            cx.op('pe', lambda e: e.transpose(pb[:G, 128:256], kbe[:, gs], cst[:, 0, :]), r=['kbe', 'cst'], w=[B0])
                            cx.op('pe', lambda e: e.transpose(pb[:G, 256:384], kd[:, gs], cst[:, 0, :]), r=['kd', 'cst'], w=[B0])
                            cx.op('act', lambda e: e.activation(out=Rm_[:G, :], in_=pb[:G, 0:256], func=AF.Copy), r=[B0], w=[N_('Rm')])
                            cx.op('act', lambda e: e.activation(out=kdtok_[:G, :], in_=pb[:G, 256:384], func=AF.Copy), r=[B0], w=[N_('kdtok')])
                            for kk_ in range(nsteps):
                                pg = kk_ % 2
                                Xc, Yc = XY_[:G, pg, 0, :G], XY_[:G, pg, 1, :G]
                                cx.op('pe', lambda e: e.matmul(pa[:G, 256:512], Xc, Rm_[:G, :], start=True, stop=True),
                                      r=[N_('XY%d' % pg), N_('Rm')], w=[A2])
                                cx.op('dve', lambda e: e.tensor_tensor(Rm_[:G, :], Rm_[:G, :], pa[:G, 256:512], ALU.add),
                                      r=[N_('Rm'), A2], w=[N_('Rm')])
                                if kk_ < nsteps - 1:
                                    Xn, Yn = XY_[:G, 1 - pg, 0, :G], XY_[:G, 1 - pg, 1, :G]
                                    cx.op('pe', lambda e: e.matmul(pa[:G, 0:G], Yc, Xc, start=True, stop=True), r=[N_('XY%d' % pg)], w=[A0])
                                    cx.op('pe', lambda e: e.matmul(pa[:G, 128:128 + G], Xc, Yc, start=True, stop=True), r=[N_('XY%d' % pg)], w=[A1])
                                    cx.op('act', lambda e: e.activation(out=Xn, in_=pa[:G, 0:G], func=AF.Copy), r=[A0], w=[N_('XY%d' % (1 - pg))])
                                    cx.op('dve', lambda e: e.tensor_copy(Yn, pa[:G, 128:128 + G]), r=[A1], w=[N_('XY%d' % (1 - pg))])
                            cx.op('pe', lambda e: e.transpose(pb[:, 0:G], Rm_[:G, 128:256], cst[:G, 0, :G]), r=[N_('Rm'), 'cst'], w=[B0])
                            cx.op('act', lambda e: e.activation(out=wkT_[:, :G], in_=pb[:, 0:G], func=AF.Copy), r=[B0], w=[N_('wkT')])
                            cx.wait_until(lambda: tick['done'] == gi)
                            if os.environ.get('GDN_OLDB', '0') == '1':
                                WS_, SU_, WSN, SUN = ps[5], ps[4][:, 0:128], PS[5], PS[4]
                            else:
                                WS_, SU_, WSN, SUN = ps[6], ps[6][:, 128:256], PS[6], PS[6]
                            for ci in range(G // L):
                                c0 = ci * L
                                cs_ = slice(c0, c0 + L)
                                cidx = (g0 + c0) // L
                                cx.op('pe', lambda e: e.matmul(WS_[:G, 0:128], wkT_[:, :G], Sb[:, hd, :], start=True, stop=True),
                                      r=[N_('wkT'), 'Sb'], w=[WSN])
                                cx.op('dve', lambda e: e.tensor_tensor(utok_[cs_, :], Rm_[cs_, 0:128], WS_[cs_, 0:128], ALU.subtract),
                                      r=[N_('Rm'), WSN], w=[N_('utok')])
                                cx._noyield += 1
                                cx.op('pe', lambda e: e.matmul(ps[7][:, cs_], utok_[cs_, :], qkT_[cs_, cs_], start=True, stop=False),
                                      r=[N_('utok'), N_('qkT')], w=[PS[7]])
                                cx._noyield -= 1
                                cx.op('pe', lambda e: e.matmul(ps[7][:, cs_], Sb[:, hd, :], qg[:, g0 + c0:g0 + c0 + L], start=False, stop=True),
                                      r=['Sb', 'qg'], w=[PS[7]])
                                cx.op('pe', lambda e: e.matmul(SU_, kdtok_[cs_, :], utok_[cs_, :], start=True, stop=True),
                                      r=[N_('kdtok'), N_('utok')], w=[SUN])
                                cx.op('dve', lambda e: e.scalar_tensor_tensor(S[:, hd, :], S[:, hd, :], edl[:, cidx:cidx + 1], SU_,
                                                                              ALU.mult, ALU.add), r=['S', 'edl', SUN], w=['S'])
                                cx.op('act', lambda e: e.activation(out=Sb[:, hd, :], in_=S[:, hd, :], func=AF.Copy), r=['S'], w=['Sb'])
                            cx.op('act', lambda e: e.activation(out=ob[:, gs], in_=ps[7][:, :G], func=AF.Copy), r=[PS[7]], w=['ob'])
                            tick['done'] += 1

                    cx.interleave([(lambda p=p: gdn_thread(p)) for p in range(nthr)])
                    if os.environ.get('GDN_BAR', '0') == '1':
                        cx.barrier()
                    stat128(ob[:, :n], 'ob', w1[:, :n], 'w1', 1.0 / 128, EPS)
                    cx.op('dve', lambda e: e.tensor_tensor(ob[:, :n], ob[:, :n], w1[:, :n], ALU.mult), r=['ob', 'w1'], w=['ob'])
                    cx.op('act', lambda e: e.activation(out=gz[:, :n], in_=gz[:, :n], func=AF.Silu), r=['gz'], w=['gz'])
                    cx.op('dve', lambda e: e.scalar_tensor_tensor(ot[:, sl, hd, :n], ob[:, :n], gdg[:, 0:1], gz[:, :n], ALU.mult, ALU.mult),
                          r=['ob', 'gdg', 'gz'], w=[OT])
                cx.dma('pool', RT(s.oT)[:, 0:4, t0:t0 + n], ot[:, sl, :, :n], r=[OT], w=['oT%d' % s.i])
            cx.dma('pool', s.gd_out[jl], S[:], r=['S'], w=['gd_out'])
            cx.dma('pool', s.gdc_out[jl], ghalo[:], r=['ghalo'], w=['gdc_out'])
        cx.barrier()

    if CD_STAGE < 3:
        return
    with ExitStack() as st:
        def T(name, shape, dt):
            return st.enter_context(SB(nc, name, shape, dt))
        TA = max(s.past + s.T for s in seqs)
        NT128 = (TA + 127) // 128
        wst = T("wst", [128, 3, 1024], F32)
        wqb = T("wqb", [128, 3, 1024], BF16)
        wkvb = T("wkvb", [128, 2, 1024], BF16)
        onesb = T("onesb", [128, 128], BF16)
        KT = T("KT", [128, TA], BF16)
        VT = T("VT", [128, NT128, 128], BF16)
        KR = T("KR", [65, TA], BF16)
        negm = T("negm", [1, 512], BF16)
        cst_ = T("cstf", [128, 512], F32)
        ckb = T("ckb", [128, 2, 512], BF16)
        ksq = T("ksq", [128, 512], F32)
        kmax = T("kmax", [128, 2], F32)
        qan = T("qan", [128, 2, 3, 512], BF16)
        rope = T("rope", [64, 2, 512], F32)
        Qn = T("Qn", [128, 512], BF16)
        Qf = T("Qf", [128, 512], F32)
        Qr = T("Qr", [65, 512], BF16)
        qr1 = T("qr1", [64, 512], F32)
        qr2 = T("qr2", [64, 512], F32)
        Pt = T("Pt", [128, 3, 512], BF16)
        rs = T("rs", [128, 512], F32)
        od = T("od", [128, 2, 512], BF16)
        for c in range(3):
            cx.dma('sp', wst[:, c, :], W['w_qb'][jl][c * 128:(c + 1) * 128, :], w=['wst'])
        cx.op('dve', lambda e: e.tensor_copy(wqb[:], wst[:]), r=['wst'], w=['wqb'])
        for c in range(2):
            cx.dma('sp', wst[:, c, :], W['w_kvb'][jl][c * 128:(c + 1) * 128, :], w=['wst'])
        cx.op('dve', lambda e: e.tensor_copy(wkvb[:], wst[:, 0:2, :]), r=['wst'], w=['wkvb'])
        cx.op('dve', lambda e: e.memset(onesb[:], 1.0), w=['onesb'])
        kq = 0
        for s in seqs:
            past = s.past
            Tall = past + s.T
            chunked = (s.T % 64 == 0)
            ktiles = tiles_of(Tall, 512)
            for (k0, nk) in ktiles:
                cx.dma('sp', cst_[0:64, :nk], s.krT[:, k0:k0 + nk], r=['krT%d' % s.i], w=['cstf'])
                cx.op('act', lambda e: e.activation(out=KR[0:64, k0:k0 + nk], in_=cst_[0:64, :nk], func=AF.Copy), r=['cstf'], w=['KR'])
            cx.op('dve', lambda e: e.memset(KR[64:65, 0:Tall], 1.0), w=['KR'])
            for hd in range(4 if M2_SUB >= 2 else 0):
                cx.op('dve', lambda e: e.memset(kmax[:], 0.0), w=['kmax'])
                for (k0, nk) in ktiles:
                    cx.dma('sp', cst_[:, :nk], s.ckvT[0:128, k0:k0 + nk], r=['ckvT%d' % s.i], w=['cstf'])
                    cx.op('act', lambda e: e.activation(out=ckb[:, 0, :nk], in_=cst_[:, :nk], func=AF.Copy), r=['cstf'], w=['ckb'])
                    cx.dma('sp', cst_[:, :nk], s.ckvT[128:256, k0:k0 + nk], r=['ckvT%d' % s.i], w=['cstf'])
                    cx.op('act', lambda e: e.activation(out=ckb[:, 1, :nk], in_=cst_[:, :nk], func=AF.Copy), r=['cstf'], w=['ckb'])
                    for c in range(2):
                        cx.op('pe', lambda e: e.matmul(ps[4][:, :nk], wkvb[:, c, hd * 256:hd * 256 + 128], ckb[:, c, :nk],
                                                       start=(c == 0), stop=(c == 1)), r=['wkvb', 'ckb'], w=[PS[4]])
                    cx.op('act', lambda e: e.activation(out=KT[:, k0:k0 + nk], in_=ps[4][:, :nk], func=AF.Copy), r=[PS[4]], w=['KT'])
                    cx.op('act', lambda e: e.activation(out=ksq[:, :nk], in_=ps[4][:, :nk], func=AF.Square), r=[PS[4]], w=['ksq'])
                    cx.op('pe', lambda e: e.matmul(ps[5][:, :nk], ones_f[:], ksq[:, :nk], start=True, stop=False),
                          r=['ksq', 'ones_f'], w=[PS[5]])
                    cx.op('act', lambda e: e.activation(out=ksq[0:64, :nk], in_=KR[0:64, k0:k0 + nk], func=AF.Square), r=['KR', 'ksq'], w=['ksq'])
                    cx.op('pe', lambda e: e.matmul(ps[5][:, :nk], ones_f[0:64, :], ksq[0:64, :nk], start=False, stop=True),
                          r=['ksq', 'ones_f'], w=[PS[5]])
                    cx.op('dve', lambda e: e.tensor_reduce(kmax[:, 1:2], ps[5][:, :nk], AX.X, ALU.max), r=[PS[5]], w=['kmax'])
                    cx.op('dve', lambda e: e.tensor_tensor(kmax[:, 0:1], kmax[:, 0:1], kmax[:, 1:2], ALU.max), r=['kmax'], w=['kmax'])
                    for a0 in range(0, nk, 128):
                        an = min(128, nk - a0)
                        ti = (k0 + a0) // 128
                        for c in range(2):
                            cx.op('pe', lambda e: e.matmul(ps[6][:an, 0:128], ckb[:, c, a0:a0 + an],
                                                           wkvb[:, c, hd * 256 + 128:hd * 256 + 256], start=(c == 0), stop=(c == 1)),
                                  r=['ckb', 'wkvb'], w=[PS[6]])
                        cx.op('dve', lambda e: e.tensor_copy(VT[:an, ti, :], ps[6][:an, 0:128]), r=[PS[6]], w=['VT'])
                for (q0, nq) in (s.tiles if M2_SUB >= 3 else []):
                    sl = kq % 2
                    kq += 1
                    QA, OD, PTn = 'qan%d' % sl, 'od%d' % sl, None
                    cx.dma('sp', qan[:, sl, :, :nq], RT(s.qanT)[:, :, q0:q0 + nq], r=['qanT%d' % s.i], w=[QA])
                    cx.dma('sp', rope[:, 0, :nq], s.ropeC[:, q0:q0 + nq], w=['rope'])
                    cx.dma('sp', rope[:, 1, :nq], s.ropeS[:, q0:q0 + nq], w=['rope'])
                    for c in range(3):
                        cx.op('pe', lambda e: e.matmul(ps[4][:, :nq], wqb[:, c, hd * 256:hd * 256 + 128], qan[:, sl, c, :nq],
                                                       start=(c == 0), stop=(c == 2)), r=['wqb', QA], w=[PS[4]])
                    cx.op('act', lambda e: e.activation(out=Qf[:, :nq], in_=ps[4][:, :nq], func=AF.Copy, scale=MLA_SCALE), r=[PS[4]], w=['Qf'])
                    cx.op('dve', lambda e: e.tensor_copy(Qn[:, :nq], Qf[:, :nq]), r=['Qf'], w=['Qn'])
                    for c in range(3):
                        cx.op('pe', lambda e: e.matmul(ps[5][0:64, :nq], wqb[:, c, hd * 256 + 128:hd * 256 + 192], qan[:, sl, c, :nq],
                                                       start=(c == 0), stop=(c == 2)), r=['wqb', QA], w=[PS[5]])
                    for c in range(3):
                        cx.op('pe', lambda e: e.matmul(ps[6][0:64, :nq], wqb[:, c, hd * 256 + 192:hd * 256 + 256], qan[:, sl, c, :nq],
                                                       start=(c == 0), stop=(c == 2)), r=['wqb', QA], w=[PS[6]])
                    cx.op('dve', lambda e: e.tensor_tensor(qr1[:, :nq], ps[5][0:64, :nq], rope[:, 0, :nq], ALU.mult), r=[PS[5], 'rope'], w=['qr1'])
                    cx.op('dve', lambda e: e.tensor_tensor(qr2[:, :nq], ps[6][0:64, :nq], rope[:, 1, :nq], ALU.mult), r=[PS[6], 'rope'], w=['qr2'])
                    cx.op('dve', lambda e: e.scalar_tensor_tensor(qr1[:, :nq], qr1[:, :nq], 1.0, qr2[:, :nq], ALU.mult, ALU.add),
                          r=['qr1', 'qr2'], w=['qr1'])
                    cx.op('act', lambda e: e.activation(out=qr1[:, :nq], in_=qr1[:, :nq], func=AF.Copy, scale=MLA_SCALE), r=['qr1'], w=['qr1'])
                    cx.op('dve', lambda e: e.tensor_copy(Qr[0:64, :nq], qr1[:, :nq]), r=['qr1'], w=['Qr'])
                    if M2_SUB == 30:
                        continue
                    cx.op('act', lambda e: e.activation(out=Qf[:, :nq], in_=Qf[:, :nq], func=AF.Square), r=['Qf'], w=['Qf'])
                    cx.op('act', lambda e: e.activation(out=qr2[:, :nq], in_=qr1[:, :nq], func=AF.Square), r=['qr1'], w=['qr2'])
                    cx.op('pe', lambda e: e.matmul(ps[7][0:1, :nq], ones_f[:, 0:1], Qf[:, :nq], start=True, stop=False), r=['Qf', 'ones_f'], w=[PS[7]])
                    cx.op('pe', lambda e: e.matmul(ps[7][0:1, :nq], ones_f[0:64, 0:1], qr2[:, :nq], start=False, stop=True), r=['qr2', 'ones_f'], w=[PS[7]])
                    cx.op('dve', lambda e: e.tensor_scalar(rs[0:1, :nq], ps[7][0:1, :nq], kmax[0:1, 0:1], None, ALU.mult),
                          r=[PS[7], 'kmax'], w=['rs'])
                    cx.op('act', lambda e: e.activation(out=rs[0:1, :nq], in_=rs[0:1, :nq], func=AF.Sqrt), r=['rs'], w=['rs'])
                    cx.op('dve', lambda e: e.tensor_scalar(negm[0:1, :nq], rs[0:1, :nq], -1.0, None, ALU.mult), r=['rs'], w=['negm'])
                    cx.dma('sp', Qr[64:65, :nq], negm[0:1, :nq], r=['negm'], w=['Qr'])
                    if M2_SUB == 31:
                        continue
                    if chunked:
                        qb = q0 // 512
                        klist = [(kt, 0, False) for kt in range(4 * qb)] + [(4 * qb + j, 128 * j, True) for j in range((nq + 127) // 128)]
                    else:
                        klist = [(kt, 0, False) for kt in range((Tall + 127) // 128)]
                    if M2_SUB < 4 or M2_SUB in (32, 33, 34, 35):
                        klist = klist[:1]
                    nkl = len(klist)
                    def emit_scores(ki):
                        kt, qlo, diag = klist[ki]
                        kn = min(128, Tall - kt * 128)
                        pn = (0, 1, 4)[ki % 3]
                        ksl = slice(kt * 128, kt * 128 + kn)
                        cx.op('pe', lambda e: e.matmul(ps[pn][:kn, qlo:nq], KT[:, ksl], Qn[:, qlo:nq], start=True, stop=False),
                              r=['KT', 'Qn'], w=[PS[pn]])
                        cx.op('pe', lambda e: e.matmul(ps[pn][:kn, qlo:nq], KR[0:65, ksl], Qr[0:65, qlo:nq], start=False, stop=True),
                              r=['KR', 'Qr'], w=[PS[pn]])

                    emit_scores(0)
                    if nkl > 1:
                        emit_scores(1)
                    for ki, (kt, qlo, diag) in enumerate(klist):
                        kn = min(128, Tall - kt * 128)
                        pn = (0, 1, 4)[ki % 3]
                        pt = ki % 3
                        PTn = 'Pt%d' % pt
                        if ki + 2 < nkl:
                            emit_scores(ki + 2)
                        cx.op('act', lambda e: e.activation(out=Pt[:kn, pt, qlo:nq], in_=ps[pn][:kn, qlo:nq], func=AF.Exp),
                              r=[PS[pn]], w=[PTn])
                        first, last = (ki == 0), (ki == nkl - 1)
                        if not diag:
                            parts = [(0, kn, qlo, nq)]
                        else:
                            parts = [(0, 64, qlo, min(qlo + 64, nq))]
                            if qlo + 64 < nq:
                                parts.append((0, 128, qlo + 64, nq))
                        for pi, (r0, r1, ca, cb) in enumerate(parts):
                            lastp = last and (pi == len(parts) - 1)
                            cx.op('pe', lambda e: e.matmul(ps[2][:, ca:cb], VT[r0:r1, kt, :], Pt[r0:r1, pt, ca:cb],
                                                           start=first, stop=lastp), r=['VT', PTn], w=[PS[2]])
                            cx.op('pe', lambda e: e.matmul(ps[3][:, ca:cb], onesb[r0:r1, :], Pt[r0:r1, pt, ca:cb],
                                                           start=first, stop=lastp), r=['onesb', PTn], w=[PS[3]])
                    if M2_SUB in (32, 33, 34, 35):
                        continue
                    cx.op('dve', lambda e: e.reciprocal(rs[:, :nq], ps[3][:, :nq]), r=[PS[3]], w=['rs'])
                    cx.op('dve', lambda e: e.tensor_tensor(od[:, sl, :nq], ps[2][:, :nq], rs[:, :nq], ALU.mult), r=[PS[2], 'rs'], w=[OD])
                    cx.dma('pool', s.oT[512 + hd * 128:512 + (hd + 1) * 128, q0:q0 + nq], od[:, sl, :nq], r=[OD], w=['oT%d' % s.i])
        cx.barrier()


def _pc(v, nchunk):
    sh = v.shape[:-1]
    return np.ascontiguousarray(np.moveaxis(v.reshape(sh + (nchunk, 128)), -1, -2))


def make_consts():
    c = np.zeros((128, 13, 128), np.float32)
    c[:, 0, :] = np.eye(128, dtype=np.float32)
    s_ = np.arange(128)[:, None]
    t_ = np.arange(128)[None, :]
    c[:, 1, :] = ((s_ // 64 == t_ // 64) & (s_ <= t_)).astype(np.float32)
    c[:, 2, :] = (s_ <= t_).astype(np.float32)
    c[:, 3, :] = (s_ < t_).astype(np.float32)
    c[:, 4, :] = ((s_ // 64 == t_ // 64) & (s_ < t_)).astype(np.float32)
    for h in range(4):
        c[64 + h, 5 + h, :] = 1.0
        c[h, 9 + h, :] = 1.0
    return c


def rope_tables(past, T):
    half = 32
    freqs = np.exp(np.float32(-math.log(10000.0)) * np.arange(half, dtype=np.float32) / np.float32(half)).astype(np.float32)
    pos = (past + np.arange(T)).astype(np.float32)
    ang = (pos[:, None] * freqs[None, :]).astype(np.float32)
    cos = np.cos(ang).astype(np.float32).T
    sin = np.sin(ang).astype(np.float32).T
    return (np.ascontiguousarray(np.concatenate([cos, cos], 0)),
            np.ascontiguousarray(np.concatenate([-sin, sin], 0)))


def make_core_inputs(inp, xs, cs, sidx, depth):
    nab = (depth + 1) // 2
    im = {}
    nseq = len(xs)
    for i in range(nseq):
        im["xT%d" % i] = np.ascontiguousarray(xs[i].T)
        if sidx[i] is not None:
            b = sidx[i]
            im["ffnc_i%d" % i] = np.ascontiguousarray(
                inp['state_ffn_conv'][:depth, b].reshape(depth, 2, 44, 128).transpose(0, 3, 2, 1))
            im["hg_i%d" % i] = np.ascontiguousarray(inp['state_hgrn'][:nab, b].transpose(0, 2, 1, 3))
            im["lru_i%d" % i] = _pc(inp['state_rglru'][:nab, b], 4)
            im["lruc_i%d" % i] = np.ascontiguousarray(
                inp['state_rglru_conv'][:nab, b].reshape(nab, 3, 4, 128).transpose(0, 3, 2, 1))
    im["cT"] = np.ascontiguousarray(np.asarray(cs).reshape(nseq, 8, 128).transpose(2, 1, 0))
    im["ada_w"] = np.ascontiguousarray(inp['ada_w'][:depth])
    im["ada_bT"] = _pc(inp['ada_b'][:depth], 48)
    im["norm_gT"] = np.ascontiguousarray(inp['norm_g'][:depth].reshape(depth, 4, 8, 128).transpose(0, 3, 1, 2))
    wo = np.zeros((depth, D, D), np.float32)
    for l in range(depth):
        wo[l] = inp['ab_w_out'][l // 2] if l % 2 == 0 else inp['cd_w_out'][l // 2]
    im["w_out"] = wo
    im["ffn_up"] = np.ascontiguousarray(inp['ffn_w_up'][:depth])
    im["ffn_cw"] = np.ascontiguousarray(inp['ffn_conv_w'][:depth].reshape(depth, 3, 44, 128).transpose(0, 3, 2, 1))
    im["ffn_dn"] = np.ascontiguousarray(inp['ffn_w_down'][:depth])
    im["consts"] = make_consts()
    im["ab_w_in"] = np.ascontiguousarray(inp['ab_w_in'][:nab])
    im["hg_lb"] = np.ascontiguousarray(inp['hgrn_lb_logits'].reshape(2, 4, 128).transpose(2, 0, 1))
    im["hg_g"] = np.ascontiguousarray(inp['hgrn_norm_g'][:nab].reshape(nab, 128, 1))
    im["lru_cw"] = np.ascontiguousarray(inp['lru_conv_w'][:nab].reshape(nab, 4, 4, 128).transpose(0, 3, 2, 1))
    vec = np.stack([inp['lru_conv_b'][:nab], inp['lru_b_a'][:nab], inp['lru_b_x'][:nab], inp['lru_lambda'][:nab]], -1)
    im["lru_vec"] = np.ascontiguousarray(vec.reshape(nab, 4, 128, 4).transpose(0, 2, 1, 3))
    gw = np.zeros((nab, 128, 2, 4, 128), np.float32)
    for k, nm in enumerate(('lru_w_a', 'lru_w_x')):
        w = inp[nm][:nab]
        for ch in range(4):
            for bb in range(2):
                gw[:, bb * 64:(bb + 1) * 64, k, ch, bb * 64:(bb + 1) * 64] = w[:, 2 * ch + bb]
    im["lru_gw"] = gw
    ncd = depth // 2
    if ncd > 0:
        src = inp['cd_w_in'][:ncd]
        w = np.zeros((ncd, D, 3072), np.float32)
        w[:, :, 0:2048] = src[:, :, 0:2048]
        w[:, :, 2048:2432] = src[:, :, 2056:2440]
        w[:, :, 2432:2688] = src[:, :, 2440:2696]
        w[:, :, 2688:2752] = src[:, :, 2696:2760]
        w[:, :, 2752:2756] = src[:, :, 2048:2052]
        w[:, :, 2816:2820] = src[:, :, 2052:2056]
        w[:, :, 2944:2976] = src[:, :, 2728:2760]
        w[:, :, 2976:3008] = src[:, :, 2696:2728]
        im["cd_w_in"] = w
        im["gd_cw"] = np.ascontiguousarray(inp['gdn_conv_w'][:ncd].reshape(ncd, 4, 12, 128).transpose(0, 3, 2, 1))
        gv = np.stack([inp['gdn_a_log'][:ncd], inp['gdn_dt_bias'][:ncd]], -1)
        im["gd_vec"] = np.ascontiguousarray(np.broadcast_to(gv[:, None], (ncd, 128, 4, 2)))
        im["gd_g"] = np.ascontiguousarray(inp['gdn_norm_g'][:ncd].reshape(ncd, 128, 1))
        im["q_g"] = _pc(inp['mla_q_norm_g'][:ncd], 3)
        im["kv_g"] = _pc(inp['mla_kv_norm_g'][:ncd], 2)
        wq = inp['mla_w_qb'][:ncd].reshape(ncd, 384, 4, 192)
        wqe = np.zeros((ncd, 384, 4, 256), np.float32)
        wqe[..., 0:192] = wq
        wqe[..., 192:224] = wq[..., 160:192]
        wqe[..., 224:256] = wq[..., 128:160]
        im["w_qb"] = np.ascontiguousarray(wqe.reshape(ncd, 384, 1024))
        im["w_kvb"] = np.ascontiguousarray(inp['mla_w_kvb'][:ncd])
        for i in range(nseq):
            T_ = xs[i].shape[0]
            past = 0 if sidx[i] is None else PAST
            im["ropeC%d" % i], im["ropeS%d" % i] = rope_tables(past, T_)
            if sidx[i] is not None:
                b = sidx[i]
                im["gd_i%d" % i] = np.ascontiguousarray(inp['state_gdn'][:ncd, b].transpose(0, 2, 1, 3))
                im["gdc_i%d" % i] = np.ascontiguousarray(
                    inp['state_gdn_conv'][:ncd, b].reshape(ncd, 3, 12, 128).transpose(0, 3, 2, 1))
                im["lat_iT%d" % i] = np.ascontiguousarray(inp['cache_mla_latent'][:ncd, b].transpose(0, 2, 1))
                im["kr_iT%d" % i] = np.ascontiguousarray(inp['cache_mla_krope'][:ncd, b].transpose(0, 2, 1))
    return im


_PROG = {}


def kernel(**inputs):
    inp = {k: np.asarray(v) for k, v in inputs.items()}
    xp, xsm = inp['x_prompt'], inp['x_sample']
    B, SEQ, _ = xp.shape
    NB, DT, _ = xsm.shape
    NS = NB // NCORES
    depth = DEPTH
    nab, ncd = (depth + 1) // 2, depth // 2
    key = (SEQ, DT, NS, depth)
    if key not in _PROG:
        _PROG[key] = build_program(SEQ, DT, NS, depth=depth)
    nc = _PROG[key]
    in_maps = []
    shared = None
    for c in range(NCORES):
        b = c * B // NCORES
        sidx = [None] + [c * NS + j for j in range(NS)]
        xs = [xp[b]] + [xsm[i] for i in sidx[1:]]
        cs = np.stack([inp['c_prompt'][b]] + [inp['c_sample'][i] for i in sidx[1:]])
        im = make_core_inputs(inp, xs, cs, sidx, depth)
        if shared is None:
            shared = im
        else:
            for k in ('ada_w', 'w_out', 'ffn_up', 'ffn_dn', 'ab_w_in', 'consts', 'cd_w_in', 'w_qb', 'w_kvb', 'ropeC0', 'ropeS0'):
                im[k] = shared[k]
        in_maps.append(im)
    res = run_bass_kernel_spmd(nc, in_maps, core_ids=list(range(NCORES))).results
    pc = [b * NCORES // B for b in range(B)]

    def un_ffnc(a):
        return a.transpose(0, 3, 2, 1).reshape(depth, 2, 2 * DFF)

    def un_hg(a):
        return a.transpose(0, 2, 1, 3)

    def un_lru(a):
        return a.transpose(0, 2, 1).reshape(nab, HALF)

    def un_gdc(a):
        return a.transpose(0, 3, 2, 1).reshape(ncd, 3, 3 * HALF)

    def un_lruc(a):
        return a.transpose(0, 3, 2, 1).reshape(nab, 3, HALF)

    def gather(fn, name, prompt):
        if prompt:
            return np.ascontiguousarray(np.stack([fn(res[pc[b]][name + "0"]) for b in range(B)], axis=1))
        return np.ascontiguousarray(np.stack(
            [fn(res[i // NS][name + str(1 + i % NS)]) for i in range(NB)], axis=1))

    y_prompt = np.ascontiguousarray(np.stack([res[pc[b]]["yT0"].T for b in range(B)]))
    y_sample = np.ascontiguousarray(np.stack([res[i // NS]["yT%d" % (1 + i % NS)].T for i in range(NB)]))
    outs = [y_prompt, y_sample]
    for prompt in (True, False):
        nb = B if prompt else NB
        tt = SEQ if prompt else DT
        outs += [
            gather(un_hg, "hg_o", prompt), gather(un_lru, "lru_o", prompt), gather(un_lruc, "lruc_o", prompt),
            gather(un_hg, "gd_o", prompt), gather(un_gdc, "gdc_o", prompt),
            gather(lambda a: a, "lat_o", prompt), gather(lambda a: a, "kr_o", prompt),
            gather(un_ffnc, "ffnc_o", prompt),
        ]
    return tuple(outs)
```

```python
import math
import os
from contextlib import ExitStack
import numpy as np
import concourse.bass as bass
import concourse.mybir as mybir
from concourse.bass_utils import run_bass_kernel_spmd

F32 = mybir.dt.float32
BF16 = mybir.dt.bfloat16
AF = mybir.ActivationFunctionType
ALU = mybir.AluOpType
AX = mybir.AxisListType

D = 1024
DEPTH = 4
HALF = 512
DFF = 2816
EPS = 1e-6
NCORES = 8
PAST = 2048
SAME_ENGINE_SYNC = os.environ.get('SES', '1') == '1'


class Ctx:
    def __init__(self, nc, es):
        self.nc = nc
        self.es = es
        self.eng = {'pe': nc.tensor, 'dve': nc.vector, 'act': nc.scalar, 'pool': nc.gpsimd, 'sp': nc.sync}
        self.sem = {}
        self.cnt = {}
        for e in self.eng:
            self.sem[e] = es.enter_context(nc.semaphore("sem_" + e))
            self.cnt[e] = 0
        self.ndma = 20
        self.dslots = {}
        for q in ('sp', 'pool'):
            self.dslots[q] = []
            for i in range(self.ndma):
                k = "d_%s_%d" % (q, i)
                self.sem[k] = es.enter_context(nc.semaphore(k))
                self.cnt[k] = 0
                self.dslots[q].append(k)
        self.dnext = {'sp': 0, 'pool': 0}
        self.waited = {e: {} for e in self.eng}
        self.lastw = {}
        self.readers = {}
        self.nins = 0
        self._il = None
        self._noyield = 0

    def _wait(self, e, deps):
        best = {}
        for (k, v) in deps:
            if v > best.get(k, 0):
                best[k] = v
        for k, v in best.items():
            if k == e and (e == 'pe' or not SAME_ENGINE_SYNC):
                continue
            if self.waited[e].get(k, 0) >= v:
                continue
            self.eng[e].wait_ge(self.sem[k], v)
            self.waited[e][k] = v

    def _deps(self, r, w):
        deps = []
        for x in r:
            if x in self.lastw:
                deps.append(self.lastw[x])
        for x in w:
            if x in self.lastw:
                deps.append(self.lastw[x])
            rd = self.readers.get(x)
            if rd:
                deps.extend(rd.items())
        return deps

    def _record(self, tok, r, w):
        for x in r:
            rd = self.readers.setdefault(x, {})
            if tok[1] > rd.get(tok[0], 0):
                rd[tok[0]] = tok[1]
        for x in w:
            self.lastw[x] = tok
            self.readers[x] = {}

    def op(self, e, fn, r=(), w=()):
        self._wait(e, self._deps(r, w))
        ins = fn(self.eng[e])
        self.cnt[e] += 1
        ins.then_inc(self.sem[e], 1)
        self._record((e, self.cnt[e]), r, w)
        self.nins += 1
        self._yield()

    def dma(self, q, out, in_, r=(), w=(), **kw):
        k = self.dslots[q][self.dnext[q]]
        self.dnext[q] = (self.dnext[q] + 1) % self.ndma
        deps = self._deps(r, w)
        if self.cnt[k] > 0:
            deps.append((k, self.cnt[k]))
        self._wait(q, deps)
        ins = self.eng[q].dma_start(out=out, in_=in_, **kw)
        self.cnt[k] += 16
        ins.then_inc(self.sem[k], 16)
        self._record((k, self.cnt[k]), r, w)
        self.nins += 1
        self._yield()

    def _yield(self):
        il = self._il
        if il is None or self._noyield:
            return
        me = il['cur']
        n = len(il['sems'])
        nxt = None
        for d in range(1, n + 1):
            c = (me + d) % n
            if il['alive'][c]:
                nxt = c
                break
        if nxt is None or nxt == me:
            return
        il['cur'] = nxt
        il['sems'][nxt].release()
        il['sems'][me].acquire()
        if il['err']:
            raise RuntimeError("interleave aborted")

    def wait_until(self, cond):
        spins = 0
        while not cond():
            if self._il is None or sum(self._il['alive']) <= 1:
                raise RuntimeError("wait_until would deadlock")
            self._yield()
            spins += 1
            if spins > 100000000:
                raise RuntimeError("wait_until spin limit")

    def interleave(self, fns):
        if len(fns) == 1:
            fns[0]()
            return
        import threading
        n = len(fns)
        il = {'sems': [threading.Semaphore(0) for _ in range(n)], 'alive': [True] * n, 'cur': 0, 'err': []}
        done = threading.Semaphore(0)

        def runner(i):
            il['sems'][i].acquire()
            try:
                if not il['err']:
                    fns[i]()
            except BaseException as ex:
                il['err'].append(ex)
            il['alive'][i] = False
            nxt = None
            for d in range(1, n + 1):
                c = (i + d) % n
                if il['alive'][c]:
                    nxt = c
                    break
            if nxt is None:
                done.release()
            else:
                il['cur'] = nxt
                il['sems'][nxt].release()

        self._il = il
        ths = [threading.Thread(target=runner, args=(i,)) for i in range(n)]
        for t in ths:
            t.start()
        il['sems'][0].release()
        done.acquire()
        for t in ths:
            t.join()
        self._il = None
        if il['err']:
            raise il['err'][0]

    def barrier(self):
        allk = [(k, v) for k, v in self.cnt.items() if v > 0]
        for e in self.eng:
            self._wait(e, [kv for kv in allk if kv[0] != e or e in ('dve', 'act', 'pool')])
        self.lastw = {}
        self.readers = {}

    def final_wait(self):
        allk = [(k, v) for k, v in self.cnt.items() if v > 0 and k != 'sp']
        self._wait('sp', allk)


def tiles_of(T, n=512):
    out = []
    t = 0
    while t < T:
        out.append((t, min(n, T - t)))
        t += n
    return out


class Seq:
    pass


_uid = [0]


def SB(nc, name, shape, dt):
    _uid[0] += 1
    return nc.sbuf_tensor("%s_u%d" % (name, _uid[0]), shape, dt)


def build_program(SEQ, DT, NS, depth=DEPTH, dbg=None):
    nc = bass.Bass("TRN2", target_bir_lowering=False)
    es = ExitStack()
    with es:
        cx = Ctx(nc, es)
        _emit(nc, es, cx, SEQ, DT, NS, depth, dbg)
        cx.final_wait()
        print("instructions:", cx.nins)
    return nc


def _emit(nc, es, cx, SEQ, DT, NS, depth, dbg):
    NSEQ = 1 + NS
    seqs = []
    for i in range(NSEQ):
        s = Seq()
        s.i = i
        s.T = SEQ if i == 0 else DT
        s.tiles = tiles_of(s.T)
        s.xin = nc.dram_tensor("xT%d" % i, [D, s.T], F32, kind="ExternalInput").ap()
        s.yout = nc.dram_tensor("yT%d" % i, [D, s.T], F32, kind="ExternalOutput").ap()
        s.ffnc_out = nc.dram_tensor("ffnc_o%d" % i, [depth, 128, 44, 2], F32, kind="ExternalOutput").ap()
        s.xT = nc.dram_tensor("s_xT%d" % i, [D, s.T], F32).ap()
        s.h2T = nc.dram_tensor("s_h2T%d" % i, [D, s.T], BF16).ap()
        s.oT = nc.dram_tensor("s_oT%d" % i, [D, s.T], BF16).ap()
        s.aT = nc.dram_tensor("s_aT%d" % i, [DFF, s.T], BF16).ap()
        nab = (depth + 1) // 2
        s.hg_out = nc.dram_tensor("hg_o%d" % i, [nab, 128, 4, 128], F32, kind="ExternalOutput").ap()
        s.lru_out = nc.dram_tensor("lru_o%d" % i, [nab, 128, 4], F32, kind="ExternalOutput").ap()
        s.lruc_out = nc.dram_tensor("lruc_o%d" % i, [nab, 128, 4, 3], F32, kind="ExternalOutput").ap()
        ncd = depth // 2
        s.past = 0 if i == 0 else PAST
        if ncd > 0:
            s.gd_out = nc.dram_tensor("gd_o%d" % i, [ncd, 128, 4, 128], F32, kind="ExternalOutput").ap()
            s.gdc_out = nc.dram_tensor("gdc_o%d" % i, [ncd, 128, 12, 3], F32, kind="ExternalOutput").ap()
            s.lat_out = nc.dram_tensor("lat_o%d" % i, [ncd, s.T, 256], F32, kind="ExternalOutput").ap()
            s.kr_out = nc.dram_tensor("kr_o%d" % i, [ncd, s.T, 64], F32, kind="ExternalOutput").ap()
            s.ropeC = nc.dram_tensor("ropeC%d" % i, [64, s.T], F32, kind="ExternalInput").ap()
            s.ropeS = nc.dram_tensor("ropeS%d" % i, [64, s.T], F32, kind="ExternalInput").ap()
            s.qanT = nc.dram_tensor("s_qanT%d" % i, [384, s.T], BF16).ap()
            s.ckvT = nc.dram_tensor("s_ckvT%d" % i, [256, s.past + s.T], F32).ap()
            s.krT = nc.dram_tensor("s_krT%d" % i, [64, s.past + s.T], F32).ap()
            if i > 0:
                s.gd_in = nc.dram_tensor("gd_i%d" % i, [ncd, 128, 4, 128], F32, kind="ExternalInput").ap()
                s.gdc_in = nc.dram_tensor("gdc_i%d" % i, [ncd, 128, 12, 3], F32, kind="ExternalInput").ap()
                s.lat_inT = nc.dram_tensor("lat_iT%d" % i, [ncd, 256, PAST], F32, kind="ExternalInput").ap()
                s.kr_inT = nc.dram_tensor("kr_iT%d" % i, [ncd, 64, PAST], F32, kind="ExternalInput").ap()
        if i > 0:
            s.ffnc_in = nc.dram_tensor("ffnc_i%d" % i, [depth, 128, 44, 2], F32, kind="ExternalInput").ap()
            s.hg_in = nc.dram_tensor("hg_i%d" % i, [nab, 128, 4, 128], F32, kind="ExternalInput").ap()
            s.lru_in = nc.dram_tensor("lru_i%d" % i, [nab, 128, 4], F32, kind="ExternalInput").ap()
            s.lruc_in = nc.dram_tensor("lruc_i%d" % i, [nab, 128, 4, 3], F32, kind="ExternalInput").ap()
        seqs.append(s)
    cT = nc.dram_tensor("cT", [128, 8, NSEQ], F32, kind="ExternalInput").ap()
    ada_w = nc.dram_tensor("ada_w", [depth, D, 6 * D], F32, kind="ExternalInput").ap()
    ada_bT = nc.dram_tensor("ada_bT", [depth, 128, 48], F32, kind="ExternalInput").ap()
    norm_gT = nc.dram_tensor("norm_gT", [depth, 128, 4, 8], F32, kind="ExternalInput").ap()
    w_out = nc.dram_tensor("w_out", [depth, D, D], F32, kind="ExternalInput").ap()
    ffn_up = nc.dram_tensor("ffn_up", [depth, D, 2 * DFF], F32, kind="ExternalInput").ap()
    ffn_cw = nc.dram_tensor("ffn_cw", [depth, 128, 44, 3], F32, kind="ExternalInput").ap()
    ffn_dn = nc.dram_tensor("ffn_dn", [depth, DFF, D], F32, kind="ExternalInput").ap()

    nab = (depth + 1) // 2
    W = {}
    W['consts'] = nc.dram_tensor("consts", [128, 13, 128], F32, kind="ExternalInput").ap()
    ncd = depth // 2
    if ncd > 0:
        W['cd_w_in'] = nc.dram_tensor("cd_w_in", [ncd, D, 3072], F32, kind="ExternalInput").ap()
        W['gd_cw'] = nc.dram_tensor("gd_cw", [ncd, 128, 12, 4], F32, kind="ExternalInput").ap()
        W['gd_vec'] = nc.dram_tensor("gd_vec", [ncd, 128, 4, 2], F32, kind="ExternalInput").ap()
        W['gd_g'] = nc.dram_tensor("gd_g", [ncd, 128, 1], F32, kind="ExternalInput").ap()
        W['q_g'] = nc.dram_tensor("q_g", [ncd, 128, 3], F32, kind="ExternalInput").ap()
        W['kv_g'] = nc.dram_tensor("kv_g", [ncd, 128, 2], F32, kind="ExternalInput").ap()
        W['w_qb'] = nc.dram_tensor("w_qb", [ncd, 384, 1024], F32, kind="ExternalInput").ap()
        W['w_kvb'] = nc.dram_tensor("w_kvb", [ncd, 256, 1024], F32, kind="ExternalInput").ap()
    W['ab_w_in'] = nc.dram_tensor("ab_w_in", [nab, D, 3072], F32, kind="ExternalInput").ap()
    W['hg_lb'] = nc.dram_tensor("hg_lb", [128, 2, 4], F32, kind="ExternalInput").ap()
    W['hg_g'] = nc.dram_tensor("hg_g", [nab, 128, 1], F32, kind="ExternalInput").ap()
    W['lru_cw'] = nc.dram_tensor("lru_cw", [nab, 128, 4, 4], F32, kind="ExternalInput").ap()
    W['lru_vec'] = nc.dram_tensor("lru_vec", [nab, 128, 4, 4], F32, kind="ExternalInput").ap()
    W['lru_gw'] = nc.dram_tensor("lru_gw", [nab, 128, 2, 4, 128], F32, kind="ExternalInput").ap()
    W['ones_b'] = None
    ps = [es.enter_context(nc.psum_tensor("ps%d" % i, [128, 512], F32)) for i in range(8)]
    PS = ["ps%d" % i for i in range(8)]
    ones_f = es.enter_context(nc.sbuf_tensor("ones_f", [128, 128], F32))
    ones_b = es.enter_context(nc.sbuf_tensor("ones_b", [128, 128], BF16))
    mod = es.enter_context(nc.sbuf_tensor("mod", [128, 48, NSEQ], F32))
    ng = es.enter_context(nc.sbuf_tensor("ng", [128, 4, 8], F32))
    gsc = es.enter_context(nc.sbuf_tensor("gsc", [128, 2, 8, NSEQ], F32))
    csil = es.enter_context(nc.sbuf_tensor("csil", [128, 8, NSEQ], F32))
    cx.op('dve', lambda e: e.memset(ones_f[:], 1.0), w=['ones_f'])
    cx.op('dve', lambda e: e.memset(ones_b[:], 1.0), w=['ones_b'])
    W['ones_b'] = ones_b
    cx.dma('sp', csil[:], cT[:, :, :], w=['csil'])
    cx.op('act', lambda e: e.activation(out=csil[:], in_=csil[:], func=AF.Silu), r=['csil'], w=['csil'])

    with SB(nc, "cp", [128, 2, 8, 512], F32) as cp:
        k = 0
        for s in seqs:
            for (t0, n) in s.tiles:
                sl = k % 2
                k += 1
                cx.dma('sp', cp[:, sl, :, :n], s.xin.rearrange("(c p) t -> p c t", p=128)[:, :, t0:t0 + n],
                       w=['cp%d' % sl])
                cx.dma('pool', s.xT.rearrange("(c p) t -> p c t", p=128)[:, :, t0:t0 + n], cp[:, sl, :, :n],
                       r=['cp%d' % sl], w=['xT%d' % s.i])
        cx.barrier()

    def rms_stats(src, n, sq, rstd, psn, srcres, nfeat_chunks=8):
        for c in range(nfeat_chunks):
            cx.op('act', lambda e, c=c: e.activation(out=sq[:, c, :n], in_=src(c), func=AF.Square),
                  r=srcres, w=['sq'])
        for c in range(nfeat_chunks):
            cx.op('pe', lambda e, c=c: e.matmul(ps[psn][:, :n], ones_b[:], sq[:, c, :n],
                                                 start=(c == 0), stop=(c == nfeat_chunks - 1)),
                  r=['sq', 'ones_b'], w=[PS[psn]])
        cx.op('dve', lambda e: e.tensor_scalar(rstd[:, :n], ps[psn][:, :n], 1.0 / (128 * nfeat_chunks), EPS,
                                               ALU.mult, ALU.add), r=[PS[psn]], w=['rstd'])
        cx.op('dve', lambda e: e.reciprocal(rstd[:, :n], rstd[:, :n]), r=['rstd'], w=['rstd'])
        cx.op('act', lambda e: e.activation(out=rstd[:, :n], in_=rstd[:, :n], func=AF.Sqrt), r=['rstd'], w=['rstd'])

    for l in range(depth):
        with SB(nc, "aw", [128, 2, 8, 768], F32) as aw, SB(nc, "adb", [128, 48], F32) as adb:
            cx.dma('sp', adb[:], ada_bT[l], w=['adb'])
            cx.dma('sp', ng[:], norm_gT[l], w=['ng'])
            for g in range(8):
                sl = g % 2
                cx.dma('sp', aw[:, sl], ada_w[l].rearrange("(c p) f -> p c f", p=128)[:, :, g * 768:(g + 1) * 768],
                       w=['aw%d' % sl])
                for j in range(6):
                    fc = g * 6 + j
                    pn = fc % 2
                    for c in range(8):
                        cx.op('pe', lambda e, c=c, j=j, sl=sl, pn=pn: e.matmul(
                            ps[pn][:, :NSEQ], aw[:, sl, c, j * 128:(j + 1) * 128], csil[:, c, :],
                            start=(c == 0), stop=(c == 7)), r=['aw%d' % sl, 'csil'], w=[PS[pn]])
                    cx.op('dve', lambda e, fc=fc, pn=pn: e.tensor_scalar(
                        mod[:, fc, :], ps[pn][:, :NSEQ], adb[:, fc:fc + 1], None, ALU.add),
                        r=[PS[pn], 'adb'], w=['mod'])
            for k2, (gi, sc0) in enumerate(((0, 8), (2, 32))):
                for c in range(8):
                    cx.op('dve', lambda e, k2=k2, gi=gi, sc0=sc0, c=c: e.tensor_scalar(
                        gsc[:, k2, c, :], mod[:, sc0 + c, :], 1.0, ng[:, gi, c:c + 1], ALU.add, ALU.mult),
                        r=['mod', 'ng'], w=['gsc'])
            cx.barrier()

        if l % 2 == 0:
            emit_mixer(nc, cx, ps, PS, l, seqs, mod, gsc, ng, ones_f, rms_stats, W)
        else:
            emit_cd(nc, cx, ps, PS, l, seqs, mod, gsc, ng, ones_f, rms_stats, W)

        with (SB(nc, "wst", [128, 2, 8, 512], F32) as wst,
              SB(nc, "wo", [128, 8, D], BF16) as wo,
              SB(nc, "ot", [128, 2, 8, 512], BF16) as ot,
              SB(nc, "xt", [128, 2, 8, 512], F32) as xt,
              SB(nc, "yt", [128, 8, 512], F32) as yt,
              SB(nc, "sq", [128, 8, 512], BF16) as sq,
              SB(nc, "rstd", [128, 512], F32) as rstd,
              SB(nc, "tmp", [128, 512], F32) as tmp,
              SB(nc, "h2", [128, 2, 8, 512], BF16) as h2):
            for g in range(2):
                cx.dma('sp', wst[:, g], w_out[l].rearrange("(c p) f -> p c f", p=128)[:, :, g * 512:(g + 1) * 512],
                       w=['wst%d' % g])
                cx.op('act', lambda e, g=g: e.activation(out=wo[:, :, g * 512:(g + 1) * 512], in_=wst[:, g],
                                                         func=AF.Copy), r=['wst%d' % g], w=['wo'])
            k = 0
            for s in seqs:
                for (t0, n) in s.tiles:
                    sl = k % 2
                    k += 1
                    XT, OT, H2 = 'xt%d' % sl, 'ot%d' % sl, 'h2%d' % sl
                    cx.dma('sp', ot[:, sl, :, :n], s.oT.rearrange("(c p) t -> p c t", p=128)[:, :, t0:t0 + n],
                           r=['oT%d' % s.i], w=[OT])
                    cx.dma('sp', xt[:, sl, :, :n], s.xT.rearrange("(c p) t -> p c t", p=128)[:, :, t0:t0 + n],
                           r=['xT%d' % s.i], w=[XT])
                    for m in range(8):
                        pn = m % 4
                        for c in range(8):
                            cx.op('pe', lambda e, m=m, c=c, pn=pn: e.matmul(
                                ps[pn][:, :n], wo[:, c, m * 128:(m + 1) * 128], ot[:, sl, c, :n],
                                start=(c == 0), stop=(c == 7)), r=['wo', OT], w=[PS[pn]])
                        cx.op('act', lambda e, m=m, pn=pn: e.activation(out=yt[:, m, :n], in_=ps[pn][:, :n],
                                                                        func=AF.Copy), r=[PS[pn]], w=['yt'])
                    rms_stats(lambda c: yt[:, c, :n], n, sq, rstd, 4, ['yt'])
                    for c in range(8):
                        cx.op('dve', lambda e, c=c: e.tensor_tensor(tmp[:, :n], yt[:, c, :n], rstd[:, :n], ALU.mult),
                              r=['yt', 'rstd'], w=['tmp'])
                        cx.op('dve', lambda e, c=c: e.tensor_scalar(
                            tmp[:, :n], tmp[:, :n], ng[:, 1, c:c + 1], mod[:, 16 + c, s.i:s.i + 1],
                            ALU.mult, ALU.mult), r=['tmp', 'ng', 'mod'], w=['tmp'])
                        cx.op('dve', lambda e, c=c: e.tensor_tensor(xt[:, sl, c, :n], xt[:, sl, c, :n], tmp[:, :n],
                                                                    ALU.add), r=['tmp', XT], w=[XT])
                    cx.dma('pool', s.xT.rearrange("(c p) t -> p c t", p=128)[:, :, t0:t0 + n], xt[:, sl, :, :n],
                           r=[XT], w=['xT%d' % s.i])
                    rms_stats(lambda c: xt[:, sl, c, :n], n, sq, rstd, 5, [XT])
                    for c in range(8):
                        cx.op('dve', lambda e, c=c: e.tensor_tensor(tmp[:, :n], xt[:, sl, c, :n], rstd[:, :n], ALU.mult),
                              r=[XT, 'rstd'], w=['tmp'])
                        cx.op('act', lambda e, c=c: e.activation(
                            out=h2[:, sl, c, :n], in_=tmp[:, :n], func=AF.Identity,
                            scale=gsc[:, 1, c, s.i:s.i + 1], bias=mod[:, 24 + c, s.i:s.i + 1]),
                            r=['tmp', 'gsc', 'mod'], w=[H2])
                    cx.dma('pool', s.h2T.rearrange("(c p) t -> p c t", p=128)[:, :, t0:t0 + n], h2[:, sl, :, :n],
                           r=[H2], w=['h2T%d' % s.i])
            cx.barrier()

        with (SB(nc, "wst", [128, 2, 2816], F32) as wst,
              SB(nc, "wu", [128, 8, 2 * DFF], BF16) as wu,
              SB(nc, "cw", [128, 44, 3], F32) as cw,
              SB(nc, "h2", [128, 2, 8, 512], BF16) as h2,
              SB(nc, "halo", [128, 44, 2], F32) as halo,
              SB(nc, "u", [128, 4, 514], F32) as u,
              SB(nc, "cv", [128, 4, 512], F32) as cv,
              SB(nc, "at", [128, 2, 22, 512], BF16) as at):
            cx.dma('sp', cw[:], ffn_cw[l], w=['cw'])
            k = 0
            for c in range(8):
                for g in range(2):
                    sl = k % 2
                    k += 1
                    cx.dma('sp', wst[:, sl], ffn_up[l][c * 128:(c + 1) * 128, g * 2816:(g + 1) * 2816],
                           w=['wst%d' % sl])
                    cx.op('act' if g else 'dve', (lambda e, c=c, g=g, sl=sl: e.activation(
                        out=wu[:, c, g * 2816:(g + 1) * 2816], in_=wst[:, sl], func=AF.Copy)) if g else
                        (lambda e, c=c, g=g, sl=sl: e.tensor_copy(wu[:, c, g * 2816:(g + 1) * 2816], wst[:, sl])),
                        r=['wst%d' % sl], w=['wu'])
            k = 0
            for s in seqs:
                if s.i == 0:
                    cx.op('dve', lambda e: e.memset(halo[:], 0.0), w=['halo'])
                else:
                    cx.dma('sp', halo[:], s.ffnc_in[l], w=['halo'])
                for (t0, n) in s.tiles:
                    sl = k % 2
                    k += 1
                    H2, AT = 'h2%d' % sl, 'at%d' % sl
                    cx.dma('sp', h2[:, sl, :, :n], s.h2T.rearrange("(c p) t -> p c t", p=128)[:, :, t0:t0 + n],
                           r=['h2T%d' % s.i], w=[H2])
                    for j in range(22):
                        for gv in range(2):
                            ch = gv * 22 + j
                            ui = (j % 2) * 2 + gv
                            pn = ui
                            U, CV = 'u%d' % ui, 'cv%d' % ui
                            for c in range(8):
                                cx.op('pe', lambda e, c=c, ch=ch, pn=pn: e.matmul(
                                    ps[pn][:, :n], wu[:, c, ch * 128:(ch + 1) * 128], h2[:, sl, c, :n],
                                    start=(c == 0), stop=(c == 7)), r=['wu', H2], w=[PS[pn]])
                            cx.op('act', lambda e, ui=ui, ch=ch: e.activation(out=u[:, ui, 0:2], in_=halo[:, ch, :],
                                                                            func=AF.Copy), r=['halo'], w=[U])
                            cx.op('act', lambda e, ui=ui, pn=pn: e.activation(out=u[:, ui, 2:2 + n], in_=ps[pn][:, :n],
                                                                            func=AF.Copy), r=[PS[pn]], w=[U])
                            cx.op('act', lambda e, ui=ui, ch=ch: e.activation(out=halo[:, ch, :], in_=u[:, ui, n:n + 2],
                                                                            func=AF.Copy), r=[U], w=['halo'])
                            cx.op('pool' if E_POOL else 'dve', lambda e, ui=ui, ch=ch: e.tensor_scalar(
                                cv[:, ui, :n], u[:, ui, 0:n], cw[:, ch, 0:1], None, ALU.mult),
                                r=[U, 'cw'], w=[CV])
                            cx.op('dve', lambda e, ui=ui, ch=ch: e.scalar_tensor_tensor(
                                cv[:, ui, :n], u[:, ui, 1:1 + n], cw[:, ch, 1:2], cv[:, ui, :n], ALU.mult, ALU.add),
                                r=[U, 'cw', CV], w=[CV])
                            cx.op('dve', lambda e, ui=ui, ch=ch: e.scalar_tensor_tensor(
                                cv[:, ui, :n], u[:, ui, 2:2 + n], cw[:, ch, 2:3], cv[:, ui, :n], ALU.mult, ALU.add),
                                r=[U, 'cw', CV], w=[CV])
                        gi = (j % 2) * 2
                        cx.op('act', lambda e, gi=gi: e.activation(out=cv[:, gi, :n], in_=cv[:, gi, :n], func=AF.Silu),
                              r=['cv%d' % gi], w=['cv%d' % gi])
                        cx.op('dve', lambda e, gi=gi, j=j: e.tensor_tensor(at[:, sl, j, :n], cv[:, gi, :n],
                                                                         cv[:, gi + 1, :n], ALU.mult),
                              r=['cv%d' % gi, 'cv%d' % (gi + 1)], w=[AT])
                    cx.dma('pool', s.aT.rearrange("(c p) t -> p c t", p=128)[:, :, t0:t0 + n], at[:, sl, :, :n],
                           r=[AT], w=['aT%d' % s.i])
                cx.dma('pool', s.ffnc_out[l], halo[:], r=['halo'], w=['ffnc_out'])
            cx.barrier()

        with (SB(nc, "wst", [128, 2, 11, 512], F32) as wst,
              SB(nc, "wd", [128, 22, D], BF16) as wd,
              SB(nc, "at", [128, 2, 22, 512], BF16) as at,
              SB(nc, "xt", [128, 2, 8, 512], F32) as xt,
              SB(nc, "yt", [128, 8, 512], F32) as yt,
              SB(nc, "sq", [128, 8, 512], BF16) as sq,
              SB(nc, "rstd", [128, 512], F32) as rstd,
              SB(nc, "tmp", [128, 512], F32) as tmp):
            k = 0
            for g in range(2):
                for hh in range(2):
                    sl = k % 2
                    k += 1
                    cx.dma('sp', wst[:, sl], ffn_dn[l].rearrange("(c p) f -> p c f", p=128)[
                        :, hh * 11:(hh + 1) * 11, g * 512:(g + 1) * 512], w=['wst%d' % sl])
                    cx.op('act', lambda e, g=g, hh=hh, sl=sl: e.activation(
                        out=wd[:, hh * 11:(hh + 1) * 11, g * 512:(g + 1) * 512], in_=wst[:, sl], func=AF.Copy),
                        r=['wst%d' % sl], w=['wd'])
            k = 0
            for s in seqs:
                for (t0, n) in s.tiles:
                    sl = k % 2
                    k += 1
                    XT, AT = 'xt%d' % sl, 'at%d' % sl
                    cx.dma('sp', at[:, sl, :, :n], s.aT.rearrange("(c p) t -> p c t", p=128)[:, :, t0:t0 + n],
                           r=['aT%d' % s.i], w=[AT])
                    cx.dma('sp', xt[:, sl, :, :n], s.xT.rearrange("(c p) t -> p c t", p=128)[:, :, t0:t0 + n],
                           r=['xT%d' % s.i], w=[XT])
                    for m in range(8):
                        pn = m % 4
                        for c in range(22):
                            cx.op('pe', lambda e, m=m, c=c, pn=pn: e.matmul(
                                ps[pn][:, :n], wd[:, c, m * 128:(m + 1) * 128], at[:, sl, c, :n],
                                start=(c == 0), stop=(c == 21)), r=['wd', AT], w=[PS[pn]])
                        cx.op('act', lambda e, m=m, pn=pn: e.activation(out=yt[:, m, :n], in_=ps[pn][:, :n],
                                                                        func=AF.Copy), r=[PS[pn]], w=['yt'])
                    rms_stats(lambda c: yt[:, c, :n], n, sq, rstd, 4, ['yt'])
                    for c in range(8):
                        cx.op('dve', lambda e, c=c: e.tensor_tensor(tmp[:, :n], yt[:, c, :n], rstd[:, :n], ALU.mult),
                              r=['yt', 'rstd'], w=['tmp'])
                        cx.op('dve', lambda e, c=c: e.tensor_scalar(
                            tmp[:, :n], tmp[:, :n], ng[:, 3, c:c + 1], mod[:, 40 + c, s.i:s.i + 1],
                            ALU.mult, ALU.mult), r=['tmp', 'ng', 'mod'], w=['tmp'])
                        cx.op('dve', lambda e, c=c: e.tensor_tensor(xt[:, sl, c, :n], xt[:, sl, c, :n], tmp[:, :n],
                                                                    ALU.add), r=['tmp', XT], w=[XT])
                    dst = s.yout if l == depth - 1 else s.xT
                    cx.dma('pool', dst.rearrange("(c p) t -> p c t", p=128)[:, :, t0:t0 + n], xt[:, sl, :, :n],
                           r=[XT], w=['xT%d' % s.i])
            cx.barrier()


def emit_mixer(nc, cx, ps, PS, l, seqs, mod, gsc, ng, ones_f, rms_stats, W):
    ab = (l % 2 == 0)
    jl = l // 2
    with ExitStack() as st:
        def T(name, shape, dt):
            return st.enter_context(SB(nc, name, shape, dt))
        xt = T("xt", [128, 2, 8, 512], F32)
        sq = T("sq", [128, 8, 512], BF16)
        rstd = T("rstd", [128, 512], F32)
        tmp = T("tmp", [128, 512], F32)
        h = T("h", [128, 8, 512], BF16)
        ot = T("otile", [128, 2, 8, 512], BF16)
        if ab:
            wst = T("wst", [128, 2, 1536], F32)
            win = T("win", [128, 8, 3072], BF16)
            P4 = T("P4", [128, 4, 512], F32)
            sig = T("sig", [128, 512], F32)
            kk = T("kk", [128, 512], F32)
            qq = T("qq", [128, 512], F32)
            lf = T("lf", [128, 512], F32)
            bb = T("bb", [128, 513], F32)
            dd = T("dd", [128, 512], F32)
            ee = T("ee", [128, 512], F32)
            qt = T("qt", [128, 512], BF16)
            kt = T("kt", [128, 512], BF16)
            qe = T("qe", [128, 512], BF16)
            kl = T("kl", [128, 512], BF16)
            vb = T("vb", [128, 512], BF16)
            edl = T("edl", [128, 8], F32)
            ob = T("ob", [128, 512], F32)
            attm = T("attm", [128, 128], BF16)
            attf = T("attf", [128, 128], F32)
            vtok = T("vtok", [128, 128], BF16)
            kltok = T("kltok", [128, 128], BF16)
            S = T("S", [128, 4, 128], F32)
            Sb = T("Sb", [128, 4, 128], BF16)
            onesr = T("onesr", [128, 512], F32)
            cst = T("cst", [128, 3, 128], F32)
            identb = T("identb", [128, 128], BF16)
            lbl = T("lbl", [128, 2, 4], F32)
            lbv = T("lbv", [128, 4], F32)
            oml = T("oml", [128, 4], F32)
            noml = T("noml", [128, 4], F32)
            hgg = T("hgg", [128, 1], F32)
            lxb = T("lxb", [128, 515], F32)
            lhalo = T("lhalo", [128, 4, 3], F32)
            lcw = T("lcw", [128, 4, 4], F32)
            lvec = T("lvec", [128, 4, 4], F32)
            m8sp = T("m8sp", [128, 4], F32)
            gst = T("gst", [128, 2, 4, 128], F32)
            gw = T("gw", [128, 2, 4, 128], BF16)
            hst = T("hst", [128, 4], F32)
            xc = T("xc", [128, 512], F32)
            xcb = T("xcb", [128, 512], BF16)
            rr = sig
            ii = kk
            aa = qq
            uu = lf
            hh = dd
            ly = ee
            gl = ob
            for c2 in range(16):
                c, hf = c2 // 2, c2 % 2
                sl = hf
                cx.dma('sp', wst[:, sl], W['ab_w_in'][jl][c * 128:(c + 1) * 128, hf * 1536:(hf + 1) * 1536],
                       w=['wst%d' % sl])
                cx.op('act' if sl else 'dve',
                      (lambda e: e.activation(out=win[:, c, hf * 1536:(hf + 1) * 1536], in_=wst[:, sl], func=AF.Copy))
                      if sl else (lambda e: e.tensor_copy(win[:, c, hf * 1536:(hf + 1) * 1536], wst[:, sl])),
                      r=['wst%d' % sl], w=['win'])
            cx.dma('sp', cst[:], W['consts'][:, 0:3, :], w=['cst'])
            cx.op('dve', lambda e: e.tensor_copy(identb[:], cst[:, 0, :]), r=['cst'], w=['identb'])
            cx.op('dve', lambda e: e.memset(onesr[:], 1.0), w=['onesr'])
            cx.dma('sp', lbl[:], W['hg_lb'][:, :, :], w=['lbl'])
            cx.dma('sp', hgg[:], W['hg_g'][jl], w=['hgg'])
            if jl == 0:
                cx.op('dve', lambda e: e.memset(lbv[:], 0.0), w=['lbv'])
            else:
                cx.op('dve', lambda e: e.tensor_tensor(lbv[:], lbl[:, 1, :], lbl[:, 0, :], ALU.subtract),
                      r=['lbl'], w=['lbv'])
                cx.op('act', lambda e: e.activation(out=lbv[:], in_=lbv[:], func=AF.Sigmoid), r=['lbv'], w=['lbv'])
            cx.op('dve', lambda e: e.tensor_scalar(oml[:], lbv[:], -1.0, 1.0, ALU.mult, ALU.add), r=['lbv'], w=['oml'])
            cx.op('dve', lambda e: e.tensor_scalar(noml[:], oml[:], -1.0, None, ALU.mult), r=['oml'], w=['noml'])
            cx.dma('sp', lcw[:], W['lru_cw'][jl], w=['lcw'])
            cx.dma('sp', lvec[:], W['lru_vec'][jl], w=['lvec'])
            cx.dma('sp', gst[:], W['lru_gw'][jl], w=['gst'])
            cx.op('dve', lambda e: e.tensor_copy(gw[:], gst[:]), r=['gst'], w=['gw'])
            cx.op('act', lambda e: e.activation(out=m8sp[:], in_=lvec[:, :, 3], func=AF.Exp, scale=-1.0),
                  r=['lvec'], w=['m8sp'])
            cx.op('dve', lambda e: e.tensor_scalar(m8sp[:], m8sp[:], 1.0, None, ALU.add), r=['m8sp'], w=['m8sp'])
            cx.op('act', lambda e: e.activation(out=m8sp[:], in_=m8sp[:], func=AF.Ln), r=['m8sp'], w=['m8sp'])
            cx.op('dve', lambda e: e.tensor_scalar(m8sp[:], m8sp[:], -8.0, None, ALU.mult), r=['m8sp'], w=['m8sp'])
        k = 0
        for s in seqs:
            L = 64 if s.T % 64 == 0 else s.T
            if ab:
                if s.i == 0:
                    cx.op('dve', lambda e: e.memset(S[:], 0.0), w=['S'])
                    cx.op('dve', lambda e: e.memset(hst[:], 0.0), w=['hst'])
                    cx.op('dve', lambda e: e.memset(lhalo[:], 0.0), w=['lhalo'])
                else:
                    cx.dma('sp', S[:], s.hg_in[jl], w=['S'])
                    cx.dma('sp', hst[:], s.lru_in[jl], w=['hst'])
                    cx.dma('sp', lhalo[:], s.lruc_in[jl], w=['lhalo'])
                cx.op('act', lambda e: e.activation(out=Sb[:], in_=S[:], func=AF.Copy), r=['S'], w=['Sb'])
            for (t0, n) in s.tiles:
                sl = k % 2
                k += 1
                XT, OT = 'xt%d' % sl, 'ot%d' % sl
                cx.dma('sp', xt[:, sl, :, :n], s.xT.rearrange("(c p) t -> p c t", p=128)[:, :, t0:t0 + n],
                       r=['xT%d' % s.i], w=[XT])
                rms_stats(lambda c: xt[:, sl, c, :n], n, sq, rstd, 6, [XT])
                for c in range(8):
                    cx.op('dve', lambda e, c=c: e.tensor_tensor(tmp[:, :n], xt[:, sl, c, :n], rstd[:, :n], ALU.mult),
                          r=[XT, 'rstd'], w=['tmp'])
                    cx.op('act', lambda e, c=c: e.activation(
                        out=(h[:, c, :n] if ab else ot[:, sl, c, :n]), in_=tmp[:, :n], func=AF.Identity,
                        scale=gsc[:, 0, c, s.i:s.i + 1], bias=mod[:, 0 + c, s.i:s.i + 1]),
                        r=['tmp', 'gsc', 'mod'], w=(['h'] if ab else [OT]))
                if ab:
                    def proj(ch, pn):
                        for c in range(8):
                            cx.op('pe', lambda e, c=c: e.matmul(ps[pn][:, :n], win[:, c, ch * 128:(ch + 1) * 128],
                                                                 h[:, c, :n], start=(c == 0), stop=(c == 7)),
                                  r=['win', 'h'], w=[PS[pn]])
                    nch = n // L
                    G = min(128, n)
                    mi = 1 if L == 64 else 2
                    for hd in range(0 if 'H' in AB_SKIP else 4):
                        for j4 in range(4):
                            pn = j4 % 2
                            proj(j4 * 4 + hd, pn)
                            cx.op('act', lambda e, j4=j4, pn=pn: e.activation(out=P4[:, j4, :n], in_=ps[pn][:, :n],
                                                                              func=AF.Copy), r=[PS[pn]], w=['P4'])
                        cx.op('act', lambda e: e.activation(out=sig[:, :n], in_=P4[:, 1, :n], func=AF.Sigmoid),
                              r=['P4'], w=['sig'])
                        cx.op('dve', lambda e: e.tensor_scalar(lf[:, :n], sig[:, :n], oml[:, hd:hd + 1], lbv[:, hd:hd + 1],
                                                               ALU.mult, ALU.add), r=['sig', 'oml', 'lbv'], w=['lf'])
                        cx.op('act', lambda e: e.activation(out=lf[:, :n], in_=lf[:, :n], func=AF.Ln), r=['lf'], w=['lf'])
                        cx.op('dve', lambda e: e.tensor_scalar(kk[:, :n], sig[:, :n], noml[:, hd:hd + 1], oml[:, hd:hd + 1],
                                                               ALU.mult, ALU.add), r=['sig', 'oml', 'noml'], w=['kk'])
                        cx.op('act', lambda e: e.activation(out=qq[:, :n], in_=P4[:, 0, :n], func=AF.Silu),
                              r=['P4'], w=['qq'])
                        cx.op('act', lambda e: e.activation(out=vb[:, :n], in_=P4[:, 2, :n], func=AF.Copy),
                              r=['P4'], w=['vb'])
                        cx.op('dve', lambda e: e.memset(bb[:, 0:1], 0.0), w=['bb'])
                        cx.op('dve', lambda e: e.tensor_tensor_scan(bb[:, 1:1 + n], onesr[:, :n], lf[:, :n], 0.0,
                                                                    ALU.mult, ALU.add), r=['onesr', 'lf'], w=['bb'])
                        b3 = bb[:, 1:1 + n].rearrange("p (c l) -> p c l", l=L)
                        mid3 = b3[:, :, L // 2:L // 2 + 1].to_broadcast([128, nch, L])
                        last3 = b3[:, :, L - 1:L].to_broadcast([128, nch, L])
                        prev3 = bb[:, 0:n].rearrange("p (c l) -> p c l", l=L)[:, :, 0:1].to_broadcast([128, nch, L])
                        d3 = dd[:, :n].rearrange("p (c l) -> p c l", l=L)

                        def expmul(ref3, scale, src, dst, DST):
                            cx.op('dve', lambda e: e.tensor_tensor(d3, b3, ref3, ALU.subtract), r=['bb'], w=['dd'])
                            cx.op('act', lambda e: e.activation(out=ee[:, :n], in_=dd[:, :n], func=AF.Exp, scale=scale),
                                  r=['dd'], w=['ee'])
                            cx.op('dve', lambda e: e.tensor_tensor(dst[:, :n], src[:, :n], ee[:, :n], ALU.mult),
                                  r=['ee', 'qq', 'kk'], w=[DST])
                        expmul(mid3, 1.0, qq, qt, 'qt')
                        expmul(mid3, -1.0, kk, kt, 'kt')
                        expmul(prev3, 1.0, qq, qe, 'qe')
                        expmul(last3, -1.0, kk, kl, 'kl')
                        cx.op('dve', lambda e: e.tensor_tensor(
                            edl[:, :nch], bb[:, 1:1 + n].rearrange("p (c l) -> p c l", l=L)[:, :, L - 1],
                            bb[:, 0:n].rearrange("p (c l) -> p c l", l=L)[:, :, 0], ALU.subtract), r=['bb'], w=['edl'])
                        cx.op('act', lambda e: e.activation(out=edl[:, :nch], in_=edl[:, :nch], func=AF.Exp),
                              r=['edl'], w=['edl'])
                        psb = ps[5].bitcast(BF16)
                        for g0 in range(0, n, G):
                            cx.op('pe', lambda e, g0=g0: e.matmul(ps[2][:G, :G], kt[:, g0:g0 + G], qt[:, g0:g0 + G],
                                                                   start=True, stop=True), r=['kt', 'qt'], w=[PS[2]])
                            cx.op('dve', lambda e: e.tensor_scalar(attf[:G, :G], ps[2][:G, :G], -1e30, 1e30, ALU.max, ALU.min),
                                  r=[PS[2]], w=['attf'])
                            cx.op('dve', lambda e: e.tensor_tensor(attm[:G, :G], attf[:G, :G], cst[:G, mi, :G], ALU.mult),
                                  r=['attf', 'cst'], w=['attm'])
                            cx.op('pe', lambda e, g0=g0: e.transpose(psb[:G, 0:128], vb[:, g0:g0 + G], identb[:]),
                                  r=['vb', 'identb'], w=[PS[5]])
                            cx.op('act', lambda e: e.activation(out=vtok[:G, :], in_=psb[:G, 0:128], func=AF.Copy),
                                  r=[PS[5]], w=['vtok'])
                            cx.op('pe', lambda e, g0=g0: e.transpose(psb[:G, 0:128], kl[:, g0:g0 + G], identb[:]),
                                  r=['kl', 'identb'], w=[PS[5]])
                            cx.op('act', lambda e: e.activation(out=kltok[:G, :], in_=psb[:G, 0:128], func=AF.Copy),
                                  r=[PS[5]], w=['kltok'])
                            for ci in range(G // L):
                                c0 = ci * L
                                cidx = (g0 + c0) // L
                                cx.op('pe', lambda e, c0=c0: e.matmul(ps[3][:, c0:c0 + L], vtok[c0:c0 + L, :],
                                                                       attm[c0:c0 + L, c0:c0 + L], start=True, stop=False),
                                      r=['vtok', 'attm'], w=[PS[3]])
                                cx.op('pe', lambda e, c0=c0, g0=g0: e.matmul(ps[3][:, c0:c0 + L], Sb[:, hd, :],
                                                                              qe[:, g0 + c0:g0 + c0 + L], start=False, stop=True),
                                      r=['Sb', 'qe'], w=[PS[3]])
                                cx.op('pe', lambda e, c0=c0: e.matmul(ps[4][:, :128], kltok[c0:c0 + L, :], vtok[c0:c0 + L, :],
                                                                       start=True, stop=True), r=['kltok', 'vtok'], w=[PS[4]])
                                cx.op('dve', lambda e, cidx=cidx: e.scalar_tensor_tensor(
                                    S[:, hd, :], S[:, hd, :], edl[:, cidx:cidx + 1], ps[4][:, :128], ALU.mult, ALU.add),
                                    r=['S', 'edl', PS[4]], w=['S'])
                                cx.op('act', lambda e: e.activation(out=Sb[:, hd, :], in_=S[:, hd, :], func=AF.Copy),
                                      r=['S'], w=['Sb'])
                            cx.op('act', lambda e, g0=g0: e.activation(out=ob[:, g0:g0 + G], in_=ps[3][:, :G], func=AF.Copy),
                                  r=[PS[3]], w=['ob'])
                        cx.op('act', lambda e: e.activation(out=dd[:, :n], in_=ob[:, :n], func=AF.Square), r=['ob'], w=['dd'])
                        cx.op('pe', lambda e: e.matmul(ps[7][:, :n], ones_f[:], dd[:, :n], start=True, stop=True),
                              r=['dd', 'ones_f'], w=[PS[7]])
                        cx.op('dve', lambda e: e.tensor_scalar(ee[:, :n], ps[7][:, :n], 1.0 / 128, EPS, ALU.mult, ALU.add),
                              r=[PS[7]], w=['ee'])
                        cx.op('dve', lambda e: e.reciprocal(ee[:, :n], ee[:, :n]), r=['ee'], w=['ee'])
                        cx.op('act', lambda e: e.activation(out=ee[:, :n], in_=ee[:, :n], func=AF.Sqrt), r=['ee'], w=['ee'])
                        cx.op('dve', lambda e: e.tensor_tensor(ob[:, :n], ob[:, :n], ee[:, :n], ALU.mult),
                              r=['ob', 'ee'], w=['ob'])
                        cx.op('act', lambda e: e.activation(out=dd[:, :n], in_=P4[:, 3, :n], func=AF.Silu), r=['P4'], w=['dd'])
                        cx.op('dve', lambda e: e.scalar_tensor_tensor(ot[:, sl, hd, :n], ob[:, :n], hgg[:, 0:1], dd[:, :n],
                                                                      ALU.mult, ALU.mult), r=['ob', 'hgg', 'dd'], w=[OT])
                    for ch in range(0 if 'L' in AB_SKIP else 4):
                        proj(16 + ch, 0)
                        cx.op('act', lambda e: e.activation(out=lxb[:, 0:3], in_=lhalo[:, ch, :], func=AF.Copy),
                              r=['lhalo'], w=['lxb'])
                        cx.op('act', lambda e: e.activation(out=lxb[:, 3:3 + n], in_=ps[0][:, :n], func=AF.Copy),
                              r=[PS[0]], w=['lxb'])
                        cx.op('act', lambda e: e.activation(out=lhalo[:, ch, :], in_=lxb[:, n:n + 3], func=AF.Copy),
                              r=['lxb'], w=['lhalo'])
                        proj(20 + ch, 1)
                        cx.op('act', lambda e: e.activation(out=ly[:, :n], in_=ps[1][:, :n], func=AF.Copy),
                              r=[PS[1]], w=['ee'])
                        cx.op('dve', lambda e: e.tensor_scalar(xc[:, :n], lxb[:, 0:n], lcw[:, ch, 0:1], lvec[:, ch, 0:1],
                                                               ALU.mult, ALU.add), r=['lxb', 'lcw', 'lvec'], w=['xc'])
                        for tp in range(1, 4):
                            cx.op('dve', lambda e, tp=tp: e.scalar_tensor_tensor(
                                xc[:, :n], lxb[:, tp:tp + n], lcw[:, ch, tp:tp + 1], xc[:, :n], ALU.mult, ALU.add),
                                r=['lxb', 'lcw', 'xc'], w=['xc'])
                        cx.op('act', lambda e: e.activation(out=xcb[:, :n], in_=xc[:, :n], func=AF.Copy), r=['xc'], w=['xcb'])
                        cx.op('pe', lambda e: e.matmul(ps[2][:, :n], gw[:, 0, ch, :], xcb[:, :n], start=True, stop=True),
                              r=['gw', 'xcb'], w=[PS[2]])
                        cx.op('act', lambda e: e.activation(out=rr[:, :n], in_=ps[2][:, :n], func=AF.Sigmoid,
                                                            bias=lvec[:, ch, 1:2]), r=[PS[2], 'lvec'], w=['sig'])
                        cx.op('pe', lambda e: e.matmul(ps[3][:, :n], gw[:, 1, ch, :], xcb[:, :n], start=True, stop=True),
                              r=['gw', 'xcb'], w=[PS[3]])
                        cx.op('act', lambda e: e.activation(out=ii[:, :n], in_=ps[3][:, :n], func=AF.Sigmoid,
                                                            bias=lvec[:, ch, 2:3]), r=[PS[3], 'lvec'], w=['kk'])
                        cx.op('dve', lambda e: e.tensor_scalar(rr[:, :n], rr[:, :n], m8sp[:, ch:ch + 1], None, ALU.mult),
                              r=['sig', 'm8sp'], w=['sig'])
                        cx.op('act', lambda e: e.activation(out=aa[:, :n], in_=rr[:, :n], func=AF.Exp), r=['sig'], w=['qq'])
                        cx.op('act', lambda e: e.activation(out=uu[:, :n], in_=rr[:, :n], func=AF.Exp, scale=2.0),
                              r=['sig'], w=['lf'])
                        cx.op('dve', lambda e: e.tensor_scalar(uu[:, :n], uu[:, :n], -1.0, 1.0, ALU.mult, ALU.add),
                              r=['lf'], w=['lf'])
                        cx.op('dve', lambda e: e.tensor_scalar(uu[:, :n], uu[:, :n], 1e-12, None, ALU.max),
                              r=['lf'], w=['lf'])
                        cx.op('act', lambda e: e.activation(out=uu[:, :n], in_=uu[:, :n], func=AF.Sqrt), r=['lf'], w=['lf'])
                        cx.op('dve', lambda e: e.tensor_tensor(uu[:, :n], uu[:, :n], ii[:, :n], ALU.mult),
                              r=['lf', 'kk'], w=['lf'])
                        cx.op('dve', lambda e: e.tensor_tensor(uu[:, :n], uu[:, :n], xc[:, :n], ALU.mult),
                              r=['lf', 'xc'], w=['lf'])
                        cx.op('dve', lambda e: e.tensor_tensor_scan(hh[:, :n], aa[:, :n], uu[:, :n], hst[:, ch:ch + 1],
                                                                    ALU.mult, ALU.add), r=['qq', 'lf', 'hst'], w=['dd'])
                        cx.op('act', lambda e: e.activation(out=hst[:, ch:ch + 1], in_=hh[:, n - 1:n], func=AF.Copy),
                              r=['dd'], w=['hst'])
                        cx.op('dve', lambda e: e.tensor_tensor(gl[:, :n], ly[:, :n], ly[:, :n], ALU.mult), r=['ee'], w=['ob'])
                        cx.op('dve', lambda e: e.tensor_scalar(gl[:, :n], gl[:, :n], 0.044715, 1.0, ALU.mult, ALU.add),
                              r=['ob'], w=['ob'])
                        cx.op('dve', lambda e: e.tensor_tensor(gl[:, :n], gl[:, :n], ly[:, :n], ALU.mult),
                              r=['ob', 'ee'], w=['ob'])
                        cx.op('act', lambda e: e.activation(out=gl[:, :n], in_=gl[:, :n], func=AF.Sigmoid,
                                                            scale=1.5957691216057308), r=['ob'], w=['ob'])
                        cx.op('dve', lambda e: e.tensor_tensor(gl[:, :n], gl[:, :n], ly[:, :n], ALU.mult),
                              r=['ob', 'ee'], w=['ob'])
                        cx.op('dve', lambda e: e.tensor_tensor(ot[:, sl, 4 + ch, :n], hh[:, :n], gl[:, :n], ALU.mult),
                              r=['dd', 'ob'], w=[OT])
                cx.dma('pool', s.oT.rearrange("(c p) t -> p c t", p=128)[:, :, t0:t0 + n], ot[:, sl, :, :n],
                       r=[OT], w=['oT%d' % s.i])
            if ab:
                cx.dma('pool', s.hg_out[jl], S[:], r=['S'], w=['hg_out'])
                cx.dma('pool', s.lru_out[jl], hst[:], r=['hst'], w=['lru_out'])
                cx.dma('pool', s.lruc_out[jl], lhalo[:], r=['lhalo'], w=['lruc_out'])
        cx.barrier()


MLA_SCALE = (128 + 64) ** -0.5
CD_STAGE = int(os.environ.get('CD_STAGE', '99'))
CD_SUB = int(os.environ.get('CD_SUB', '99'))
M2_SUB = int(os.environ.get('M2_SUB', '99'))
AB_SKIP = os.environ.get('AB_SKIP', '')
E_POOL = os.environ.get('E_POOL', '0') == '1'


def emit_cd(nc, cx, ps, PS, l, seqs, mod, gsc, ng, ones_f, rms_stats, W):
    jl = l // 2
    ones_b = W['ones_b']
    RT = lambda a: a.rearrange("(c p) t -> p c t", p=128)
    with ExitStack() as st:
        def T(name, shape, dt):
            return st.enter_context(SB(nc, name, shape, dt))
        xt = T("xt", [128, 8, 512], F32)
        sq = T("sq", [128, 8, 512], BF16)
        rstd = T("rstd", [128, 512], F32)
        tmp = T("tmp", [128, 512], F32)
        h = T("h", [128, 8, 512], BF16)
        ot = T("otile", [128, 2, 4, 512], BF16)
        wst = T("wst", [128, 2, 1536], F32)
        win = T("win", [128, 8, 3072], BF16)
        cst = T("cst", [128, 13, 128], F32)
        identb = T("identb", [128, 128], BF16)
        onesr = T("onesr", [128, 512], F32)
        pre = T("pre", [128, 3, 515], F32)
        cvx = T("cvx", [128, 3, 512], F32)
        ghalo = T("ghalo", [128, 12, 3], F32)
        gcw = T("gcw", [128, 12, 4], F32)
        gvec = T("gvec", [128, 4, 2], F32)
        nexpa = T("nexpa", [128, 4], F32)
        gdg = T("gdg", [128, 1], F32)
        c21 = T("c21", [128, 512], F32)
        c22 = T("c22", [128, 512], F32)
        c23 = T("c23", [128, 512], F32)
        gz = T("gz", [128, 512], F32)
        betaB = T("betaB", [128, 512], F32)
        gg = T("gg", [128, 513], F32)
        egr = T("egr", [128, 512], F32)
        ela = T("ela", [128, 512], F32)
        edl = T("edl", [128, 8], F32)
        w1 = T("w1", [128, 512], F32)
        w2 = T("w2", [128, 512], F32)
        vbet = T("vbet", [128, 512], F32)
        kbe = T("kbe", [128, 512], F32)
        kd = T("kd", [128, 512], F32)
        qbf = T("qbf", [128, 512], BF16)
        kbf = T("kbf", [128, 512], BF16)
        qg = T("qg", [128, 512], BF16)
        ob = T("ob", [128, 512], F32)
        gcolS = [T("gcol%d" % i_, [128, 1], F32) for i_ in range(3)]
        DBeS = [T("DBe%d" % i_, [128, 128], F32) for i_ in range(3)]
        XYS = [T("XY%d" % i_, [128, 2, 2, 128], F32) for i_ in range(3)]
        RmS = [T("Rm%d" % i_, [128, 256], F32) for i_ in range(3)]
        qkTS = [T("qkT%d" % i_, [128, 128], BF16) for i_ in range(3)]
        kdtokS = [T("kdtok%d" % i_, [128, 128], BF16) for i_ in range(3)]
        wkTS = [T("wkT%d" % i_, [128, 128], BF16) for i_ in range(3)]
        utokS = [T("utok%d" % i_, [128, 128], BF16) for i_ in range(3)]
        S = T("S", [128, 4, 128], F32)
        Sb = T("Sb", [128, 4, 128], BF16)
        PQ = T("PQ", [128, 3, 512], F32)
        qng = T("qng", [128, 3], F32)
        kvg = T("kvg", [128, 2], F32)
        qan = T("qan", [128, 3, 512], BF16)
        ckv = T("ckv", [128, 2, 512], F32)
        tokst = T("tokst", [128, 4, 320], F32)
        rope = T("rope", [64, 2, 512], F32)
        krr = T("krr", [64, 512], F32)
        for c2 in range(16):
            c, hf = c2 // 2, c2 % 2
            sl = hf
            cx.dma('sp', wst[:, sl], W['cd_w_in'][jl][c * 128:(c + 1) * 128, hf * 1536:(hf + 1) * 1536],
                   w=['wst%d' % sl])
            cx.op('act' if sl else 'dve',
                  (lambda e: e.activation(out=win[:, c, hf * 1536:(hf + 1) * 1536], in_=wst[:, sl], func=AF.Copy))
                  if sl else (lambda e: e.tensor_copy(win[:, c, hf * 1536:(hf + 1) * 1536], wst[:, sl])),
                  r=['wst%d' % sl], w=['win'])
        cx.dma('sp', cst[:], W['consts'][:, :, :], w=['cst'])
        cx.op('dve', lambda e: e.tensor_copy(identb[:], cst[:, 0, :]), r=['cst'], w=['identb'])
        cx.op('dve', lambda e: e.memset(onesr[:], 1.0), w=['onesr'])
        cx.dma('sp', gcw[:], W['gd_cw'][jl], w=['gcw'])
        cx.dma('sp', gvec[:], W['gd_vec'][jl], w=['gvec'])
        cx.dma('sp', gdg[:], W['gd_g'][jl], w=['gdg'])
        cx.dma('sp', qng[:], W['q_g'][jl], w=['qng'])
        cx.dma('sp', kvg[:], W['kv_g'][jl], w=['kvg'])
        cx.op('act', lambda e: e.activation(out=nexpa[:], in_=gvec[:, :, 0], func=AF.Exp), r=['gvec'], w=['nexpa'])
        cx.op('dve', lambda e: e.tensor_scalar(nexpa[:], nexpa[:], -1.0, None, ALU.mult), r=['nexpa'], w=['nexpa'])
        k = 0
        for s in seqs:
            L = 64 if s.T % 64 == 0 else s.T
            nsteps = int(round(math.log2(L)))
            past = s.past
            if s.i == 0:
                cx.op('dve', lambda e: e.memset(S[:], 0.0), w=['S'])
                cx.op('dve', lambda e: e.memset(ghalo[:], 0.0), w=['ghalo'])
            else:
                cx.dma('sp', S[:], s.gd_in[jl], w=['S'])
                cx.dma('sp', ghalo[:], s.gdc_in[jl], w=['ghalo'])
                cx.dma('pool', s.ckvT[:, 0:past], s.lat_inT[jl], w=['ckvT%d' % s.i])
                cx.dma('pool', s.krT[:, 0:past], s.kr_inT[jl], w=['krT%d' % s.i])
            cx.op('act', lambda e: e.activation(out=Sb[:], in_=S[:], func=AF.Copy), r=['S'], w=['Sb'])
            for (t0, n) in s.tiles:
                sl = k % 2
                k += 1
                OT = 'ot%d' % sl
                cx.dma('sp', xt[:, :, :n], RT(s.xT)[:, :, t0:t0 + n], r=['xT%d' % s.i], w=['xt'])
                cx.dma('sp', rope[:, 0, :n], s.ropeC[:, t0:t0 + n], w=['rope'])
                cx.dma('sp', rope[:, 1, :n], s.ropeS[:, t0:t0 + n], w=['rope'])
                rms_stats(lambda c: xt[:, c, :n], n, sq, rstd, 6, ['xt'])
                for c in range(8):
                    cx.op('dve', lambda e: e.tensor_tensor(tmp[:, :n], xt[:, c, :n], rstd[:, :n], ALU.mult),
                          r=['xt', 'rstd'], w=['tmp'])
                    cx.op('act', lambda e: e.activation(out=h[:, c, :n], in_=tmp[:, :n], func=AF.Identity,
                                                        scale=gsc[:, 0, c, s.i:s.i + 1], bias=mod[:, c, s.i:s.i + 1]),
                          r=['tmp', 'gsc', 'mod'], w=['h'])

                def proj(ch, pn):
                    for c in range(8):
                        cx.op('pe', lambda e, c=c: e.matmul(ps[pn][:, :n], win[:, c, ch * 128:(ch + 1) * 128],
                                                             h[:, c, :n], start=(c == 0), stop=(c == 7)),
                              r=['win', 'h'], w=[PS[pn]])

                def pcopy(ch, pn, dst, DST):
                    proj(ch, pn)
                    cx.op('act', lambda e: e.activation(out=dst, in_=ps[pn][:, :n], func=AF.Copy), r=[PS[pn]], w=[DST])

                def stat128(src, SRC, dst, DST, scl, bias_eps):
                    cx.op('act', lambda e: e.activation(out=w2[:, :n], in_=src, func=AF.Square), r=[SRC], w=['w2'])
                    cx.op('pe', lambda e: e.matmul(ps[6][:, :n], ones_f[:], w2[:, :n], start=True, stop=True),
                          r=['w2', 'ones_f'], w=[PS[6]])
                    cx.op('dve', lambda e: e.tensor_scalar(dst, ps[6][:, :n], scl, bias_eps, ALU.mult, ALU.add),
                          r=[PS[6]], w=[DST])
                    cx.op('dve', lambda e: e.reciprocal(dst, dst), r=[DST], w=[DST])
                    cx.op('act', lambda e: e.activation(out=dst, in_=dst, func=AF.Sqrt), r=[DST], w=[DST])

                pcopy(21, 0, c21[:, :n], 'c21')
                pcopy(22, 1, c22[:, :n], 'c22')
                pcopy(23, 0, c23[:, :n], 'c23')
                for c in range(3):
                    pcopy(16 + c, c % 2, PQ[:, c, :n], 'PQ')
                for c in range(3):
                    cx.op('act', lambda e: e.activation(out=sq[:, c, :n], in_=PQ[:, c, :n], func=AF.Square), r=['PQ'], w=['sq'])
                for c in range(3):
                    cx.op('pe', lambda e: e.matmul(ps[6][:, :n], ones_b[:], sq[:, c, :n], start=(c == 0), stop=(c == 2)),
                          r=['sq', 'ones_b'], w=[PS[6]])
                cx.op('dve', lambda e: e.tensor_scalar(w1[:, :n], ps[6][:, :n], 1.0 / 384, EPS, ALU.mult, ALU.add),
                      r=[PS[6]], w=['w1'])
                cx.op('dve', lambda e: e.reciprocal(w1[:, :n], w1[:, :n]), r=['w1'], w=['w1'])
                cx.op('act', lambda e: e.activation(out=w1[:, :n], in_=w1[:, :n], func=AF.Sqrt), r=['w1'], w=['w1'])
                for c in range(3):
                    cx.op('dve', lambda e: e.tensor_tensor(PQ[:, c, :n], PQ[:, c, :n], w1[:, :n], ALU.mult),
                          r=['PQ', 'w1'], w=['PQ'])
                    cx.op('dve', lambda e: e.tensor_scalar(qan[:, c, :n], PQ[:, c, :n], qng[:, c:c + 1], None, ALU.mult),
                          r=['PQ', 'qng'], w=['qan'])
                cx.dma('pool', RT(s.qanT)[:, :, t0:t0 + n], qan[:, :, :n], r=['qan'], w=['qanT%d' % s.i])
                for c in range(2):
                    pcopy(19 + c, c % 2, PQ[:, c, :n], 'PQ')
                for c in range(2):
                    cx.op('act', lambda e: e.activation(out=sq[:, c, :n], in_=PQ[:, c, :n], func=AF.Square), r=['PQ'], w=['sq'])
                for c in range(2):
                    cx.op('pe', lambda e: e.matmul(ps[6][:, :n], ones_b[:], sq[:, c, :n], start=(c == 0), stop=(c == 1)),
                          r=['sq', 'ones_b'], w=[PS[6]])
                cx.op('dve', lambda e: e.tensor_scalar(w1[:, :n], ps[6][:, :n], 1.0 / 256, EPS, ALU.mult, ALU.add),
                      r=[PS[6]], w=['w1'])
                cx.op('dve', lambda e: e.reciprocal(w1[:, :n], w1[:, :n]), r=['w1'], w=['w1'])
                cx.op('act', lambda e: e.activation(out=w1[:, :n], in_=w1[:, :n], func=AF.Sqrt), r=['w1'], w=['w1'])
                for c in range(2):
                    cx.op('dve', lambda e: e.tensor_tensor(PQ[:, c, :n], PQ[:, c, :n], w1[:, :n], ALU.mult),
                          r=['PQ', 'w1'], w=['PQ'])
                    cx.op('dve', lambda e: e.tensor_scalar(ckv[:, c, :n], PQ[:, c, :n], kvg[:, c:c + 1], None, ALU.mult),
                          r=['PQ', 'kvg'], w=['ckv'])
                cx.dma('pool', RT(s.ckvT)[:, :, past + t0:past + t0 + n], ckv[:, :, :n], r=['ckv'], w=['ckvT%d' % s.i])
                cx.op('dve', lambda e: e.tensor_tensor(krr[:, :n], c21[0:64, :n], rope[:, 0, :n], ALU.mult),
                      r=['c21', 'rope'], w=['krr'])
                cx.op('dve', lambda e: e.tensor_tensor(w1[0:64, :n], c23[0:64, :n], rope[:, 1, :n], ALU.mult),
                      r=['c23', 'rope'], w=['w1'])
                cx.op('dve', lambda e: e.tensor_tensor(krr[:, :n], krr[:, :n], w1[0:64, :n], ALU.add),
                      r=['krr', 'w1'], w=['krr'])
                cx.dma('pool', s.krT[:, past + t0:past + t0 + n], krr[:, :n], r=['krr'], w=['krT%d' % s.i])
                nsub = (n + 127) // 128
                for sb_ in range(nsub):
                    a0 = sb_ * 128
                    an = min(128, n - a0)
                    for c in range(2):
                        cx.op('pe', lambda e: e.transpose(ps[4][:an, c * 128:(c + 1) * 128], ckv[:, c, a0:a0 + an], cst[:, 0, :]),
                              r=['ckv', 'cst'], w=[PS[4]])
                    cx.op('pe', lambda e: e.transpose(ps[4][:an, 256:320], krr[:, a0:a0 + an], cst[0:64, 0, 0:64]),
                          r=['krr', 'cst'], w=[PS[4]])
                    cx.op('act', lambda e: e.activation(out=tokst[:an, sb_, :], in_=ps[4][:an, 0:320], func=AF.Copy),
                          r=[PS[4]], w=['tokst'])
                    cx.dma('pool', s.lat_out[jl][t0 + a0:t0 + a0 + an, :], tokst[:an, sb_, 0:256], r=['tokst'], w=['lat_out'])
                    cx.dma('pool', s.kr_out[jl][t0 + a0:t0 + a0 + an, :], tokst[:an, sb_, 256:320], r=['tokst'], w=['kr_out'])
                nch = n // L
                G = min(128, n)
                mI, mS = (1, 4) if L == 64 else (2, 3)
                for hd in range(4 if CD_STAGE >= 2 else 0):
                    for idx, ch in enumerate((hd, 4 + hd, 8 + hd)):
                        proj(ch, idx % 2)
                        cx.op('act', lambda e: e.activation(out=pre[:, idx, 0:3], in_=ghalo[:, ch, :], func=AF.Copy),
                              r=['ghalo'], w=['pre'])
                        cx.op('act', lambda e: e.activation(out=pre[:, idx, 3:3 + n], in_=ps[idx % 2][:, :n], func=AF.Copy),
                              r=[PS[idx % 2]], w=['pre'])
                        cx.op('act', lambda e: e.activation(out=ghalo[:, ch, :], in_=pre[:, idx, n:n + 3], func=AF.Copy),
                              r=['pre'], w=['ghalo'])
                        cx.op('dve', lambda e: e.tensor_scalar(cvx[:, idx, :n], pre[:, idx, 0:n], gcw[:, ch, 0:1], None, ALU.mult),
                              r=['pre', 'gcw'], w=['cvx'])
                        for tp in range(1, 4):
                            cx.op('dve', lambda e: e.scalar_tensor_tensor(cvx[:, idx, :n], pre[:, idx, tp:tp + n],
                                                                          gcw[:, ch, tp:tp + 1], cvx[:, idx, :n], ALU.mult, ALU.add),
                                  r=['pre', 'gcw', 'cvx'], w=['cvx'])
                        cx.op('act', lambda e: e.activation(out=cvx[:, idx, :n], in_=cvx[:, idx, :n], func=AF.Silu),
                              r=['cvx'], w=['cvx'])
                    pcopy(12 + hd, 0, gz[:, :n], 'gz')
                    stat128(cvx[:, 0, :n], 'cvx', w1[:, :n], 'w1', 1.0, EPS)
                    cx.op('dve', lambda e: e.scalar_tensor_tensor(cvx[:, 0, :n], cvx[:, 0, :n], 128 ** -0.5, w1[:, :n],
                                                                  ALU.mult, ALU.mult), r=['cvx', 'w1'], w=['cvx'])
                    stat128(cvx[:, 1, :n], 'cvx', w1[:, :n], 'w1', 1.0, EPS)
                    cx.op('dve', lambda e: e.tensor_tensor(cvx[:, 1, :n], cvx[:, 1, :n], w1[:, :n], ALU.mult),
                          r=['cvx', 'w1'], w=['cvx'])
                    cx.op('act', lambda e: e.activation(out=qbf[:, :n], in_=cvx[:, 0, :n], func=AF.Copy), r=['cvx'], w=['qbf'])
                    cx.op('act', lambda e: e.activation(out=kbf[:, :n], in_=cvx[:, 1, :n], func=AF.Copy), r=['cvx'], w=['kbf'])
                    cx.op('pe', lambda e: e.matmul(ps[2][:, :n], cst[:, 5 + hd, :], c21[:, :n], start=True, stop=True),
                          r=['cst', 'c21'], w=[PS[2]])
                    cx.op('act', lambda e: e.activation(out=betaB[:, :n], in_=ps[2][:, :n], func=AF.Sigmoid),
                          r=[PS[2]], w=['betaB'])
                    cx.op('pe', lambda e: e.matmul(ps[3][:, :n], cst[:, 9 + hd, :], c22[:, :n], start=True, stop=True),
                          r=['cst', 'c22'], w=[PS[3]])
                    cx.op('dve', lambda e: e.tensor_scalar(w1[:, :n], ps[3][:, :n], gvec[:, hd, 1:2], None, ALU.add),
                          r=[PS[3], 'gvec'], w=['w1'])
                    cx.op('act', lambda e: e.activation(out=w2[:, :n], in_=w1[:, :n], func=AF.Abs), r=['w1'], w=['w2'])
                    cx.op('act', lambda e: e.activation(out=w2[:, :n], in_=w2[:, :n], func=AF.Exp, scale=-1.0), r=['w2'], w=['w2'])
                    cx.op('dve', lambda e: e.tensor_scalar(w2[:, :n], w2[:, :n], 1.0, None, ALU.add), r=['w2'], w=['w2'])
                    cx.op('act', lambda e: e.activation(out=w2[:, :n], in_=w2[:, :n], func=AF.Ln), r=['w2'], w=['w2'])
                    cx.op('dve', lambda e: e.scalar_tensor_tensor(w1[:, :n], w1[:, :n], 0.0, w2[:, :n], ALU.max, ALU.add),
                          r=['w1', 'w2'], w=['w1'])
                    cx.op('dve', lambda e: e.tensor_scalar(w1[:, :n], w1[:, :n], nexpa[:, hd:hd + 1], None, ALU.mult),
                          r=['w1', 'nexpa'], w=['w1'])
                    cx.op('dve', lambda e: e.memset(gg[:, 0:1], 0.0), w=['gg'])
                    cx.op('dve', lambda e: e.tensor_tensor_scan(gg[:, 1:1 + n], onesr[:, :n], w1[:, :n], 0.0, ALU.mult, ALU.add),
                          r=['onesr', 'w1'], w=['gg'])
                    b3 = gg[:, 1:1 + n].rearrange("p (c l) -> p c l", l=L)
                    last3 = b3[:, :, L - 1:L].to_broadcast([128, nch, L])
                    prev3 = gg[:, 0:n].rearrange("p (c l) -> p c l", l=L)[:, :, 0:1].to_broadcast([128, nch, L])
                    cx.op('dve', lambda e: e.tensor_tensor(egr[:, :n].rearrange("p (c l) -> p c l", l=L), b3, prev3, ALU.subtract),
                          r=['gg'], w=['egr'])
                    cx.op('act', lambda e: e.activation(out=egr[:, :n], in_=egr[:, :n], func=AF.Exp), r=['egr'], w=['egr'])
                    cx.op('dve', lambda e: e.tensor_tensor(ela[:, :n].rearrange("p (c l) -> p c l", l=L), b3, last3, ALU.subtract),
                          r=['gg'], w=['ela'])
                    cx.op('act', lambda e: e.activation(out=ela[:, :n], in_=ela[:, :n], func=AF.Exp, scale=-1.0), r=['ela'], w=['ela'])
                    cx.op('dve', lambda e: e.tensor_tensor(
                        edl[:, :nch], gg[:, 1:1 + n].rearrange("p (c l) -> p c l", l=L)[:, :, L - 1],
                        gg[:, 0:n].rearrange("p (c l) -> p c l", l=L)[:, :, 0], ALU.subtract), r=['gg'], w=['edl'])
                    cx.op('act', lambda e: e.activation(out=edl[:, :nch], in_=edl[:, :nch], func=AF.Exp), r=['edl'], w=['edl'])
                    cx.op('dve', lambda e: e.tensor_tensor(vbet[:, :n], cvx[:, 2, :n], betaB[:, :n], ALU.mult),
                          r=['cvx', 'betaB'], w=['vbet'])
                    cx.op('dve', lambda e: e.tensor_tensor(kbe[:, :n], cvx[:, 1, :n], betaB[:, :n], ALU.mult),
                          r=['cvx', 'betaB'], w=['kbe'])
                    cx.op('dve', lambda e: e.tensor_tensor(kbe[:, :n], kbe[:, :n], egr[:, :n], ALU.mult), r=['kbe', 'egr'], w=['kbe'])
                    cx.op('dve', lambda e: e.tensor_tensor(kd[:, :n], cvx[:, 1, :n], ela[:, :n], ALU.mult), r=['cvx', 'ela'], w=['kd'])
                    cx.op('dve', lambda e: e.tensor_tensor(qg[:, :n], cvx[:, 0, :n], egr[:, :n], ALU.mult), r=['cvx', 'egr'], w=['qg'])
                    groups = list(range(0, n, G))
                    nthr = min(int(os.environ.get('GDN_THR', '3')), len(groups))
                    if os.environ.get('GDN_BAR', '0') == '1':
                        cx.barrier()
                    tick = {'done': 0}

                    def gdn_thread(p):
                        ba, bb_ = ((2, 3), (4, 5), (0, 1))[p]
                        pa, pb = ps[ba], ps[bb_]
                        A0 = A1 = A2 = PS[ba]
                        B0 = B1 = PS[bb_]
                        gcol_, DBe_, XY_, Rm_, qkT_, kdtok_, wkT_, utok_ = [b_[p] for b_ in
                                                                           (gcolS, DBeS, XYS, RmS, qkTS, kdtokS, wkTS, utokS)]
                        N_ = lambda nm: '%s_%d' % (nm, p)
                        for gi in range(p, len(groups), nthr):
                            g0 = groups[gi]
                            gs = slice(g0, g0 + G)
                            cx.op('pe', lambda e: e.transpose(pb[:G, 0:128], gg[:, 1 + g0:1 + g0 + G], cst[:, 0, :]),
                                  r=['gg', 'cst'], w=[B0])
                            cx.op('act', lambda e: e.activation(out=gcol_[:G, :], in_=pb[:G, 0:1], func=AF.Copy), r=[B0], w=[N_('gcol')])
                            cx.op('dve', lambda e: e.tensor_scalar(DBe_[:G, :G], gg[:G, 1 + g0:1 + g0 + G], gcol_[:G, 0:1], 0.0,
                                                                   ALU.subtract, ALU.min), r=['gg', N_('gcol')], w=[N_('DBe')])
                            cx.op('act', lambda e: e.activation(out=DBe_[:G, :G], in_=DBe_[:G, :G], func=AF.Exp), r=[N_('DBe')], w=[N_('DBe')])
                            cx.op('pe', lambda e: e.matmul(pa[:G, 0:G], kbf[:, gs], kbf[:, gs], start=True, stop=True), r=['kbf'], w=[A0])
                            X0, Y0 = XY_[:G, 0, 0, :G], XY_[:G, 0, 1, :G]
                            cx.op('dve', lambda e: e.scalar_tensor_tensor(X0, pa[:G, 0:G], -1.0, DBe_[:G, :G], ALU.mult, ALU.mult),
                                  r=[A0, N_('DBe')], w=[N_('XY0')])
                            cx.op('dve', lambda e: e.tensor_tensor(X0, X0, betaB[:G, gs], ALU.mult), r=[N_('XY0'), 'betaB'], w=[N_('XY0')])
                            cx.op('dve', lambda e: e.tensor_tensor(X0, X0, cst[:G, mS, :G], ALU.mult), r=[N_('XY0'), 'cst'], w=[N_('XY0')])
                            cx.op('pe', lambda e: e.transpose(pb[:G, 384:384 + G], X0, cst[:G, 0, :G]), r=[N_('XY0'), 'cst'], w=[B1])
                            cx.op('act', lambda e: e.activation(out=Y0, in_=pb[:G, 384:384 + G], func=AF.Copy), r=[B1], w=[N_('XY0')])
                            cx.op('pe', lambda e: e.matmul(pa[:G, 128:128 + G], kbf[:, gs], qbf[:, gs], start=True, stop=True),
                                  r=['kbf', 'qbf'], w=[A1])
                            cx.op('dve', lambda e: e.tensor_tensor(DBe_[:G, :G], DBe_[:G, :G], cst[:G, mI, :G], ALU.mult),
                                  r=[N_('DBe'), 'cst'], w=[N_('DBe')])
                            cx.op('dve', lambda e: e.tensor_tensor(qkT_[:G, :G], pa[:G, 128:128 + G], DBe_[:G, :G], ALU.mult),
                                  r=[A1, N_('DBe')], w=[N_('qkT')])
                            cx.op('pe', lambda e: e.transpose(pb[:G, 0:128], vbet[:, gs], cst[:, 0, :]), r=['vbet', 'cst'], w=[B0])
                            cx.op('pe', lambda e: e.transpose(pb[:G, 128:256], kbe[:, gs], cst[:, 0, :]), r=['kbe', 'cst'], w=[B0])
                            cx.op('pe', lambda e: e.transpose(pb[:G, 256:384], kd[:, gs], cst[:, 0, :]), r=['kd', 'cst'], w=[B0])
                            cx.op('act', lambda e: e.activation(out=Rm_[:G, :], in_=pb[:G, 0:256], func=AF.Copy), r=[B0], w=[N_('Rm')])
                            cx.op('act', lambda e: e.activation(out=kdtok_[:G, :], in_=pb[:G, 256:384], func=AF.Copy), r=[B0], w=[N_('kdtok')])
                            for kk_ in range(nsteps):
                                pg = kk_ % 2
                                Xc, Yc = XY_[:G, pg, 0, :G], XY_[:G, pg, 1, :G]
                                cx.op('pe', lambda e: e.matmul(pa[:G, 256:512], Xc, Rm_[:G, :], start=True, stop=True),
                                      r=[N_('XY%d' % pg), N_('Rm')], w=[A2])
                                cx.op('dve', lambda e: e.tensor_tensor(Rm_[:G, :], Rm_[:G, :], pa[:G, 256:512], ALU.add),
                                      r=[N_('Rm'), A2], w=[N_('Rm')])
                                if kk_ < nsteps - 1:
                                    Xn, Yn = XY_[:G, 1 - pg, 0, :G], XY_[:G, 1 - pg, 1, :G]
                                    cx.op('pe', lambda e: e.matmul(pa[:G, 0:G], Yc, Xc, start=True, stop=True), r=[N_('XY%d' % pg)], w=[A0])
                                    cx.op('pe', lambda e: e.matmul(pa[:G, 128:128 + G], Xc, Yc, start=True, stop=True), r=[N_('XY%d' % pg)], w=[A1])
                                    cx.op('act', lambda e: e.activation(out=Xn, in_=pa[:G, 0:G], func=AF.Copy), r=[A0], w=[N_('XY%d' % (1 - pg))])
                                    cx.op('dve', lambda e: e.tensor_copy(Yn, pa[:G, 128:128 + G]), r=[A1], w=[N_('XY%d' % (1 - pg))])
                            cx.op('pe', lambda e: e.transpose(pb[:, 0:G], Rm_[:G, 128:256], cst[:G, 0, :G]), r=[N_('Rm'), 'cst'], w=[B0])
                            cx.op('act', lambda e: e.activation(out=wkT_[:, :G], in_=pb[:, 0:G], func=AF.Copy), r=[B0], w=[N_('wkT')])
                            cx.wait_until(lambda: tick['done'] == gi)
                            if os.environ.get('GDN_OLDB', '0') == '1':
                                WS_, SU_, WSN, SUN = ps[5], ps[4][:, 0:128], PS[5], PS[4]
                            else:
                                WS_, SU_, WSN, SUN = ps[6], ps[6][:, 128:256], PS[6], PS[6]
                            for ci in range(G // L):
                                c0 = ci * L
                                cs_ = slice(c0, c0 + L)
                                cidx = (g0 + c0) // L
                                cx.op('pe', lambda e: e.matmul(WS_[:G, 0:128], wkT_[:, :G], Sb[:, hd, :], start=True, stop=True),
                                      r=[N_('wkT'), 'Sb'], w=[WSN])
                                cx.op('dve', lambda e: e.tensor_tensor(utok_[cs_, :], Rm_[cs_, 0:128], WS_[cs_, 0:128], ALU.subtract),
                                      r=[N_('Rm'), WSN], w=[N_('utok')])
                                cx._noyield += 1
                                cx.op('pe', lambda e: e.matmul(ps[7][:, cs_], utok_[cs_, :], qkT_[cs_, cs_], start=True, stop=False),
                                      r=[N_('utok'), N_('qkT')], w=[PS[7]])
                                cx._noyield -= 1
                                cx.op('pe', lambda e: e.matmul(ps[7][:, cs_], Sb[:, hd, :], qg[:, g0 + c0:g0 + c0 + L], start=False, stop=True),
                                      r=['Sb', 'qg'], w=[PS[7]])
                                cx.op('pe', lambda e: e.matmul(SU_, kdtok_[cs_, :], utok_[cs_, :], start=True, stop=True),
                                      r=[N_('kdtok'), N_('utok')], w=[SUN])
                                cx.op('dve', lambda e: e.scalar_tensor_tensor(S[:, hd, :], S[:, hd, :], edl[:, cidx:cidx + 1], SU_,
                                                                              ALU.mult, ALU.add), r=['S', 'edl', SUN], w=['S'])
                                cx.op('act', lambda e: e.activation(out=Sb[:, hd, :], in_=S[:, hd, :], func=AF.Copy), r=['S'], w=['Sb'])
                            cx.op('act', lambda e: e.activation(out=ob[:, gs], in_=ps[7][:, :G], func=AF.Copy), r=[PS[7]], w=['ob'])
                            tick['done'] += 1

                    cx.interleave([(lambda p=p: gdn_thread(p)) for p in range(nthr)])
                    if os.environ.get('GDN_BAR', '0') == '1':
                        cx.barrier()
                    stat128(ob[:, :n], 'ob', w1[:, :n], 'w1', 1.0 / 128, EPS)
                    cx.op('dve', lambda e: e.tensor_tensor(ob[:, :n], ob[:, :n], w1[:, :n], ALU.mult), r=['ob', 'w1'], w=['ob'])
                    cx.op('act', lambda e: e.activation(out=gz[:, :n], in_=gz[:, :n], func=AF.Silu), r=['gz'], w=['gz'])
                    cx.op('dve', lambda e: e.scalar_tensor_tensor(ot[:, sl, hd, :n], ob[:, :n], gdg[:, 0:1], gz[:, :n], ALU.mult, ALU.mult),
                          r=['ob', 'gdg', 'gz'], w=[OT])
                cx.dma('pool', RT(s.oT)[:, 0:4, t0:t0 + n], ot[:, sl, :, :n], r=[OT], w=['oT%d' % s.i])
            cx.dma('pool', s.gd_out[jl], S[:], r=['S'], w=['gd_out'])
            cx.dma('pool', s.gdc_out[jl], ghalo[:], r=['ghalo'], w=['gdc_out'])
        cx.barrier()

    if CD_STAGE < 3:
        return
    with ExitStack() as st:
        def T(name, shape, dt):
            return st.enter_context(SB(nc, name, shape, dt))
        TA = max(s.past + s.T for s in seqs)
        NT128 = (TA + 127) // 128
        wst = T("wst", [128, 3, 1024], F32)
        wqb = T("wqb", [128, 3, 1024], BF16)
        wkvb = T("wkvb", [128, 2, 1024], BF16)
        onesb = T("onesb", [128, 128], BF16)
        KT = T("KT", [128, TA], BF16)
        VT = T("VT", [128, NT128, 128], BF16)
        KR = T("KR", [65, TA], BF16)
        negm = T("negm", [1, 512], BF16)
        cst_ = T("cstf", [128, 512], F32)
        ckb = T("ckb", [128, 2, 512], BF16)
        ksq = T("ksq", [128, 512], F32)
        kmax = T("kmax", [128, 2], F32)
        qan = T("qan", [128, 2, 3, 512], BF16)
        rope = T("rope", [64, 2, 512], F32)
        Qn = T("Qn", [128, 512], BF16)
        Qf = T("Qf", [128, 512], F32)
        Qr = T("Qr", [65, 512], BF16)
        qr1 = T("qr1", [64, 512], F32)
        qr2 = T("qr2", [64, 512], F32)
        Pt = T("Pt", [128, 3, 512], BF16)
        rs = T("rs", [128, 512], F32)
        od = T("od", [128, 2, 512], BF16)
        for c in range(3):
            cx.dma('sp', wst[:, c, :], W['w_qb'][jl][c * 128:(c + 1) * 128, :], w=['wst'])
        cx.op('dve', lambda e: e.tensor_copy(wqb[:], wst[:]), r=['wst'], w=['wqb'])
        for c in range(2):
            cx.dma('sp', wst[:, c, :], W['w_kvb'][jl][c * 128:(c + 1) * 128, :], w=['wst'])
        cx.op('dve', lambda e: e.tensor_copy(wkvb[:], wst[:, 0:2, :]), r=['wst'], w=['wkvb'])
        cx.op('dve', lambda e: e.memset(onesb[:], 1.0), w=['onesb'])
        kq = 0
        for s in seqs:
            past = s.past
            Tall = past + s.T
            chunked = (s.T % 64 == 0)
            ktiles = tiles_of(Tall, 512)
            for (k0, nk) in ktiles:
                cx.dma('sp', cst_[0:64, :nk], s.krT[:, k0:k0 + nk], r=['krT%d' % s.i], w=['cstf'])
                cx.op('act', lambda e: e.activation(out=KR[0:64, k0:k0 + nk], in_=cst_[0:64, :nk], func=AF.Copy), r=['cstf'], w=['KR'])
            cx.op('dve', lambda e: e.memset(KR[64:65, 0:Tall], 1.0), w=['KR'])
            for hd in range(4 if M2_SUB >= 2 else 0):
                cx.op('dve', lambda e: e.memset(kmax[:], 0.0), w=['kmax'])
                for (k0, nk) in ktiles:
                    cx.dma('sp', cst_[:, :nk], s.ckvT[0:128, k0:k0 + nk], r=['ckvT%d' % s.i], w=['cstf'])
                    cx.op('act', lambda e: e.activation(out=ckb[:, 0, :nk], in_=cst_[:, :nk], func=AF.Copy), r=['cstf'], w=['ckb'])
                    cx.dma('sp', cst_[:, :nk], s.ckvT[128:256, k0:k0 + nk], r=['ckvT%d' % s.i], w=['cstf'])
                    cx.op('act', lambda e: e.activation(out=ckb[:, 1, :nk], in_=cst_[:, :nk], func=AF.Copy), r=['cstf'], w=['ckb'])
                    for c in range(2):
                        cx.op('pe', lambda e: e.matmul(ps[4][:, :nk], wkvb[:, c, hd * 256:hd * 256 + 128], ckb[:, c, :nk],
                                                       start=(c == 0), stop=(c == 1)), r=['wkvb', 'ckb'], w=[PS[4]])
                    cx.op('act', lambda e: e.activation(out=KT[:, k0:k0 + nk], in_=ps[4][:, :nk], func=AF.Copy), r=[PS[4]], w=['KT'])
                    cx.op('act', lambda e: e.activation(out=ksq[:, :nk], in_=ps[4][:, :nk], func=AF.Square), r=[PS[4]], w=['ksq'])
                    cx.op('pe', lambda e: e.matmul(ps[5][:, :nk], ones_f[:], ksq[:, :nk], start=True, stop=False),
                          r=['ksq', 'ones_f'], w=[PS[5]])
                    cx.op('act', lambda e: e.activation(out=ksq[0:64, :nk], in_=KR[0:64, k0:k0 + nk], func=AF.Square), r=['KR', 'ksq'], w=['ksq'])
                    cx.op('pe', lambda e: e.matmul(ps[5][:, :nk], ones_f[0:64, :], ksq[0:64, :nk], start=False, stop=True),
                          r=['ksq', 'ones_f'], w=[PS[5]])
                    cx.op('dve', lambda e: e.tensor_reduce(kmax[:, 1:2], ps[5][:, :nk], AX.X, ALU.max), r=[PS[5]], w=['kmax'])
                    cx.op('dve', lambda e: e.tensor_tensor(kmax[:, 0:1], kmax[:, 0:1], kmax[:, 1:2], ALU.max), r=['kmax'], w=['kmax'])
                    for a0 in range(0, nk, 128):
                        an = min(128, nk - a0)
                        ti = (k0 + a0) // 128
                        for c in range(2):
                            cx.op('pe', lambda e: e.matmul(ps[6][:an, 0:128], ckb[:, c, a0:a0 + an],
                                                           wkvb[:, c, hd * 256 + 128:hd * 256 + 256], start=(c == 0), stop=(c == 1)),
                                  r=['ckb', 'wkvb'], w=[PS[6]])
                        cx.op('dve', lambda e: e.tensor_copy(VT[:an, ti, :], ps[6][:an, 0:128]), r=[PS[6]], w=['VT'])
                for (q0, nq) in (s.tiles if M2_SUB >= 3 else []):
                    sl = kq % 2
                    kq += 1
                    QA, OD, PTn = 'qan%d' % sl, 'od%d' % sl, None
                    cx.dma('sp', qan[:, sl, :, :nq], RT(s.qanT)[:, :, q0:q0 + nq], r=['qanT%d' % s.i], w=[QA])
                    cx.dma('sp', rope[:, 0, :nq], s.ropeC[:, q0:q0 + nq], w=['rope'])
                    cx.dma('sp', rope[:, 1, :nq], s.ropeS[:, q0:q0 + nq], w=['rope'])
                    for c in range(3):
                        cx.op('pe', lambda e: e.matmul(ps[4][:, :nq], wqb[:, c, hd * 256:hd * 256 + 128], qan[:, sl, c, :nq],
                                                       start=(c == 0), stop=(c == 2)), r=['wqb', QA], w=[PS[4]])
                    cx.op('act', lambda e: e.activation(out=Qf[:, :nq], in_=ps[4][:, :nq], func=AF.Copy, scale=MLA_SCALE), r=[PS[4]], w=['Qf'])
                    cx.op('dve', lambda e: e.tensor_copy(Qn[:, :nq], Qf[:, :nq]), r=['Qf'], w=['Qn'])
                    for c in range(3):
                        cx.op('pe', lambda e: e.matmul(ps[5][0:64, :nq], wqb[:, c, hd * 256 + 128:hd * 256 + 192], qan[:, sl, c, :nq],
                                                       start=(c == 0), stop=(c == 2)), r=['wqb', QA], w=[PS[5]])
                    for c in range(3):
                        cx.op('pe', lambda e: e.matmul(ps[6][0:64, :nq], wqb[:, c, hd * 256 + 192:hd * 256 + 256], qan[:, sl, c, :nq],
                                                       start=(c == 0), stop=(c == 2)), r=['wqb', QA], w=[PS[6]])
                    cx.op('dve', lambda e: e.tensor_tensor(qr1[:, :nq], ps[5][0:64, :nq], rope[:, 0, :nq], ALU.mult), r=[PS[5], 'rope'], w=['qr1'])
                    cx.op('dve', lambda e: e.tensor_tensor(qr2[:, :nq], ps[6][0:64, :nq], rope[:, 1, :nq], ALU.mult), r=[PS[6], 'rope'], w=['qr2'])
                    cx.op('dve', lambda e: e.scalar_tensor_tensor(qr1[:, :nq], qr1[:, :nq], 1.0, qr2[:, :nq], ALU.mult, ALU.add),
                          r=['qr1', 'qr2'], w=['qr1'])
                    cx.op('act', lambda e: e.activation(out=qr1[:, :nq], in_=qr1[:, :nq], func=AF.Copy, scale=MLA_SCALE), r=['qr1'], w=['qr1'])
                    cx.op('dve', lambda e: e.tensor_copy(Qr[0:64, :nq], qr1[:, :nq]), r=['qr1'], w=['Qr'])
                    if M2_SUB == 30:
                        continue
                    cx.op('act', lambda e: e.activation(out=Qf[:, :nq], in_=Qf[:, :nq], func=AF.Square), r=['Qf'], w=['Qf'])
                    cx.op('act', lambda e: e.activation(out=qr2[:, :nq], in_=qr1[:, :nq], func=AF.Square), r=['qr1'], w=['qr2'])
                    cx.op('pe', lambda e: e.matmul(ps[7][0:1, :nq], ones_f[:, 0:1], Qf[:, :nq], start=True, stop=False), r=['Qf', 'ones_f'], w=[PS[7]])
                    cx.op('pe', lambda e: e.matmul(ps[7][0:1, :nq], ones_f[0:64, 0:1], qr2[:, :nq], start=False, stop=True), r=['qr2', 'ones_f'], w=[PS[7]])
                    cx.op('dve', lambda e: e.tensor_scalar(rs[0:1, :nq], ps[7][0:1, :nq], kmax[0:1, 0:1], None, ALU.mult),
                          r=[PS[7], 'kmax'], w=['rs'])
                    cx.op('act', lambda e: e.activation(out=rs[0:1, :nq], in_=rs[0:1, :nq], func=AF.Sqrt), r=['rs'], w=['rs'])
                    cx.op('dve', lambda e: e.tensor_scalar(negm[0:1, :nq], rs[0:1, :nq], -1.0, None, ALU.mult), r=['rs'], w=['negm'])
                    cx.dma('sp', Qr[64:65, :nq], negm[0:1, :nq], r=['negm'], w=['Qr'])
                    if M2_SUB == 31:
                        continue
                    if chunked:
                        qb = q0 // 512
                        klist = [(kt, 0, False) for kt in range(4 * qb)] + [(4 * qb + j, 128 * j, True) for j in range((nq + 127) // 128)]
                    else:
                        klist = [(kt, 0, False) for kt in range((Tall + 127) // 128)]
                    if M2_SUB < 4 or M2_SUB in (32, 33, 34, 35):
                        klist = klist[:1]
                    nkl = len(klist)
                    def emit_scores(ki):
                        kt, qlo, diag = klist[ki]
                        kn = min(128, Tall - kt * 128)
                        pn = (0, 1, 4)[ki % 3]
                        ksl = slice(kt * 128, kt * 128 + kn)
                        cx.op('pe', lambda e: e.matmul(ps[pn][:kn, qlo:nq], KT[:, ksl], Qn[:, qlo:nq], start=True, stop=False),
                              r=['KT', 'Qn'], w=[PS[pn]])
                        cx.op('pe', lambda e: e.matmul(ps[pn][:kn, qlo:nq], KR[0:65, ksl], Qr[0:65, qlo:nq], start=False, stop=True),
                              r=['KR', 'Qr'], w=[PS[pn]])

                    emit_scores(0)
                    if nkl > 1:
                        emit_scores(1)
                    for ki, (kt, qlo, diag) in enumerate(klist):
                        kn = min(128, Tall - kt * 128)
                        pn = (0, 1, 4)[ki % 3]
                        pt = ki % 3
                        PTn = 'Pt%d' % pt
                        if ki + 2 < nkl:
                            emit_scores(ki + 2)
                        cx.op('act', lambda e: e.activation(out=Pt[:kn, pt, qlo:nq], in_=ps[pn][:kn, qlo:nq], func=AF.Exp),
                              r=[PS[pn]], w=[PTn])
                        first, last = (ki == 0), (ki == nkl - 1)
                        if not diag:
                            parts = [(0, kn, qlo, nq)]
                        else:
                            parts = [(0, 64, qlo, min(qlo + 64, nq))]
                            if qlo + 64 < nq:
                                parts.append((0, 128, qlo + 64, nq))
                        for pi, (r0, r1, ca, cb) in enumerate(parts):
                            lastp = last and (pi == len(parts) - 1)
                            cx.op('pe', lambda e: e.matmul(ps[2][:, ca:cb], VT[r0:r1, kt, :], Pt[r0:r1, pt, ca:cb],
                                                           start=first, stop=lastp), r=['VT', PTn], w=[PS[2]])
                            cx.op('pe', lambda e: e.matmul(ps[3][:, ca:cb], onesb[r0:r1, :], Pt[r0:r1, pt, ca:cb],
                                                           start=first, stop=lastp), r=['onesb', PTn], w=[PS[3]])
                    if M2_SUB in (32, 33, 34, 35):
                        continue
                    cx.op('dve', lambda e: e.reciprocal(rs[:, :nq], ps[3][:, :nq]), r=[PS[3]], w=['rs'])
                    cx.op('dve', lambda e: e.tensor_tensor(od[:, sl, :nq], ps[2][:, :nq], rs[:, :nq], ALU.mult), r=[PS[2], 'rs'], w=[OD])
                    cx.dma('pool', s.oT[512 + hd * 128:512 + (hd + 1) * 128, q0:q0 + nq], od[:, sl, :nq], r=[OD], w=['oT%d' % s.i])
        cx.barrier()


def _pc(v, nchunk):
    sh = v.shape[:-1]
    return np.ascontiguousarray(np.moveaxis(v.reshape(sh + (nchunk, 128)), -1, -2))


def make_consts():
    c = np.zeros((128, 13, 128), np.float32)
    c[:, 0, :] = np.eye(128, dtype=np.float32)
    s_ = np.arange(128)[:, None]
    t_ = np.arange(128)[None, :]
    c[:, 1, :] = ((s_ // 64 == t_ // 64) & (s_ <= t_)).astype(np.float32)
    c[:, 2, :] = (s_ <= t_).astype(np.float32)
    c[:, 3, :] = (s_ < t_).astype(np.float32)
    c[:, 4, :] = ((s_ // 64 == t_ // 64) & (s_ < t_)).astype(np.float32)
    for h in range(4):
        c[64 + h, 5 + h, :] = 1.0
        c[h, 9 + h, :] = 1.0
    return c


def rope_tables(past, T):
    half = 32
    freqs = np.exp(np.float32(-math.log(10000.0)) * np.arange(half, dtype=np.float32) / np.float32(half)).astype(np.float32)
    pos = (past + np.arange(T)).astype(np.float32)
    ang = (pos[:, None] * freqs[None, :]).astype(np.float32)
    cos = np.cos(ang).astype(np.float32).T
    sin = np.sin(ang).astype(np.float32).T
    return (np.ascontiguousarray(np.concatenate([cos, cos], 0)),
            np.ascontiguousarray(np.concatenate([-sin, sin], 0)))


def make_core_inputs(inp, xs, cs, sidx, depth):
    nab = (depth + 1) // 2
    im = {}
    nseq = len(xs)
    for i in range(nseq):
        im["xT%d" % i] = np.ascontiguousarray(xs[i].T)
        if sidx[i] is not None:
            b = sidx[i]
            im["ffnc_i%d" % i] = np.ascontiguousarray(
                inp['state_ffn_conv'][:depth, b].reshape(depth, 2, 44, 128).transpose(0, 3, 2, 1))
            im["hg_i%d" % i] = np.ascontiguousarray(inp['state_hgrn'][:nab, b].transpose(0, 2, 1, 3))
            im["lru_i%d" % i] = _pc(inp['state_rglru'][:nab, b], 4)
            im["lruc_i%d" % i] = np.ascontiguousarray(
                inp['state_rglru_conv'][:nab, b].reshape(nab, 3, 4, 128).transpose(0, 3, 2, 1))
    im["cT"] = np.ascontiguousarray(np.asarray(cs).reshape(nseq, 8, 128).transpose(2, 1, 0))
    im["ada_w"] = np.ascontiguousarray(inp['ada_w'][:depth])
    im["ada_bT"] = _pc(inp['ada_b'][:depth], 48)
    im["norm_gT"] = np.ascontiguousarray(inp['norm_g'][:depth].reshape(depth, 4, 8, 128).transpose(0, 3, 1, 2))
    wo = np.zeros((depth, D, D), np.float32)
    for l in range(depth):
        wo[l] = inp['ab_w_out'][l // 2] if l % 2 == 0 else inp['cd_w_out'][l // 2]
    im["w_out"] = wo
    im["ffn_up"] = np.ascontiguousarray(inp['ffn_w_up'][:depth])
    im["ffn_cw"] = np.ascontiguousarray(inp['ffn_conv_w'][:depth].reshape(depth, 3, 44, 128).transpose(0, 3, 2, 1))
    im["ffn_dn"] = np.ascontiguousarray(inp['ffn_w_down'][:depth])
    im["consts"] = make_consts()
    im["ab_w_in"] = np.ascontiguousarray(inp['ab_w_in'][:nab])
    im["hg_lb"] = np.ascontiguousarray(inp['hgrn_lb_logits'].reshape(2, 4, 128).transpose(2, 0, 1))
    im["hg_g"] = np.ascontiguousarray(inp['hgrn_norm_g'][:nab].reshape(nab, 128, 1))
    im["lru_cw"] = np.ascontiguousarray(inp['lru_conv_w'][:nab].reshape(nab, 4, 4, 128).transpose(0, 3, 2, 1))
    vec = np.stack([inp['lru_conv_b'][:nab], inp['lru_b_a'][:nab], inp['lru_b_x'][:nab], inp['lru_lambda'][:nab]], -1)
    im["lru_vec"] = np.ascontiguousarray(vec.reshape(nab, 4, 128, 4).transpose(0, 2, 1, 3))
    gw = np.zeros((nab, 128, 2, 4, 128), np.float32)
    for k, nm in enumerate(('lru_w_a', 'lru_w_x')):
        w = inp[nm][:nab]
        for ch in range(4):
            for bb in range(2):
                gw[:, bb * 64:(bb + 1) * 64, k, ch, bb * 64:(bb + 1) * 64] = w[:, 2 * ch + bb]
    im["lru_gw"] = gw
    ncd = depth // 2
    if ncd > 0:
        src = inp['cd_w_in'][:ncd]
        w = np.zeros((ncd, D, 3072), np.float32)
        w[:, :, 0:2048] = src[:, :, 0:2048]
        w[:, :, 2048:2432] = src[:, :, 2056:2440]
        w[:, :, 2432:2688] = src[:, :, 2440:2696]
        w[:, :, 2688:2752] = src[:, :, 2696:2760]
        w[:, :, 2752:2756] = src[:, :, 2048:2052]
        w[:, :, 2816:2820] = src[:, :, 2052:2056]
        w[:, :, 2944:2976] = src[:, :, 2728:2760]
        w[:, :, 2976:3008] = src[:, :, 2696:2728]
        im["cd_w_in"] = w
        im["gd_cw"] = np.ascontiguousarray(inp['gdn_conv_w'][:ncd].reshape(ncd, 4, 12, 128).transpose(0, 3, 2, 1))
        gv = np.stack([inp['gdn_a_log'][:ncd], inp['gdn_dt_bias'][:ncd]], -1)
        im["gd_vec"] = np.ascontiguousarray(np.broadcast_to(gv[:, None], (ncd, 128, 4, 2)))
        im["gd_g"] = np.ascontiguousarray(inp['gdn_norm_g'][:ncd].reshape(ncd, 128, 1))
        im["q_g"] = _pc(inp['mla_q_norm_g'][:ncd], 3)
        im["kv_g"] = _pc(inp['mla_kv_norm_g'][:ncd], 2)
        wq = inp['mla_w_qb'][:ncd].reshape(ncd, 384, 4, 192)
        wqe = np.zeros((ncd, 384, 4, 256), np.float32)
        wqe[..., 0:192] = wq
        wqe[..., 192:224] = wq[..., 160:192]
        wqe[..., 224:256] = wq[..., 128:160]
        im["w_qb"] = np.ascontiguousarray(wqe.reshape(ncd, 384, 1024))
        im["w_kvb"] = np.ascontiguousarray(inp['mla_w_kvb'][:ncd])
        for i in range(nseq):
            T_ = xs[i].shape[0]
            past = 0 if sidx[i] is None else PAST
            im["ropeC%d" % i], im["ropeS%d" % i] = rope_tables(past, T_)
            if sidx[i] is not None:
                b = sidx[i]
                im["gd_i%d" % i] = np.ascontiguousarray(inp['state_gdn'][:ncd, b].transpose(0, 2, 1, 3))
                im["gdc_i%d" % i] = np.ascontiguousarray(
                    inp['state_gdn_conv'][:ncd, b].reshape(ncd, 3, 12, 128).transpose(0, 3, 2, 1))
                im["lat_iT%d" % i] = np.ascontiguousarray(inp['cache_mla_latent'][:ncd, b].transpose(0, 2, 1))
                im["kr_iT%d" % i] = np.ascontiguousarray(inp['cache_mla_krope'][:ncd, b].transpose(0, 2, 1))
    return im


_PROG = {}


def kernel(**inputs):
    inp = {k: np.asarray(v) for k, v in inputs.items()}
    xp, xsm = inp['x_prompt'], inp['x_sample']
    B, SEQ, _ = xp.shape
    NB, DT, _ = xsm.shape
    NS = NB // NCORES
    depth = DEPTH
    nab, ncd = (depth + 1) // 2, depth // 2
    key = (SEQ, DT, NS, depth)
    if key not in _PROG:
        _PROG[key] = build_program(SEQ, DT, NS, depth=depth)
    nc = _PROG[key]
    in_maps = []
    shared = None
    for c in range(NCORES):
        b = c * B // NCORES
        sidx = [None] + [c * NS + j for j in range(NS)]
        xs = [xp[b]] + [xsm[i] for i in sidx[1:]]
        cs = np.stack([inp['c_prompt'][b]] + [inp['c_sample'][i] for i in sidx[1:]])
        im = make_core_inputs(inp, xs, cs, sidx, depth)
        if shared is None:
            shared = im
        else:
            for k in ('ada_w', 'w_out', 'ffn_up', 'ffn_dn', 'ab_w_in', 'consts', 'cd_w_in', 'w_qb', 'w_kvb', 'ropeC0', 'ropeS0'):
                im[k] = shared[k]
        in_maps.append(im)
    res = run_bass_kernel_spmd(nc, in_maps, core_ids=list(range(NCORES))).results
    pc = [b * NCORES // B for b in range(B)]

    def un_ffnc(a):
        return a.transpose(0, 3, 2, 1).reshape(depth, 2, 2 * DFF)

    def un_hg(a):
        return a.transpose(0, 2, 1, 3)

    def un_lru(a):
        return a.transpose(0, 2, 1).reshape(nab, HALF)

    def un_gdc(a):
        return a.transpose(0, 3, 2, 1).reshape(ncd, 3, 3 * HALF)

    def un_lruc(a):
        return a.transpose(0, 3, 2, 1).reshape(nab, 3, HALF)

    def gather(fn, name, prompt):
        if prompt:
            return np.ascontiguousarray(np.stack([fn(res[pc[b]][name + "0"]) for b in range(B)], axis=1))
        return np.ascontiguousarray(np.stack(
            [fn(res[i // NS][name + str(1 + i % NS)]) for i in range(NB)], axis=1))

    y_prompt = np.ascontiguousarray(np.stack([res[pc[b]]["yT0"].T for b in range(B)]))
    y_sample = np.ascontiguousarray(np.stack([res[i // NS]["yT%d" % (1 + i % NS)].T for i in range(NB)]))
    outs = [y_prompt, y_sample]
    for prompt in (True, False):
        nb = B if prompt else NB
        tt = SEQ if prompt else DT
        outs += [
            gather(un_hg, "hg_o", prompt), gather(un_lru, "lru_o", prompt), gather(un_lruc, "lruc_o", prompt),
            gather(un_hg, "gd_o", prompt), gather(un_gdc, "gdc_o", prompt),
            gather(lambda a: a, "lat_o", prompt), gather(lambda a: a, "kr_o", prompt),
            gather(un_ffnc, "ffnc_o", prompt),
        ]
    return tuple(outs)
```

```python
import math
import os
from contextlib import ExitStack
import numpy as np
import concourse.bass as bass
import concourse.mybir as mybir
from concourse.bass_utils import run_bass_kernel_spmd

F32 = mybir.dt.float32
BF16 = mybir.dt.bfloat16
AF = mybir.ActivationFunctionType
ALU = mybir.AluOpType
AX = mybir.AxisListType

D = 1024
DEPTH = 4
HALF = 512
DFF = 2816
EPS = 1e-6
NCORES = 8
PAST = 2048
SAME_ENGINE_SYNC = os.environ.get('SES', '1') == '1'


class Ctx:
    def __init__(self, nc, es):
        self.nc = nc
        self.es = es
        self.eng = {'pe': nc.tensor, 'dve': nc.vector, 'act': nc.scalar, 'pool': nc.gpsimd, 'sp': nc.sync}
        self.sem = {}
        self.cnt = {}
        for e in self.eng:
            self.sem[e] = es.enter_context(nc.semaphore("sem_" + e))
            self.cnt[e] = 0
        self.ndma = 20
        self.dslots = {}
        for q in ('sp', 'pool'):
            self.dslots[q] = []
            for i in range(self.ndma):
                k = "d_%s_%d" % (q, i)
                self.sem[k] = es.enter_context(nc.semaphore(k))
                self.cnt[k] = 0
                self.dslots[q].append(k)
        self.dnext = {'sp': 0, 'pool': 0}
        self.waited = {e: {} for e in self.eng}
        self.lastw = {}
        self.readers = {}
        self.nins = 0
        self._il = None
        self._noyield = 0

    def _wait(self, e, deps):
        best = {}
        for (k, v) in deps:
            if v > best.get(k, 0):
                best[k] = v
        for k, v in best.items():
            if k == e and (e == 'pe' or not SAME_ENGINE_SYNC):
                continue
            if self.waited[e].get(k, 0) >= v:
                continue
            self.eng[e].wait_ge(self.sem[k], v)
            self.waited[e][k] = v

    def _deps(self, r, w):
        deps = []
        for x in r:
            if x in self.lastw:
                deps.append(self.lastw[x])
        for x in w:
            if x in self.lastw:
                deps.append(self.lastw[x])
            rd = self.readers.get(x)
            if rd:
                deps.extend(rd.items())
        return deps

    def _record(self, tok, r, w):
        for x in r:
            rd = self.readers.setdefault(x, {})
            if tok[1] > rd.get(tok[0], 0):
                rd[tok[0]] = tok[1]
        for x in w:
            self.lastw[x] = tok
            self.readers[x] = {}

    def op(self, e, fn, r=(), w=()):
        self._wait(e, self._deps(r, w))
        ins = fn(self.eng[e])
        self.cnt[e] += 1
        ins.then_inc(self.sem[e], 1)
        self._record((e, self.cnt[e]), r, w)
        self.nins += 1
        self._yield()

    def dma(self, q, out, in_, r=(), w=(), **kw):
        k = self.dslots[q][self.dnext[q]]
        self.dnext[q] = (self.dnext[q] + 1) % self.ndma
        deps = self._deps(r, w)
        if self.cnt[k] > 0:
            deps.append((k, self.cnt[k]))
        self._wait(q, deps)
        ins = self.eng[q].dma_start(out=out, in_=in_, **kw)
        self.cnt[k] += 16
        ins.then_inc(self.sem[k], 16)
        self._record((k, self.cnt[k]), r, w)
        self.nins += 1
        self._yield()

    def _yield(self):
        il = self._il
        if il is None or self._noyield:
            return
        me = il['cur']
        n = len(il['sems'])
        nxt = None
        for d in range(1, n + 1):
            c = (me + d) % n
            if il['alive'][c]:
                nxt = c
                break
        if nxt is None or nxt == me:
            return
        il['cur'] = nxt
        il['sems'][nxt].release()
        il['sems'][me].acquire()
        if il['err']:
            raise RuntimeError("interleave aborted")

    def wait_until(self, cond):
        spins = 0
        while not cond():
            if self._il is None or sum(self._il['alive']) <= 1:
                raise RuntimeError("wait_until would deadlock")
            self._yield()
            spins += 1
            if spins > 100000000:
                raise RuntimeError("wait_until spin limit")

    def interleave(self, fns):
        if len(fns) == 1:
            fns[0]()
            return
        import threading
        n = len(fns)
        il = {'sems': [threading.Semaphore(0) for _ in range(n)], 'alive': [True] * n, 'cur': 0, 'err': []}
        done = threading.Semaphore(0)

        def runner(i):
            il['sems'][i].acquire()
            try:
                if not il['err']:
                    fns[i]()
            except BaseException as ex:
                il['err'].append(ex)
            il['alive'][i] = False
            nxt = None
            for d in range(1, n + 1):
                c = (i + d) % n
                if il['alive'][c]:
                    nxt = c
                    break
            if nxt is None:
                done.release()
            else:
                il['cur'] = nxt
                il['sems'][nxt].release()

        self._il = il
        ths = [threading.Thread(target=runner, args=(i,)) for i in range(n)]
        for t in ths:
            t.start()
        il['sems'][0].release()
        done.acquire()
        for t in ths:
            t.join()
        self._il = None
        if il['err']:
            raise il['err'][0]

    def barrier(self):
        allk = [(k, v) for k, v in self.cnt.items() if v > 0]
        for e in self.eng:
            self._wait(e, [kv for kv in allk if kv[0] != e or e in ('dve', 'act', 'pool')])
        self.lastw = {}
        self.readers = {}

    def final_wait(self):
        allk = [(k, v) for k, v in self.cnt.items() if v > 0 and k != 'sp']
        self._wait('sp', allk)


def tiles_of(T, n=512):
    out = []
    t = 0
    while t < T:
        out.append((t, min(n, T - t)))
        t += n
    return out


class Seq:
    pass


_uid = [0]


def SB(nc, name, shape, dt):
    _uid[0] += 1
    return nc.sbuf_tensor("%s_u%d" % (name, _uid[0]), shape, dt)


def build_program(SEQ, DT, NS, depth=DEPTH, dbg=None):
    nc = bass.Bass("TRN2", target_bir_lowering=False)
    es = ExitStack()
    with es:
        cx = Ctx(nc, es)
        _emit(nc, es, cx, SEQ, DT, NS, depth, dbg)
        cx.final_wait()
        print("instructions:", cx.nins)
    return nc


def _emit(nc, es, cx, SEQ, DT, NS, depth, dbg):
    NSEQ = 1 + NS
    seqs = []
    for i in range(NSEQ):
        s = Seq()
        s.i = i
        s.T = SEQ if i == 0 else DT
        s.tiles = tiles_of(s.T)
        s.xin = nc.dram_tensor("xT%d" % i, [D, s.T], F32, kind="ExternalInput").ap()
        s.yout = nc.dram_tensor("yT%d" % i, [D, s.T], F32, kind="ExternalOutput").ap()
        s.ffnc_out = nc.dram_tensor("ffnc_o%d" % i, [depth, 128, 44, 2], F32, kind="ExternalOutput").ap()
        s.xT = nc.dram_tensor("s_xT%d" % i, [D, s.T], F32).ap()
        s.h2T = nc.dram_tensor("s_h2T%d" % i, [D, s.T], BF16).ap()
        s.oT = nc.dram_tensor("s_oT%d" % i, [D, s.T], BF16).ap()
        s.aT = nc.dram_tensor("s_aT%d" % i, [DFF, s.T], BF16).ap()
        nab = (depth + 1) // 2
        s.hg_out = nc.dram_tensor("hg_o%d" % i, [nab, 128, 4, 128], F32, kind="ExternalOutput").ap()
        s.lru_out = nc.dram_tensor("lru_o%d" % i, [nab, 128, 4], F32, kind="ExternalOutput").ap()
        s.lruc_out = nc.dram_tensor("lruc_o%d" % i, [nab, 128, 4, 3], F32, kind="ExternalOutput").ap()
        ncd = depth // 2
        s.past = 0 if i == 0 else PAST
        if ncd > 0:
            s.gd_out = nc.dram_tensor("gd_o%d" % i, [ncd, 128, 4, 128], F32, kind="ExternalOutput").ap()
            s.gdc_out = nc.dram_tensor("gdc_o%d" % i, [ncd, 128, 12, 3], F32, kind="ExternalOutput").ap()
            s.lat_out = nc.dram_tensor("lat_o%d" % i, [ncd, s.T, 256], F32, kind="ExternalOutput").ap()
            s.kr_out = nc.dram_tensor("kr_o%d" % i, [ncd, s.T, 64], F32, kind="ExternalOutput").ap()
            s.ropeC = nc.dram_tensor("ropeC%d" % i, [64, s.T], F32, kind="ExternalInput").ap()
            s.ropeS = nc.dram_tensor("ropeS%d" % i, [64, s.T], F32, kind="ExternalInput").ap()
            s.qanT = nc.dram_tensor("s_qanT%d" % i, [384, s.T], BF16).ap()
            s.ckvT = nc.dram_tensor("s_ckvT%d" % i, [256, s.past + s.T], F32).ap()
            s.krT = nc.dram_tensor("s_krT%d" % i, [64, s.past + s.T], F32).ap()
            if i > 0:
                s.gd_in = nc.dram_tensor("gd_i%d" % i, [ncd, 128, 4, 128], F32, kind="ExternalInput").ap()
                s.gdc_in = nc.dram_tensor("gdc_i%d" % i, [ncd, 128, 12, 3], F32, kind="ExternalInput").ap()
                s.lat_inT = nc.dram_tensor("lat_iT%d" % i, [ncd, 256, PAST], F32, kind="ExternalInput").ap()
                s.kr_inT = nc.dram_tensor("kr_iT%d" % i, [ncd, 64, PAST], F32, kind="ExternalInput").ap()
        if i > 0:
            s.ffnc_in = nc.dram_tensor("ffnc_i%d" % i, [depth, 128, 44, 2], F32, kind="ExternalInput").ap()
            s.hg_in = nc.dram_tensor("hg_i%d" % i, [nab, 128, 4, 128], F32, kind="ExternalInput").ap()
            s.lru_in = nc.dram_tensor("lru_i%d" % i, [nab, 128, 4], F32, kind="ExternalInput").ap()
            s.lruc_in = nc.dram_tensor("lruc_i%d" % i, [nab, 128, 4, 3], F32, kind="ExternalInput").ap()
        seqs.append(s)
    cT = nc.dram_tensor("cT", [128, 8, NSEQ], F32, kind="ExternalInput").ap()
    ada_w = nc.dram_tensor("ada_w", [depth, D, 6 * D], F32, kind="ExternalInput").ap()
    ada_bT = nc.dram_tensor("ada_bT", [depth, 128, 48], F32, kind="ExternalInput").ap()
    norm_gT = nc.dram_tensor("norm_gT", [depth, 128, 4, 8], F32, kind="ExternalInput").ap()
    w_out = nc.dram_tensor("w_out", [depth, D, D], F32, kind="ExternalInput").ap()
    ffn_up = nc.dram_tensor("ffn_up", [depth, D, 2 * DFF], F32, kind="ExternalInput").ap()
    ffn_cw = nc.dram_tensor("ffn_cw", [depth, 128, 44, 3], F32, kind="ExternalInput").ap()
    ffn_dn = nc.dram_tensor("ffn_dn", [depth, DFF, D], F32, kind="ExternalInput").ap()

    nab = (depth + 1) // 2
    W = {}
    W['consts'] = nc.dram_tensor("consts", [128, 13, 128], F32, kind="ExternalInput").ap()
    ncd = depth // 2
    if ncd > 0:
        W['cd_w_in'] = nc.dram_tensor("cd_w_in", [ncd, D, 3072], F32, kind="ExternalInput").ap()
        W['gd_cw'] = nc.dram_tensor("gd_cw", [ncd, 128, 12, 4], F32, kind="ExternalInput").ap()
        W['gd_vec'] = nc.dram_tensor("gd_vec", [ncd, 128, 4, 2], F32, kind="ExternalInput").ap()
        W['gd_g'] = nc.dram_tensor("gd_g", [ncd, 128, 1], F32, kind="ExternalInput").ap()
        W['q_g'] = nc.dram_tensor("q_g", [ncd, 128, 3], F32, kind="ExternalInput").ap()
        W['kv_g'] = nc.dram_tensor("kv_g", [ncd, 128, 2], F32, kind="ExternalInput").ap()
        W['w_qb'] = nc.dram_tensor("w_qb", [ncd, 384, 1024], F32, kind="ExternalInput").ap()
        W['w_kvb'] = nc.dram_tensor("w_kvb", [ncd, 256, 1024], F32, kind="ExternalInput").ap()
    W['ab_w_in'] = nc.dram_tensor("ab_w_in", [nab, D, 3072], F32, kind="ExternalInput").ap()
    W['hg_lb'] = nc.dram_tensor("hg_lb", [128, 2, 4], F32, kind="ExternalInput").ap()
    W['hg_g'] = nc.dram_tensor("hg_g", [nab, 128, 1], F32, kind="ExternalInput").ap()
    W['lru_cw'] = nc.dram_tensor("lru_cw", [nab, 128, 4, 4], F32, kind="ExternalInput").ap()
    W['lru_vec'] = nc.dram_tensor("lru_vec", [nab, 128, 4, 4], F32, kind="ExternalInput").ap()
    W['lru_gw'] = nc.dram_tensor("lru_gw", [nab, 128, 2, 4, 128], F32, kind="ExternalInput").ap()
    W['ones_b'] = None
    ps = [es.enter_context(nc.psum_tensor("ps%d" % i, [128, 512], F32)) for i in range(8)]
    PS = ["ps%d" % i for i in range(8)]
    ones_f = es.enter_context(nc.sbuf_tensor("ones_f", [128, 128], F32))
    ones_b = es.enter_context(nc.sbuf_tensor("ones_b", [128, 128], BF16))
    mod = es.enter_context(nc.sbuf_tensor("mod", [128, 48, NSEQ], F32))
    ng = es.enter_context(nc.sbuf_tensor("ng", [128, 4, 8], F32))
    gsc = es.enter_context(nc.sbuf_tensor("gsc", [128, 2, 8, NSEQ], F32))
    csil = es.enter_context(nc.sbuf_tensor("csil", [128, 8, NSEQ], F32))
    cx.op('dve', lambda e: e.memset(ones_f[:], 1.0), w=['ones_f'])
    cx.op('dve', lambda e: e.memset(ones_b[:], 1.0), w=['ones_b'])
    W['ones_b'] = ones_b
    cx.dma('sp', csil[:], cT[:, :, :], w=['csil'])
    cx.op('act', lambda e: e.activation(out=csil[:], in_=csil[:], func=AF.Silu), r=['csil'], w=['csil'])

    with SB(nc, "cp", [128, 2, 8, 512], F32) as cp:
        k = 0
        for s in seqs:
            for (t0, n) in s.tiles:
                sl = k % 2
                k += 1
                cx.dma('sp', cp[:, sl, :, :n], s.xin.rearrange("(c p) t -> p c t", p=128)[:, :, t0:t0 + n],
                       w=['cp%d' % sl])
                cx.dma('pool', s.xT.rearrange("(c p) t -> p c t", p=128)[:, :, t0:t0 + n], cp[:, sl, :, :n],
                       r=['cp%d' % sl], w=['xT%d' % s.i])
        cx.barrier()

    def rms_stats(src, n, sq, rstd, psn, srcres, nfeat_chunks=8):
        for c in range(nfeat_chunks):
            cx.op('act', lambda e, c=c: e.activation(out=sq[:, c, :n], in_=src(c), func=AF.Square),
                  r=srcres, w=['sq'])
        for c in range(nfeat_chunks):
            cx.op('pe', lambda e, c=c: e.matmul(ps[psn][:, :n], ones_b[:], sq[:, c, :n],
                                                 start=(c == 0), stop=(c == nfeat_chunks - 1)),
                  r=['sq', 'ones_b'], w=[PS[psn]])
        cx.op('dve', lambda e: e.tensor_scalar(rstd[:, :n], ps[psn][:, :n], 1.0 / (128 * nfeat_chunks), EPS,
                                               ALU.mult, ALU.add), r=[PS[psn]], w=['rstd'])
        cx.op('dve', lambda e: e.reciprocal(rstd[:, :n], rstd[:, :n]), r=['rstd'], w=['rstd'])
        cx.op('act', lambda e: e.activation(out=rstd[:, :n], in_=rstd[:, :n], func=AF.Sqrt), r=['rstd'], w=['rstd'])

    for l in range(depth):
        with SB(nc, "aw", [128, 2, 8, 768], F32) as aw, SB(nc, "adb", [128, 48], F32) as adb:
            cx.dma('sp', adb[:], ada_bT[l], w=['adb'])
            cx.dma('sp', ng[:], norm_gT[l], w=['ng'])
            for g in range(8):
                sl = g % 2
                cx.dma('sp', aw[:, sl], ada_w[l].rearrange("(c p) f -> p c f", p=128)[:, :, g * 768:(g + 1) * 768],
                       w=['aw%d' % sl])
                for j in range(6):
                    fc = g * 6 + j
                    pn = fc % 2
                    for c in range(8):
                        cx.op('pe', lambda e, c=c, j=j, sl=sl, pn=pn: e.matmul(
                            ps[pn][:, :NSEQ], aw[:, sl, c, j * 128:(j + 1) * 128], csil[:, c, :],
                            start=(c == 0), stop=(c == 7)), r=['aw%d' % sl, 'csil'], w=[PS[pn]])
                    cx.op('dve', lambda e, fc=fc, pn=pn: e.tensor_scalar(
                        mod[:, fc, :], ps[pn][:, :NSEQ], adb[:, fc:fc + 1], None, ALU.add),
                        r=[PS[pn], 'adb'], w=['mod'])
            for k2, (gi, sc0) in enumerate(((0, 8), (2, 32))):
                for c in range(8):
                    cx.op('dve', lambda e, k2=k2, gi=gi, sc0=sc0, c=c: e.tensor_scalar(
                        gsc[:, k2, c, :], mod[:, sc0 + c, :], 1.0, ng[:, gi, c:c + 1], ALU.add, ALU.mult),
                        r=['mod', 'ng'], w=['gsc'])
            cx.barrier()

        if l % 2 == 0:
            emit_mixer(nc, cx, ps, PS, l, seqs, mod, gsc, ng, ones_f, rms_stats, W)
        else:
            emit_cd(nc, cx, ps, PS, l, seqs, mod, gsc, ng, ones_f, rms_stats, W)

        with (SB(nc, "wst", [128, 2, 8, 512], F32) as wst,
              SB(nc, "wo", [128, 8, D], BF16) as wo,
              SB(nc, "ot", [128, 2, 8, 512], BF16) as ot,
              SB(nc, "xt", [128, 2, 8, 512], F32) as xt,
              SB(nc, "yt", [128, 8, 512], F32) as yt,
              SB(nc, "sq", [128, 8, 512], BF16) as sq,
              SB(nc, "rstd", [128, 512], F32) as rstd,
              SB(nc, "tmp", [128, 512], F32) as tmp,
              SB(nc, "tmpb", [128, 2, 512], F32) as tmpb,
              SB(nc, "h2", [128, 2, 8, 512], BF16) as h2):
            for g in range(2):
                cx.dma('sp', wst[:, g], w_out[l].rearrange("(c p) f -> p c f", p=128)[:, :, g * 512:(g + 1) * 512],
                       w=['wst%d' % g])
                cx.op('act', lambda e, g=g: e.activation(out=wo[:, :, g * 512:(g + 1) * 512], in_=wst[:, g],
                                                         func=AF.Copy), r=['wst%d' % g], w=['wo'])
            k = 0
            for s in seqs:
                for (t0, n) in s.tiles:
                    sl = k % 2
                    k += 1
                    XT, OT, H2 = 'xt%d' % sl, 'ot%d' % sl, 'h2%d' % sl
                    cx.dma('sp', ot[:, sl, :, :n], s.oT.rearrange("(c p) t -> p c t", p=128)[:, :, t0:t0 + n],
                           r=['oT%d' % s.i], w=[OT])
                    cx.dma('sp', xt[:, sl, :, :n], s.xT.rearrange("(c p) t -> p c t", p=128)[:, :, t0:t0 + n],
                           r=['xT%d' % s.i], w=[XT])
                    for m in range(8):
                        pn = m % 4
                        for c in range(8):
                            cx.op('pe', lambda e, m=m, c=c, pn=pn: e.matmul(
                                ps[pn][:, :n], wo[:, c, m * 128:(m + 1) * 128], ot[:, sl, c, :n],
                                start=(c == 0), stop=(c == 7)), r=['wo', OT], w=[PS[pn]])
                        cx.op('act', lambda e, m=m, pn=pn: e.activation(out=yt[:, m, :n], in_=ps[pn][:, :n],
                                                                        func=AF.Copy), r=[PS[pn]], w=['yt'])
                    rms_stats(lambda c: yt[:, c, :n], n, sq, rstd, 4, ['yt'])
                    for c in range(8):
                        cx.op('dve', lambda e, c=c: e.tensor_tensor(tmp[:, :n], yt[:, c, :n], rstd[:, :n], ALU.mult),
                              r=['yt', 'rstd'], w=['tmp'])
                        cx.op('dve', lambda e, c=c: e.tensor_scalar(
                            tmp[:, :n], tmp[:, :n], ng[:, 1, c:c + 1], mod[:, 16 + c, s.i:s.i + 1],
                            ALU.mult, ALU.mult), r=['tmp', 'ng', 'mod'], w=['tmp'])
                        cx.op('dve', lambda e, c=c: e.tensor_tensor(xt[:, sl, c, :n], xt[:, sl, c, :n], tmp[:, :n],
                                                                    ALU.add), r=['tmp', XT], w=[XT])
                    cx.dma('pool', s.xT.rearrange("(c p) t -> p c t", p=128)[:, :, t0:t0 + n], xt[:, sl, :, :n],
                           r=[XT], w=['xT%d' % s.i])
                    rms_stats(lambda c: xt[:, sl, c, :n], n, sq, rstd, 5, [XT])
                    for c in range(8):
                        cx.op('dve', lambda e, c=c: e.tensor_tensor(tmpb[:, c % 2, :n], xt[:, sl, c, :n], rstd[:, :n], ALU.mult),
                              r=[XT, 'rstd'], w=['tmpb%d' % (c % 2)])
                        cx.op('act', lambda e, c=c: e.activation(
                            out=h2[:, sl, c, :n], in_=tmpb[:, c % 2, :n], func=AF.Identity,
                            scale=gsc[:, 1, c, s.i:s.i + 1], bias=mod[:, 24 + c, s.i:s.i + 1]),
                            r=['tmpb%d' % (c % 2), 'gsc', 'mod'], w=[H2])
                    cx.dma('pool', s.h2T.rearrange("(c p) t -> p c t", p=128)[:, :, t0:t0 + n], h2[:, sl, :, :n],
                           r=[H2], w=['h2T%d' % s.i])
            cx.barrier()

        with (SB(nc, "wst", [128, 2, 2816], F32) as wst,
              SB(nc, "wu", [128, 8, 2 * DFF], BF16) as wu,
              SB(nc, "cw", [128, 44, 3], F32) as cw,
              SB(nc, "h2", [128, 2, 8, 512], BF16) as h2,
              SB(nc, "halo", [128, 44, 2], F32) as halo,
              SB(nc, "u", [128, 4, 514], F32) as u,
              SB(nc, "cv", [128, 4, 512], F32) as cv,
              SB(nc, "at", [128, 2, 22, 512], BF16) as at):
            cx.dma('sp', cw[:], ffn_cw[l], w=['cw'])
            k = 0
            for c in range(8):
                for g in range(2):
                    sl = k % 2
                    k += 1
                    cx.dma('sp', wst[:, sl], ffn_up[l][c * 128:(c + 1) * 128, g * 2816:(g + 1) * 2816],
                           w=['wst%d' % sl])
                    cx.op('act' if g else 'dve', (lambda e, c=c, g=g, sl=sl: e.activation(
                        out=wu[:, c, g * 2816:(g + 1) * 2816], in_=wst[:, sl], func=AF.Copy)) if g else
                        (lambda e, c=c, g=g, sl=sl: e.tensor_copy(wu[:, c, g * 2816:(g + 1) * 2816], wst[:, sl])),
                        r=['wst%d' % sl], w=['wu'])
            k = 0
            for s in seqs:
                if s.i == 0:
                    cx.op('dve', lambda e: e.memset(halo[:], 0.0), w=['halo'])
                else:
                    cx.dma('sp', halo[:], s.ffnc_in[l], w=['halo'])
                for (t0, n) in s.tiles:
                    sl = k % 2
                    k += 1
                    H2, AT = 'h2%d' % sl, 'at%d' % sl
                    cx.dma('sp', h2[:, sl, :, :n], s.h2T.rearrange("(c p) t -> p c t", p=128)[:, :, t0:t0 + n],
                           r=['h2T%d' % s.i], w=[H2])
                    for j in range(22):
                        for gv in range(2):
                            ch = gv * 22 + j
                            ui = (j % 2) * 2 + gv
                            pn = ui
                            U, CV = 'u%d' % ui, 'cv%d' % ui
                            for c in range(8):
                                cx.op('pe', lambda e, c=c, ch=ch, pn=pn: e.matmul(
                                    ps[pn][:, :n], wu[:, c, ch * 128:(ch + 1) * 128], h2[:, sl, c, :n],
                                    start=(c == 0), stop=(c == 7)), r=['wu', H2], w=[PS[pn]])
                            cx.op('act', lambda e, ui=ui, ch=ch: e.activation(out=u[:, ui, 0:2], in_=halo[:, ch, :],
                                                                            func=AF.Copy), r=['halo'], w=[U])
                            cx.op('act', lambda e, ui=ui, pn=pn: e.activation(out=u[:, ui, 2:2 + n], in_=ps[pn][:, :n],
                                                                            func=AF.Copy), r=[PS[pn]], w=[U])
                            cx.op('act', lambda e, ui=ui, ch=ch: e.activation(out=halo[:, ch, :], in_=u[:, ui, n:n + 2],
                                                                            func=AF.Copy), r=[U], w=['halo'])
                            cx.op('pool' if E_POOL else 'dve', lambda e, ui=ui, ch=ch: e.tensor_scalar(
                                cv[:, ui, :n], u[:, ui, 0:n], cw[:, ch, 0:1], None, ALU.mult),
                                r=[U, 'cw'], w=[CV])
                            cx.op('dve', lambda e, ui=ui, ch=ch: e.scalar_tensor_tensor(
                                cv[:, ui, :n], u[:, ui, 1:1 + n], cw[:, ch, 1:2], cv[:, ui, :n], ALU.mult, ALU.add),
                                r=[U, 'cw', CV], w=[CV])
                            cx.op('dve', lambda e, ui=ui, ch=ch: e.scalar_tensor_tensor(
                                cv[:, ui, :n], u[:, ui, 2:2 + n], cw[:, ch, 2:3], cv[:, ui, :n], ALU.mult, ALU.add),
                                r=[U, 'cw', CV], w=[CV])
                        gi = (j % 2) * 2
                        cx.op('act', lambda e, gi=gi: e.activation(out=cv[:, gi, :n], in_=cv[:, gi, :n], func=AF.Silu),
                              r=['cv%d' % gi], w=['cv%d' % gi])
                        cx.op('dve', lambda e, gi=gi, j=j: e.tensor_tensor(at[:, sl, j, :n], cv[:, gi, :n],
                                                                         cv[:, gi + 1, :n], ALU.mult),
                              r=['cv%d' % gi, 'cv%d' % (gi + 1)], w=[AT])
                    cx.dma('pool', s.aT.rearrange("(c p) t -> p c t", p=128)[:, :, t0:t0 + n], at[:, sl, :, :n],
                           r=[AT], w=['aT%d' % s.i])
                cx.dma('pool', s.ffnc_out[l], halo[:], r=['halo'], w=['ffnc_out'])
            cx.barrier()

        with (SB(nc, "wst", [128, 2, 11, 512], F32) as wst,
              SB(nc, "wd", [128, 22, D], BF16) as wd,
              SB(nc, "at", [128, 2, 22, 512], BF16) as at,
              SB(nc, "xt", [128, 2, 8, 512], F32) as xt,
              SB(nc, "yt", [128, 8, 512], F32) as yt,
              SB(nc, "sq", [128, 8, 512], BF16) as sq,
              SB(nc, "rstd", [128, 512], F32) as rstd,
              SB(nc, "tmp", [128, 512], F32) as tmp):
            k = 0
            for g in range(2):
                for hh in range(2):
                    sl = k % 2
                    k += 1
                    cx.dma('sp', wst[:, sl], ffn_dn[l].rearrange("(c p) f -> p c f", p=128)[
                        :, hh * 11:(hh + 1) * 11, g * 512:(g + 1) * 512], w=['wst%d' % sl])
                    cx.op('act', lambda e, g=g, hh=hh, sl=sl: e.activation(
                        out=wd[:, hh * 11:(hh + 1) * 11, g * 512:(g + 1) * 512], in_=wst[:, sl], func=AF.Copy),
                        r=['wst%d' % sl], w=['wd'])
            k = 0
            for s in seqs:
                for (t0, n) in s.tiles:
                    sl = k % 2
                    k += 1
                    XT, AT = 'xt%d' % sl, 'at%d' % sl
                    cx.dma('sp', at[:, sl, :, :n], s.aT.rearrange("(c p) t -> p c t", p=128)[:, :, t0:t0 + n],
                           r=['aT%d' % s.i], w=[AT])
                    cx.dma('sp', xt[:, sl, :, :n], s.xT.rearrange("(c p) t -> p c t", p=128)[:, :, t0:t0 + n],
                           r=['xT%d' % s.i], w=[XT])
                    for m in range(8):
                        pn = m % 4
                        for c in range(22):
                            cx.op('pe', lambda e, m=m, c=c, pn=pn: e.matmul(
                                ps[pn][:, :n], wd[:, c, m * 128:(m + 1) * 128], at[:, sl, c, :n],
                                start=(c == 0), stop=(c == 21)), r=['wd', AT], w=[PS[pn]])
                        cx.op('act', lambda e, m=m, pn=pn: e.activation(out=yt[:, m, :n], in_=ps[pn][:, :n],
                                                                        func=AF.Copy), r=[PS[pn]], w=['yt'])
                    rms_stats(lambda c: yt[:, c, :n], n, sq, rstd, 4, ['yt'])
                    for c in range(8):
                        cx.op('dve', lambda e, c=c: e.tensor_tensor(tmp[:, :n], yt[:, c, :n], rstd[:, :n], ALU.mult),
                              r=['yt', 'rstd'], w=['tmp'])
                        cx.op('dve', lambda e, c=c: e.tensor_scalar(
                            tmp[:, :n], tmp[:, :n], ng[:, 3, c:c + 1], mod[:, 40 + c, s.i:s.i + 1],
                            ALU.mult, ALU.mult), r=['tmp', 'ng', 'mod'], w=['tmp'])
                        cx.op('dve', lambda e, c=c: e.tensor_tensor(xt[:, sl, c, :n], xt[:, sl, c, :n], tmp[:, :n],
                                                                    ALU.add), r=['tmp', XT], w=[XT])
                    dst = s.yout if l == depth - 1 else s.xT
                    cx.dma('pool', dst.rearrange("(c p) t -> p c t", p=128)[:, :, t0:t0 + n], xt[:, sl, :, :n],
                           r=[XT], w=['xT%d' % s.i])
            cx.barrier()


def emit_mixer(nc, cx, ps, PS, l, seqs, mod, gsc, ng, ones_f, rms_stats, W):
    ab = (l % 2 == 0)
    jl = l // 2
    with ExitStack() as st:
        def T(name, shape, dt):
            return st.enter_context(SB(nc, name, shape, dt))
        xt = T("xt", [128, 2, 8, 512], F32)
        sq = T("sq", [128, 8, 512], BF16)
        rstd = T("rstd", [128, 512], F32)
        tmp = T("tmp", [128, 512], F32)
        tmpb = T("tmpb", [128, 2, 512], F32)
        h = T("h", [128, 8, 512], BF16)
        ot = T("otile", [128, 2, 8, 512], BF16)
        if ab:
            wst = T("wst", [128, 2, 1536], F32)
            win = T("win", [128, 8, 3072], BF16)
            P4 = T("P4", [128, 4, 512], F32)
            sig = T("sig", [128, 512], F32)
            kk = T("kk", [128, 512], F32)
            qq = T("qq", [128, 512], F32)
            lf = T("lf", [128, 512], F32)
            bb = T("bb", [128, 513], F32)
            dd = T("dd", [128, 512], F32)
            ee = T("ee", [128, 512], F32)
            qt = T("qt", [128, 512], BF16)
            kt = T("kt", [128, 512], BF16)
            qe = T("qe", [128, 512], BF16)
            kl = T("kl", [128, 512], BF16)
            vb = T("vb", [128, 512], BF16)
            edl = T("edl", [128, 8], F32)
            ob = T("ob", [128, 512], F32)
            attm = T("attm", [128, 128], BF16)
            attf = T("attf", [128, 128], F32)
            vtok = T("vtok", [128, 128], BF16)
            kltok = T("kltok", [128, 128], BF16)
            S = T("S", [128, 4, 128], F32)
            Sb = T("Sb", [128, 4, 128], BF16)
            onesr = T("onesr", [128, 512], F32)
            cst = T("cst", [128, 3, 128], F32)
            identb = T("identb", [128, 128], BF16)
            lbl = T("lbl", [128, 2, 4], F32)
            lbv = T("lbv", [128, 4], F32)
            oml = T("oml", [128, 4], F32)
            noml = T("noml", [128, 4], F32)
            hgg = T("hgg", [128, 1], F32)
            lxb = T("lxb", [128, 515], F32)
            lhalo = T("lhalo", [128, 4, 3], F32)
            lcw = T("lcw", [128, 4, 4], F32)
            lvec = T("lvec", [128, 4, 4], F32)
            m8sp = T("m8sp", [128, 4], F32)
            gst = T("gst", [128, 2, 4, 128], F32)
            gw = T("gw", [128, 2, 4, 128], BF16)
            hst = T("hst", [128, 4], F32)
            xc = T("xc", [128, 512], F32)
            xcb = T("xcb", [128, 512], BF16)
            rr = sig
            ii = kk
            aa = qq
            uu = lf
            hh = dd
            ly = ee
            gl = ob
            for c2 in range(16):
                c, hf = c2 // 2, c2 % 2
                sl = hf
                cx.dma('sp', wst[:, sl], W['ab_w_in'][jl][c * 128:(c + 1) * 128, hf * 1536:(hf + 1) * 1536],
                       w=['wst%d' % sl])
                cx.op('act' if sl else 'dve',
                      (lambda e: e.activation(out=win[:, c, hf * 1536:(hf + 1) * 1536], in_=wst[:, sl], func=AF.Copy))
                      if sl else (lambda e: e.tensor_copy(win[:, c, hf * 1536:(hf + 1) * 1536], wst[:, sl])),
                      r=['wst%d' % sl], w=['win'])
            cx.dma('sp', cst[:], W['consts'][:, 0:3, :], w=['cst'])
            cx.op('dve', lambda e: e.tensor_copy(identb[:], cst[:, 0, :]), r=['cst'], w=['identb'])
            cx.op('dve', lambda e: e.memset(onesr[:], 1.0), w=['onesr'])
            cx.dma('sp', lbl[:], W['hg_lb'][:, :, :], w=['lbl'])
            cx.dma('sp', hgg[:], W['hg_g'][jl], w=['hgg'])
            if jl == 0:
                cx.op('dve', lambda e: e.memset(lbv[:], 0.0), w=['lbv'])
            else:
                cx.op('dve', lambda e: e.tensor_tensor(lbv[:], lbl[:, 1, :], lbl[:, 0, :], ALU.subtract),
                      r=['lbl'], w=['lbv'])
                cx.op('act', lambda e: e.activation(out=lbv[:], in_=lbv[:], func=AF.Sigmoid), r=['lbv'], w=['lbv'])
            cx.op('dve', lambda e: e.tensor_scalar(oml[:], lbv[:], -1.0, 1.0, ALU.mult, ALU.add), r=['lbv'], w=['oml'])
            cx.op('dve', lambda e: e.tensor_scalar(noml[:], oml[:], -1.0, None, ALU.mult), r=['oml'], w=['noml'])
            cx.dma('sp', lcw[:], W['lru_cw'][jl], w=['lcw'])
            cx.dma('sp', lvec[:], W['lru_vec'][jl], w=['lvec'])
            cx.dma('sp', gst[:], W['lru_gw'][jl], w=['gst'])
            cx.op('dve', lambda e: e.tensor_copy(gw[:], gst[:]), r=['gst'], w=['gw'])
            cx.op('act', lambda e: e.activation(out=m8sp[:], in_=lvec[:, :, 3], func=AF.Exp, scale=-1.0),
                  r=['lvec'], w=['m8sp'])
            cx.op('dve', lambda e: e.tensor_scalar(m8sp[:], m8sp[:], 1.0, None, ALU.add), r=['m8sp'], w=['m8sp'])
            cx.op('act', lambda e: e.activation(out=m8sp[:], in_=m8sp[:], func=AF.Ln), r=['m8sp'], w=['m8sp'])
            cx.op('dve', lambda e: e.tensor_scalar(m8sp[:], m8sp[:], -8.0, None, ALU.mult), r=['m8sp'], w=['m8sp'])
        k = 0
        for s in seqs:
            L = 64 if s.T % 64 == 0 else s.T
            if ab:
                if s.i == 0:
                    cx.op('dve', lambda e: e.memset(S[:], 0.0), w=['S'])
                    cx.op('dve', lambda e: e.memset(hst[:], 0.0), w=['hst'])
                    cx.op('dve', lambda e: e.memset(lhalo[:], 0.0), w=['lhalo'])
                else:
                    cx.dma('sp', S[:], s.hg_in[jl], w=['S'])
                    cx.dma('sp', hst[:], s.lru_in[jl], w=['hst'])
                    cx.dma('sp', lhalo[:], s.lruc_in[jl], w=['lhalo'])
                cx.op('act', lambda e: e.activation(out=Sb[:], in_=S[:], func=AF.Copy), r=['S'], w=['Sb'])
            for (t0, n) in s.tiles:
                sl = k % 2
                k += 1
                XT, OT = 'xt%d' % sl, 'ot%d' % sl
                cx.dma('sp', xt[:, sl, :, :n], s.xT.rearrange("(c p) t -> p c t", p=128)[:, :, t0:t0 + n],
                       r=['xT%d' % s.i], w=[XT])
                rms_stats(lambda c: xt[:, sl, c, :n], n, sq, rstd, 6, [XT])
                for c in range(8):
                    cx.op('dve', lambda e, c=c: e.tensor_tensor(tmpb[:, c % 2, :n], xt[:, sl, c, :n], rstd[:, :n], ALU.mult),
                          r=[XT, 'rstd'], w=['tmpb%d' % (c % 2)])
                    cx.op('act', lambda e, c=c: e.activation(
                        out=(h[:, c, :n] if ab else ot[:, sl, c, :n]), in_=tmpb[:, c % 2, :n], func=AF.Identity,
                        scale=gsc[:, 0, c, s.i:s.i + 1], bias=mod[:, 0 + c, s.i:s.i + 1]),
                        r=['tmpb%d' % (c % 2), 'gsc', 'mod'], w=(['h'] if ab else [OT]))
                if ab:
                    def proj(ch, pn):
                        for c in range(8):
                            cx.op('pe', lambda e, c=c: e.matmul(ps[pn][:, :n], win[:, c, ch * 128:(ch + 1) * 128],
                                                                 h[:, c, :n], start=(c == 0), stop=(c == 7)),
                                  r=['win', 'h'], w=[PS[pn]])
                    nch = n // L
                    G = min(128, n)
                    mi = 1 if L == 64 else 2
                    for hd in range(0 if 'H' in AB_SKIP else 4):
                        for j4 in range(4):
                            pn = j4 % 2
                            proj(j4 * 4 + hd, pn)
                            cx.op('act', lambda e, j4=j4, pn=pn: e.activation(out=P4[:, j4, :n], in_=ps[pn][:, :n],
                                                                              func=AF.Copy), r=[PS[pn]], w=['P4'])
                        cx.op('act', lambda e: e.activation(out=sig[:, :n], in_=P4[:, 1, :n], func=AF.Sigmoid),
                              r=['P4'], w=['sig'])
                        cx.op('dve', lambda e: e.tensor_scalar(lf[:, :n], sig[:, :n], oml[:, hd:hd + 1], lbv[:, hd:hd + 1],
                                                               ALU.mult, ALU.add), r=['sig', 'oml', 'lbv'], w=['lf'])
                        cx.op('act', lambda e: e.activation(out=lf[:, :n], in_=lf[:, :n], func=AF.Ln), r=['lf'], w=['lf'])
                        cx.op('dve', lambda e: e.tensor_scalar(kk[:, :n], sig[:, :n], noml[:, hd:hd + 1], oml[:, hd:hd + 1],
                                                               ALU.mult, ALU.add), r=['sig', 'oml', 'noml'], w=['kk'])
                        cx.op('act', lambda e: e.activation(out=qq[:, :n], in_=P4[:, 0, :n], func=AF.Silu),
                              r=['P4'], w=['qq'])
                        cx.op('act', lambda e: e.activation(out=vb[:, :n], in_=P4[:, 2, :n], func=AF.Copy),
                              r=['P4'], w=['vb'])
                        cx.op('dve', lambda e: e.memset(bb[:, 0:1], 0.0), w=['bb'])
                        cx.op('dve', lambda e: e.tensor_tensor_scan(bb[:, 1:1 + n], onesr[:, :n], lf[:, :n], 0.0,
                                                                    ALU.mult, ALU.add), r=['onesr', 'lf'], w=['bb'])
                        b3 = bb[:, 1:1 + n].rearrange("p (c l) -> p c l", l=L)
                        mid3 = b3[:, :, L // 2:L // 2 + 1].to_broadcast([128, nch, L])
                        last3 = b3[:, :, L - 1:L].to_broadcast([128, nch, L])
                        prev3 = bb[:, 0:n].rearrange("p (c l) -> p c l", l=L)[:, :, 0:1].to_broadcast([128, nch, L])
                        d3 = dd[:, :n].rearrange("p (c l) -> p c l", l=L)

                        def expmul(ref3, scale, src, dst, DST):
                            cx.op('dve', lambda e: e.tensor_tensor(d3, b3, ref3, ALU.subtract), r=['bb'], w=['dd'])
                            cx.op('act', lambda e: e.activation(out=ee[:, :n], in_=dd[:, :n], func=AF.Exp, scale=scale),
                                  r=['dd'], w=['ee'])
                            cx.op('dve', lambda e: e.tensor_tensor(dst[:, :n], src[:, :n], ee[:, :n], ALU.mult),
                                  r=['ee', 'qq', 'kk'], w=[DST])
                        expmul(mid3, 1.0, qq, qt, 'qt')
                        expmul(mid3, -1.0, kk, kt, 'kt')
                        expmul(prev3, 1.0, qq, qe, 'qe')
                        expmul(last3, -1.0, kk, kl, 'kl')
                        cx.op('dve', lambda e: e.tensor_tensor(
                            edl[:, :nch], bb[:, 1:1 + n].rearrange("p (c l) -> p c l", l=L)[:, :, L - 1],
                            bb[:, 0:n].rearrange("p (c l) -> p c l", l=L)[:, :, 0], ALU.subtract), r=['bb'], w=['edl'])
                        cx.op('act', lambda e: e.activation(out=edl[:, :nch], in_=edl[:, :nch], func=AF.Exp),
                              r=['edl'], w=['edl'])
                        psb = ps[5].bitcast(BF16)
                        for g0 in range(0, n, G):
                            cx.op('pe', lambda e, g0=g0: e.matmul(ps[2][:G, :G], kt[:, g0:g0 + G], qt[:, g0:g0 + G],
                                                                   start=True, stop=True), r=['kt', 'qt'], w=[PS[2]])
                            cx.op('dve', lambda e: e.tensor_scalar(attf[:G, :G], ps[2][:G, :G], -1e30, 1e30, ALU.max, ALU.min),
                                  r=[PS[2]], w=['attf'])
                            cx.op('dve', lambda e: e.tensor_tensor(attm[:G, :G], attf[:G, :G], cst[:G, mi, :G], ALU.mult),
                                  r=['attf', 'cst'], w=['attm'])
                            cx.op('pe', lambda e, g0=g0: e.transpose(psb[:G, 0:128], vb[:, g0:g0 + G], identb[:]),
                                  r=['vb', 'identb'], w=[PS[5]])
                            cx.op('act', lambda e: e.activation(out=vtok[:G, :], in_=psb[:G, 0:128], func=AF.Copy),
                                  r=[PS[5]], w=['vtok'])
                            cx.op('pe', lambda e, g0=g0: e.transpose(psb[:G, 0:128], kl[:, g0:g0 + G], identb[:]),
                                  r=['kl', 'identb'], w=[PS[5]])
                            cx.op('act', lambda e: e.activation(out=kltok[:G, :], in_=psb[:G, 0:128], func=AF.Copy),
                                  r=[PS[5]], w=['kltok'])
                            for ci in range(G // L):
                                c0 = ci * L
                                cidx = (g0 + c0) // L
                                cx.op('pe', lambda e, c0=c0: e.matmul(ps[3][:, c0:c0 + L], vtok[c0:c0 + L, :],
                                                                       attm[c0:c0 + L, c0:c0 + L], start=True, stop=False),
                                      r=['vtok', 'attm'], w=[PS[3]])
                                cx.op('pe', lambda e, c0=c0, g0=g0: e.matmul(ps[3][:, c0:c0 + L], Sb[:, hd, :],
                                                                              qe[:, g0 + c0:g0 + c0 + L], start=False, stop=True),
                                      r=['Sb', 'qe'], w=[PS[3]])
                                cx.op('pe', lambda e, c0=c0: e.matmul(ps[4][:, :128], kltok[c0:c0 + L, :], vtok[c0:c0 + L, :],
                                                                       start=True, stop=True), r=['kltok', 'vtok'], w=[PS[4]])
                                cx.op('dve', lambda e, cidx=cidx: e.scalar_tensor_tensor(
                                    S[:, hd, :], S[:, hd, :], edl[:, cidx:cidx + 1], ps[4][:, :128], ALU.mult, ALU.add),
                                    r=['S', 'edl', PS[4]], w=['S'])
                                cx.op('act', lambda e: e.activation(out=Sb[:, hd, :], in_=S[:, hd, :], func=AF.Copy),
                                      r=['S'], w=['Sb'])
                            cx.op('act', lambda e, g0=g0: e.activation(out=ob[:, g0:g0 + G], in_=ps[3][:, :G], func=AF.Copy),
                                  r=[PS[3]], w=['ob'])
                        cx.op('act', lambda e: e.activation(out=dd[:, :n], in_=ob[:, :n], func=AF.Square), r=['ob'], w=['dd'])
                        cx.op('pe', lambda e: e.matmul(ps[7][:, :n], ones_f[:], dd[:, :n], start=True, stop=True),
                              r=['dd', 'ones_f'], w=[PS[7]])
                        cx.op('dve', lambda e: e.tensor_scalar(ee[:, :n], ps[7][:, :n], 1.0 / 128, EPS, ALU.mult, ALU.add),
                              r=[PS[7]], w=['ee'])
                        cx.op('dve', lambda e: e.reciprocal(ee[:, :n], ee[:, :n]), r=['ee'], w=['ee'])
                        cx.op('act', lambda e: e.activation(out=ee[:, :n], in_=ee[:, :n], func=AF.Sqrt), r=['ee'], w=['ee'])
                        cx.op('dve', lambda e: e.tensor_tensor(ob[:, :n], ob[:, :n], ee[:, :n], ALU.mult),
                              r=['ob', 'ee'], w=['ob'])
                        cx.op('act', lambda e: e.activation(out=dd[:, :n], in_=P4[:, 3, :n], func=AF.Silu), r=['P4'], w=['dd'])
                        cx.op('dve', lambda e: e.scalar_tensor_tensor(ot[:, sl, hd, :n], ob[:, :n], hgg[:, 0:1], dd[:, :n],
                                                                      ALU.mult, ALU.mult), r=['ob', 'hgg', 'dd'], w=[OT])
                    for ch in range(0 if 'L' in AB_SKIP else 4):
                        proj(16 + ch, 0)
                        cx.op('act', lambda e: e.activation(out=lxb[:, 0:3], in_=lhalo[:, ch, :], func=AF.Copy),
                              r=['lhalo'], w=['lxb'])
                        cx.op('act', lambda e: e.activation(out=lxb[:, 3:3 + n], in_=ps[0][:, :n], func=AF.Copy),
                              r=[PS[0]], w=['lxb'])
                        cx.op('act', lambda e: e.activation(out=lhalo[:, ch, :], in_=lxb[:, n:n + 3], func=AF.Copy),
                              r=['lxb'], w=['lhalo'])
                        proj(20 + ch, 1)
                        cx.op('act', lambda e: e.activation(out=ly[:, :n], in_=ps[1][:, :n], func=AF.Copy),
                              r=[PS[1]], w=['ee'])
                        cx.op('dve', lambda e: e.tensor_scalar(xc[:, :n], lxb[:, 0:n], lcw[:, ch, 0:1], lvec[:, ch, 0:1],
                                                               ALU.mult, ALU.add), r=['lxb', 'lcw', 'lvec'], w=['xc'])
                        for tp in range(1, 4):
                            cx.op('dve', lambda e, tp=tp: e.scalar_tensor_tensor(
                                xc[:, :n], lxb[:, tp:tp + n], lcw[:, ch, tp:tp + 1], xc[:, :n], ALU.mult, ALU.add),
                                r=['lxb', 'lcw', 'xc'], w=['xc'])
                        cx.op('act', lambda e: e.activation(out=xcb[:, :n], in_=xc[:, :n], func=AF.Copy), r=['xc'], w=['xcb'])
                        cx.op('pe', lambda e: e.matmul(ps[2][:, :n], gw[:, 0, ch, :], xcb[:, :n], start=True, stop=True),
                              r=['gw', 'xcb'], w=[PS[2]])
                        cx.op('act', lambda e: e.activation(out=rr[:, :n], in_=ps[2][:, :n], func=AF.Sigmoid,
                                                            bias=lvec[:, ch, 1:2]), r=[PS[2], 'lvec'], w=['sig'])
                        cx.op('pe', lambda e: e.matmul(ps[3][:, :n], gw[:, 1, ch, :], xcb[:, :n], start=True, stop=True),
                              r=['gw', 'xcb'], w=[PS[3]])
                        cx.op('act', lambda e: e.activation(out=ii[:, :n], in_=ps[3][:, :n], func=AF.Sigmoid,
                                                            bias=lvec[:, ch, 2:3]), r=[PS[3], 'lvec'], w=['kk'])
                        cx.op('dve', lambda e: e.tensor_scalar(rr[:, :n], rr[:, :n], m8sp[:, ch:ch + 1], None, ALU.mult),
                              r=['sig', 'm8sp'], w=['sig'])
                        cx.op('act', lambda e: e.activation(out=aa[:, :n], in_=rr[:, :n], func=AF.Exp), r=['sig'], w=['qq'])
                        cx.op('act', lambda e: e.activation(out=uu[:, :n], in_=rr[:, :n], func=AF.Exp, scale=2.0),
                              r=['sig'], w=['lf'])
                        cx.op('dve', lambda e: e.tensor_scalar(uu[:, :n], uu[:, :n], -1.0, 1.0, ALU.mult, ALU.add),
                              r=['lf'], w=['lf'])
                        cx.op('dve', lambda e: e.tensor_scalar(uu[:, :n], uu[:, :n], 1e-12, None, ALU.max),
                              r=['lf'], w=['lf'])
                        cx.op('act', lambda e: e.activation(out=uu[:, :n], in_=uu[:, :n], func=AF.Sqrt), r=['lf'], w=['lf'])
                        cx.op('dve', lambda e: e.tensor_tensor(uu[:, :n], uu[:, :n], ii[:, :n], ALU.mult),
                              r=['lf', 'kk'], w=['lf'])
                        cx.op('dve', lambda e: e.tensor_tensor(uu[:, :n], uu[:, :n], xc[:, :n], ALU.mult),
                              r=['lf', 'xc'], w=['lf'])
                        cx.op('dve', lambda e: e.tensor_tensor_scan(hh[:, :n], aa[:, :n], uu[:, :n], hst[:, ch:ch + 1],
                                                                    ALU.mult, ALU.add), r=['qq', 'lf', 'hst'], w=['dd'])
                        cx.op('act', lambda e: e.activation(out=hst[:, ch:ch + 1], in_=hh[:, n - 1:n], func=AF.Copy),
                              r=['dd'], w=['hst'])
                        cx.op('dve', lambda e: e.tensor_tensor(gl[:, :n], ly[:, :n], ly[:, :n], ALU.mult), r=['ee'], w=['ob'])
                        cx.op('dve', lambda e: e.tensor_scalar(gl[:, :n], gl[:, :n], 0.044715, 1.0, ALU.mult, ALU.add),
                              r=['ob'], w=['ob'])
                        cx.op('dve', lambda e: e.tensor_tensor(gl[:, :n], gl[:, :n], ly[:, :n], ALU.mult),
                              r=['ob', 'ee'], w=['ob'])
                        cx.op('act', lambda e: e.activation(out=gl[:, :n], in_=gl[:, :n], func=AF.Sigmoid,
                                                            scale=1.5957691216057308), r=['ob'], w=['ob'])
                        cx.op('dve', lambda e: e.tensor_tensor(gl[:, :n], gl[:, :n], ly[:, :n], ALU.mult),
                              r=['ob', 'ee'], w=['ob'])
                        cx.op('dve', lambda e: e.tensor_tensor(ot[:, sl, 4 + ch, :n], hh[:, :n], gl[:, :n], ALU.mult),
                              r=['dd', 'ob'], w=[OT])
                cx.dma('pool', s.oT.rearrange("(c p) t -> p c t", p=128)[:, :, t0:t0 + n], ot[:, sl, :, :n],
                       r=[OT], w=['oT%d' % s.i])
            if ab:
                cx.dma('pool', s.hg_out[jl], S[:], r=['S'], w=['hg_out'])
                cx.dma('pool', s.lru_out[jl], hst[:], r=['hst'], w=['lru_out'])
                cx.dma('pool', s.lruc_out[jl], lhalo[:], r=['lhalo'], w=['lruc_out'])
        cx.barrier()


MLA_SCALE = (128 + 64) ** -0.5
CD_STAGE = int(os.environ.get('CD_STAGE', '99'))
CD_SUB = int(os.environ.get('CD_SUB', '99'))
M2_SUB = int(os.environ.get('M2_SUB', '99'))
AB_SKIP = os.environ.get('AB_SKIP', '')
E_POOL = os.environ.get('E_POOL', '0') == '1'


def emit_cd(nc, cx, ps, PS, l, seqs, mod, gsc, ng, ones_f, rms_stats, W):
    jl = l // 2
    ones_b = W['ones_b']
    RT = lambda a: a.rearrange("(c p) t -> p c t", p=128)
    with ExitStack() as st:
        def T(name, shape, dt):
            return st.enter_context(SB(nc, name, shape, dt))
        xt = T("xt", [128, 8, 512], F32)
        sq = T("sq", [128, 8, 512], BF16)
        rstd = T("rstd", [128, 512], F32)
        tmp = T("tmp", [128, 512], F32)
        tmpb = T("tmpb", [128, 2, 512], F32)
        h = T("h", [128, 8, 512], BF16)
        ot = T("otile", [128, 2, 4, 512], BF16)
        wst = T("wst", [128, 2, 1536], F32)
        win = T("win", [128, 8, 3072], BF16)
        cst = T("cst", [128, 13, 128], F32)
        identb = T("identb", [128, 128], BF16)
        onesr = T("onesr", [128, 512], F32)
        pre = T("pre", [128, 3, 515], F32)
        cvx = T("cvx", [128, 3, 512], F32)
        ghalo = T("ghalo", [128, 12, 3], F32)
        gcw = T("gcw", [128, 12, 4], F32)
        gvec = T("gvec", [128, 4, 2], F32)
        nexpa = T("nexpa", [128, 4], F32)
        gdg = T("gdg", [128, 1], F32)
        c21 = T("c21", [128, 512], F32)
        c22 = T("c22", [128, 512], F32)
        c23 = T("c23", [128, 512], F32)
        gz = T("gz", [128, 512], F32)
        betaB = T("betaB", [128, 512], F32)
        gg = T("gg", [128, 513], F32)
        egr = T("egr", [128, 512], F32)
        ela = T("ela", [128, 512], F32)
        edl = T("edl", [128, 8], F32)
        w1 = T("w1", [128, 512], F32)
        w2 = T("w2", [128, 512], F32)
        vbet = T("vbet", [128, 512], F32)
        kbe = T("kbe", [128, 512], F32)
        kd = T("kd", [128, 512], F32)
        qbf = T("qbf", [128, 512], BF16)
        kbf = T("kbf", [128, 512], BF16)
        qg = T("qg", [128, 512], BF16)
        ob = T("ob", [128, 512], F32)
        gcolS = [T("gcol%d" % i_, [128, 1], F32) for i_ in range(3)]
        DBeS = [T("DBe%d" % i_, [128, 128], F32) for i_ in range(3)]
        XYS = [T("XY%d" % i_, [128, 2, 2, 128], F32) for i_ in range(3)]
        RmS = [T("Rm%d" % i_, [128, 256], F32) for i_ in range(3)]
        qkTS = [T("qkT%d" % i_, [128, 128], BF16) for i_ in range(3)]
        kdtokS = [T("kdtok%d" % i_, [128, 128], BF16) for i_ in range(3)]
        wkTS = [T("wkT%d" % i_, [128, 128], BF16) for i_ in range(3)]
        utokS = [T("utok%d" % i_, [128, 128], BF16) for i_ in range(3)]
        S = T("S", [128, 4, 128], F32)
        Sb = T("Sb", [128, 4, 128], BF16)
        PQ = T("PQ", [128, 3, 512], F32)
        qng = T("qng", [128, 3], F32)
        kvg = T("kvg", [128, 2], F32)
        qan = T("qan", [128, 3, 512], BF16)
        ckv = T("ckv", [128, 2, 512], F32)
        tokst = T("tokst", [128, 4, 320], F32)
        rope = T("rope", [64, 2, 512], F32)
        krr = T("krr", [64, 512], F32)
        for c2 in range(16):
            c, hf = c2 // 2, c2 % 2
            sl = hf
            cx.dma('sp', wst[:, sl], W['cd_w_in'][jl][c * 128:(c + 1) * 128, hf * 1536:(hf + 1) * 1536],
                   w=['wst%d' % sl])
            cx.op('act' if sl else 'dve',
                  (lambda e: e.activation(out=win[:, c, hf * 1536:(hf + 1) * 1536], in_=wst[:, sl], func=AF.Copy))
                  if sl else (lambda e: e.tensor_copy(win[:, c, hf * 1536:(hf + 1) * 1536], wst[:, sl])),
                  r=['wst%d' % sl], w=['win'])
        cx.dma('sp', cst[:], W['consts'][:, :, :], w=['cst'])
        cx.op('dve', lambda e: e.tensor_copy(identb[:], cst[:, 0, :]), r=['cst'], w=['identb'])
        cx.op('dve', lambda e: e.memset(onesr[:], 1.0), w=['onesr'])
        cx.dma('sp', gcw[:], W['gd_cw'][jl], w=['gcw'])
        cx.dma('sp', gvec[:], W['gd_vec'][jl], w=['gvec'])
        cx.dma('sp', gdg[:], W['gd_g'][jl], w=['gdg'])
        cx.dma('sp', qng[:], W['q_g'][jl], w=['qng'])
        cx.dma('sp', kvg[:], W['kv_g'][jl], w=['kvg'])
        cx.op('act', lambda e: e.activation(out=nexpa[:], in_=gvec[:, :, 0], func=AF.Exp), r=['gvec'], w=['nexpa'])
        cx.op('dve', lambda e: e.tensor_scalar(nexpa[:], nexpa[:], -1.0, None, ALU.mult), r=['nexpa'], w=['nexpa'])
        k = 0
        for s in seqs:
            L = 64 if s.T % 64 == 0 else s.T
            nsteps = int(round(math.log2(L)))
            past = s.past
            if s.i == 0:
                cx.op('dve', lambda e: e.memset(S[:], 0.0), w=['S'])
                cx.op('dve', lambda e: e.memset(ghalo[:], 0.0), w=['ghalo'])
            else:
                cx.dma('sp', S[:], s.gd_in[jl], w=['S'])
                cx.dma('sp', ghalo[:], s.gdc_in[jl], w=['ghalo'])
                cx.dma('pool', s.ckvT[:, 0:past], s.lat_inT[jl], w=['ckvT%d' % s.i])
                cx.dma('pool', s.krT[:, 0:past], s.kr_inT[jl], w=['krT%d' % s.i])
            cx.op('act', lambda e: e.activation(out=Sb[:], in_=S[:], func=AF.Copy), r=['S'], w=['Sb'])
            for (t0, n) in s.tiles:
                sl = k % 2
                k += 1
                OT = 'ot%d' % sl
                cx.dma('sp', xt[:, :, :n], RT(s.xT)[:, :, t0:t0 + n], r=['xT%d' % s.i], w=['xt'])
                cx.dma('sp', rope[:, 0, :n], s.ropeC[:, t0:t0 + n], w=['rope'])
                cx.dma('sp', rope[:, 1, :n], s.ropeS[:, t0:t0 + n], w=['rope'])
                rms_stats(lambda c: xt[:, c, :n], n, sq, rstd, 6, ['xt'])
                for c in range(8):
                    cx.op('dve', lambda e: e.tensor_tensor(tmpb[:, c % 2, :n], xt[:, c, :n], rstd[:, :n], ALU.mult),
                          r=['xt', 'rstd'], w=['tmpb%d' % (c % 2)])
                    cx.op('act', lambda e: e.activation(out=h[:, c, :n], in_=tmpb[:, c % 2, :n], func=AF.Identity,
                                                        scale=gsc[:, 0, c, s.i:s.i + 1], bias=mod[:, c, s.i:s.i + 1]),
                          r=['tmpb%d' % (c % 2), 'gsc', 'mod'], w=['h'])

                def proj(ch, pn):
                    for c in range(8):
                        cx.op('pe', lambda e, c=c: e.matmul(ps[pn][:, :n], win[:, c, ch * 128:(ch + 1) * 128],
                                                             h[:, c, :n], start=(c == 0), stop=(c == 7)),
                              r=['win', 'h'], w=[PS[pn]])

                def pcopy(ch, pn, dst, DST):
                    proj(ch, pn)
                    cx.op('act', lambda e: e.activation(out=dst, in_=ps[pn][:, :n], func=AF.Copy), r=[PS[pn]], w=[DST])

                def stat128(src, SRC, dst, DST, scl, bias_eps):
                    cx.op('act', lambda e: e.activation(out=w2[:, :n], in_=src, func=AF.Square), r=[SRC], w=['w2'])
                    cx.op('pe', lambda e: e.matmul(ps[6][:, :n], ones_f[:], w2[:, :n], start=True, stop=True),
                          r=['w2', 'ones_f'], w=[PS[6]])
                    cx.op('dve', lambda e: e.tensor_scalar(dst, ps[6][:, :n], scl, bias_eps, ALU.mult, ALU.add),
                          r=[PS[6]], w=[DST])
                    cx.op('dve', lambda e: e.reciprocal(dst, dst), r=[DST], w=[DST])
                    cx.op('act', lambda e: e.activation(out=dst, in_=dst, func=AF.Sqrt), r=[DST], w=[DST])

                pcopy(21, 0, c21[:, :n], 'c21')
                pcopy(22, 1, c22[:, :n], 'c22')
                pcopy(23, 0, c23[:, :n], 'c23')
                for c in range(3):
                    pcopy(16 + c, c % 2, PQ[:, c, :n], 'PQ')
                for c in range(3):
                    cx.op('act', lambda e: e.activation(out=sq[:, c, :n], in_=PQ[:, c, :n], func=AF.Square), r=['PQ'], w=['sq'])
                for c in range(3):
                    cx.op('pe', lambda e: e.matmul(ps[6][:, :n], ones_b[:], sq[:, c, :n], start=(c == 0), stop=(c == 2)),
                          r=['sq', 'ones_b'], w=[PS[6]])
                cx.op('dve', lambda e: e.tensor_scalar(w1[:, :n], ps[6][:, :n], 1.0 / 384, EPS, ALU.mult, ALU.add),
                      r=[PS[6]], w=['w1'])
                cx.op('dve', lambda e: e.reciprocal(w1[:, :n], w1[:, :n]), r=['w1'], w=['w1'])
                cx.op('act', lambda e: e.activation(out=w1[:, :n], in_=w1[:, :n], func=AF.Sqrt), r=['w1'], w=['w1'])
                for c in range(3):
                    cx.op('dve', lambda e: e.tensor_tensor(PQ[:, c, :n], PQ[:, c, :n], w1[:, :n], ALU.mult),
                          r=['PQ', 'w1'], w=['PQ'])
                    cx.op('dve', lambda e: e.tensor_scalar(qan[:, c, :n], PQ[:, c, :n], qng[:, c:c + 1], None, ALU.mult),
                          r=['PQ', 'qng'], w=['qan'])
                cx.dma('pool', RT(s.qanT)[:, :, t0:t0 + n], qan[:, :, :n], r=['qan'], w=['qanT%d' % s.i])
                for c in range(2):
                    pcopy(19 + c, c % 2, PQ[:, c, :n], 'PQ')
                for c in range(2):
                    cx.op('act', lambda e: e.activation(out=sq[:, c, :n], in_=PQ[:, c, :n], func=AF.Square), r=['PQ'], w=['sq'])
                for c in range(2):
                    cx.op('pe', lambda e: e.matmul(ps[6][:, :n], ones_b[:], sq[:, c, :n], start=(c == 0), stop=(c == 1)),
                          r=['sq', 'ones_b'], w=[PS[6]])
                cx.op('dve', lambda e: e.tensor_scalar(w1[:, :n], ps[6][:, :n], 1.0 / 256, EPS, ALU.mult, ALU.add),
                      r=[PS[6]], w=['w1'])
                cx.op('dve', lambda e: e.reciprocal(w1[:, :n], w1[:, :n]), r=['w1'], w=['w1'])
                cx.op('act', lambda e: e.activation(out=w1[:, :n], in_=w1[:, :n], func=AF.Sqrt), r=['w1'], w=['w1'])
                for c in range(2):
                    cx.op('dve', lambda e: e.tensor_tensor(PQ[:, c, :n], PQ[:, c, :n], w1[:, :n], ALU.mult),
                          r=['PQ', 'w1'], w=['PQ'])
                    cx.op('dve', lambda e: e.tensor_scalar(ckv[:, c, :n], PQ[:, c, :n], kvg[:, c:c + 1], None, ALU.mult),
                          r=['PQ', 'kvg'], w=['ckv'])
                cx.dma('pool', RT(s.ckvT)[:, :, past + t0:past + t0 + n], ckv[:, :, :n], r=['ckv'], w=['ckvT%d' % s.i])
                cx.op('dve', lambda e: e.tensor_tensor(krr[:, :n], c21[0:64, :n], rope[:, 0, :n], ALU.mult),
                      r=['c21', 'rope'], w=['krr'])
                cx.op('dve', lambda e: e.tensor_tensor(w1[0:64, :n], c23[0:64, :n], rope[:, 1, :n], ALU.mult),
                      r=['c23', 'rope'], w=['w1'])
                cx.op('dve', lambda e: e.tensor_tensor(krr[:, :n], krr[:, :n], w1[0:64, :n], ALU.add),
                      r=['krr', 'w1'], w=['krr'])
                cx.dma('pool', s.krT[:, past + t0:past + t0 + n], krr[:, :n], r=['krr'], w=['krT%d' % s.i])
                nsub = (n + 127) // 128
                for sb_ in range(nsub):
                    a0 = sb_ * 128
                    an = min(128, n - a0)
                    for c in range(2):
                        cx.op('pe', lambda e: e.transpose(ps[4][:an, c * 128:(c + 1) * 128], ckv[:, c, a0:a0 + an], cst[:, 0, :]),
                              r=['ckv', 'cst'], w=[PS[4]])
                    cx.op('pe', lambda e: e.transpose(ps[4][:an, 256:320], krr[:, a0:a0 + an], cst[0:64, 0, 0:64]),
                          r=['krr', 'cst'], w=[PS[4]])
                    cx.op('act', lambda e: e.activation(out=tokst[:an, sb_, :], in_=ps[4][:an, 0:320], func=AF.Copy),
                          r=[PS[4]], w=['tokst'])
                    cx.dma('pool', s.lat_out[jl][t0 + a0:t0 + a0 + an, :], tokst[:an, sb_, 0:256], r=['tokst'], w=['lat_out'])
                    cx.dma('pool', s.kr_out[jl][t0 + a0:t0 + a0 + an, :], tokst[:an, sb_, 256:320], r=['tokst'], w=['kr_out'])
                nch = n // L
                G = min(128, n)
                mI, mS = (1, 4) if L == 64 else (2, 3)
                for hd in range(4 if CD_STAGE >= 2 else 0):
                    for idx, ch in enumerate((hd, 4 + hd, 8 + hd)):
                        proj(ch, idx % 2)
                        cx.op('act', lambda e: e.activation(out=pre[:, idx, 0:3], in_=ghalo[:, ch, :], func=AF.Copy),
                              r=['ghalo'], w=['pre'])
                        cx.op('act', lambda e: e.activation(out=pre[:, idx, 3:3 + n], in_=ps[idx % 2][:, :n], func=AF.Copy),
                              r=[PS[idx % 2]], w=['pre'])
                        cx.op('act', lambda e: e.activation(out=ghalo[:, ch, :], in_=pre[:, idx, n:n + 3], func=AF.Copy),
                              r=['pre'], w=['ghalo'])
                        cx.op('dve', lambda e: e.tensor_scalar(cvx[:, idx, :n], pre[:, idx, 0:n], gcw[:, ch, 0:1], None, ALU.mult),
                              r=['pre', 'gcw'], w=['cvx'])
                        for tp in range(1, 4):
                            cx.op('dve', lambda e: e.scalar_tensor_tensor(cvx[:, idx, :n], pre[:, idx, tp:tp + n],
                                                                          gcw[:, ch, tp:tp + 1], cvx[:, idx, :n], ALU.mult, ALU.add),
                                  r=['pre', 'gcw', 'cvx'], w=['cvx'])
                        cx.op('act', lambda e: e.activation(out=cvx[:, idx, :n], in_=cvx[:, idx, :n], func=AF.Silu),
                              r=['cvx'], w=['cvx'])
                    pcopy(12 + hd, 0, gz[:, :n], 'gz')
                    stat128(cvx[:, 0, :n], 'cvx', w1[:, :n], 'w1', 1.0, EPS)
                    cx.op('dve', lambda e: e.scalar_tensor_tensor(cvx[:, 0, :n], cvx[:, 0, :n], 128 ** -0.5, w1[:, :n],
                                                                  ALU.mult, ALU.mult), r=['cvx', 'w1'], w=['cvx'])
                    stat128(cvx[:, 1, :n], 'cvx', w1[:, :n], 'w1', 1.0, EPS)
                    cx.op('dve', lambda e: e.tensor_tensor(cvx[:, 1, :n], cvx[:, 1, :n], w1[:, :n], ALU.mult),
                          r=['cvx', 'w1'], w=['cvx'])
                    cx.op('act', lambda e: e.activation(out=qbf[:, :n], in_=cvx[:, 0, :n], func=AF.Copy), r=['cvx'], w=['qbf'])
                    cx.op('act', lambda e: e.activation(out=kbf[:, :n], in_=cvx[:, 1, :n], func=AF.Copy), r=['cvx'], w=['kbf'])
                    cx.op('pe', lambda e: e.matmul(ps[2][:, :n], cst[:, 5 + hd, :], c21[:, :n], start=True, stop=True),
                          r=['cst', 'c21'], w=[PS[2]])
                    cx.op('act', lambda e: e.activation(out=betaB[:, :n], in_=ps[2][:, :n], func=AF.Sigmoid),
                          r=[PS[2]], w=['betaB'])
                    cx.op('pe', lambda e: e.matmul(ps[3][:, :n], cst[:, 9 + hd, :], c22[:, :n], start=True, stop=True),
                          r=['cst', 'c22'], w=[PS[3]])
                    cx.op('dve', lambda e: e.tensor_scalar(w1[:, :n], ps[3][:, :n], gvec[:, hd, 1:2], None, ALU.add),
                          r=[PS[3], 'gvec'], w=['w1'])
                    cx.op('act', lambda e: e.activation(out=w2[:, :n], in_=w1[:, :n], func=AF.Abs), r=['w1'], w=['w2'])
                    cx.op('act', lambda e: e.activation(out=w2[:, :n], in_=w2[:, :n], func=AF.Exp, scale=-1.0), r=['w2'], w=['w2'])
                    cx.op('dve', lambda e: e.tensor_scalar(w2[:, :n], w2[:, :n], 1.0, None, ALU.add), r=['w2'], w=['w2'])
                    cx.op('act', lambda e: e.activation(out=w2[:, :n], in_=w2[:, :n], func=AF.Ln), r=['w2'], w=['w2'])
                    cx.op('dve', lambda e: e.scalar_tensor_tensor(w1[:, :n], w1[:, :n], 0.0, w2[:, :n], ALU.max, ALU.add),
                          r=['w1', 'w2'], w=['w1'])
                    cx.op('dve', lambda e: e.tensor_scalar(w1[:, :n], w1[:, :n], nexpa[:, hd:hd + 1], None, ALU.mult),
                          r=['w1', 'nexpa'], w=['w1'])
                    cx.op('dve', lambda e: e.memset(gg[:, 0:1], 0.0), w=['gg'])
                    cx.op('dve', lambda e: e.tensor_tensor_scan(gg[:, 1:1 + n], onesr[:, :n], w1[:, :n], 0.0, ALU.mult, ALU.add),
                          r=['onesr', 'w1'], w=['gg'])
                    b3 = gg[:, 1:1 + n].rearrange("p (c l) -> p c l", l=L)
                    last3 = b3[:, :, L - 1:L].to_broadcast([128, nch, L])
                    prev3 = gg[:, 0:n].rearrange("p (c l) -> p c l", l=L)[:, :, 0:1].to_broadcast([128, nch, L])
                    cx.op('dve', lambda e: e.tensor_tensor(egr[:, :n].rearrange("p (c l) -> p c l", l=L), b3, prev3, ALU.subtract),
                          r=['gg'], w=['egr'])
                    cx.op('act', lambda e: e.activation(out=egr[:, :n], in_=egr[:, :n], func=AF.Exp), r=['egr'], w=['egr'])
                    cx.op('dve', lambda e: e.tensor_tensor(ela[:, :n].rearrange("p (c l) -> p c l", l=L), b3, last3, ALU.subtract),
                          r=['gg'], w=['ela'])
                    cx.op('act', lambda e: e.activation(out=ela[:, :n], in_=ela[:, :n], func=AF.Exp, scale=-1.0), r=['ela'], w=['ela'])
                    cx.op('dve', lambda e: e.tensor_tensor(
                        edl[:, :nch], gg[:, 1:1 + n].rearrange("p (c l) -> p c l", l=L)[:, :, L - 1],
                        gg[:, 0:n].rearrange("p (c l) -> p c l", l=L)[:, :, 0], ALU.subtract), r=['gg'], w=['edl'])
                    cx.op('act', lambda e: e.activation(out=edl[:, :nch], in_=edl[:, :nch], func=AF.Exp), r=['edl'], w=['edl'])
                    cx.op('dve', lambda e: e.tensor_tensor(vbet[:, :n], cvx[:, 2, :n], betaB[:, :n], ALU.mult),
                          r=['cvx', 'betaB'], w=['vbet'])
                    cx.op('dve', lambda e: e.tensor_tensor(kbe[:, :n], cvx[:, 1, :n], betaB[:, :n], ALU.mult),
                          r=['cvx', 'betaB'], w=['kbe'])
                    cx.op('dve', lambda e: e.tensor_tensor(kbe[:, :n], kbe[:, :n], egr[:, :n], ALU.mult), r=['kbe', 'egr'], w=['kbe'])
                    cx.op('dve', lambda e: e.tensor_tensor(kd[:, :n], cvx[:, 1, :n], ela[:, :n], ALU.mult), r=['cvx', 'ela'], w=['kd'])
                    cx.op('dve', lambda e: e.tensor_tensor(qg[:, :n], cvx[:, 0, :n], egr[:, :n], ALU.mult), r=['cvx', 'egr'], w=['qg'])
                    groups = list(range(0, n, G))
                    nthr = min(int(os.environ.get('GDN_THR', '3')), len(groups))
                    if os.environ.get('GDN_BAR', '0') == '1':
                        cx.barrier()
                    tick = {'done': 0}

                    def gdn_thread(p):
                        ba, bb_ = ((2, 3), (4, 5), (0, 1))[p]
                        pa, pb = ps[ba], ps[bb_]
                        A0 = A1 = A2 = PS[ba]
                        B0 = B1 = PS[bb_]
                        gcol_, DBe_, XY_, Rm_, qkT_, kdtok_, wkT_, utok_ = [b_[p] for b_ in
                                                                           (gcolS, DBeS, XYS, RmS, qkTS, kdtokS, wkTS, utokS)]
                        N_ = lambda nm: '%s_%d' % (nm, p)
                        for gi in range(p, len(groups), nthr):
                            g0 = groups[gi]
                            gs = slice(g0, g0 + G)
                            cx.op('pe', lambda e: e.transpose(pb[:G, 0:128], gg[:, 1 + g0:1 + g0 + G], cst[:, 0, :]),
                                  r=['gg', 'cst'], w=[B0])
                            cx.op('act', lambda e: e.activation(out=gcol_[:G, :], in_=pb[:G, 0:1], func=AF.Copy), r=[B0], w=[N_('gcol')])
                            cx.op('dve', lambda e: e.tensor_scalar(DBe_[:G, :G], gg[:G, 1 + g0:1 + g0 + G], gcol_[:G, 0:1], 0.0,
                                                                   ALU.subtract, ALU.min), r=['gg', N_('gcol')], w=[N_('DBe')])
                            cx.op('act', lambda e: e.activation(out=DBe_[:G, :G], in_=DBe_[:G, :G], func=AF.Exp), r=[N_('DBe')], w=[N_('DBe')])
                            cx.op('pe', lambda e: e.matmul(pa[:G, 0:G], kbf[:, gs], kbf[:, gs], start=True, stop=True), r=['kbf'], w=[A0])
                            X0, Y0 = XY_[:G, 0, 0, :G], XY_[:G, 0, 1, :G]
                            cx.op('dve', lambda e: e.scalar_tensor_tensor(X0, pa[:G, 0:G], -1.0, DBe_[:G, :G], ALU.mult, ALU.mult),
                                  r=[A0, N_('DBe')], w=[N_('XY0')])
                            cx.op('dve', lambda e: e.tensor_tensor(X0, X0, betaB[:G, gs], ALU.mult), r=[N_('XY0'), 'betaB'], w=[N_('XY0')])
                            cx.op('dve', lambda e: e.tensor_tensor(X0, X0, cst[:G, mS, :G], ALU.mult), r=[N_('XY0'), 'cst'], w=[N_('XY0')])
                            cx.op('pe', lambda e: e.transpose(pb[:G, 384:384 + G], X0, cst[:G, 0, :G]), r=[N_('XY0'), 'cst'], w=[B1])
                            cx.op('act', lambda e: e.activation(out=Y0, in_=pb[:G, 384:384 + G], func=AF.Copy), r=[B1], w=[N_('XY0')])
                            cx.op('pe', lambda e: e.matmul(pa[:G, 128:128 + G], kbf[:, gs], qbf[:, gs], start=True, stop=True),
                                  r=['kbf', 'qbf'], w=[A1])
                            cx.op('dve', lambda e: e.tensor_tensor(DBe_[:G, :G], DBe_[:G, :G], cst[:G, mI, :G], ALU.mult),
                                  r=[N_('DBe'), 'cst'], w=[N_('DBe')])
                            cx.op('dve', lambda e: e.tensor_tensor(qkT_[:G, :G], pa[:G, 128:128 + G], DBe_[:G, :G], ALU.mult),
                                  r=[A1, N_('DBe')], w=[N_('qkT')])
                            cx.op('pe', lambda e: e.transpose(pb[:G, 0:128], vbet[:, gs], cst[:, 0, :]), r=['vbet', 'cst'], w=[B0])
                            cx.op('pe', lambda e: e.transpose(pb[:G, 128:256], kbe[:, gs], cst[:, 0, :]), r=['kbe', 'cst'], w=[B0])
                            cx.op('pe', lambda e: e.transpose(pb[:G, 256:384], kd[:, gs], cst[:, 0, :]), r=['kd', 'cst'], w=[B0])
                            cx.op('act', lambda e: e.activation(out=Rm_[:G, :], in_=pb[:G, 0:256], func=AF.Copy), r=[B0], w=[N_('Rm')])
                            cx.op('act', lambda e: e.activation(out=kdtok_[:G, :], in_=pb[:G, 256:384], func=AF.Copy), r=[B0], w=[N_('kdtok')])
                            for kk_ in range(nsteps):
                                pg = kk_ % 2
                                Xc, Yc = XY_[:G, pg, 0, :G], XY_[:G, pg, 1, :G]
                                cx.op('pe', lambda e: e.matmul(pa[:G, 256:512], Xc, Rm_[:G, :], start=True, stop=True),
                                      r=[N_('XY%d' % pg), N_('Rm')], w=[A2])
                                cx.op('dve', lambda e: e.tensor_tensor(Rm_[:G, :], Rm_[:G, :], pa[:G, 256:512], ALU.add),
                                      r=[N_('Rm'), A2], w=[N_('Rm')])
                                if kk_ < nsteps - 1:
                                    Xn, Yn = XY_[:G, 1 - pg, 0, :G], XY_[:G, 1 - pg, 1, :G]
                                    cx.op('pe', lambda e: e.matmul(pa[:G, 0:G], Yc, Xc, start=True, stop=True), r=[N_('XY%d' % pg)], w=[A0])
                                    cx.op('pe', lambda e: e.matmul(pa[:G, 128:128 + G], Xc, Yc, start=True, stop=True), r=[N_('XY%d' % pg)], w=[A1])
                                    cx.op('act', lambda e: e.activation(out=Xn, in_=pa[:G, 0:G], func=AF.Copy), r=[A0], w=[N_('XY%d' % (1 - pg))])
                                    cx.op('dve', lambda e: e.tensor_copy(Yn, pa[:G, 128:128 + G]), r=[A1], w=[N_('XY%d' % (1 - pg))])
                            cx.op('pe', lambda e: e.transpose(pb[:, 0:G], Rm_[:G, 128:256], cst[:G, 0, :G]), r=[N_('Rm'), 'cst'], w=[B0])
                            cx.op('act', lambda e: e.activation(out=wkT_[:, :G], in_=pb[:, 0:G], func=AF.Copy), r=[B0], w=[N_('wkT')])
                            cx.wait_until(lambda: tick['done'] == gi)
                            if os.environ.get('GDN_OLDB', '0') == '1':
                                WS_, SU_, WSN, SUN = ps[5], ps[4][:, 0:128], PS[5], PS[4]
                            else:
                                WS_, SU_, WSN, SUN = ps[6], ps[6][:, 128:256], PS[6], PS[6]
                            for ci in range(G // L):
                                c0 = ci * L
                                cs_ = slice(c0, c0 + L)
                                cidx = (g0 + c0) // L
                                cx.op('pe', lambda e: e.matmul(WS_[:G, 0:128], wkT_[:, :G], Sb[:, hd, :], start=True, stop=True),
                                      r=[N_('wkT'), 'Sb'], w=[WSN])
                                cx.op('dve', lambda e: e.tensor_tensor(utok_[cs_, :], Rm_[cs_, 0:128], WS_[cs_, 0:128], ALU.subtract),
                                      r=[N_('Rm'), WSN], w=[N_('utok')])
                                cx._noyield += 1
                                cx.op('pe', lambda e: e.matmul(ps[7][:, cs_], utok_[cs_, :], qkT_[cs_, cs_], start=True, stop=False),
                                      r=[N_('utok'), N_('qkT')], w=[PS[7]])
                                cx._noyield -= 1
                                cx.op('pe', lambda e: e.matmul(ps[7][:, cs_], Sb[:, hd, :], qg[:, g0 + c0:g0 + c0 + L], start=False, stop=True),
                                      r=['Sb', 'qg'], w=[PS[7]])
                                cx.op('pe', lambda e: e.matmul(SU_, kdtok_[cs_, :], utok_[cs_, :], start=True, stop=True),
                                      r=[N_('kdtok'), N_('utok')], w=[SUN])
                                cx.op('dve', lambda e: e.scalar_tensor_tensor(S[:, hd, :], S[:, hd, :], edl[:, cidx:cidx + 1], SU_,
                                                                              ALU.mult, ALU.add), r=['S', 'edl', SUN], w=['S'])
                                cx.op('act', lambda e: e.activation(out=Sb[:, hd, :], in_=S[:, hd, :], func=AF.Copy), r=['S'], w=['Sb'])
                            cx.op('act', lambda e: e.activation(out=ob[:, gs], in_=ps[7][:, :G], func=AF.Copy), r=[PS[7]], w=['ob'])
                            tick['done'] += 1

                    cx.interleave([(lambda p=p: gdn_thread(p)) for p in range(nthr)])
                    if os.environ.get('GDN_BAR', '0') == '1':
                        cx.barrier()
                    stat128(ob[:, :n], 'ob', w1[:, :n], 'w1', 1.0 / 128, EPS)
                    cx.op('dve', lambda e: e.tensor_tensor(ob[:, :n], ob[:, :n], w1[:, :n], ALU.mult), r=['ob', 'w1'], w=['ob'])
                    cx.op('act', lambda e: e.activation(out=gz[:, :n], in_=gz[:, :n], func=AF.Silu), r=['gz'], w=['gz'])
                    cx.op('dve', lambda e: e.scalar_tensor_tensor(ot[:, sl, hd, :n], ob[:, :n], gdg[:, 0:1], gz[:, :n], ALU.mult, ALU.mult),
                          r=['ob', 'gdg', 'gz'], w=[OT])
                cx.dma('pool', RT(s.oT)[:, 0:4, t0:t0 + n], ot[:, sl, :, :n], r=[OT], w=['oT%d' % s.i])
            cx.dma('pool', s.gd_out[jl], S[:], r=['S'], w=['gd_out'])
            cx.dma('pool', s.gdc_out[jl], ghalo[:], r=['ghalo'], w=['gdc_out'])
        cx.barrier()

    if CD_STAGE < 3:
        return
    with ExitStack() as st:
        def T(name, shape, dt):
            return st.enter_context(SB(nc, name, shape, dt))
        TA = max(s.past + s.T for s in seqs)
        NT128 = (TA + 127) // 128
        wst = T("wst", [128, 3, 1024], F32)
        wqb = T("wqb", [128, 3, 1024], BF16)
        wkvb = T("wkvb", [128, 2, 1024], BF16)
        onesb = T("onesb", [128, 128], BF16)
        KT = T("KT", [128, TA], BF16)
        VT = T("VT", [128, NT128, 128], BF16)
        KR = T("KR", [65, TA], BF16)
        negm = T("negm", [1, 512], BF16)
        cst_ = T("cstf", [128, 512], F32)
        ckb = T("ckb", [128, 2, 512], BF16)
        ksq = T("ksq", [128, 512], F32)
        kmax = T("kmax", [128, 2], F32)
        qan = T("qan", [128, 2, 3, 512], BF16)
        rope = T("rope", [64, 2, 512], F32)
        Qn = T("Qn", [128, 512], BF16)
        Qf = T("Qf", [128, 512], F32)
        Qr = T("Qr", [65, 512], BF16)
        qr1 = T("qr1", [64, 512], F32)
        qr2 = T("qr2", [64, 512], F32)
        Pt = T("Pt", [128, 3, 512], BF16)
        rs = T("rs", [128, 512], F32)
        od = T("od", [128, 2, 512], BF16)
        for c in range(3):
            cx.dma('sp', wst[:, c, :], W['w_qb'][jl][c * 128:(c + 1) * 128, :], w=['wst'])
        cx.op('dve', lambda e: e.tensor_copy(wqb[:], wst[:]), r=['wst'], w=['wqb'])
        for c in range(2):
            cx.dma('sp', wst[:, c, :], W['w_kvb'][jl][c * 128:(c + 1) * 128, :], w=['wst'])
        cx.op('dve', lambda e: e.tensor_copy(wkvb[:], wst[:, 0:2, :]), r=['wst'], w=['wkvb'])
        cx.op('dve', lambda e: e.memset(onesb[:], 1.0), w=['onesb'])
        kq = 0
        for s in seqs:
            past = s.past
            Tall = past + s.T
            chunked = (s.T % 64 == 0)
            ktiles = tiles_of(Tall, 512)
            for (k0, nk) in ktiles:
                cx.dma('sp', cst_[0:64, :nk], s.krT[:, k0:k0 + nk], r=['krT%d' % s.i], w=['cstf'])
                cx.op('act', lambda e: e.activation(out=KR[0:64, k0:k0 + nk], in_=cst_[0:64, :nk], func=AF.Copy), r=['cstf'], w=['KR'])
            cx.op('dve', lambda e: e.memset(KR[64:65, 0:Tall], 1.0), w=['KR'])
            for hd in range(4 if M2_SUB >= 2 else 0):
                cx.op('dve', lambda e: e.memset(kmax[:], 0.0), w=['kmax'])
                for (k0, nk) in ktiles:
                    cx.dma('sp', cst_[:, :nk], s.ckvT[0:128, k0:k0 + nk], r=['ckvT%d' % s.i], w=['cstf'])
                    cx.op('act', lambda e: e.activation(out=ckb[:, 0, :nk], in_=cst_[:, :nk], func=AF.Copy), r=['cstf'], w=['ckb'])
                    cx.dma('sp', cst_[:, :nk], s.ckvT[128:256, k0:k0 + nk], r=['ckvT%d' % s.i], w=['cstf'])
                    cx.op('act', lambda e: e.activation(out=ckb[:, 1, :nk], in_=cst_[:, :nk], func=AF.Copy), r=['cstf'], w=['ckb'])
                    for c in range(2):
                        cx.op('pe', lambda e: e.matmul(ps[4][:, :nk], wkvb[:, c, hd * 256:hd * 256 + 128], ckb[:, c, :nk],
                                                       start=(c == 0), stop=(c == 1)), r=['wkvb', 'ckb'], w=[PS[4]])
                    cx.op('act', lambda e: e.activation(out=KT[:, k0:k0 + nk], in_=ps[4][:, :nk], func=AF.Copy), r=[PS[4]], w=['KT'])
                    cx.op('act', lambda e: e.activation(out=ksq[:, :nk], in_=ps[4][:, :nk], func=AF.Square), r=[PS[4]], w=['ksq'])
                    cx.op('pe', lambda e: e.matmul(ps[5][:, :nk], ones_f[:], ksq[:, :nk], start=True, stop=False),
                          r=['ksq', 'ones_f'], w=[PS[5]])
                    cx.op('act', lambda e: e.activation(out=ksq[0:64, :nk], in_=KR[0:64, k0:k0 + nk], func=AF.Square), r=['KR', 'ksq'], w=['ksq'])
                    cx.op('pe', lambda e: e.matmul(ps[5][:, :nk], ones_f[0:64, :], ksq[0:64, :nk], start=False, stop=True),
                          r=['ksq', 'ones_f'], w=[PS[5]])
                    cx.op('dve', lambda e: e.tensor_reduce(kmax[:, 1:2], ps[5][:, :nk], AX.X, ALU.max), r=[PS[5]], w=['kmax'])
                    cx.op('dve', lambda e: e.tensor_tensor(kmax[:, 0:1], kmax[:, 0:1], kmax[:, 1:2], ALU.max), r=['kmax'], w=['kmax'])
                    for a0 in range(0, nk, 128):
                        an = min(128, nk - a0)
                        ti = (k0 + a0) // 128
                        for c in range(2):
                            cx.op('pe', lambda e: e.matmul(ps[6][:an, 0:128], ckb[:, c, a0:a0 + an],
                                                           wkvb[:, c, hd * 256 + 128:hd * 256 + 256], start=(c == 0), stop=(c == 1)),
                                  r=['ckb', 'wkvb'], w=[PS[6]])
                        cx.op('dve', lambda e: e.tensor_copy(VT[:an, ti, :], ps[6][:an, 0:128]), r=[PS[6]], w=['VT'])
                for (q0, nq) in (s.tiles if M2_SUB >= 3 else []):
                    sl = kq % 2
                    kq += 1
                    QA, OD, PTn = 'qan%d' % sl, 'od%d' % sl, None
                    cx.dma('sp', qan[:, sl, :, :nq], RT(s.qanT)[:, :, q0:q0 + nq], r=['qanT%d' % s.i], w=[QA])
                    cx.dma('sp', rope[:, 0, :nq], s.ropeC[:, q0:q0 + nq], w=['rope'])
                    cx.dma('sp', rope[:, 1, :nq], s.ropeS[:, q0:q0 + nq], w=['rope'])
                    for c in range(3):
                        cx.op('pe', lambda e: e.matmul(ps[4][:, :nq], wqb[:, c, hd * 256:hd * 256 + 128], qan[:, sl, c, :nq],
                                                       start=(c == 0), stop=(c == 2)), r=['wqb', QA], w=[PS[4]])
                    cx.op('act', lambda e: e.activation(out=Qf[:, :nq], in_=ps[4][:, :nq], func=AF.Copy, scale=MLA_SCALE), r=[PS[4]], w=['Qf'])
                    cx.op('dve', lambda e: e.tensor_copy(Qn[:, :nq], Qf[:, :nq]), r=['Qf'], w=['Qn'])
                    for c in range(3):
                        cx.op('pe', lambda e: e.matmul(ps[5][0:64, :nq], wqb[:, c, hd * 256 + 128:hd * 256 + 192], qan[:, sl, c, :nq],
                                                       start=(c == 0), stop=(c == 2)), r=['wqb', QA], w=[PS[5]])
                    for c in range(3):
                        cx.op('pe', lambda e: e.matmul(ps[6][0:64, :nq], wqb[:, c, hd * 256 + 192:hd * 256 + 256], qan[:, sl, c, :nq],
                                                       start=(c == 0), stop=(c == 2)), r=['wqb', QA], w=[PS[6]])
                    cx.op('dve', lambda e: e.tensor_tensor(qr1[:, :nq], ps[5][0:64, :nq], rope[:, 0, :nq], ALU.mult), r=[PS[5], 'rope'], w=['qr1'])
                    cx.op('dve', lambda e: e.tensor_tensor(qr2[:, :nq], ps[6][0:64, :nq], rope[:, 1, :nq], ALU.mult), r=[PS[6], 'rope'], w=['qr2'])
                    cx.op('dve', lambda e: e.scalar_tensor_tensor(qr1[:, :nq], qr1[:, :nq], 1.0, qr2[:, :nq], ALU.mult, ALU.add),
                          r=['qr1', 'qr2'], w=['qr1'])
                    cx.op('act', lambda e: e.activation(out=qr1[:, :nq], in_=qr1[:, :nq], func=AF.Copy, scale=MLA_SCALE), r=['qr1'], w=['qr1'])
                    cx.op('dve', lambda e: e.tensor_copy(Qr[0:64, :nq], qr1[:, :nq]), r=['qr1'], w=['Qr'])
                    if M2_SUB == 30:
                        continue
                    cx.op('act', lambda e: e.activation(out=Qf[:, :nq], in_=Qf[:, :nq], func=AF.Square), r=['Qf'], w=['Qf'])
                    cx.op('act', lambda e: e.activation(out=qr2[:, :nq], in_=qr1[:, :nq], func=AF.Square), r=['qr1'], w=['qr2'])
                    cx.op('pe', lambda e: e.matmul(ps[7][0:1, :nq], ones_f[:, 0:1], Qf[:, :nq], start=True, stop=False), r=['Qf', 'ones_f'], w=[PS[7]])
                    cx.op('pe', lambda e: e.matmul(ps[7][0:1, :nq], ones_f[0:64, 0:1], qr2[:, :nq], start=False, stop=True), r=['qr2', 'ones_f'], w=[PS[7]])
                    cx.op('dve', lambda e: e.tensor_scalar(rs[0:1, :nq], ps[7][0:1, :nq], kmax[0:1, 0:1], None, ALU.mult),
                          r=[PS[7], 'kmax'], w=['rs'])
                    cx.op('act', lambda e: e.activation(out=rs[0:1, :nq], in_=rs[0:1, :nq], func=AF.Sqrt), r=['rs'], w=['rs'])
                    cx.op('dve', lambda e: e.tensor_scalar(negm[0:1, :nq], rs[0:1, :nq], -1.0, None, ALU.mult), r=['rs'], w=['negm'])
                    cx.dma('sp', Qr[64:65, :nq], negm[0:1, :nq], r=['negm'], w=['Qr'])
                    if M2_SUB == 31:
                        continue
                    if chunked:
                        qb = q0 // 512
                        klist = [(kt, 0, False) for kt in range(4 * qb)] + [(4 * qb + j, 128 * j, True) for j in range((nq + 127) // 128)]
                    else:
                        klist = [(kt, 0, False) for kt in range((Tall + 127) // 128)]
                    if M2_SUB < 4 or M2_SUB in (32, 33, 34, 35):
                        klist = klist[:1]
                    nkl = len(klist)
                    def emit_scores(ki):
                        kt, qlo, diag = klist[ki]
                        kn = min(128, Tall - kt * 128)
                        pn = (0, 1, 4)[ki % 3]
                        ksl = slice(kt * 128, kt * 128 + kn)
                        cx.op('pe', lambda e: e.matmul(ps[pn][:kn, qlo:nq], KT[:, ksl], Qn[:, qlo:nq], start=True, stop=False),
                              r=['KT', 'Qn'], w=[PS[pn]])
                        cx.op('pe', lambda e: e.matmul(ps[pn][:kn, qlo:nq], KR[0:65, ksl], Qr[0:65, qlo:nq], start=False, stop=True),
                              r=['KR', 'Qr'], w=[PS[pn]])

                    emit_scores(0)
                    if nkl > 1:
                        emit_scores(1)
                    for ki, (kt, qlo, diag) in enumerate(klist):
                        kn = min(128, Tall - kt * 128)
                        pn = (0, 1, 4)[ki % 3]
                        pt = ki % 3
                        PTn = 'Pt%d' % pt
                        if ki + 2 < nkl:
                            emit_scores(ki + 2)
                        cx.op('act', lambda e: e.activation(out=Pt[:kn, pt, qlo:nq], in_=ps[pn][:kn, qlo:nq], func=AF.Exp),
                              r=[PS[pn]], w=[PTn])
                        first, last = (ki == 0), (ki == nkl - 1)
                        if not diag:
                            parts = [(0, kn, qlo, nq)]
                        else:
                            parts = [(0, 64, qlo, min(qlo + 64, nq))]
                            if qlo + 64 < nq:
                                parts.append((0, 128, qlo + 64, nq))
                        for pi, (r0, r1, ca, cb) in enumerate(parts):
                            lastp = last and (pi == len(parts) - 1)
                            cx.op('pe', lambda e: e.matmul(ps[2][:, ca:cb], VT[r0:r1, kt, :], Pt[r0:r1, pt, ca:cb],
                                                           start=first, stop=lastp), r=['VT', PTn], w=[PS[2]])
                            cx.op('pe', lambda e: e.matmul(ps[3][:, ca:cb], onesb[r0:r1, :], Pt[r0:r1, pt, ca:cb],
                                                           start=first, stop=lastp), r=['onesb', PTn], w=[PS[3]])
                    if M2_SUB in (32, 33, 34, 35):
                        continue
                    cx.op('dve', lambda e: e.reciprocal(rs[:, :nq], ps[3][:, :nq]), r=[PS[3]], w=['rs'])
                    cx.op('dve', lambda e: e.tensor_tensor(od[:, sl, :nq], ps[2][:, :nq], rs[:, :nq], ALU.mult), r=[PS[2], 'rs'], w=[OD])
                    cx.dma('pool', s.oT[512 + hd * 128:512 + (hd + 1) * 128, q0:q0 + nq], od[:, sl, :nq], r=[OD], w=['oT%d' % s.i])
        cx.barrier()


def _pc(v, nchunk):
    sh = v.shape[:-1]
    return np.ascontiguousarray(np.moveaxis(v.reshape(sh + (nchunk, 128)), -1, -2))


def make_consts():
    c = np.zeros((128, 13, 128), np.float32)
    c[:, 0, :] = np.eye(128, dtype=np.float32)
    s_ = np.arange(128)[:, None]
    t_ = np.arange(128)[None, :]
    c[:, 1, :] = ((s_ // 64 == t_ // 64) & (s_ <= t_)).astype(np.float32)
    c[:, 2, :] = (s_ <= t_).astype(np.float32)
    c[:, 3, :] = (s_ < t_).astype(np.float32)
    c[:, 4, :] = ((s_ // 64 == t_ // 64) & (s_ < t_)).astype(np.float32)
    for h in range(4):
        c[64 + h, 5 + h, :] = 1.0
        c[h, 9 + h, :] = 1.0
    return c


def rope_tables(past, T):
    half = 32
    freqs = np.exp(np.float32(-math.log(10000.0)) * np.arange(half, dtype=np.float32) / np.float32(half)).astype(np.float32)
    pos = (past + np.arange(T)).astype(np.float32)
    ang = (pos[:, None] * freqs[None, :]).astype(np.float32)
    cos = np.cos(ang).astype(np.float32).T
    sin = np.sin(ang).astype(np.float32).T
    return (np.ascontiguousarray(np.concatenate([cos, cos], 0)),
            np.ascontiguousarray(np.concatenate([-sin, sin], 0)))


def make_core_inputs(inp, xs, cs, sidx, depth):
    nab = (depth + 1) // 2
    im = {}
    nseq = len(xs)
    for i in range(nseq):
        im["xT%d" % i] = np.ascontiguousarray(xs[i].T)
        if sidx[i] is not None:
            b = sidx[i]
            im["ffnc_i%d" % i] = np.ascontiguousarray(
                inp['state_ffn_conv'][:depth, b].reshape(depth, 2, 44, 128).transpose(0, 3, 2, 1))
            im["hg_i%d" % i] = np.ascontiguousarray(inp['state_hgrn'][:nab, b].transpose(0, 2, 1, 3))
            im["lru_i%d" % i] = _pc(inp['state_rglru'][:nab, b], 4)
            im["lruc_i%d" % i] = np.ascontiguousarray(
                inp['state_rglru_conv'][:nab, b].reshape(nab, 3, 4, 128).transpose(0, 3, 2, 1))
    im["cT"] = np.ascontiguousarray(np.asarray(cs).reshape(nseq, 8, 128).transpose(2, 1, 0))
    im["ada_w"] = np.ascontiguousarray(inp['ada_w'][:depth])
    im["ada_bT"] = _pc(inp['ada_b'][:depth], 48)
    im["norm_gT"] = np.ascontiguousarray(inp['norm_g'][:depth].reshape(depth, 4, 8, 128).transpose(0, 3, 1, 2))
    wo = np.zeros((depth, D, D), np.float32)
    for l in range(depth):
        wo[l] = inp['ab_w_out'][l // 2] if l % 2 == 0 else inp['cd_w_out'][l // 2]
    im["w_out"] = wo
    im["ffn_up"] = np.ascontiguousarray(inp['ffn_w_up'][:depth])
    im["ffn_cw"] = np.ascontiguousarray(inp['ffn_conv_w'][:depth].reshape(depth, 3, 44, 128).transpose(0, 3, 2, 1))
    im["ffn_dn"] = np.ascontiguousarray(inp['ffn_w_down'][:depth])
    im["consts"] = make_consts()
    im["ab_w_in"] = np.ascontiguousarray(inp['ab_w_in'][:nab])
    im["hg_lb"] = np.ascontiguousarray(inp['hgrn_lb_logits'].reshape(2, 4, 128).transpose(2, 0, 1))
    im["hg_g"] = np.ascontiguousarray(inp['hgrn_norm_g'][:nab].reshape(nab, 128, 1))
    im["lru_cw"] = np.ascontiguousarray(inp['lru_conv_w'][:nab].reshape(nab, 4, 4, 128).transpose(0, 3, 2, 1))
    vec = np.stack([inp['lru_conv_b'][:nab], inp['lru_b_a'][:nab], inp['lru_b_x'][:nab], inp['lru_lambda'][:nab]], -1)
    im["lru_vec"] = np.ascontiguousarray(vec.reshape(nab, 4, 128, 4).transpose(0, 2, 1, 3))
    gw = np.zeros((nab, 128, 2, 4, 128), np.float32)
    for k, nm in enumerate(('lru_w_a', 'lru_w_x')):
        w = inp[nm][:nab]
        for ch in range(4):
            for bb in range(2):
                gw[:, bb * 64:(bb + 1) * 64, k, ch, bb * 64:(bb + 1) * 64] = w[:, 2 * ch + bb]
    im["lru_gw"] = gw
    ncd = depth // 2
    if ncd > 0:
        src = inp['cd_w_in'][:ncd]
        w = np.zeros((ncd, D, 3072), np.float32)
        w[:, :, 0:2048] = src[:, :, 0:2048]
        w[:, :, 2048:2432] = src[:, :, 2056:2440]
        w[:, :, 2432:2688] = src[:, :, 2440:2696]
        w[:, :, 2688:2752] = src[:, :, 2696:2760]
        w[:, :, 2752:2756] = src[:, :, 2048:2052]
        w[:, :, 2816:2820] = src[:, :, 2052:2056]
        w[:, :, 2944:2976] = src[:, :, 2728:2760]
        w[:, :, 2976:3008] = src[:, :, 2696:2728]
        im["cd_w_in"] = w
        im["gd_cw"] = np.ascontiguousarray(inp['gdn_conv_w'][:ncd].reshape(ncd, 4, 12, 128).transpose(0, 3, 2, 1))
        gv = np.stack([inp['gdn_a_log'][:ncd], inp['gdn_dt_bias'][:ncd]], -1)
        im["gd_vec"] = np.ascontiguousarray(np.broadcast_to(gv[:, None], (ncd, 128, 4, 2)))
        im["gd_g"] = np.ascontiguousarray(inp['gdn_norm_g'][:ncd].reshape(ncd, 128, 1))
        im["q_g"] = _pc(inp['mla_q_norm_g'][:ncd], 3)
        im["kv_g"] = _pc(inp['mla_kv_norm_g'][:ncd], 2)
        wq = inp['mla_w_qb'][:ncd].reshape(ncd, 384, 4, 192)
        wqe = np.zeros((ncd, 384, 4, 256), np.float32)
        wqe[..., 0:192] = wq
        wqe[..., 192:224] = wq[..., 160:192]
        wqe[..., 224:256] = wq[..., 128:160]
        im["w_qb"] = np.ascontiguousarray(wqe.reshape(ncd, 384, 1024))
        im["w_kvb"] = np.ascontiguousarray(inp['mla_w_kvb'][:ncd])
        for i in range(nseq):
            T_ = xs[i].shape[0]
            past = 0 if sidx[i] is None else PAST
            im["ropeC%d" % i], im["ropeS%d" % i] = rope_tables(past, T_)
            if sidx[i] is not None:
                b = sidx[i]
                im["gd_i%d" % i] = np.ascontiguousarray(inp['state_gdn'][:ncd, b].transpose(0, 2, 1, 3))
                im["gdc_i%d" % i] = np.ascontiguousarray(
                    inp['state_gdn_conv'][:ncd, b].reshape(ncd, 3, 12, 128).transpose(0, 3, 2, 1))
                im["lat_iT%d" % i] = np.ascontiguousarray(inp['cache_mla_latent'][:ncd, b].transpose(0, 2, 1))
                im["kr_iT%d" % i] = np.ascontiguousarray(inp['cache_mla_krope'][:ncd, b].transpose(0, 2, 1))
    return im


_PROG = {}


def kernel(**inputs):
    inp = {k: np.asarray(v) for k, v in inputs.items()}
    xp, xsm = inp['x_prompt'], inp['x_sample']
    B, SEQ, _ = xp.shape
    NB, DT, _ = xsm.shape
    NS = NB // NCORES
    depth = DEPTH
    nab, ncd = (depth + 1) // 2, depth // 2
    key = (SEQ, DT, NS, depth)
    if key not in _PROG:
        _PROG[key] = build_program(SEQ, DT, NS, depth=depth)
    nc = _PROG[key]
    in_maps = []
    shared = None
    for c in range(NCORES):
        b = c * B // NCORES
        sidx = [None] + [c * NS + j for j in range(NS)]
        xs = [xp[b]] + [xsm[i] for i in sidx[1:]]
        cs = np.stack([inp['c_prompt'][b]] + [inp['c_sample'][i] for i in sidx[1:]])
        im = make_core_inputs(inp, xs, cs, sidx, depth)
        if shared is None:
            shared = im
        else:
            for k in ('ada_w', 'w_out', 'ffn_up', 'ffn_dn', 'ab_w_in', 'consts', 'cd_w_in', 'w_qb', 'w_kvb', 'ropeC0', 'ropeS0'):
                im[k] = shared[k]
        in_maps.append(im)
    res = run_bass_kernel_spmd(nc, in_maps, core_ids=list(range(NCORES))).results
    pc = [b * NCORES // B for b in range(B)]

    def un_ffnc(a):
        return a.transpose(0, 3, 2, 1).reshape(depth, 2, 2 * DFF)

    def un_hg(a):
        return a.transpose(0, 2, 1, 3)

    def un_lru(a):
        return a.transpose(0, 2, 1).reshape(nab, HALF)

    def un_gdc(a):
        return a.transpose(0, 3, 2, 1).reshape(ncd, 3, 3 * HALF)

    def un_lruc(a):
        return a.transpose(0, 3, 2, 1).reshape(nab, 3, HALF)

    def gather(fn, name, prompt):
        if prompt:
            return np.ascontiguousarray(np.stack([fn(res[pc[b]][name + "0"]) for b in range(B)], axis=1))
        return np.ascontiguousarray(np.stack(
            [fn(res[i // NS][name + str(1 + i % NS)]) for i in range(NB)], axis=1))

    y_prompt = np.ascontiguousarray(np.stack([res[pc[b]]["yT0"].T for b in range(B)]))
    y_sample = np.ascontiguousarray(np.stack([res[i // NS]["yT%d" % (1 + i % NS)].T for i in range(NB)]))
    outs = [y_prompt, y_sample]
    for prompt in (True, False):
        nb = B if prompt else NB
        tt = SEQ if prompt else DT
        outs += [
            gather(un_hg, "hg_o", prompt), gather(un_lru, "lru_o", prompt), gather(un_lruc, "lruc_o", prompt),
            gather(un_hg, "gd_o", prompt), gather(un_gdc, "gdc_o", prompt),
            gather(lambda a: a, "lat_o", prompt), gather(lambda a: a, "kr_o", prompt),
            gather(un_ffnc, "ffnc_o", prompt),
        ]
    return tuple(outs)
```
